# Optimizing a Trainium2 kernel written in Bass

```python
import jax, jax.numpy as jnp
from jax import lax
import numpy as np


D_MODEL = 1024
BATCH = 4
SEQ = 4096
DEPTH = 4

D_MIX = D_MODEL
D_CONV = D_MIX // 4
D_CONF = D_MIX // 4
D_DN = D_MIX // 2
N_CONV_GROUPS = 4
N_CONF_GROUPS = 4
DN_HEADS = 4
DN_HEAD_DIM = D_DN // DN_HEADS
SHORT_CONV_W = 3
CONF_CONV_W = 31
DN_CONV_W = 4
DN_CHUNK = 64
D_FF = ((8 * D_MODEL + 3 * 256 - 1) // (3 * 256)) * 256
IN_COLS = 3 * D_CONV + 2 * D_CONF + 4 * D_DN + 2 * DN_HEADS
N_MOD = 6
EPS = 1e-6

kernel_name = 'hymba_conv_conformer_gdn_adaln_trunk'


def rmsnorm(x, g):
    xf = x.astype(jnp.float32)
    y = xf * lax.rsqrt(jnp.mean(xf * xf, axis=-1, keepdims=True) + EPS)
    return y.astype(x.dtype) * g


def layernorm(x, g, b):
    xf = x.astype(jnp.float32)
    mu = jnp.mean(xf, axis=-1, keepdims=True)
    var = jnp.mean(jnp.square(xf - mu), axis=-1, keepdims=True)
    return ((xf - mu) * lax.rsqrt(var + 1e-5)).astype(x.dtype) * g + b


def causal_dwconv(x, w):
    K, C = w.shape
    xp = jnp.pad(x, ((0, 0), (K - 1, 0), (0, 0)))
    return lax.conv_general_dilated(xp, w[:, None, :].astype(x.dtype), window_strides=(1,), padding='VALID',
                                    dimension_numbers=('NWC', 'WIO', 'NWC'), feature_group_count=C)


def l2norm(x):
    return x * lax.rsqrt(jnp.sum(x * x, axis=-1, keepdims=True) + EPS)


def chunk_gated_delta_rule(q, k, v, g, beta):
    Bsz, T, H, Dk = q.shape
    C = DN_CHUNK
    N = T // C
    def to_chunks(t):
        return t.reshape(Bsz, N, C, H, -1).transpose(0, 3, 1, 2, 4)
    q = to_chunks(q) * (Dk ** -0.5)
    k = to_chunks(k)
    v = to_chunks(v)
    beta = beta.reshape(Bsz, N, C, H).transpose(0, 3, 1, 2)
    g = jnp.cumsum(g.reshape(Bsz, N, C, H).transpose(0, 3, 1, 2), axis=-1)
    causal = jnp.tril(jnp.ones((C, C), dtype=bool))
    strict = jnp.tril(jnp.ones((C, C), dtype=bool), -1)
    diff = g[..., :, None] - g[..., None, :]
    decay = jnp.where(causal, jnp.exp(jnp.where(causal, diff, 0.0)), 0.0)
    k_beta = k * beta[..., None]
    v_beta = v * beta[..., None]
    Lm = jnp.where(strict, jnp.einsum('bhncd,bhnsd->bhncs', k_beta, k) * decay, 0.0)
    eye = jnp.eye(C, dtype=q.dtype)
    Tm = lax.linalg.triangular_solve(eye + Lm, jnp.broadcast_to(eye, Lm.shape), left_side=True,
                                     lower=True, unit_diagonal=True)
    u = jnp.einsum('bhncs,bhnse->bhnce', Tm, v_beta)
    w = jnp.einsum('bhncs,bhnsd->bhncd', Tm, k_beta * jnp.exp(g)[..., None])
    qk = jnp.where(causal, jnp.einsum('bhncd,bhnsd->bhncs', q, k) * decay, 0.0)

    def step(S, inp):
        q_i, k_i, u_i, w_i, g_i, qk_i = inp
        v_new = u_i - jnp.einsum('bhcd,bhde->bhce', w_i, S)
        o = (jnp.einsum('bhcd,bhde->bhce', q_i * jnp.exp(g_i)[..., None], S)
             + jnp.einsum('bhcs,bhse->bhce', qk_i, v_new))
        g_last = g_i[..., -1]
        S = (S * jnp.exp(g_last)[..., None, None]
             + jnp.einsum('bhcd,bhce->bhde', k_i * jnp.exp(g_last[..., None] - g_i)[..., None], v_new))
        return S, o

    xs = tuple(jnp.moveaxis(t, 2, 0) for t in (q, k, u, w, g, qk))
    S0 = jnp.zeros((Bsz, H, Dk, v.shape[-1]), jnp.float32)
    _, o = lax.scan(step, S0, xs)
    return o.transpose(1, 0, 3, 2, 4).reshape(Bsz, T, H, -1)


def setup_inputs(seed: int = 0) -> dict:
    key = jax.random.key(seed)
    ks = jax.random.split(key, 24)
    f32 = jnp.float32
    def nrm(k, shape, scale):
        return jax.random.normal(k, shape, f32) * scale
    L, D = DEPTH, D_MODEL
    dt = jax.random.uniform(ks[13], (L, DN_HEADS), f32, minval=1e-3, maxval=0.1)
    return {
        'x': nrm(ks[0], (BATCH, SEQ, D), 1.0),
        'c': nrm(ks[1], (BATCH, D), 1.0),
        'w_ada': nrm(ks[2], (L, D, N_MOD * D), 0.5 * D ** -0.5),
        'b_ada': nrm(ks[3], (L, N_MOD * D), 0.02),
        'norm_mix_g': 1.0 + nrm(ks[4], (L, D), 0.02),
        'norm_ffn_g': 1.0 + nrm(ks[5], (L, D), 0.02),
        'w_in': nrm(ks[6], (L, D, IN_COLS), D ** -0.5),
        'conv_a_w': nrm(ks[7], (L, SHORT_CONV_W, D_CONV), SHORT_CONV_W ** -0.5),
        'conf_dw_w': nrm(ks[8], (L, CONF_CONV_W, D_CONF), CONF_CONV_W ** -0.5),
        'conf_dw_b': nrm(ks[9], (L, D_CONF), 0.02),
        'conf_ln_g': 1.0 + nrm(ks[10], (L, D_CONF), 0.02),
        'conf_ln_b': nrm(ks[11], (L, D_CONF), 0.02),
        'dn_conv_w': nrm(ks[12], (L, DN_CONV_W, 3 * D_DN), DN_CONV_W ** -0.5),
        'dn_a_log': jnp.log(jax.random.uniform(ks[14], (L, DN_HEADS), f32, minval=1.0, maxval=16.0)),
        'dn_dt_bias': dt + jnp.log(-jnp.expm1(-dt)),
        'dn_norm_g': 1.0 + nrm(ks[15], (L, DN_HEAD_DIM), 0.02),
        'w_out': nrm(ks[16], (L, D_MIX, D), D_MIX ** -0.5),
        'w_ffn_in': nrm(ks[17], (L, D, 2 * D_FF), D ** -0.5),
        'w_ffn_out': nrm(ks[18], (L, D_FF, D), D_FF ** -0.5),
        'final_norm_g': 1.0 + nrm(ks[19], (D,), 0.02),
    }


def reference(x, c, w_ada, b_ada, norm_mix_g, norm_ffn_g, w_in, conv_a_w, conf_dw_w, conf_dw_b,
              conf_ln_g, conf_ln_b, dn_conv_w, dn_a_log, dn_dt_bias, dn_norm_g, w_out,
              w_ffn_in, w_ffn_out, final_norm_g):
    Bsz, T, _ = x.shape
    c_act = jax.nn.silu(c)
    sizes = [D_CONV] * 3 + [D_CONF] * 2 + [D_DN] * 4 + [DN_HEADS] * 2
    split_idx = np.cumsum(sizes)[:-1].tolist()
    for l in range(DEPTH):
        mod = (c_act @ w_ada[l] + b_ada[l])[:, None, :]
        sh1, sc1, g1, sh2, sc2, g2 = jnp.split(mod, N_MOD, axis=-1)

        h = rmsnorm(x, norm_mix_g[l]) * (1.0 + sc1) + sh1
        proj = h @ w_in[l]
        (a_b, a_c, a_v, b_a, b_g, c_q, c_k, c_v, c_z, c_alpha, c_beta) = jnp.split(proj, split_idx, axis=-1)

        y_a = a_b * causal_dwconv(a_c * a_v, conv_a_w[l])

        u = b_a * jax.nn.sigmoid(b_g)
        u = causal_dwconv(u, conf_dw_w[l]) + conf_dw_b[l]
        y_b = jax.nn.silu(layernorm(u, conf_ln_g[l], conf_ln_b[l]))

        qkv = jax.nn.silu(causal_dwconv(jnp.concatenate([c_q, c_k, c_v], axis=-1), dn_conv_w[l]))
        q, k, v = jnp.split(qkv.astype(jnp.float32), 3, axis=-1)
        q = l2norm(q.reshape(Bsz, T, DN_HEADS, DN_HEAD_DIM))
        k = l2norm(k.reshape(Bsz, T, DN_HEADS, DN_HEAD_DIM))
        v = v.reshape(Bsz, T, DN_HEADS, DN_HEAD_DIM)
        gdec = -jnp.exp(dn_a_log[l].astype(jnp.float32)) * jax.nn.softplus(
            c_alpha.astype(jnp.float32) + dn_dt_bias[l].astype(jnp.float32))
        beta = jax.nn.sigmoid(c_beta.astype(jnp.float32))
        o = chunk_gated_delta_rule(q, k, v, gdec, beta).astype(x.dtype)
        z = c_z.reshape(Bsz, T, DN_HEADS, DN_HEAD_DIM)
        y_c = (rmsnorm(o, dn_norm_g[l]) * jax.nn.silu(z)).reshape(Bsz, T, D_DN)

        mix = jnp.concatenate([y_a, y_b, y_c], axis=-1) @ w_out[l]
        x = x + g1 * mix

        h = rmsnorm(x, norm_ffn_g[l]) * (1.0 + sc2) + sh2
        gate, up = jnp.split(h @ w_ffn_in[l], 2, axis=-1)
        x = x + g2 * ((jax.nn.silu(gate) * up) @ w_ffn_out[l])
    return rmsnorm(x, final_norm_g)
```

```python
from contextlib import ExitStack

import numpy as np
import concourse.bass as bass
import concourse.mybir as mybir
from concourse.bass_utils import run_bass_kernel_spmd

F32 = mybir.dt.float32
BF16 = mybir.dt.bfloat16
AF = mybir.ActivationFunctionType
ALU = mybir.AluOpType


class Buf:
    __slots__ = ("name", "w", "r", "excl")

    def __init__(self, name="", excl=False):
        self.name = name
        self.w = None
        self.r = []
        self.excl = excl


class _Op:
    __slots__ = ("eng", "fn", "deps", "is_dma", "sem", "val", "signal", "emitted")


class Sched:
    CENG = ("pe", "act", "dve", "pool")
    ENG = ("pe", "act", "dve", "pool", "sp")
    DMAQ = ("sp", "pool", "act")

    def __init__(self, nc, strict=True, dma_k=6):
        self.nc = nc
        self.strict = strict
        self.K = dma_k
        self.es = ExitStack()
        self.csem = {e: self.es.enter_context(nc.semaphore("c_" + e)) for e in self.CENG}
        self.ccount = {e: 0 for e in self.CENG}
        self.dsem = {q: [self.es.enter_context(nc.semaphore("d_%s%d" % (q, i))) for i in range(dma_k)]
                     for q in self.DMAQ}
        self.dcount = {q: 0 for q in self.DMAQ}
        self.dhist = {q: [] for q in self.DMAQ}
        self.known = {e: {} for e in self.ENG}
        self.pending = {e: [] for e in self.ENG}
        self.last = {e: None for e in self.CENG}
        self.barrier_ops = []
        self.nops = 0
        self.muted = False

    def _mk(self, eng, fn, reads, writes, is_dma):
        op = _Op()
        op.eng = eng
        op.fn = fn
        op.is_dma = is_dma
        op.signal = is_dma
        op.sem = None
        op.val = None
        op.emitted = False
        deps = list(self.barrier_ops)
        for b in reads:
            if b.w is not None:
                deps.append(b.w)
            if b.excl:
                deps.extend(r for r in b.r if r.eng != eng)
        for b in writes:
            if b.w is not None:
                deps.append(b.w)
            deps.extend(b.r)
        op.deps = deps
        for b in reads:
            b.r.append(op)
        for b in writes:
            b.w = op
            b.r = []
        self.pending[eng].append(op)
        self.nops += 1
        return op

    def add(self, eng, fn, reads=(), writes=()):
        if self.muted:
            return None
        op = self._mk(eng, fn, reads, writes, False)
        self.last[eng] = op
        return op

    def dma(self, q, out, in_, reads=(), writes=(), **kw):
        if self.muted:
            return None
        def fn(e, out=out, in_=in_, kw=kw):
            return e.dma_start(out=out, in_=in_, **kw)
        op = self._mk(q, fn, reads, writes, True)
        i = self.dcount[q]
        self.dcount[q] += 1
        op.sem = self.dsem[q][i % self.K]
        op.val = 16 * (i // self.K + 1)
        if i >= self.K:
            op.deps.append(self.dhist[q][i - self.K])
        self.dhist[q].append(op)
        return op

    def barrier(self):
        if self.muted:
            return
        ops = [self.last[e] for e in self.CENG if self.last[e] is not None]
        for q in self.DMAQ:
            ops.extend(self.dhist[q][-self.K:])
        self.barrier_ops = ops

    def _need(self, op, dep):
        if dep.is_dma:
            return True
        if dep.eng != op.eng:
            return True
        if op.eng == "pe":
            return False
        if op.is_dma:
            return True
        return self.strict

    def flush(self, final=False):
        nc = self.nc
        if not final and not any(self.pending[e] for e in self.ENG):
            return
        for e in self.ENG:
            for op in self.pending[e]:
                for d in op.deps:
                    if not d.is_dma and not d.emitted and self._need(op, d):
                        d.signal = True
        for e in self.CENG:
            comp = [o for o in self.pending[e] if not o.is_dma]
            if comp:
                comp[-1].signal = True
        for e in self.CENG:
            c = self.ccount[e]
            comp = [o for o in self.pending[e] if not o.is_dma]
            for o in comp:
                if o.signal:
                    c += 1
                    o.val = c
                o.sem = self.csem[e]
            self.ccount[e] = c
            nxt = None
            for o in reversed(comp):
                if o.signal:
                    nxt = o.val
                else:
                    o.val = nxt
        getter = {"pe": "tensor", "act": "scalar", "dve": "vector", "pool": "gpsimd", "sp": "sync"}

        def run(e, eng):
            known = self.known[e]
            for op in self.pending[e]:
                waits = {}
                for d in op.deps:
                    if not self._need(op, d):
                        continue
                    key = id(d.sem)
                    if known.get(key, 0) >= d.val:
                        continue
                    if key not in waits or waits[key][1] < d.val:
                        waits[key] = (d.sem, d.val)
                for key, (sem, val) in waits.items():
                    eng.wait_ge(sem, val)
                    known[key] = val
                inst = op.fn(eng)
                if op.signal:
                    inst.then_inc(op.sem, 16 if op.is_dma else 1)
                op.emitted = True
                op.fn = None
            if final and e == "sp":
                for q in self.DMAQ:
                    n = self.dcount[q]
                    for j in range(self.K):
                        cnt = len(range(j, n, self.K))
                        if cnt:
                            eng.wait_ge(self.dsem[q][j], 16 * cnt)

        with nc.Block() as blk:
            for e in self.ENG:
                if not self.pending[e] and not (final and e == "sp"):
                    continue
                getattr(blk, getter[e])(lambda eng, e=e: run(e, eng))
        self.pending = {e: [] for e in self.ENG}

    def close(self):
        self.es.close()


D = 1024
T = 4096
L = 4
NG = T // 512
NCH = T // 128
DFF = 2816
INC = 3336
EPS = 1e-6
NCORES = 8


class _Stop(Exception):
    pass


def _levels_masks():
    i = np.arange(128)
    s_, c_ = np.meshgrid(i, i, indexing="ij")
    out = {}
    out["ident"] = (s_ == c_).astype(np.float32)
    out["uincl"] = (s_ <= c_).astype(np.float32)
    out["masksl"] = (s_ > c_).astype(np.float32)
    out["negm"] = np.where(c_ < s_, -30000.0, 0.0).astype(np.float32)
    out["strictu"] = (s_ < c_).astype(np.float32)
    out["ones"] = np.ones((128, 128), np.float32)
    lev = []
    for b in (1, 2, 4, 8, 16, 32, 64):
        m = ((s_ // (2 * b)) == (c_ // (2 * b))) & ((s_ // b) % 2 == 0) & ((c_ // b) % 2 == 1)
        lev.append(m.astype(np.float32))
    out["levU"] = np.stack(lev)
    return out


def build_program(depth=L, stop=None):
    nc = bass.Bass("TRN2", target_bir_lowering=False)
    dt = nc.dram_tensor

    def din(name, shape, dtype=F32):
        return dt(name, list(shape), dtype, kind="ExternalInput").ap()

    xT_in = din("xT", [D, T])
    cT_in = din("cT", [128, 8])
    w_ada = din("w_ada", [depth, D, 6 * D])
    b_ada = din("b_ada", [depth, 6 * D])
    w_in = din("w_in", [depth, D, INC])
    w_out = din("w_out", [depth, D, D])
    w_fi = din("w_ffn_in", [depth, D, 2 * DFF])
    w_fo = din("w_ffn_out", [depth, DFF, D])
    gm_in = din("gm", [128, L * 8])
    gf_in = din("gf", [128, L * 8])
    gfin_in = din("gfin", [128, 8])
    caw_in = din("caw", [128, L * 2 * 3])
    cfw_in = din("cfw", [128, L * 2 * 31])
    cfb_in = din("cfb", [128, L * 2])
    cfg_in = din("cfg", [128, L * 2])
    cfbb_in = din("cfbb", [128, L * 2])
    dnw_in = din("dnw", [128, L * 12 * 4])
    alog_in = din("alog", [128, L * 128])
    dtb_in = din("dtb", [128, L * 128])
    gdn_in = din("gdn", [128, L * 128])
    cm_in = din("cmask", [128, 6 * 128])
    lev_in = din("levU4", [128, 7 * 512])
    levT_in = din("levL4", [128, 7 * 512])
    i4_in = din("ident4", [128, 512])
    su4_in = din("strictu4", [128, 512])
    outT = dt("outT", [D, T], F32, kind="ExternalOutput").ap()

    xs = dt("xs", [D, T], F32, kind="Internal").ap()
    ys = dt("ys", [D, T], BF16, kind="Internal").ap()
    dnin = dt("dnin", [128, NCH, 16, 128], BF16, kind="Internal").ap()

    import os as _os2
    s = Sched(nc, strict=(_os2.environ.get("MK_STRICT", "1") == "1"))
    top = ExitStack()

    _cnt = [0]

    def sb(es, name, shape, dtype=F32):
        _cnt[0] += 1
        return es.enter_context(nc.sbuf_tensor("%s_%d" % (name, _cnt[0]), list(shape), dtype))

    ps = [top.enter_context(nc.psum_tensor("ps%d" % i, [128, 512], F32)) for i in range(6)]
    ps6b = top.enter_context(nc.psum_tensor("ps6b", [128, 1024], BF16))
    ps7b = top.enter_context(nc.psum_tensor("ps7b", [128, 1024], BF16))
    bps = [Buf("ps%d" % i, excl=True) for i in range(6)]
    bps6, bps7 = Buf("ps6b", excl=True), Buf("ps7b", excl=True)

    modT = sb(top, "modT", [128, L, 48])
    A1 = sb(top, "A1", [128, L * 8]); A2 = sb(top, "A2", [128, L * 8])
    gm = sb(top, "gm_t", [128, L * 8]); gf = sb(top, "gf_t", [128, L * 8]); gfin = sb(top, "gfin_t", [128, 8])
    caw = sb(top, "caw_t", [128, L * 6]); cfw = sb(top, "cfw_t", [128, L * 62])
    cfb = sb(top, "cfb_t", [128, L * 2]); cfg = sb(top, "cfg_t", [128, L * 2]); cfbb = sb(top, "cfbb_t", [128, L * 2])
    dnw = sb(top, "dnw_t", [128, L * 48])
    alog = sb(top, "alog_t", [128, L * 128]); dtb = sb(top, "dtb_t", [128, L * 128]); gdn = sb(top, "gdn_t", [128, L * 128])
    cmF = sb(top, "cmF", [128, 6 * 128])
    identB = sb(top, "identB", [128, 128], BF16)
    onesB = sb(top, "onesB", [128, 128], BF16)
    ab_tok = sb(top, "ab_tok", [128, NCH * 8])
    bconst = Buf("const")
    bmod = Buf("mod")
    bab = Buf("abtok")
    identF = cmF[:, 0:128]; uincl = cmF[:, 128:256]; masksl = cmF[:, 256:384]
    negm = cmF[:, 384:512]; strictu = cmF[:, 512:640]; onesF = cmF[:, 640:768]

    for (t_, src) in ((gm, gm_in), (gf, gf_in), (gfin, gfin_in), (caw, caw_in), (cfw, cfw_in), (cfb, cfb_in),
                      (cfg, cfg_in), (cfbb, cfbb_in), (dnw, dnw_in), (alog, alog_in), (dtb, dtb_in),
                      (gdn, gdn_in), (cmF, cm_in)):
        s.dma("sp", t_[:], src, writes=[bconst])
    s.dma("pool", identB[:], cm_in[:, 0:128], writes=[bconst])
    s.dma("pool", onesB[:], cm_in[:, 640:768], writes=[bconst])
    bxs = Buf("xs")
    for j in range(8):
        s.dma("sp", xs[j * 128:(j + 1) * 128, :], xT_in[j * 128:(j + 1) * 128, :], writes=[bxs])

    with ExitStack() as es:
        cT = sb(es, "cT_t", [128, 8]); cact = sb(es, "cact", [128, 8])
        wt = [sb(es, "wada%d" % i, [128, 2048]) for i in range(6)]
        bwt = [Buf() for _ in range(6)]
        modrow = sb(es, "modrow", [1, 6 * D]); brow = sb(es, "brow", [1, 6 * D])
        one11 = sb(es, "one11", [1, 1])
        bc, bmr, bbr = Buf(), Buf(), Buf()
        s.dma("sp", cT[:], cT_in, writes=[bc])
        s.add("act", lambda e: e.activation(cact[:], cT[:], AF.Silu), reads=[bc], writes=[bc])
        s.add("dve", lambda e: e.memset(one11[:], 1.0), writes=[bc])
        it = 0
        for l in range(depth):
            s.dma("sp", brow[:], b_ada[l:l + 1, :], reads=[], writes=[bbr])
            for cg in range(3):
                for k in range(8):
                    w_ = wt[it % 6]; bw_ = bwt[it % 6]; it += 1
                    s.dma(("sp", "act", "pool")[it % 3], w_[:], w_ada[l, k * 128:(k + 1) * 128, cg * 2048:(cg + 1) * 2048],
                          writes=[bw_])
                    for i in range(4):
                        s.add("pe", lambda e, i=i, w_=w_, k=k: e.matmul(
                            ps[i][0:1, :], cact[:, k:k + 1], w_[:, i * 512:(i + 1) * 512],
                            start=(k == 0), stop=(k == 7)), reads=[bw_, bc], writes=[bps[i]])
                for i in range(4):
                    c0 = cg * 2048 + i * 512
                    s.add("dve", lambda e, i=i, c0=c0: e.tensor_tensor(
                        modrow[0:1, c0:c0 + 512], ps[i][0:1, :], brow[0:1, c0:c0 + 512], ALU.add),
                        reads=[bps[i], bbr], writes=[bmr])
            for j in range(48):
                s.add("pe", lambda e, j=j: e.matmul(ps[4][:, j:j + 1], modrow[0:1, j * 128:(j + 1) * 128],
                                                     one11[0:1, 0:1], start=True, stop=True),
                      reads=[bmr, bc], writes=[bps[4]])
            s.add("act", lambda e, l=l: e.activation(modT[:, l, :], ps[4][:, 0:48], AF.Copy),
                  reads=[bps[4]], writes=[bmod])
            s.add("dve", lambda e, l=l: e.scalar_tensor_tensor(
                A1[:, l * 8:(l + 1) * 8], modT[:, l, 8:16], 1.0, gm[:, l * 8:(l + 1) * 8], ALU.add, ALU.mult),
                reads=[bmod, bconst], writes=[bmod])
            s.add("dve", lambda e, l=l: e.scalar_tensor_tensor(
                A2[:, l * 8:(l + 1) * 8], modT[:, l, 32:40], 1.0, gf[:, l * 8:(l + 1) * 8], ALU.add, ALU.mult),
                reads=[bmod, bconst], writes=[bmod])
        s.barrier()
        s.flush()

    def B1(l, j): return modT[:, l, j:j + 1]
    def G1(l, j): return modT[:, l, 16 + j:17 + j]
    def B2(l, j): return modT[:, l, 24 + j:25 + j]
    def G2(l, j): return modT[:, l, 40 + j:41 + j]

    xs_v = xs.rearrange("(j p) t -> p j t", p=128)
    ys_v = ys.rearrange("(j p) t -> p j t", p=128)
    out_v = outT.rearrange("(j p) t -> p j t", p=128)

    def norm_group(es_tiles, l, g, Acol, Bcol, hdst, hbuf, xt, bxt, load=True, pn=5):
        sq, bsq, rs, brs = es_tiles
        sl = slice(g * 512, (g + 1) * 512)
        if load:
            s.dma("sp", xt[:], xs_v[:, :, sl], reads=[bxs], writes=[bxt])
        s.add("act", lambda e: e.activation(sq[:], xt[:], AF.Square), reads=[bxt], writes=[bsq])
        for j in range(8):
            s.add("pe", lambda e, j=j: e.matmul(ps[pn][:], onesB[:], sq[:, j, :], start=(j == 0), stop=(j == 7)),
                  reads=[bsq, bconst], writes=[bps[pn]])
        s.add("act", lambda e: e.activation(rs[:], ps[pn][:], AF.Sqrt, bias=EPS, scale=1.0 / D),
              reads=[bps[pn]], writes=[brs])
        s.add("dve", lambda e: e.reciprocal(rs[:], rs[:]), reads=[brs], writes=[brs])
        return sl

    def ck(name):
        if stop == name:
            s.muted = True
    ck('pro')
    for l in range(depth):
        with ExitStack() as es:
            hT = sb(es, "hT", [128, 8, T], BF16); bh = Buf("hT")
            with ExitStack() as es2:
                xt = [sb(es2, "xt%d" % i, [128, 8, 512]) for i in range(2)]; bxt = [Buf(), Buf()]
                sqs = [sb(es2, "sq", [128, 8, 512], BF16) for _ in range(2)]; bsqs = [Buf(), Buf()]
                rss = [sb(es2, "rs", [128, 512]) for _ in range(2)]; brss = [Buf(), Buf()]
                for g in range(NG):
                    x_ = xt[g % 2]; bx_ = bxt[g % 2]
                    sq = sqs[g % 2]; bsq = bsqs[g % 2]; rs = rss[g % 2]; brs = brss[g % 2]
                    sl = norm_group((sq, bsq, rs, brs), l, g, None, None, None, None, x_, bx_, pn=4 + g % 2)
                    for j in range(8):
                        s.add("dve", lambda e, j=j, x_=x_, rs=rs: e.scalar_tensor_tensor(
                            x_[:, j, :], x_[:, j, :], A1[:, l * 8 + j:l * 8 + j + 1], rs[:], ALU.mult, ALU.mult),
                            reads=[bx_, brs, bmod], writes=[bx_])
                        s.add("act", lambda e, j=j, x_=x_, sl=sl: e.activation(
                            hT[:, j, sl], x_[:, j, :], AF.Identity, bias=B1(l, j), scale=1.0),
                            reads=[bx_, bmod], writes=[bh])
                s.barrier(); s.flush()
            ck('n1')

            wv = w_in[l].rearrange("(k p) c -> p k c", p=128)

            def proj(wtile, bw, g, pidx):
                for k in range(8):
                    s.add("pe", lambda e, k=k: e.matmul(ps[pidx][:], wtile[:, k, :], hT[:, k, g * 512:(g + 1) * 512],
                                                         start=(k == 0), stop=(k == 7)),
                          reads=[bw, bh], writes=[bps[pidx]])

            with ExitStack() as es2:
                wa = [sb(es2, "wa%d" % i, [128, 8, 128], BF16) for i in range(3)]; bwa = [Buf() for _ in range(3)]
                ppad = sb(es2, "ppad", [128, T + 2]); bpp = Buf()
                abt = sb(es2, "abt", [128, T]); bab_ = Buf()
                acc = sb(es2, "acc", [128, T]); bacc = Buf()
                ybf = sb(es2, "ybf", [128, T], BF16); bybf = Buf()
                tmpc = sb(es2, "tmpc", [128, 512]); btc = Buf()
                s.add("pool", lambda e: e.memset(ppad[:, 0:2], 0.0), writes=[bpp])
                for cc in range(2):
                    for i, base in enumerate((256, 512, 0)):
                        c0 = base + cc * 128
                        s.dma("pool", wa[i][:], wv[:, :, c0:c0 + 128], writes=[bwa[i]])
                    for g in range(NG):
                        sl = slice(g * 512, (g + 1) * 512)
                        proj(wa[0], bwa[0], g, 0)
                        proj(wa[1], bwa[1], g, 1)
                        proj(wa[2], bwa[2], g, 2)
                        s.add("act", lambda e: e.activation(tmpc[:], ps[0][:], AF.Copy), reads=[bps[0]], writes=[btc])
                        s.add("dve", lambda e, g=g: e.tensor_tensor(ppad[:, 2 + g * 512:2 + (g + 1) * 512], tmpc[:],
                                                                  ps[1][:], ALU.mult),
                              reads=[btc, bps[1]], writes=[bpp])
                        s.add("act", lambda e, sl=sl: e.activation(abt[:, sl], ps[2][:], AF.Copy),
                              reads=[bps[2]], writes=[bab_])
                    for sg in range(4):
                        o = sg * 1024
                        wc = lambda k: caw[:, l * 6 + cc * 3 + k:l * 6 + cc * 3 + k + 1]
                        s.add("dve", lambda e, o=o, w0=wc(0): e.tensor_scalar(
                            acc[:, o:o + 1024], ppad[:, o:o + 1024], w0, None, ALU.mult),
                            reads=[bpp, bconst], writes=[bacc])
                        for k in (1, 2):
                            s.add("dve", lambda e, o=o, k=k, wk=wc(k): e.scalar_tensor_tensor(
                                acc[:, o:o + 1024], ppad[:, o + k:o + k + 1024], wk, acc[:, o:o + 1024],
                                ALU.mult, ALU.add), reads=[bpp, bacc, bconst], writes=[bacc])
                        s.add("pool", lambda e, o=o: e.tensor_tensor(ybf[:, o:o + 1024], acc[:, o:o + 1024],
                                                                  abt[:, o:o + 1024], ALU.mult),
                              reads=[bacc, bab_], writes=[bybf])
                    s.dma("sp", ys[cc * 128:(cc + 1) * 128, :], ybf[:], reads=[bybf], writes=[])
                s.barrier(); s.flush()
            ck('mixA')

            with ExitStack() as es2:
                wb = [sb(es2, "wb%d" % i, [128, 8, 128], BF16) for i in range(2)]; bwb = [Buf(), Buf()]
                upad = sb(es2, "upad", [128, 2, T + 30], BF16); bup = Buf()
                dg = sb(es2, "dg", [128, 62, 128], BF16); bdg = Buf()
                v32 = sb(es2, "v32", [128, 2, T]); bv32 = Buf()
                sgt = sb(es2, "sgt", [128, 512]); bsg = Buf()
                sq2s = [sb(es2, "sq2", [128, 2, 512], BF16) for _ in range(2)]; bsq2s = [Buf(), Buf()]
                rs2s = [sb(es2, "rs2", [128, 512]) for _ in range(2)]; brs2s = [Buf(), Buf()]
                t2s = [sb(es2, "t2", [128, 512]) for _ in range(4)]; bt2s = [Buf() for _ in range(4)]
                yb2 = [sb(es2, "yb2_%d" % i, [128, 2, 512], BF16) for i in range(2)]; byb2 = [Buf(), Buf()]
                for cc in range(2):
                    s.add("pool", lambda e, cc=cc: e.memset(upad[:, cc, 0:30], 0.0), writes=[bup])
                    for k in range(31):
                        s.add("pool", lambda e, cc=cc, k=k: e.tensor_scalar(
                            dg[:, cc * 31 + k, :], identB[:], cfw[:, l * 62 + cc * 31 + k:l * 62 + cc * 31 + k + 1],
                            None, ALU.mult), reads=[bconst], writes=[bdg])
                for cc in range(2):
                    s.dma("pool", wb[0][:], wv[:, :, 768 + cc * 128:768 + (cc + 1) * 128], writes=[bwb[0]])
                    s.dma("pool", wb[1][:], wv[:, :, 1024 + cc * 128:1024 + (cc + 1) * 128], writes=[bwb[1]])
                    for g in range(NG):
                        proj(wb[0], bwb[0], g, 0)
                        proj(wb[1], bwb[1], g, 1)
                        s.add("act", lambda e: e.activation(sgt[:], ps[1][:], AF.Sigmoid), reads=[bps[1]], writes=[bsg])
                        s.add("dve", lambda e, cc=cc, g=g: e.tensor_tensor(
                            upad[:, cc, 30 + g * 512:30 + (g + 1) * 512], sgt[:], ps[0][:], ALU.mult),
                            reads=[bsg, bps[0]], writes=[bup])
                for g in range(NG):
                    sl = slice(g * 512, (g + 1) * 512)
                    for cc in range(2):
                        pi = 2 + cc
                        for k in range(31):
                            s.add("pe", lambda e, cc=cc, k=k, g=g, pi=pi: e.matmul(
                                ps[pi][:], dg[:, cc * 31 + k, :], upad[:, cc, g * 512 + k:g * 512 + k + 512],
                                start=(k == 0), stop=(k == 30)), reads=[bdg, bup], writes=[bps[pi]])
                        s.add("act", lambda e, cc=cc, sl=sl, pi=pi: e.activation(
                            v32[:, cc, sl], ps[pi][:], AF.Identity, bias=cfb[:, l * 2 + cc:l * 2 + cc + 1], scale=1.0),
                            reads=[bps[pi], bconst], writes=[bv32])
                    pm, pv = (4, 5) if g % 2 == 0 else (0, 1)
                    sq2 = sq2s[g % 2]; bsq2 = bsq2s[g % 2]; rs2 = rs2s[g % 2]; brs2 = brs2s[g % 2]
                    for cc in range(2):
                        s.add("pe", lambda e, cc=cc, sl=sl, pm=pm: e.matmul(ps[pm][:], onesF, v32[:, cc, sl],
                                                                  start=(cc == 0), stop=(cc == 1)),
                              reads=[bv32, bconst], writes=[bps[pm]])
                    for cc in range(2):
                        s.add("dve", lambda e, cc=cc, sl=sl, pm=pm: e.scalar_tensor_tensor(
                            v32[:, cc, sl], ps[pm][:], -1.0 / 256, v32[:, cc, sl], ALU.mult, ALU.add),
                            reads=[bps[pm], bv32], writes=[bv32])
                    s.add("act", lambda e, sl=sl, sq2=sq2: e.activation(sq2[:], v32[:, :, sl], AF.Square),
                          reads=[bv32], writes=[bsq2])
                    for cc in range(2):
                        s.add("pe", lambda e, cc=cc, pv=pv, sq2=sq2: e.matmul(ps[pv][:], onesB[:], sq2[:, cc, :],
                                                             start=(cc == 0), stop=(cc == 1)),
                              reads=[bsq2, bconst], writes=[bps[pv]])
                    s.add("act", lambda e, pv=pv, rs2=rs2: e.activation(rs2[:], ps[pv][:], AF.Sqrt, bias=1e-5, scale=1.0 / 256),
                          reads=[bps[pv]], writes=[brs2])
                    s.add("dve", lambda e, rs2=rs2: e.reciprocal(rs2[:], rs2[:]), reads=[brs2], writes=[brs2])
                    yb_ = yb2[g % 2]; by_ = byb2[g % 2]
                    for cc in range(2):
                        t2 = t2s[(g % 2) * 2 + cc]; bt2 = bt2s[(g % 2) * 2 + cc]
                        s.add("dve", lambda e, cc=cc, sl=sl, t2=t2, rs2=rs2: e.tensor_tensor(t2[:], v32[:, cc, sl], rs2[:], ALU.mult),
                              reads=[bv32, brs2], writes=[bt2])
                        s.add("act", lambda e, cc=cc, yb_=yb_, t2=t2: e.activation(
                            yb_[:, cc, :], t2[:], AF.Silu, bias=cfbb[:, l * 2 + cc:l * 2 + cc + 1],
                            scale=cfg[:, l * 2 + cc:l * 2 + cc + 1]), reads=[bt2, bconst], writes=[by_])
                    s.dma("sp", ys_v[:, 2:4, sl], yb_[:], reads=[by_], writes=[])
                s.barrier(); s.flush()
            ck('mixB')

            with ExitStack() as es2:
                wc_ = [sb(es2, "wc%d" % i, [128, 8, 128], BF16) for i in range(2)]; bwc = [Buf(), Buf()]
                cpad = sb(es2, "cpad", [128, T + 3], BF16); bcp = Buf()
                dg4 = sb(es2, "dg4", [128, 4, 128], BF16); bdg4 = Buf()
                sils = [sb(es2, "sil", [128, 512]) for _ in range(2)]; bsils = [Buf(), Buf()]
                sq3s = [sb(es2, "sq3", [128, 512], BF16) for _ in range(2)]; bsq3s = [Buf(), Buf()]
                rs3s = [sb(es2, "rs3", [128, 512]) for _ in range(2)]; brs3s = [Buf(), Buf()]
                ob = [sb(es2, "ob%d" % i, [128, T], BF16) for i in range(2)]; bob = [Buf(), Buf()]
                wab = sb(es2, "wab", [128, 8, 8], BF16); bwab = Buf()
                s.add("pool", lambda e: e.memset(cpad[:, 0:3], 0.0), writes=[bcp])
                jobs = []
                for h in range(4):
                    jobs.append(("k", 1792 + h * 128, 4 + h, 0 + h))
                    jobs.append(("q", 1280 + h * 128, 0 + h, 4 + h))
                    jobs.append(("v", 2304 + h * 128, 8 + h, 8 + h))
                    jobs.append(("z", 2816 + h * 128, None, 12 + h))
                for ji, (ty, c0, ci, dj) in enumerate(jobs):
                    w_ = wc_[ji % 2]; bw_ = bwc[ji % 2]
                    o_ = ob[ji % 2]; bo_ = bob[ji % 2]
                    s.dma("pool", w_[:], wv[:, :, c0:c0 + 128], writes=[bw_])
                    if ty == "z":
                        for g in range(NG):
                            sl = slice(g * 512, (g + 1) * 512)
                            proj(w_, bw_, g, g % 2)
                            s.add("act", lambda e, sl=sl, g=g, o_=o_: e.activation(o_[:, sl], ps[g % 2][:], AF.Silu),
                                  reads=[bps[g % 2]], writes=[bo_])
                    else:
                        for k in range(4):
                            s.add("pool", lambda e, k=k, ci=ci: e.tensor_scalar(
                                dg4[:, k, :], identB[:], dnw[:, l * 48 + ci * 4 + k:l * 48 + ci * 4 + k + 1],
                                None, ALU.mult), reads=[bconst], writes=[bdg4])
                        for g in range(NG):
                            proj(w_, bw_, g, g % 2)
                            s.add("act", lambda e, g=g: e.activation(cpad[:, 3 + g * 512:3 + (g + 1) * 512],
                                                                    ps[g % 2][:], AF.Copy),
                                  reads=[bps[g % 2]], writes=[bcp])
                        for g in range(NG):
                            sl = slice(g * 512, (g + 1) * 512)
                            pi = 2 + g % 2
                            for k in range(4):
                                s.add("pe", lambda e, k=k, g=g, pi=pi: e.matmul(
                                    ps[pi][:], dg4[:, k, :], cpad[:, g * 512 + k:g * 512 + k + 512],
                                    start=(k == 0), stop=(k == 3)), reads=[bdg4, bcp], writes=[bps[pi]])
                            if ty == "v":
                                s.add("act", lambda e, sl=sl, pi=pi, o_=o_: e.activation(o_[:, sl], ps[pi][:], AF.Silu),
                                      reads=[bps[pi]], writes=[bo_])
                            else:
                                sil = sils[g % 2]; bsil = bsils[g % 2]; sq3 = sq3s[g % 2]; bsq3 = bsq3s[g % 2]
                                rs3 = rs3s[g % 2]; brs3 = brs3s[g % 2]; pq = 4 + g % 2
                                s.add("act", lambda e, pi=pi, sil=sil: e.activation(sil[:], ps[pi][:], AF.Silu),
                                      reads=[bps[pi]], writes=[bsil])
                                s.add("pool", lambda e, sil=sil, sq3=sq3: e.tensor_tensor(sq3[:], sil[:], sil[:], ALU.mult),
                                      reads=[bsil], writes=[bsq3])
                                s.add("pe", lambda e, sq3=sq3, pq=pq: e.matmul(ps[pq][:], onesB[:], sq3[:], start=True, stop=True),
                                      reads=[bsq3, bconst], writes=[bps[pq]])
                                s.add("act", lambda e, rs3=rs3, pq=pq: e.activation(rs3[:], ps[pq][:], AF.Sqrt, bias=EPS, scale=1.0),
                                      reads=[bps[pq]], writes=[brs3])
                                s.add("dve", lambda e, rs3=rs3: e.reciprocal(rs3[:], rs3[:]), reads=[brs3], writes=[brs3])
                                sc = 128.0 ** -0.5 if ty == "q" else 1.0
                                s.add("dve", lambda e, sl=sl, sc=sc, o_=o_, sil=sil, rs3=rs3: e.scalar_tensor_tensor(
                                    o_[:, sl], sil[:], sc, rs3[:], ALU.mult, ALU.mult),
                                    reads=[bsil, brs3], writes=[bo_])
                    s.dma("sp", dnin[:, :, dj, :], o_[:].rearrange("p (n t) -> p n t", t=128), reads=[bo_], writes=[])
                s.dma("pool", wab[:], wv[:, :, 3328:3336], writes=[bwab])
                for n in range(NCH):
                    for k in range(8):
                        s.add("pe", lambda e, n=n, k=k: e.matmul(
                            ps[5][:, n * 8:(n + 1) * 8], hT[:, k, n * 128:(n + 1) * 128], wab[:, k, :],
                            start=(k == 0), stop=(k == 7)), reads=[bwab, bh], writes=[bps[5]])
                s.add("act", lambda e: e.activation(ab_tok[:], ps[5][:, 0:NCH * 8], AF.Copy),
                      reads=[bps[5]], writes=[bab])
                s.barrier(); s.flush()

        ck('dnin')
        with ExitStack() as es:
            def t_(name, shape, dtype=F32):
                return sb(es, name, shape, dtype)
            mU4 = t_("mU4", [128, 7, 512], BF16); mL4 = t_("mL4", [128, 7, 512], BF16)
            i4 = t_("i4", [128, 512], BF16); su4 = t_("su4", [128, 512])
            bm = Buf("masks")
            s.dma("pool", mU4[:], lev_in.rearrange("p (a b) -> p a b", b=512), writes=[bm])
            s.dma("pool", mL4[:], levT_in.rearrange("p (a b) -> p a b", b=512), writes=[bm])
            s.dma("pool", i4[:], i4_in, writes=[bm])
            s.dma("sp", su4[:], su4_in, writes=[bm])
            beta = t_("beta", [128, NCH, 4]); gtok = t_("gtok", [128, NCH, 4])
            xsp = t_("xsp", [128, NCH, 4]); axs = t_("axs", [128, NCH, 4]); lg = t_("lg", [128, NCH, 4])
            nega = t_("nega", [128, 128])
            gam = t_("gam", [128, 128]); eg = t_("eg", [128, 128]); negeg = t_("negeg", [128, 128])
            decl = t_("decl", [128, 128]); eglast = t_("eglast", [128, 128])
            bsc = Buf("scal")
            abv = ab_tok[:].rearrange("p (n c) -> p n c", c=8)
            dtbv = dtb[:, l * 128:(l + 1) * 128].rearrange("p (n c) -> p n c", c=4)
            s.add("act", lambda e: e.activation(beta[:], abv[:, :, 4:8], AF.Sigmoid), reads=[bab], writes=[bsc])
            s.add("dve", lambda e: e.tensor_tensor(xsp[:], abv[:, :, 0:4], dtbv, ALU.add), reads=[bab, bconst], writes=[bsc])
            s.add("act", lambda e: e.activation(axs[:], xsp[:], AF.Abs), reads=[bsc], writes=[bsc])
            s.add("act", lambda e: e.activation(axs[:], axs[:], AF.Exp, scale=-1.0), reads=[bsc], writes=[bsc])
            s.add("act", lambda e: e.activation(lg[:], axs[:], AF.Ln, bias=1.0, scale=1.0), reads=[bsc], writes=[bsc])
            s.add("dve", lambda e: e.tensor_scalar(xsp[:], xsp[:], 0.0, None, ALU.max), reads=[bsc], writes=[bsc])
            s.add("dve", lambda e: e.tensor_tensor(xsp[:], xsp[:], lg[:], ALU.add), reads=[bsc], writes=[bsc])
            s.add("act", lambda e: e.activation(nega[:], alog[:, l * 128:(l + 1) * 128], AF.Exp), reads=[bconst], writes=[bsc])
            s.add("dve", lambda e: e.scalar_tensor_tensor(
                gtok[:].rearrange("p n c -> p (n c)"), xsp[:].rearrange("p n c -> p (n c)"), -1.0, nega[:],
                ALU.mult, ALU.mult), reads=[bsc], writes=[bsc])
            gflat = gtok[:].rearrange("p n c -> p (n c)")
            bflat = beta[:].rearrange("p n c -> p (n c)")
            s.add("pe", lambda e: e.matmul(ps[0][:, 0:128], uincl, gflat, start=True, stop=True),
                  reads=[bsc, bconst], writes=[bps[0]])
            s.add("pe", lambda e: e.matmul(ps[1][:, 0:128], onesF, gflat, start=True, stop=True),
                  reads=[bsc, bconst], writes=[bps[1]])
            s.add("act", lambda e: e.activation(gam[:], ps[0][:, 0:128], AF.Copy), reads=[bps[0]], writes=[bsc])
            s.add("act", lambda e: e.activation(eg[:], ps[0][:, 0:128], AF.Exp), reads=[bps[0]], writes=[bsc])
            s.add("dve", lambda e: e.tensor_scalar(negeg[:], eg[:], -1.0, None, ALU.mult), reads=[bsc], writes=[bsc])
            s.add("dve", lambda e: e.tensor_tensor(decl[:], ps[1][:, 0:128], gam[:], ALU.subtract),
                  reads=[bps[1], bsc], writes=[bsc])
            s.add("act", lambda e: e.activation(decl[:], decl[:], AF.Exp), reads=[bsc], writes=[bsc])
            s.add("act", lambda e: e.activation(eglast[:], ps[1][:, 0:128], AF.Exp), reads=[bps[1]], writes=[bsc])

            ck('D0')
            import os as _os
            NCH_RUN = int(_os.environ.get('DBG_NCH', NCH))
            NL = 4
            S32 = t_("S32", [128, 512]); bS = Buf()
            Sbf = t_("Sbf", [128, 512], BF16); bSb = Buf()
            s.add("dve", lambda e: e.memset(S32[:], 0.0), writes=[bS])
            s.add("pool", lambda e: e.memset(Sbf[:], 0.0), writes=[bSb])
            gd = gdn[:, l * 128:(l + 1) * 128]
            H = lambda a, h: a[:, h * 128:(h + 1) * 128]

            class Lane:
                pass
            lanes = []
            for li_ in range(NL):
                ln = Lane()
                ln.inp = t_("inp", [128, 16, 128], BF16); ln.binp = Buf()
                ln.Mt = t_("Mt", [128, 4, 128]); ln.bMt = Buf()
                ln.E = t_("E", [128, 512]); ln.bE = Buf()
                ln.Es = t_("Es", [128, 512]); ln.bEs = Buf()
                for nm in ("Pm", "Qm", "X", "Y", "R1", "QKm", "kdec", "rp", "vnew", "on", "yc"):
                    setattr(ln, nm, t_(nm, [128, 512], BF16)); setattr(ln, "b" + nm, Buf())
                ln.Qoff = t_("Qoff", [128, 6, 512], BF16); ln.bQoff = Buf()
                for nm in ("vtok", "o2s", "o_t"):
                    setattr(ln, nm, t_(nm, [128, 512])); setattr(ln, "b" + nm, Buf())
                ln.junk = t_("junk", [128, 128]); ln.bjunk = Buf()
                ln.ss = t_("ss", [128, 4]); ln.bss = Buf()
                if li_ % 2 == 0:
                    ln.p = (ps[0], ps[1], ps[2]); ln.bp = (bps[0], bps[1], bps[2]); ln.pT = ps6b; ln.bpT = bps6
                else:
                    ln.p = (ps[3], ps[4], ps[5]); ln.bp = (bps[3], bps[4], bps[5]); ln.pT = ps7b; ln.bpT = bps7
                lanes.append(ln)

            owner = {}

            def acq(me, banks):
                while any(owner.get(id(b_)) not in (None, me) for b_ in banks):
                    yield
                for b_ in banks:
                    owner[id(b_)] = me

            def rel(me, banks):
                for b_ in banks:
                    if owner.get(id(b_)) == me:
                        owner[id(b_)] = None

            def d1_gen(n, ln):
                ip = ln.inp; bip = ln.binp
                p0, p1, p2 = ln.p; b0, b1, b2 = ln.bp; pT = ln.pT; bT = ln.bpT
                col = lambda a, h: a[:, n * 4 + h:n * 4 + h + 1]
                s.dma("sp", ip[:], dnin[:, n, :, :], reads=[], writes=[bip])
                yield
                yield from acq(n, (b0, b1))
                for h in range(4):
                    s.add("pe", lambda e, h=h: e.matmul(H(p0, h), ip[:, h, :], ip[:, h, :], start=True, stop=True),
                          reads=[bip], writes=[b0])
                for h in range(4):
                    s.add("pe", lambda e, h=h: e.matmul(H(p1, h), ip[:, h, :], ip[:, 4 + h, :], start=True, stop=True),
                          reads=[bip], writes=[b1])
                for h in range(4):
                    s.add("pool", lambda e, h=h, g_=col(gflat, h): e.tensor_scalar(ln.Mt[:, h, :], masksl, g_, None, ALU.mult),
                          reads=[bsc, bconst], writes=[ln.bMt])
                yield
                yield from acq(n, (b2,))
                for h in range(4):
                    s.add("pe", lambda e, h=h: e.matmul(H(p2, h), ln.Mt[:, h, :], uincl, start=True, stop=False),
                          reads=[ln.bMt, bconst], writes=[b2])
                    s.add("pe", lambda e, h=h: e.matmul(H(p2, h), identF, negm, start=False, stop=True),
                          reads=[ln.bMt, bconst], writes=[b2])
                yield
                s.add("act", lambda e: e.activation(ln.E[:], p2[:], AF.Exp), reads=[b2], writes=[ln.bE])
                rel(n, (b2,))
                yield
                s.add("pool", lambda e: e.tensor_tensor(ln.Es[:], ln.E[:], su4[:], ALU.mult), reads=[ln.bE, bm], writes=[ln.bEs])
                s.add("dve", lambda e: e.tensor_tensor(ln.QKm[:], p1[:], ln.E[:], ALU.mult), reads=[b1, ln.bE], writes=[ln.bQKm])
                rel(n, (b1,))
                yield
                for h in range(4):
                    s.add("dve", lambda e, h=h, b_=col(bflat, h): e.scalar_tensor_tensor(
                        H(ln.Pm, h), H(p0, h), b_, H(ln.Es, h), ALU.mult, ALU.mult),
                        reads=[b0, ln.bEs, bsc], writes=[ln.bPm])
                rel(n, (b0,))
                yield
                yield from acq(n, (bT,))
                for h in range(4):
                    s.add("pe", lambda e, h=h: e.transpose(H(pT, h), H(ln.Pm, h), identB[:]), reads=[ln.bPm, bconst], writes=[bT])
                s.add("pool", lambda e: e.tensor_tensor(ln.X[:], ln.Pm[:], mU4[:, 0, :], ALU.mult), reads=[ln.bPm, bm], writes=[ln.bX])
                s.add("pool", lambda e: e.tensor_tensor(ln.X[:], i4[:], ln.X[:], ALU.subtract), reads=[ln.bX, bm], writes=[ln.bX])
                yield
                s.add("act", lambda e: e.activation(ln.Qm[:], pT[:, 0:512], AF.Copy), reads=[bT], writes=[ln.bQm])
                rel(n, (bT,))
                yield
                s.add("pool", lambda e: e.tensor_tensor(ln.Y[:], ln.Qm[:], mL4[:, 0, :], ALU.mult), reads=[ln.bQm, bm], writes=[ln.bY])
                s.add("pool", lambda e: e.tensor_tensor(ln.Y[:], i4[:], ln.Y[:], ALU.subtract), reads=[ln.bY, bm], writes=[ln.bY])
                for li in range(1, 7):
                    s.add("pool", lambda e, li=li: e.tensor_tensor(ln.Qoff[:, li - 1, :], ln.Qm[:], mL4[:, li, :], ALU.mult),
                          reads=[ln.bQm, bm], writes=[ln.bQoff])
                yield
                for li in range(1, 7):
                    last = (li == 6)
                    yield from acq(n, (b0,))
                    for h in range(4):
                        s.add("pe", lambda e, h=h, li=li: e.matmul(H(p0, h), ln.Qoff[:, li - 1, h * 128:(h + 1) * 128], H(ln.X, h),
                                                                 start=True, stop=True), reads=[ln.bQoff, ln.bX], writes=[b0])
                    yield
                    s.add("act", lambda e: e.activation(ln.R1[:], p0[:], AF.Copy), reads=[b0], writes=[ln.bR1])
                    rel(n, (b0,))
                    yield
                    yield from acq(n, (b1, b2))
                    for h in range(4):
                        s.add("pe", lambda e, h=h: e.matmul(H(p1, h), H(ln.Y, h), H(ln.R1, h), start=True, stop=True),
                              reads=[ln.bY, ln.bR1], writes=[b1])
                    if not last:
                        for h in range(4):
                            s.add("pe", lambda e, h=h: e.matmul(H(p2, h), H(ln.R1, h), H(ln.Y, h), start=True, stop=True),
                                  reads=[ln.bY, ln.bR1], writes=[b2])
                    yield
                    s.add("dve", lambda e: e.tensor_tensor(ln.X[:], ln.X[:], p1[:], ALU.subtract), reads=[ln.bX, b1], writes=[ln.bX])
                    if not last:
                        s.add("dve", lambda e: e.tensor_tensor(ln.Y[:], ln.Y[:], p2[:], ALU.subtract),
                              reads=[ln.bY, b2], writes=[ln.bY])
                    rel(n, (b1, b2))
                    yield

            def d2_gen(n, ln):
                ip = ln.inp; bip = ln.binp
                p0, p1, p2 = ln.p; b0, b1, b2 = ln.bp; pT = ln.pT; bT = ln.bpT
                col = lambda a, h: a[:, n * 4 + h:n * 4 + h + 1]
                yield from acq(n, (bT, b0, b1))
                for h in range(4):
                    s.add("pe", lambda e, h=h: e.transpose(H(pT, h), ip[:, h, :], identB[:]), reads=[bip, bconst], writes=[bT])
                for h in range(4):
                    s.add("pe", lambda e, h=h: e.transpose(H(pT, 4 + h), ip[:, 8 + h, :], identB[:]), reads=[bip, bconst], writes=[bT])
                for h in range(4):
                    s.add("pe", lambda e, h=h: e.matmul(H(p0, h), ip[:, h, :], H(Sbf, h), start=True, stop=True),
                          reads=[bip, bSb], writes=[b0])
                for h in range(4):
                    s.add("pe", lambda e, h=h: e.matmul(H(p1, h), ip[:, 4 + h, :], H(Sbf, h), start=True, stop=True),
                          reads=[bip, bSb], writes=[b1])
                yield
                s.add("act", lambda e: e.activation(ln.vtok[:], pT[:, 512:1024], AF.Copy), reads=[bT], writes=[ln.bvtok])
                for h in range(4):
                    s.add("dve", lambda e, h=h, d_=col(decl, h): e.tensor_scalar(H(ln.kdec, h), H(pT, h), d_, None, ALU.mult),
                          reads=[bT, bsc], writes=[ln.bkdec])
                rel(n, (bT,))
                yield
                for h in range(4):
                    s.add("dve", lambda e, h=h, ne_=col(negeg, h): e.scalar_tensor_tensor(
                        H(ln.rp, h), H(p0, h), ne_, H(ln.vtok, h), ALU.mult, ALU.add),
                        reads=[b0, ln.bvtok, bsc], writes=[ln.brp])
                rel(n, (b0,))
                yield
                yield from acq(n, (b2,))
                for h in range(4):
                    s.add("pe", lambda e, h=h: e.matmul(H(p2, h), H(ln.X, h), H(ln.rp, h), start=True, stop=True),
                          reads=[ln.bX, ln.brp], writes=[b2])
                yield
                for h in range(4):
                    s.add("act", lambda e, h=h, b_=col(bflat, h): e.activation(H(ln.vnew, h), H(p2, h), AF.Identity, bias=0.0, scale=b_),
                          reads=[b2, bsc], writes=[ln.bvnew])
                yield
                yield from acq(n, (b0,))
                for h in range(4):
                    s.add("pe", lambda e, h=h: e.matmul(H(p2, h), H(ln.kdec, h), H(ln.vnew, h), start=True, stop=True),
                          reads=[ln.bkdec, ln.bvnew], writes=[b2])
                for h in range(4):
                    s.add("pe", lambda e, h=h: e.matmul(H(p0, h), H(ln.QKm, h), H(ln.vnew, h), start=True, stop=True),
                          reads=[ln.bQKm, ln.bvnew], writes=[b0])
                yield
                for h in range(4):
                    s.add("dve", lambda e, h=h, el_=col(eglast, h): e.scalar_tensor_tensor(
                        H(S32, h), H(S32, h), el_, H(p2, h), ALU.mult, ALU.add),
                        reads=[bS, b2, bsc], writes=[bS])
                s.add("act", lambda e: e.activation(ln.o2s[:], p0[:], AF.Copy), reads=[b0], writes=[ln.bo2s])
                rel(n, (b0, b2))
                yield
                s.add("pool", lambda e: e.tensor_copy(Sbf[:], S32[:]), reads=[bS], writes=[bSb])
                for h in range(4):
                    s.add("dve", lambda e, h=h, eg_=col(eg, h): e.scalar_tensor_tensor(
                        H(ln.o_t, h), H(p1, h), eg_, H(ln.o2s, h), ALU.mult, ALU.add),
                        reads=[b1, ln.bo2s, bsc], writes=[ln.bo_t])
                rel(n, (b1,))
                s.add("pool", lambda e: e.memset(ln.ss[:], 0.0), writes=[ln.bss])
                yield

            def d3_gen(n, ln):
                ip = ln.inp; bip = ln.binp
                pT = ln.pT; bT = ln.bpT
                for h in range(4):
                    s.add("act", lambda e, h=h: e.activation(ln.junk[:], H(ln.o_t, h), AF.Square, accum_out=ln.ss[:, h:h + 1]),
                          reads=[ln.bo_t, ln.bss], writes=[ln.bjunk, ln.bss])
                yield
                s.add("act", lambda e: e.activation(ln.ss[:], ln.ss[:], AF.Sqrt, bias=EPS, scale=1.0 / 128), reads=[ln.bss], writes=[ln.bss])
                yield
                s.add("dve", lambda e: e.reciprocal(ln.ss[:], ln.ss[:]), reads=[ln.bss], writes=[ln.bss])
                yield
                for h in range(4):
                    s.add("dve", lambda e, h=h: e.scalar_tensor_tensor(H(ln.on, h), H(ln.o_t, h), ln.ss[:, h:h + 1], gd, ALU.mult, ALU.mult),
                          reads=[ln.bo_t, ln.bss, bconst], writes=[ln.bon])
                yield
                yield from acq(n, (bT,))
                for h in range(4):
                    s.add("pe", lambda e, h=h: e.transpose(H(pT, 4 + h), H(ln.on, h), identB[:]), reads=[ln.bon, bconst], writes=[bT])
                yield
                s.add("dve", lambda e: e.tensor_tensor(
                    ln.yc[:], pT[:, 512:1024], ip[:, 12:16, :].rearrange("p a b -> p (a b)"), ALU.mult),
                    reads=[bT, bip], writes=[ln.byc])
                rel(n, (bT,))
                s.dma("sp", ys_v[:, 4:8, n * 128:(n + 1) * 128], ln.yc[:].rearrange("p (a b) -> p a b", b=128),
                      reads=[ln.byc], writes=[])
                yield

            def chain(n, ln):
                yield from d1_gen(n, ln)
                while d2_turn[0] != n:
                    yield
                yield from d2_gen(n, ln)
                d2_turn[0] = n + 1
                yield from d3_gen(n, ln)

            d2_turn = [0]
            active = {}
            nxt = 0
            while nxt < NCH_RUN or active:
                while nxt < NCH_RUN and (nxt % NL) not in active:
                    active[nxt % NL] = chain(nxt, lanes[nxt % NL]); nxt += 1
                    break
                for k in sorted(active.keys(), key=lambda kk: kk):
                    try:
                        next(active[k])
                    except StopIteration:
                        del active[k]
            s.barrier(); s.flush()

        ck("mixy%d" % l)
        with ExitStack() as es:
            wo = sb(es, "wo", [128, 8, D], BF16); bwo = Buf()
            wov = w_out[l].rearrange("(k p) c -> p k c", p=128)
            for k in range(8):
                s.dma("pool", wo[:, k, :], wov[:, k, :], writes=[bwo])
            yt = [sb(es, "yt%d" % i, [128, 8, 512], BF16) for i in range(2)]; byt = [Buf(), Buf()]
            xt = [sb(es, "xo%d" % i, [128, 8, 512]) for i in range(2)]; bxt = [Buf(), Buf()]
            for g in range(NG):
                sl = slice(g * 512, (g + 1) * 512)
                y_ = yt[g % 2]; by_ = byt[g % 2]; x_ = xt[g % 2]; bx_ = bxt[g % 2]
                s.dma("sp", y_[:], ys_v[:, :, sl], writes=[by_])
                s.dma("sp", x_[:], xs_v[:, :, sl], writes=[bx_])
                for oc in range(8):
                    pi = oc % 4
                    for k in range(8):
                        s.add("pe", lambda e, k=k, oc=oc, pi=pi, y_=y_: e.matmul(
                            ps[pi][:], wo[:, k, oc * 128:(oc + 1) * 128], y_[:, k, :], start=(k == 0), stop=(k == 7)),
                            reads=[bwo, by_], writes=[bps[pi]])
                    s.add("dve", lambda e, oc=oc, pi=pi, x_=x_: e.scalar_tensor_tensor(
                        x_[:, oc, :], ps[pi][:], G1(l, oc), x_[:, oc, :], ALU.mult, ALU.add),
                        reads=[bps[pi], bx_, bmod], writes=[bx_])
                s.dma("sp", xs_v[:, :, sl], x_[:], reads=[bx_], writes=[])
            s.barrier(); s.flush()
        ck("mix%d" % l)

        with ExitStack() as es:
            wfi = sb(es, "wfi", [128, 8, 2 * DFF], BF16); bwfi = Buf()
            wfo = sb(es, "wfo", [128, 22, D], BF16); bwfo = Buf()
            wfiv = w_fi[l].rearrange("(k p) c -> p k c", p=128)
            wfov = w_fo[l].rearrange("(k p) c -> p k c", p=128)
            for k in range(8):
                for hh in range(2):
                    s.dma("pool", wfi[:, k, hh * DFF:(hh + 1) * DFF], wfiv[:, k, hh * DFF:(hh + 1) * DFF], writes=[bwfi])
            for k in range(22):
                s.dma("pool", wfo[:, k, :], wfov[:, k, :], writes=[bwfo])
            xt = sb(es, "xf", [128, 8, 512]); bxt = Buf()
            sq = sb(es, "sqf", [128, 8, 512], BF16); bsq = Buf()
            rs = sb(es, "rsf", [128, 512]); brs = Buf()
            hT2 = sb(es, "hT2", [128, 8, 512], BF16); bh2 = Buf()
            aT = sb(es, "aT", [128, 22, 512], BF16); baT = Buf()
            sgf = [sb(es, "sgf%d" % i, [128, 512]) for i in range(2)]; bsgf = [Buf(), Buf()]
            for g in range(NG):
                sl = norm_group((sq, bsq, rs, brs), l, g, None, None, None, None, xt, bxt)
                for j in range(8):
                    s.add("dve", lambda e, j=j: e.scalar_tensor_tensor(
                        sq[:, j, :], xt[:, j, :], A2[:, l * 8 + j:l * 8 + j + 1], rs[:], ALU.mult, ALU.mult),
                        reads=[bxt, brs, bmod, bsq], writes=[bsq])
                    s.add("act", lambda e, j=j: e.activation(
                        hT2[:, j, :], sq[:, j, :], AF.Identity, bias=B2(l, j), scale=1.0),
                        reads=[bsq, bmod], writes=[bh2])
                for j in range(22):
                    pg, pu = (0, 1) if j % 2 == 0 else (2, 3)
                    for k in range(8):
                        s.add("pe", lambda e, k=k, j=j, pg=pg: e.matmul(
                            ps[pg][:], wfi[:, k, j * 128:(j + 1) * 128], hT2[:, k, :], start=(k == 0), stop=(k == 7)),
                            reads=[bwfi, bh2], writes=[bps[pg]])
                    for k in range(8):
                        s.add("pe", lambda e, k=k, j=j, pu=pu: e.matmul(
                            ps[pu][:], wfi[:, k, DFF + j * 128:DFF + (j + 1) * 128], hT2[:, k, :],
                            start=(k == 0), stop=(k == 7)), reads=[bwfi, bh2], writes=[bps[pu]])
                    sg_ = sgf[j % 2]; bsg_ = bsgf[j % 2]
                    s.add("act", lambda e, pg=pg, sg_=sg_: e.activation(sg_[:], ps[pg][:], AF.Silu),
                          reads=[bps[pg]], writes=[bsg_])
                    s.add("dve", lambda e, j=j, pu=pu, sg_=sg_: e.tensor_tensor(aT[:, j, :], sg_[:], ps[pu][:], ALU.mult),
                          reads=[bsg_, bps[pu]], writes=[baT])
                for oc in range(8):
                    pi = 4 + oc % 2
                    for k in range(22):
                        s.add("pe", lambda e, k=k, oc=oc, pi=pi: e.matmul(
                            ps[pi][:], wfo[:, k, oc * 128:(oc + 1) * 128], aT[:, k, :], start=(k == 0), stop=(k == 21)),
                            reads=[bwfo, baT], writes=[bps[pi]])
                    s.add("dve", lambda e, oc=oc, pi=pi: e.scalar_tensor_tensor(
                        xt[:, oc, :], ps[pi][:], G2(l, oc), xt[:, oc, :], ALU.mult, ALU.add),
                        reads=[bps[pi], bxt, bmod], writes=[bxt])
                s.dma("sp", xs_v[:, :, sl], xt[:], reads=[bxt], writes=[bxs])
            s.barrier(); s.flush()
        ck("ffn%d" % l)

    s.muted = False
    dumpy = stop in ('mixA', 'mixB') or (stop is not None and stop.startswith('mixy'))
    with ExitStack() as es:
        xt = [sb(es, "xz%d" % i, [128, 8, 512]) for i in range(2)]; bxt = [Buf(), Buf()]
        sq = sb(es, "sqz", [128, 8, 512], BF16); bsq = Buf()
        rs = sb(es, "rsz", [128, 512]); brs = Buf()
        if dumpy:
            yt = [sb(es, "yz%d" % i, [128, 8, 512], BF16) for i in range(2)]; byt = [Buf(), Buf()]
        for g in range(NG):
            x_ = xt[g % 2]; bx_ = bxt[g % 2]
            if stop is None:
                sl = norm_group((sq, bsq, rs, brs), 0, g, None, None, None, None, x_, bx_)
                for j in range(8):
                    s.add("dve", lambda e, j=j, x_=x_: e.scalar_tensor_tensor(
                        x_[:, j, :], x_[:, j, :], gfin[:, j:j + 1], rs[:], ALU.mult, ALU.mult),
                        reads=[bx_, brs, bconst], writes=[bx_])
            elif dumpy:
                sl = slice(g * 512, (g + 1) * 512)
                y_ = yt[g % 2]; by_ = byt[g % 2]
                s.dma("sp", y_[:], ys_v[:, :, sl], writes=[by_])
                s.add("dve", lambda e, x_=x_, y_=y_: e.tensor_copy(x_[:], y_[:]), reads=[by_], writes=[bx_])
            else:
                sl = slice(g * 512, (g + 1) * 512)
                s.dma("sp", x_[:], xs_v[:, :, sl], writes=[bx_])
            s.dma("sp", out_v[:, :, sl], x_[:], reads=[bx_], writes=[])
        s.flush(final=True)
    top.close()
    s.close()
    nc._nops = s.nops
    return nc


def _pp(a):
    a = np.asarray(a, np.float32)
    lead = a.shape[:-1]
    n = a.shape[-1] // 128
    a = a.reshape(lead + (n, 128))
    a = np.moveaxis(a, -1, 0)
    return np.ascontiguousarray(a.reshape(128, -1))


def make_in_maps(inputs, depth=L, ncores=NCORES):
    f = lambda k: np.asarray(inputs[k], np.float32)
    m = _levels_masks()
    common = {
        "w_ada": f("w_ada")[:depth], "b_ada": f("b_ada")[:depth], "w_in": f("w_in")[:depth], "w_out": f("w_out")[:depth],
        "w_ffn_in": f("w_ffn_in")[:depth], "w_ffn_out": f("w_ffn_out")[:depth],
        "gm": _pp(f("norm_mix_g")), "gf": _pp(f("norm_ffn_g")), "gfin": _pp(f("final_norm_g")),
        "caw": _pp(np.transpose(f("conv_a_w"), (0, 2, 1)).reshape(L, 2, 128, 3).transpose(0, 1, 3, 2)),
        "cfw": _pp(np.transpose(f("conf_dw_w"), (0, 2, 1)).reshape(L, 2, 128, 31).transpose(0, 1, 3, 2)),
        "cfb": _pp(f("conf_dw_b")), "cfg": _pp(f("conf_ln_g")), "cfbb": _pp(f("conf_ln_b")),
        "dnw": _pp(np.transpose(f("dn_conv_w"), (0, 2, 1)).reshape(L, 12, 128, 4).transpose(0, 1, 3, 2)),
        "alog": np.ascontiguousarray(np.broadcast_to(np.tile(f("dn_a_log"), (1, NCH)).reshape(1, L * 128), (128, L * 128))),
        "dtb": np.ascontiguousarray(np.broadcast_to(np.tile(f("dn_dt_bias"), (1, NCH)).reshape(1, L * 128), (128, L * 128))),
        "gdn": np.ascontiguousarray(np.broadcast_to(f("dn_norm_g").reshape(1, L * 128), (128, L * 128))),
        "cmask": np.ascontiguousarray(np.concatenate(
            [m["ident"], m["uincl"], m["masksl"], m["negm"], m["strictu"], m["ones"]], axis=1)),
        "levU4": np.ascontiguousarray(np.concatenate([np.tile(m["levU"][i], (1, 4)) for i in range(7)], axis=1)),
        "levL4": np.ascontiguousarray(np.concatenate([np.tile(m["levU"][i].T, (1, 4)) for i in range(7)], axis=1)),
        "ident4": np.ascontiguousarray(np.tile(m["ident"], (1, 4))),
        "strictu4": np.ascontiguousarray(np.tile(m["strictu"], (1, 4))),
    }
    x = f("x"); c = f("c")
    maps = []
    for core in range(ncores):
        b = core % 4
        d = dict(common)
        d["xT"] = np.ascontiguousarray(x[b].T)
        d["cT"] = np.ascontiguousarray(c[b].reshape(8, 128).T)
        maps.append(d)
    return maps


_NC_CACHE = {}


def kernel(**inputs):
    if "nc" not in _NC_CACHE:
        _NC_CACHE["nc"] = build_program()
    nc = _NC_CACHE["nc"]
    in_maps = make_in_maps(inputs)
    res = run_bass_kernel_spmd(nc, in_maps, core_ids=list(range(NCORES)))
    out = np.stack([np.asarray(res.results[b]["outT"], np.float32).T for b in range(4)], axis=0)
    return np.ascontiguousarray(out)
```

```python
from contextlib import ExitStack

import numpy as np
import concourse.bass as bass
import concourse.mybir as mybir
from concourse.bass_utils import run_bass_kernel_spmd

F32 = mybir.dt.float32
BF16 = mybir.dt.bfloat16
AF = mybir.ActivationFunctionType
ALU = mybir.AluOpType


class Buf:
    __slots__ = ("name", "w", "r", "excl")

    def __init__(self, name="", excl=False):
        self.name = name
        self.w = None
        self.r = []
        self.excl = excl


class _Op:
    __slots__ = ("eng", "fn", "deps", "is_dma", "sem", "val", "signal", "emitted")


class Sched:
    CENG = ("pe", "act", "dve", "pool")
    ENG = ("pe", "act", "dve", "pool", "sp")
    DMAQ = ("sp", "pool", "act")

    def __init__(self, nc, strict=True, dma_k=6):
        self.nc = nc
        self.strict = strict
        self.K = dma_k
        self.es = ExitStack()
        self.csem = {e: self.es.enter_context(nc.semaphore("c_" + e)) for e in self.CENG}
        self.ccount = {e: 0 for e in self.CENG}
        self.dsem = {q: [self.es.enter_context(nc.semaphore("d_%s%d" % (q, i))) for i in range(dma_k)]
                     for q in self.DMAQ}
        self.dcount = {q: 0 for q in self.DMAQ}
        self.dhist = {q: [] for q in self.DMAQ}
        self.known = {e: {} for e in self.ENG}
        self.pending = {e: [] for e in self.ENG}
        self.last = {e: None for e in self.CENG}
        self.barrier_ops = []
        self.nops = 0
        self.muted = False

    def _mk(self, eng, fn, reads, writes, is_dma):
        op = _Op()
        op.eng = eng
        op.fn = fn
        op.is_dma = is_dma
        op.signal = is_dma
        op.sem = None
        op.val = None
        op.emitted = False
        deps = list(self.barrier_ops)
        for b in reads:
            if b.w is not None:
                deps.append(b.w)
            if b.excl:
                deps.extend(r for r in b.r if r.eng != eng)
        for b in writes:
            if b.w is not None:
                deps.append(b.w)
            deps.extend(b.r)
        op.deps = deps
        for b in reads:
            b.r.append(op)
        for b in writes:
            b.w = op
            b.r = []
        self.pending[eng].append(op)
        self.nops += 1
        return op

    def add(self, eng, fn, reads=(), writes=()):
        if self.muted:
            return None
        op = self._mk(eng, fn, reads, writes, False)
        self.last[eng] = op
        return op

    def dma(self, q, out, in_, reads=(), writes=(), **kw):
        if self.muted:
            return None
        def fn(e, out=out, in_=in_, kw=kw):
            return e.dma_start(out=out, in_=in_, **kw)
        op = self._mk(q, fn, reads, writes, True)
        i = self.dcount[q]
        self.dcount[q] += 1
        op.sem = self.dsem[q][i % self.K]
        op.val = 16 * (i // self.K + 1)
        if i >= self.K:
            op.deps.append(self.dhist[q][i - self.K])
        self.dhist[q].append(op)
        return op

    def barrier(self):
        if self.muted:
            return
        ops = [self.last[e] for e in self.CENG if self.last[e] is not None]
        for q in self.DMAQ:
            ops.extend(self.dhist[q][-self.K:])
        self.barrier_ops = ops

    def _need(self, op, dep):
        if dep.is_dma:
            return True
        if dep.eng != op.eng:
            return True
        if op.eng == "pe":
            return False
        if op.is_dma:
            return True
        return self.strict

    def flush(self, final=False):
        nc = self.nc
        if not final and not any(self.pending[e] for e in self.ENG):
            return
        for e in self.ENG:
            for op in self.pending[e]:
                for d in op.deps:
                    if not d.is_dma and not d.emitted and self._need(op, d):
                        d.signal = True
        for e in self.CENG:
            comp = [o for o in self.pending[e] if not o.is_dma]
            if comp:
                comp[-1].signal = True
        for e in self.CENG:
            c = self.ccount[e]
            comp = [o for o in self.pending[e] if not o.is_dma]
            for o in comp:
                if o.signal:
                    c += 1
                    o.val = c
                o.sem = self.csem[e]
            self.ccount[e] = c
            nxt = None
            for o in reversed(comp):
                if o.signal:
                    nxt = o.val
                else:
                    o.val = nxt
        getter = {"pe": "tensor", "act": "scalar", "dve": "vector", "pool": "gpsimd", "sp": "sync"}

        def run(e, eng):
            known = self.known[e]
            for op in self.pending[e]:
                waits = {}
                for d in op.deps:
                    if not self._need(op, d):
                        continue
                    key = id(d.sem)
                    if known.get(key, 0) >= d.val:
                        continue
                    if key not in waits or waits[key][1] < d.val:
                        waits[key] = (d.sem, d.val)
                for key, (sem, val) in waits.items():
                    eng.wait_ge(sem, val)
                    known[key] = val
                inst = op.fn(eng)
                if op.signal:
                    inst.then_inc(op.sem, 16 if op.is_dma else 1)
                op.emitted = True
                op.fn = None
            if final and e == "sp":
                for q in self.DMAQ:
                    n = self.dcount[q]
                    for j in range(self.K):
                        cnt = len(range(j, n, self.K))
                        if cnt:
                            eng.wait_ge(self.dsem[q][j], 16 * cnt)

        with nc.Block() as blk:
            for e in self.ENG:
                if not self.pending[e] and not (final and e == "sp"):
                    continue
                getattr(blk, getter[e])(lambda eng, e=e: run(e, eng))
        self.pending = {e: [] for e in self.ENG}

    def close(self):
        self.es.close()


D = 1024
T = 4096
L = 4
NG = T // 512
NCH = T // 128
DFF = 2816
INC = 3336
EPS = 1e-6
NCORES = 8


class _Stop(Exception):
    pass


def _levels_masks():
    i = np.arange(128)
    s_, c_ = np.meshgrid(i, i, indexing="ij")
    out = {}
    out["ident"] = (s_ == c_).astype(np.float32)
    out["uincl"] = (s_ <= c_).astype(np.float32)
    out["masksl"] = (s_ > c_).astype(np.float32)
    out["negm"] = np.where(c_ < s_, -30000.0, 0.0).astype(np.float32)
    out["strictu"] = (s_ < c_).astype(np.float32)
    out["ones"] = np.ones((128, 128), np.float32)
    lev = []
    for b in (1, 2, 4, 8, 16, 32, 64):
        m = ((s_ // (2 * b)) == (c_ // (2 * b))) & ((s_ // b) % 2 == 0) & ((c_ // b) % 2 == 1)
        lev.append(m.astype(np.float32))
    out["levU"] = np.stack(lev)
    return out


def build_program(depth=L, stop=None):
    nc = bass.Bass("TRN2", target_bir_lowering=False)
    dt = nc.dram_tensor

    def din(name, shape, dtype=F32):
        return dt(name, list(shape), dtype, kind="ExternalInput").ap()

    xT_in = din("xT", [D, T])
    cT_in = din("cT", [128, 8])
    w_ada = din("w_ada", [depth, D, 6 * D])
    b_ada = din("b_ada", [depth, 6 * D])
    w_in = din("w_in", [depth, D, INC])
    w_out = din("w_out", [depth, D, D])
    w_fi = din("w_ffn_in", [depth, D, 2 * DFF])
    w_fo = din("w_ffn_out", [depth, DFF, D])
    gm_in = din("gm", [128, L * 8])
    gf_in = din("gf", [128, L * 8])
    gfin_in = din("gfin", [128, 8])
    caw_in = din("caw", [128, L * 2 * 3])
    cfw_in = din("cfw", [128, L * 2 * 31])
    cfb_in = din("cfb", [128, L * 2])
    cfg_in = din("cfg", [128, L * 2])
    cfbb_in = din("cfbb", [128, L * 2])
    dnw_in = din("dnw", [128, L * 12 * 4])
    alog_in = din("alog", [128, L * 128])
    dtb_in = din("dtb", [128, L * 128])
    gdn_in = din("gdn", [128, L * 128])
    cm_in = din("cmask", [128, 6 * 128])
    lev_in = din("levU4", [128, 7 * 512])
    levT_in = din("levL4", [128, 7 * 512])
    i4_in = din("ident4", [128, 512])
    su4_in = din("strictu4", [128, 512])
    outT = dt("outT", [D, T], F32, kind="ExternalOutput").ap()

    xs = dt("xs", [D, T], F32, kind="Internal").ap()
    ys = dt("ys", [D, T], BF16, kind="Internal").ap()
    dnin = dt("dnin", [128, NCH, 16, 128], BF16, kind="Internal").ap()

    import os as _os2
    s = Sched(nc, strict=(_os2.environ.get("MK_STRICT", "1") == "1"))
    top = ExitStack()

    _cnt = [0]

    def sb(es, name, shape, dtype=F32):
        _cnt[0] += 1
        return es.enter_context(nc.sbuf_tensor("%s_%d" % (name, _cnt[0]), list(shape), dtype))

    ps = [top.enter_context(nc.psum_tensor("ps%d" % i, [128, 512], F32)) for i in range(6)]
    ps6b = top.enter_context(nc.psum_tensor("ps6b", [128, 1024], BF16))
    ps7b = top.enter_context(nc.psum_tensor("ps7b", [128, 1024], BF16))
    bps = [Buf("ps%d" % i, excl=True) for i in range(6)]
    bps6, bps7 = Buf("ps6b", excl=True), Buf("ps7b", excl=True)

    modT = sb(top, "modT", [128, L, 48])
    A1 = sb(top, "A1", [128, L * 8]); A2 = sb(top, "A2", [128, L * 8])
    gm = sb(top, "gm_t", [128, L * 8]); gf = sb(top, "gf_t", [128, L * 8]); gfin = sb(top, "gfin_t", [128, 8])
    caw = sb(top, "caw_t", [128, L * 6]); cfw = sb(top, "cfw_t", [128, L * 62])
    cfb = sb(top, "cfb_t", [128, L * 2]); cfg = sb(top, "cfg_t", [128, L * 2]); cfbb = sb(top, "cfbb_t", [128, L * 2])
    dnw = sb(top, "dnw_t", [128, L * 48])
    alog = sb(top, "alog_t", [128, L * 128]); dtb = sb(top, "dtb_t", [128, L * 128]); gdn = sb(top, "gdn_t", [128, L * 128])
    cmF = sb(top, "cmF", [128, 6 * 128])
    identB = sb(top, "identB", [128, 128], BF16)
    onesB = sb(top, "onesB", [128, 128], BF16)
    ab_tok = sb(top, "ab_tok", [128, NCH * 8])
    bconst = Buf("const")
    bmod = Buf("mod")
    bab = Buf("abtok")
    identF = cmF[:, 0:128]; uincl = cmF[:, 128:256]; masksl = cmF[:, 256:384]
    negm = cmF[:, 384:512]; strictu = cmF[:, 512:640]; onesF = cmF[:, 640:768]

    for (t_, src) in ((gm, gm_in), (gf, gf_in), (gfin, gfin_in), (caw, caw_in), (cfw, cfw_in), (cfb, cfb_in),
                      (cfg, cfg_in), (cfbb, cfbb_in), (dnw, dnw_in), (alog, alog_in), (dtb, dtb_in),
                      (gdn, gdn_in), (cmF, cm_in)):
        s.dma("sp", t_[:], src, writes=[bconst])
    s.dma("pool", identB[:], cm_in[:, 0:128], writes=[bconst])
    s.dma("pool", onesB[:], cm_in[:, 640:768], writes=[bconst])
    bxs = Buf("xs")
    for j in range(8):
        s.dma("sp", xs[j * 128:(j + 1) * 128, :], xT_in[j * 128:(j + 1) * 128, :], writes=[bxs])

    with ExitStack() as es:
        cT = sb(es, "cT_t", [128, 8]); cact = sb(es, "cact", [128, 8])
        wt = [sb(es, "wada%d" % i, [128, 2048]) for i in range(6)]
        bwt = [Buf() for _ in range(6)]
        modrow = sb(es, "modrow", [1, 6 * D]); brow = sb(es, "brow", [1, 6 * D])
        one11 = sb(es, "one11", [1, 1])
        bc, bmr, bbr = Buf(), Buf(), Buf()
        s.dma("sp", cT[:], cT_in, writes=[bc])
        s.add("act", lambda e: e.activation(cact[:], cT[:], AF.Silu), reads=[bc], writes=[bc])
        s.add("dve", lambda e: e.memset(one11[:], 1.0), writes=[bc])
        it = 0
        for l in range(depth):
            s.dma("sp", brow[:], b_ada[l:l + 1, :], reads=[], writes=[bbr])
            for cg in range(3):
                for k in range(8):
                    w_ = wt[it % 6]; bw_ = bwt[it % 6]; it += 1
                    s.dma(("sp", "act", "pool")[it % 3], w_[:], w_ada[l, k * 128:(k + 1) * 128, cg * 2048:(cg + 1) * 2048],
                          writes=[bw_])
                    for i in range(4):
                        s.add("pe", lambda e, i=i, w_=w_, k=k: e.matmul(
                            ps[i][0:1, :], cact[:, k:k + 1], w_[:, i * 512:(i + 1) * 512],
                            start=(k == 0), stop=(k == 7)), reads=[bw_, bc], writes=[bps[i]])
                for i in range(4):
                    c0 = cg * 2048 + i * 512
                    s.add("dve", lambda e, i=i, c0=c0: e.tensor_tensor(
                        modrow[0:1, c0:c0 + 512], ps[i][0:1, :], brow[0:1, c0:c0 + 512], ALU.add),
                        reads=[bps[i], bbr], writes=[bmr])
            for j in range(48):
                s.add("pe", lambda e, j=j: e.matmul(ps[4][:, j:j + 1], modrow[0:1, j * 128:(j + 1) * 128],
                                                     one11[0:1, 0:1], start=True, stop=True),
                      reads=[bmr, bc], writes=[bps[4]])
            s.add("act", lambda e, l=l: e.activation(modT[:, l, :], ps[4][:, 0:48], AF.Copy),
                  reads=[bps[4]], writes=[bmod])
            s.add("dve", lambda e, l=l: e.scalar_tensor_tensor(
                A1[:, l * 8:(l + 1) * 8], modT[:, l, 8:16], 1.0, gm[:, l * 8:(l + 1) * 8], ALU.add, ALU.mult),
                reads=[bmod, bconst], writes=[bmod])
            s.add("dve", lambda e, l=l: e.scalar_tensor_tensor(
                A2[:, l * 8:(l + 1) * 8], modT[:, l, 32:40], 1.0, gf[:, l * 8:(l + 1) * 8], ALU.add, ALU.mult),
                reads=[bmod, bconst], writes=[bmod])
        s.barrier()
        s.flush()

    def B1(l, j): return modT[:, l, j:j + 1]
    def G1(l, j): return modT[:, l, 16 + j:17 + j]
    def B2(l, j): return modT[:, l, 24 + j:25 + j]
    def G2(l, j): return modT[:, l, 40 + j:41 + j]

    xs_v = xs.rearrange("(j p) t -> p j t", p=128)
    ys_v = ys.rearrange("(j p) t -> p j t", p=128)
    out_v = outT.rearrange("(j p) t -> p j t", p=128)

    def norm_group(es_tiles, l, g, Acol, Bcol, hdst, hbuf, xt, bxt, load=True, pn=5):
        sq, bsq, rs, brs = es_tiles
        sl = slice(g * 512, (g + 1) * 512)
        if load:
            s.dma("sp", xt[:], xs_v[:, :, sl], reads=[bxs], writes=[bxt])
        s.add("act", lambda e: e.activation(sq[:], xt[:], AF.Square), reads=[bxt], writes=[bsq])
        for j in range(8):
            s.add("pe", lambda e, j=j: e.matmul(ps[pn][:], onesB[:], sq[:, j, :], start=(j == 0), stop=(j == 7)),
                  reads=[bsq, bconst], writes=[bps[pn]])
        s.add("act", lambda e: e.activation(rs[:], ps[pn][:], AF.Sqrt, bias=EPS, scale=1.0 / D),
              reads=[bps[pn]], writes=[brs])
        s.add("dve", lambda e: e.reciprocal(rs[:], rs[:]), reads=[brs], writes=[brs])
        return sl

    def ck(name):
        if stop == name:
            s.muted = True
    ck('pro')
    for l in range(depth):
        with ExitStack() as es:
            hT = sb(es, "hT", [128, 8, T], BF16); bh = Buf("hT")
            with ExitStack() as es2:
                xt = [sb(es2, "xt%d" % i, [128, 8, 512]) for i in range(2)]; bxt = [Buf(), Buf()]
                sqs = [sb(es2, "sq", [128, 8, 512], BF16) for _ in range(2)]; bsqs = [Buf(), Buf()]
                rss = [sb(es2, "rs", [128, 512]) for _ in range(2)]; brss = [Buf(), Buf()]
                for g in range(NG):
                    x_ = xt[g % 2]; bx_ = bxt[g % 2]
                    sq = sqs[g % 2]; bsq = bsqs[g % 2]; rs = rss[g % 2]; brs = brss[g % 2]
                    sl = norm_group((sq, bsq, rs, brs), l, g, None, None, None, None, x_, bx_, pn=4 + g % 2)
                    for j in range(8):
                        s.add("dve", lambda e, j=j, x_=x_, rs=rs: e.scalar_tensor_tensor(
                            x_[:, j, :], x_[:, j, :], A1[:, l * 8 + j:l * 8 + j + 1], rs[:], ALU.mult, ALU.mult),
                            reads=[bx_, brs, bmod], writes=[bx_])
                        s.add("act", lambda e, j=j, x_=x_, sl=sl: e.activation(
                            hT[:, j, sl], x_[:, j, :], AF.Identity, bias=B1(l, j), scale=1.0),
                            reads=[bx_, bmod], writes=[bh])
                s.barrier(); s.flush()
            ck('n1')

            wv = w_in[l].rearrange("(k p) c -> p k c", p=128)

            def proj(wtile, bw, g, pidx):
                for k in range(8):
                    s.add("pe", lambda e, k=k: e.matmul(ps[pidx][:], wtile[:, k, :], hT[:, k, g * 512:(g + 1) * 512],
                                                         start=(k == 0), stop=(k == 7)),
                          reads=[bw, bh], writes=[bps[pidx]])

            with ExitStack() as es2:
                wa = [sb(es2, "wa%d" % i, [128, 8, 128], BF16) for i in range(3)]; bwa = [Buf() for _ in range(3)]
                ppad = sb(es2, "ppad", [128, T + 2]); bpp = Buf()
                abt = sb(es2, "abt", [128, T]); bab_ = Buf()
                acc = sb(es2, "acc", [128, T]); bacc = Buf()
                ybf = sb(es2, "ybf", [128, T], BF16); bybf = Buf()
                tmpc = sb(es2, "tmpc", [128, 512]); btc = Buf()
                s.add("pool", lambda e: e.memset(ppad[:, 0:2], 0.0), writes=[bpp])
                for cc in range(2):
                    for i, base in enumerate((256, 512, 0)):
                        c0 = base + cc * 128
                        s.dma("pool", wa[i][:], wv[:, :, c0:c0 + 128], writes=[bwa[i]])
                    for g in range(NG):
                        sl = slice(g * 512, (g + 1) * 512)
                        proj(wa[0], bwa[0], g, 0)
                        proj(wa[1], bwa[1], g, 1)
                        proj(wa[2], bwa[2], g, 2)
                        s.add("act", lambda e: e.activation(tmpc[:], ps[0][:], AF.Copy), reads=[bps[0]], writes=[btc])
                        s.add("dve", lambda e, g=g: e.tensor_tensor(ppad[:, 2 + g * 512:2 + (g + 1) * 512], tmpc[:],
                                                                  ps[1][:], ALU.mult),
                              reads=[btc, bps[1]], writes=[bpp])
                        s.add("act", lambda e, sl=sl: e.activation(abt[:, sl], ps[2][:], AF.Copy),
                              reads=[bps[2]], writes=[bab_])
                    for sg in range(4):
                        o = sg * 1024
                        wc = lambda k: caw[:, l * 6 + cc * 3 + k:l * 6 + cc * 3 + k + 1]
                        s.add("dve", lambda e, o=o, w0=wc(0): e.tensor_scalar(
                            acc[:, o:o + 1024], ppad[:, o:o + 1024], w0, None, ALU.mult),
                            reads=[bpp, bconst], writes=[bacc])
                        for k in (1, 2):
                            s.add("dve", lambda e, o=o, k=k, wk=wc(k): e.scalar_tensor_tensor(
                                acc[:, o:o + 1024], ppad[:, o + k:o + k + 1024], wk, acc[:, o:o + 1024],
                                ALU.mult, ALU.add), reads=[bpp, bacc, bconst], writes=[bacc])
                        s.add("pool", lambda e, o=o: e.tensor_tensor(ybf[:, o:o + 1024], acc[:, o:o + 1024],
                                                                  abt[:, o:o + 1024], ALU.mult),
                              reads=[bacc, bab_], writes=[bybf])
                    s.dma("sp", ys[cc * 128:(cc + 1) * 128, :], ybf[:], reads=[bybf], writes=[])
                s.barrier(); s.flush()
            ck('mixA')

            with ExitStack() as es2:
                wb = [sb(es2, "wb%d" % i, [128, 8, 128], BF16) for i in range(2)]; bwb = [Buf(), Buf()]
                upad = sb(es2, "upad", [128, 2, T + 30], BF16); bup = Buf()
                dg = sb(es2, "dg", [128, 62, 128], BF16); bdg = Buf()
                v32 = sb(es2, "v32", [128, 2, T]); bv32 = Buf()
                sgt = sb(es2, "sgt", [128, 512]); bsg = Buf()
                sq2s = [sb(es2, "sq2", [128, 2, 512], BF16) for _ in range(2)]; bsq2s = [Buf(), Buf()]
                rs2s = [sb(es2, "rs2", [128, 512]) for _ in range(2)]; brs2s = [Buf(), Buf()]
                t2s = [sb(es2, "t2", [128, 512]) for _ in range(4)]; bt2s = [Buf() for _ in range(4)]
                yb2 = [sb(es2, "yb2_%d" % i, [128, 2, 512], BF16) for i in range(2)]; byb2 = [Buf(), Buf()]
                for cc in range(2):
                    s.add("pool", lambda e, cc=cc: e.memset(upad[:, cc, 0:30], 0.0), writes=[bup])
                    for k in range(31):
                        s.add("pool", lambda e, cc=cc, k=k: e.tensor_scalar(
                            dg[:, cc * 31 + k, :], identB[:], cfw[:, l * 62 + cc * 31 + k:l * 62 + cc * 31 + k + 1],
                            None, ALU.mult), reads=[bconst], writes=[bdg])
                for cc in range(2):
                    s.dma("pool", wb[0][:], wv[:, :, 768 + cc * 128:768 + (cc + 1) * 128], writes=[bwb[0]])
                    s.dma("pool", wb[1][:], wv[:, :, 1024 + cc * 128:1024 + (cc + 1) * 128], writes=[bwb[1]])
                    for g in range(NG):
                        proj(wb[0], bwb[0], g, 0)
                        proj(wb[1], bwb[1], g, 1)
                        s.add("act", lambda e: e.activation(sgt[:], ps[1][:], AF.Sigmoid), reads=[bps[1]], writes=[bsg])
                        s.add("dve", lambda e, cc=cc, g=g: e.tensor_tensor(
                            upad[:, cc, 30 + g * 512:30 + (g + 1) * 512], sgt[:], ps[0][:], ALU.mult),
                            reads=[bsg, bps[0]], writes=[bup])
                for g in range(NG):
                    sl = slice(g * 512, (g + 1) * 512)
                    for cc in range(2):
                        pi = 2 + cc
                        for k in range(31):
                            s.add("pe", lambda e, cc=cc, k=k, g=g, pi=pi: e.matmul(
                                ps[pi][:], dg[:, cc * 31 + k, :], upad[:, cc, g * 512 + k:g * 512 + k + 512],
                                start=(k == 0), stop=(k == 30)), reads=[bdg, bup], writes=[bps[pi]])
                        s.add("act", lambda e, cc=cc, sl=sl, pi=pi: e.activation(
                            v32[:, cc, sl], ps[pi][:], AF.Identity, bias=cfb[:, l * 2 + cc:l * 2 + cc + 1], scale=1.0),
                            reads=[bps[pi], bconst], writes=[bv32])
                    pm, pv = (4, 5) if g % 2 == 0 else (0, 1)
                    sq2 = sq2s[g % 2]; bsq2 = bsq2s[g % 2]; rs2 = rs2s[g % 2]; brs2 = brs2s[g % 2]
                    for cc in range(2):
                        s.add("pe", lambda e, cc=cc, sl=sl, pm=pm: e.matmul(ps[pm][:], onesF, v32[:, cc, sl],
                                                                  start=(cc == 0), stop=(cc == 1)),
                              reads=[bv32, bconst], writes=[bps[pm]])
                    for cc in range(2):
                        s.add("dve", lambda e, cc=cc, sl=sl, pm=pm: e.scalar_tensor_tensor(
                            v32[:, cc, sl], ps[pm][:], -1.0 / 256, v32[:, cc, sl], ALU.mult, ALU.add),
                            reads=[bps[pm], bv32], writes=[bv32])
                    s.add("act", lambda e, sl=sl, sq2=sq2: e.activation(sq2[:], v32[:, :, sl], AF.Square),
                          reads=[bv32], writes=[bsq2])
                    for cc in range(2):
                        s.add("pe", lambda e, cc=cc, pv=pv, sq2=sq2: e.matmul(ps[pv][:], onesB[:], sq2[:, cc, :],
                                                             start=(cc == 0), stop=(cc == 1)),
                              reads=[bsq2, bconst], writes=[bps[pv]])
                    s.add("act", lambda e, pv=pv, rs2=rs2: e.activation(rs2[:], ps[pv][:], AF.Sqrt, bias=1e-5, scale=1.0 / 256),
                          reads=[bps[pv]], writes=[brs2])
                    s.add("dve", lambda e, rs2=rs2: e.reciprocal(rs2[:], rs2[:]), reads=[brs2], writes=[brs2])
                    yb_ = yb2[g % 2]; by_ = byb2[g % 2]
                    for cc in range(2):
                        t2 = t2s[(g % 2) * 2 + cc]; bt2 = bt2s[(g % 2) * 2 + cc]
                        s.add("dve", lambda e, cc=cc, sl=sl, t2=t2, rs2=rs2: e.tensor_tensor(t2[:], v32[:, cc, sl], rs2[:], ALU.mult),
                              reads=[bv32, brs2], writes=[bt2])
                        s.add("act", lambda e, cc=cc, yb_=yb_, t2=t2: e.activation(
                            yb_[:, cc, :], t2[:], AF.Silu, bias=cfbb[:, l * 2 + cc:l * 2 + cc + 1],
                            scale=cfg[:, l * 2 + cc:l * 2 + cc + 1]), reads=[bt2, bconst], writes=[by_])
                    s.dma("sp", ys_v[:, 2:4, sl], yb_[:], reads=[by_], writes=[])
                s.barrier(); s.flush()
            ck('mixB')

            with ExitStack() as es2:
                wc_ = [sb(es2, "wc%d" % i, [128, 8, 128], BF16) for i in range(2)]; bwc = [Buf(), Buf()]
                cpad = sb(es2, "cpad", [128, T + 3], BF16); bcp = Buf()
                dg4 = sb(es2, "dg4", [128, 4, 128], BF16); bdg4 = Buf()
                sils = [sb(es2, "sil", [128, 512]) for _ in range(2)]; bsils = [Buf(), Buf()]
                sq3s = [sb(es2, "sq3", [128, 512], BF16) for _ in range(2)]; bsq3s = [Buf(), Buf()]
                rs3s = [sb(es2, "rs3", [128, 512]) for _ in range(2)]; brs3s = [Buf(), Buf()]
                ob = [sb(es2, "ob%d" % i, [128, T], BF16) for i in range(2)]; bob = [Buf(), Buf()]
                wab = sb(es2, "wab", [128, 8, 8], BF16); bwab = Buf()
                s.add("pool", lambda e: e.memset(cpad[:, 0:3], 0.0), writes=[bcp])
                jobs = []
                for h in range(4):
                    jobs.append(("k", 1792 + h * 128, 4 + h, 0 + h))
                    jobs.append(("q", 1280 + h * 128, 0 + h, 4 + h))
                    jobs.append(("v", 2304 + h * 128, 8 + h, 8 + h))
                    jobs.append(("z", 2816 + h * 128, None, 12 + h))
                for ji, (ty, c0, ci, dj) in enumerate(jobs):
                    w_ = wc_[ji % 2]; bw_ = bwc[ji % 2]
                    o_ = ob[ji % 2]; bo_ = bob[ji % 2]
                    s.dma("pool", w_[:], wv[:, :, c0:c0 + 128], writes=[bw_])
                    if ty == "z":
                        for g in range(NG):
                            sl = slice(g * 512, (g + 1) * 512)
                            proj(w_, bw_, g, g % 2)
                            s.add("act", lambda e, sl=sl, g=g, o_=o_: e.activation(o_[:, sl], ps[g % 2][:], AF.Silu),
                                  reads=[bps[g % 2]], writes=[bo_])
                    else:
                        for k in range(4):
                            s.add("pool", lambda e, k=k, ci=ci: e.tensor_scalar(
                                dg4[:, k, :], identB[:], dnw[:, l * 48 + ci * 4 + k:l * 48 + ci * 4 + k + 1],
                                None, ALU.mult), reads=[bconst], writes=[bdg4])
                        for g in range(NG):
                            proj(w_, bw_, g, g % 2)
                            s.add("act", lambda e, g=g: e.activation(cpad[:, 3 + g * 512:3 + (g + 1) * 512],
                                                                    ps[g % 2][:], AF.Copy),
                                  reads=[bps[g % 2]], writes=[bcp])
                        for g in range(NG):
                            sl = slice(g * 512, (g + 1) * 512)
                            pi = 2 + g % 2
                            for k in range(4):
                                s.add("pe", lambda e, k=k, g=g, pi=pi: e.matmul(
                                    ps[pi][:], dg4[:, k, :], cpad[:, g * 512 + k:g * 512 + k + 512],
                                    start=(k == 0), stop=(k == 3)), reads=[bdg4, bcp], writes=[bps[pi]])
                            if ty == "v":
                                s.add("act", lambda e, sl=sl, pi=pi, o_=o_: e.activation(o_[:, sl], ps[pi][:], AF.Silu),
                                      reads=[bps[pi]], writes=[bo_])
                            else:
                                sil = sils[g % 2]; bsil = bsils[g % 2]; sq3 = sq3s[g % 2]; bsq3 = bsq3s[g % 2]
                                rs3 = rs3s[g % 2]; brs3 = brs3s[g % 2]; pq = 4 + g % 2
                                s.add("act", lambda e, pi=pi, sil=sil: e.activation(sil[:], ps[pi][:], AF.Silu),
                                      reads=[bps[pi]], writes=[bsil])
                                s.add("pool", lambda e, sil=sil, sq3=sq3: e.tensor_tensor(sq3[:], sil[:], sil[:], ALU.mult),
                                      reads=[bsil], writes=[bsq3])
                                s.add("pe", lambda e, sq3=sq3, pq=pq: e.matmul(ps[pq][:], onesB[:], sq3[:], start=True, stop=True),
                                      reads=[bsq3, bconst], writes=[bps[pq]])
                                s.add("act", lambda e, rs3=rs3, pq=pq: e.activation(rs3[:], ps[pq][:], AF.Sqrt, bias=EPS, scale=1.0),
                                      reads=[bps[pq]], writes=[brs3])
                                s.add("dve", lambda e, rs3=rs3: e.reciprocal(rs3[:], rs3[:]), reads=[brs3], writes=[brs3])
                                sc = 128.0 ** -0.5 if ty == "q" else 1.0
                                s.add("dve", lambda e, sl=sl, sc=sc, o_=o_, sil=sil, rs3=rs3: e.scalar_tensor_tensor(
                                    o_[:, sl], sil[:], sc, rs3[:], ALU.mult, ALU.mult),
                                    reads=[bsil, brs3], writes=[bo_])
                    s.dma("sp", dnin[:, :, dj, :], o_[:].rearrange("p (n t) -> p n t", t=128), reads=[bo_], writes=[])
                s.dma("pool", wab[:], wv[:, :, 3328:3336], writes=[bwab])
                for n in range(NCH):
                    for k in range(8):
                        s.add("pe", lambda e, n=n, k=k: e.matmul(
                            ps[5][:, n * 8:(n + 1) * 8], hT[:, k, n * 128:(n + 1) * 128], wab[:, k, :],
                            start=(k == 0), stop=(k == 7)), reads=[bwab, bh], writes=[bps[5]])
                s.add("act", lambda e: e.activation(ab_tok[:], ps[5][:, 0:NCH * 8], AF.Copy),
                      reads=[bps[5]], writes=[bab])
                s.barrier(); s.flush()

        ck('dnin')
        with ExitStack() as es:
            def t_(name, shape, dtype=F32):
                return sb(es, name, shape, dtype)
            mU4 = t_("mU4", [128, 7, 512], BF16); mL4 = t_("mL4", [128, 7, 512], BF16)
            i4 = t_("i4", [128, 512], BF16); su4 = t_("su4", [128, 512])
            bm = Buf("masks")
            s.dma("pool", mU4[:], lev_in.rearrange("p (a b) -> p a b", b=512), writes=[bm])
            s.dma("pool", mL4[:], levT_in.rearrange("p (a b) -> p a b", b=512), writes=[bm])
            s.dma("pool", i4[:], i4_in, writes=[bm])
            s.dma("sp", su4[:], su4_in, writes=[bm])
            beta = t_("beta", [128, NCH, 4]); gtok = t_("gtok", [128, NCH, 4])
            xsp = t_("xsp", [128, NCH, 4]); axs = t_("axs", [128, NCH, 4]); lg = t_("lg", [128, NCH, 4])
            nega = t_("nega", [128, 128])
            gam = t_("gam", [128, 128]); eg = t_("eg", [128, 128]); negeg = t_("negeg", [128, 128])
            decl = t_("decl", [128, 128]); eglast = t_("eglast", [128, 128])
            bsc = Buf("scal")
            abv = ab_tok[:].rearrange("p (n c) -> p n c", c=8)
            dtbv = dtb[:, l * 128:(l + 1) * 128].rearrange("p (n c) -> p n c", c=4)
            s.add("act", lambda e: e.activation(beta[:], abv[:, :, 4:8], AF.Sigmoid), reads=[bab], writes=[bsc])
            s.add("dve", lambda e: e.tensor_tensor(xsp[:], abv[:, :, 0:4], dtbv, ALU.add), reads=[bab, bconst], writes=[bsc])
            s.add("act", lambda e: e.activation(axs[:], xsp[:], AF.Abs), reads=[bsc], writes=[bsc])
            s.add("act", lambda e: e.activation(axs[:], axs[:], AF.Exp, scale=-1.0), reads=[bsc], writes=[bsc])
            s.add("act", lambda e: e.activation(lg[:], axs[:], AF.Ln, bias=1.0, scale=1.0), reads=[bsc], writes=[bsc])
            s.add("dve", lambda e: e.tensor_scalar(xsp[:], xsp[:], 0.0, None, ALU.max), reads=[bsc], writes=[bsc])
            s.add("dve", lambda e: e.tensor_tensor(xsp[:], xsp[:], lg[:], ALU.add), reads=[bsc], writes=[bsc])
            s.add("act", lambda e: e.activation(nega[:], alog[:, l * 128:(l + 1) * 128], AF.Exp), reads=[bconst], writes=[bsc])
            s.add("dve", lambda e: e.scalar_tensor_tensor(
                gtok[:].rearrange("p n c -> p (n c)"), xsp[:].rearrange("p n c -> p (n c)"), -1.0, nega[:],
                ALU.mult, ALU.mult), reads=[bsc], writes=[bsc])
            gflat = gtok[:].rearrange("p n c -> p (n c)")
            bflat = beta[:].rearrange("p n c -> p (n c)")
            s.add("pe", lambda e: e.matmul(ps[0][:, 0:128], uincl, gflat, start=True, stop=True),
                  reads=[bsc, bconst], writes=[bps[0]])
            s.add("pe", lambda e: e.matmul(ps[1][:, 0:128], onesF, gflat, start=True, stop=True),
                  reads=[bsc, bconst], writes=[bps[1]])
            s.add("act", lambda e: e.activation(gam[:], ps[0][:, 0:128], AF.Copy), reads=[bps[0]], writes=[bsc])
            s.add("act", lambda e: e.activation(eg[:], ps[0][:, 0:128], AF.Exp), reads=[bps[0]], writes=[bsc])
            s.add("dve", lambda e: e.tensor_scalar(negeg[:], eg[:], -1.0, None, ALU.mult), reads=[bsc], writes=[bsc])
            s.add("dve", lambda e: e.tensor_tensor(decl[:], ps[1][:, 0:128], gam[:], ALU.subtract),
                  reads=[bps[1], bsc], writes=[bsc])
            s.add("act", lambda e: e.activation(decl[:], decl[:], AF.Exp), reads=[bsc], writes=[bsc])
            s.add("act", lambda e: e.activation(eglast[:], ps[1][:, 0:128], AF.Exp), reads=[bps[1]], writes=[bsc])

            ck('D0')
            import os as _os
            NCH_RUN = int(_os.environ.get('DBG_NCH', NCH))
            NL = 3
            S32 = t_("S32", [128, 512]); bS = Buf()
            Sbf = t_("Sbf", [128, 512], BF16); bSb = Buf()
            s.add("dve", lambda e: e.memset(S32[:], 0.0), writes=[bS])
            s.add("pool", lambda e: e.memset(Sbf[:], 0.0), writes=[bSb])
            gd = gdn[:, l * 128:(l + 1) * 128]
            H = lambda a, h: a[:, h * 128:(h + 1) * 128]

            class Lane:
                pass
            lanes = []
            for li_ in range(NL):
                ln = Lane()
                ln.inp = t_("inp", [128, 16, 128], BF16); ln.binp = Buf()
                ln.Mt = t_("Mt", [128, 4, 128]); ln.bMt = Buf()
                ln.E = t_("E", [128, 512]); ln.bE = Buf()
                ln.Es = t_("Es", [128, 512]); ln.bEs = Buf()
                for nm in ("Pm", "Qm", "X", "Y", "R1", "QKm", "kdec", "rp", "vnew", "on", "yc"):
                    setattr(ln, nm, t_(nm, [128, 512], BF16)); setattr(ln, "b" + nm, Buf())
                ln.Qoff = t_("Qoff", [128, 6, 512], BF16); ln.bQoff = Buf()
                for nm in ("vtok", "o2s", "o_t"):
                    setattr(ln, nm, t_(nm, [128, 512])); setattr(ln, "b" + nm, Buf())
                ln.junk = t_("junk", [128, 128]); ln.bjunk = Buf()
                ln.ss = t_("ss", [128, 4]); ln.bss = Buf()
                ln.p = (ps[2 * li_], ps[2 * li_ + 1]); ln.bp = (bps[2 * li_], bps[2 * li_ + 1])
                if li_ % 2 == 0:
                    ln.pT = ps6b; ln.bpT = bps6
                else:
                    ln.pT = ps7b; ln.bpT = bps7
                lanes.append(ln)

            owner = {}

            def acq(me, banks):
                while any(owner.get(id(b_)) not in (None, me) for b_ in banks):
                    yield
                for b_ in banks:
                    owner[id(b_)] = me

            def rel(me, banks):
                for b_ in banks:
                    if owner.get(id(b_)) == me:
                        owner[id(b_)] = None

            def d1_gen(n, ln):
                ip = ln.inp; bip = ln.binp
                p0, p1 = ln.p; b0, b1 = ln.bp; pT = ln.pT; bT = ln.bpT
                col = lambda a, h: a[:, n * 4 + h:n * 4 + h + 1]
                s.dma("sp", ip[:], dnin[:, n, :, :], reads=[], writes=[bip])
                for h in range(4):
                    s.add("pool", lambda e, h=h, g_=col(gflat, h): e.tensor_scalar(ln.Mt[:, h, :], masksl, g_, None, ALU.mult),
                          reads=[bsc, bconst], writes=[ln.bMt])
                yield
                for h in range(4):
                    s.add("pe", lambda e, h=h: e.matmul(H(p0, h), ln.Mt[:, h, :], uincl, start=True, stop=False),
                          reads=[ln.bMt, bconst], writes=[b0])
                    s.add("pe", lambda e, h=h: e.matmul(H(p0, h), identF, negm, start=False, stop=True),
                          reads=[ln.bMt, bconst], writes=[b0])
                yield
                s.add("act", lambda e: e.activation(ln.E[:], p0[:], AF.Exp), reads=[b0], writes=[ln.bE])
                yield
                for h in range(4):
                    s.add("pe", lambda e, h=h: e.matmul(H(p0, h), ip[:, h, :], ip[:, h, :], start=True, stop=True),
                          reads=[bip], writes=[b0])
                for h in range(4):
                    s.add("pe", lambda e, h=h: e.matmul(H(p1, h), ip[:, h, :], ip[:, 4 + h, :], start=True, stop=True),
                          reads=[bip], writes=[b1])
                s.add("pool", lambda e: e.tensor_tensor(ln.Es[:], ln.E[:], su4[:], ALU.mult), reads=[ln.bE, bm], writes=[ln.bEs])
                yield
                for h in range(4):
                    s.add("dve", lambda e, h=h, b_=col(bflat, h): e.scalar_tensor_tensor(
                        H(ln.Pm, h), H(p0, h), b_, H(ln.Es, h), ALU.mult, ALU.mult),
                        reads=[b0, ln.bEs, bsc], writes=[ln.bPm])
                s.add("dve", lambda e: e.tensor_tensor(ln.QKm[:], p1[:], ln.E[:], ALU.mult), reads=[b1, ln.bE], writes=[ln.bQKm])
                yield
                yield from acq(n, (bT,))
                for h in range(4):
                    s.add("pe", lambda e, h=h: e.transpose(H(pT, h), H(ln.Pm, h), identB[:]), reads=[ln.bPm, bconst], writes=[bT])
                s.add("pool", lambda e: e.tensor_tensor(ln.X[:], ln.Pm[:], mU4[:, 0, :], ALU.mult), reads=[ln.bPm, bm], writes=[ln.bX])
                s.add("pool", lambda e: e.tensor_tensor(ln.X[:], i4[:], ln.X[:], ALU.subtract), reads=[ln.bX, bm], writes=[ln.bX])
                yield
                s.add("act", lambda e: e.activation(ln.Qm[:], pT[:, 0:512], AF.Copy), reads=[bT], writes=[ln.bQm])
                rel(n, (bT,))
                yield
                s.add("pool", lambda e: e.tensor_tensor(ln.Y[:], ln.Qm[:], mL4[:, 0, :], ALU.mult), reads=[ln.bQm, bm], writes=[ln.bY])
                s.add("pool", lambda e: e.tensor_tensor(ln.Y[:], i4[:], ln.Y[:], ALU.subtract), reads=[ln.bY, bm], writes=[ln.bY])
                for li in range(1, 7):
                    s.add("pool", lambda e, li=li: e.tensor_tensor(ln.Qoff[:, li - 1, :], ln.Qm[:], mL4[:, li, :], ALU.mult),
                          reads=[ln.bQm, bm], writes=[ln.bQoff])
                yield
                for li in range(1, 7):
                    last = (li == 6)
                    for h in range(4):
                        s.add("pe", lambda e, h=h, li=li: e.matmul(H(p0, h), ln.Qoff[:, li - 1, h * 128:(h + 1) * 128], H(ln.X, h),
                                                                 start=True, stop=True), reads=[ln.bQoff, ln.bX], writes=[b0])
                    yield
                    s.add("act", lambda e: e.activation(ln.R1[:], p0[:], AF.Copy), reads=[b0], writes=[ln.bR1])
                    yield
                    for h in range(4):
                        s.add("pe", lambda e, h=h: e.matmul(H(p0, h), H(ln.Y, h), H(ln.R1, h), start=True, stop=True),
                              reads=[ln.bY, ln.bR1], writes=[b0])
                    if not last:
                        for h in range(4):
                            s.add("pe", lambda e, h=h: e.matmul(H(p1, h), H(ln.R1, h), H(ln.Y, h), start=True, stop=True),
                                  reads=[ln.bY, ln.bR1], writes=[b1])
                    yield
                    s.add("dve", lambda e: e.tensor_tensor(ln.X[:], ln.X[:], p0[:], ALU.subtract), reads=[ln.bX, b0], writes=[ln.bX])
                    if not last:
                        s.add("dve", lambda e: e.tensor_tensor(ln.Y[:], ln.Y[:], p1[:], ALU.subtract),
                              reads=[ln.bY, b1], writes=[ln.bY])
                    yield

            def d2_gen(n, ln):
                ip = ln.inp; bip = ln.binp
                p0, p1 = ln.p; b0, b1 = ln.bp; pT = ln.pT; bT = ln.bpT
                col = lambda a, h: a[:, n * 4 + h:n * 4 + h + 1]
                yield from acq(n, (bT,))
                for h in range(4):
                    s.add("pe", lambda e, h=h: e.transpose(H(pT, h), ip[:, h, :], identB[:]), reads=[bip, bconst], writes=[bT])
                for h in range(4):
                    s.add("pe", lambda e, h=h: e.transpose(H(pT, 4 + h), ip[:, 8 + h, :], identB[:]), reads=[bip, bconst], writes=[bT])
                for h in range(4):
                    s.add("pe", lambda e, h=h: e.matmul(H(p0, h), ip[:, h, :], H(Sbf, h), start=True, stop=True),
                          reads=[bip, bSb], writes=[b0])
                for h in range(4):
                    s.add("pe", lambda e, h=h: e.matmul(H(p1, h), ip[:, 4 + h, :], H(Sbf, h), start=True, stop=True),
                          reads=[bip, bSb], writes=[b1])
                yield
                s.add("act", lambda e: e.activation(ln.vtok[:], pT[:, 512:1024], AF.Copy), reads=[bT], writes=[ln.bvtok])
                for h in range(4):
                    s.add("dve", lambda e, h=h, d_=col(decl, h): e.tensor_scalar(H(ln.kdec, h), H(pT, h), d_, None, ALU.mult),
                          reads=[bT, bsc], writes=[ln.bkdec])
                rel(n, (bT,))
                yield
                for h in range(4):
                    s.add("dve", lambda e, h=h, ne_=col(negeg, h): e.scalar_tensor_tensor(
                        H(ln.rp, h), H(p0, h), ne_, H(ln.vtok, h), ALU.mult, ALU.add),
                        reads=[b0, ln.bvtok, bsc], writes=[ln.brp])
                s.add("act", lambda e: e.activation(ln.o2s[:], p1[:], AF.Copy), reads=[b1], writes=[ln.bo2s])
                yield
                for h in range(4):
                    s.add("pe", lambda e, h=h: e.matmul(H(p0, h), H(ln.X, h), H(ln.rp, h), start=True, stop=True),
                          reads=[ln.bX, ln.brp], writes=[b0])
                yield
                for h in range(4):
                    s.add("act", lambda e, h=h, b_=col(bflat, h): e.activation(H(ln.vnew, h), H(p0, h), AF.Identity, bias=0.0, scale=b_),
                          reads=[b0, bsc], writes=[ln.bvnew])
                yield
                for h in range(4):
                    s.add("pe", lambda e, h=h: e.matmul(H(p0, h), H(ln.kdec, h), H(ln.vnew, h), start=True, stop=True),
                          reads=[ln.bkdec, ln.bvnew], writes=[b0])
                for h in range(4):
                    s.add("pe", lambda e, h=h: e.matmul(H(p1, h), H(ln.QKm, h), H(ln.vnew, h), start=True, stop=True),
                          reads=[ln.bQKm, ln.bvnew], writes=[b1])
                yield
                for h in range(4):
                    s.add("dve", lambda e, h=h, el_=col(eglast, h): e.scalar_tensor_tensor(
                        H(S32, h), H(S32, h), el_, H(p0, h), ALU.mult, ALU.add),
                        reads=[bS, b0, bsc], writes=[bS])
                yield
                s.add("pool", lambda e: e.tensor_copy(Sbf[:], S32[:]), reads=[bS], writes=[bSb])
                for h in range(4):
                    s.add("dve", lambda e, h=h, eg_=col(eg, h): e.scalar_tensor_tensor(
                        H(ln.o_t, h), H(ln.o2s, h), eg_, H(p1, h), ALU.mult, ALU.add),
                        reads=[b1, ln.bo2s, bsc], writes=[ln.bo_t])
                s.add("pool", lambda e: e.memset(ln.ss[:], 0.0), writes=[ln.bss])
                yield

            def d3_gen(n, ln):
                ip = ln.inp; bip = ln.binp
                pT = ln.pT; bT = ln.bpT
                for h in range(4):
                    s.add("act", lambda e, h=h: e.activation(ln.junk[:], H(ln.o_t, h), AF.Square, accum_out=ln.ss[:, h:h + 1]),
                          reads=[ln.bo_t, ln.bss], writes=[ln.bjunk, ln.bss])
                yield
                s.add("act", lambda e: e.activation(ln.ss[:], ln.ss[:], AF.Sqrt, bias=EPS, scale=1.0 / 128), reads=[ln.bss], writes=[ln.bss])
                yield
                s.add("dve", lambda e: e.reciprocal(ln.ss[:], ln.ss[:]), reads=[ln.bss], writes=[ln.bss])
                yield
                for h in range(4):
                    s.add("dve", lambda e, h=h: e.scalar_tensor_tensor(H(ln.on, h), H(ln.o_t, h), ln.ss[:, h:h + 1], gd, ALU.mult, ALU.mult),
                          reads=[ln.bo_t, ln.bss, bconst], writes=[ln.bon])
                yield
                yield from acq(n, (bT,))
                for h in range(4):
                    s.add("pe", lambda e, h=h: e.transpose(H(pT, 4 + h), H(ln.on, h), identB[:]), reads=[ln.bon, bconst], writes=[bT])
                yield
                s.add("dve", lambda e: e.tensor_tensor(
                    ln.yc[:], pT[:, 512:1024], ip[:, 12:16, :].rearrange("p a b -> p (a b)"), ALU.mult),
                    reads=[bT, bip], writes=[ln.byc])
                rel(n, (bT,))
                s.dma("sp", ys_v[:, 4:8, n * 128:(n + 1) * 128], ln.yc[:].rearrange("p (a b) -> p a b", b=128),
                      reads=[ln.byc], writes=[])
                yield

            def chain(n, ln):
                yield from d1_gen(n, ln)
                while d2_turn[0] != n:
                    yield
                yield from d2_gen(n, ln)
                d2_turn[0] = n + 1
                yield from d3_gen(n, ln)

            d2_turn = [0]
            active = {}
            nxt = 0
            while nxt < NCH_RUN or active:
                while nxt < NCH_RUN and (nxt % NL) not in active:
                    active[nxt % NL] = chain(nxt, lanes[nxt % NL]); nxt += 1
                    break
                for k in sorted(active.keys(), key=lambda kk: kk):
                    try:
                        next(active[k])
                    except StopIteration:
                        del active[k]
            s.barrier(); s.flush()

        ck("mixy%d" % l)
        esF = ExitStack()
        wfi = sb(esF, "wfi", [128, 8, 2 * DFF], BF16); bwfi = Buf()
        wfiv = w_fi[l].rearrange("(k p) c -> p k c", p=128)
        with ExitStack() as es:
            wo = sb(es, "wo", [128, 8, D], BF16); bwo = Buf()
            wov = w_out[l].rearrange("(k p) c -> p k c", p=128)
            for k in range(8):
                s.dma("pool", wo[:, k, :], wov[:, k, :], writes=[bwo])
            for k in range(8):
                for hh in range(2):
                    s.dma("pool", wfi[:, k, hh * DFF:(hh + 1) * DFF], wfiv[:, k, hh * DFF:(hh + 1) * DFF], writes=[bwfi])
            yt = [sb(es, "yt%d" % i, [128, 8, 512], BF16) for i in range(2)]; byt = [Buf(), Buf()]
            xt = [sb(es, "xo%d" % i, [128, 8, 512]) for i in range(2)]; bxt = [Buf(), Buf()]
            for g in range(NG):
                sl = slice(g * 512, (g + 1) * 512)
                y_ = yt[g % 2]; by_ = byt[g % 2]; x_ = xt[g % 2]; bx_ = bxt[g % 2]
                s.dma("sp", y_[:], ys_v[:, :, sl], writes=[by_])
                s.dma("sp", x_[:], xs_v[:, :, sl], writes=[bx_])
                for oc in range(8):
                    pi = oc % 4
                    for k in range(8):
                        s.add("pe", lambda e, k=k, oc=oc, pi=pi, y_=y_: e.matmul(
                            ps[pi][:], wo[:, k, oc * 128:(oc + 1) * 128], y_[:, k, :], start=(k == 0), stop=(k == 7)),
                            reads=[bwo, by_], writes=[bps[pi]])
                    s.add("dve", lambda e, oc=oc, pi=pi, x_=x_: e.scalar_tensor_tensor(
                        x_[:, oc, :], ps[pi][:], G1(l, oc), x_[:, oc, :], ALU.mult, ALU.add),
                        reads=[bps[pi], bx_, bmod], writes=[bx_])
                s.dma("sp", xs_v[:, :, sl], x_[:], reads=[bx_], writes=[])
            s.barrier(); s.flush()
        ck("mix%d" % l)

        with ExitStack() as es:
            wfo = sb(es, "wfo", [128, 22, D], BF16); bwfo = Buf()
            wfov = w_fo[l].rearrange("(k p) c -> p k c", p=128)
            for k in range(22):
                s.dma("pool", wfo[:, k, :], wfov[:, k, :], writes=[bwfo])
            xt = sb(es, "xf", [128, 8, 512]); bxt = Buf()
            sq = sb(es, "sqf", [128, 8, 512], BF16); bsq = Buf()
            rs = sb(es, "rsf", [128, 512]); brs = Buf()
            hT2 = sb(es, "hT2", [128, 8, 512], BF16); bh2 = Buf()
            aT = sb(es, "aT", [128, 22, 512], BF16); baT = Buf()
            sgf = [sb(es, "sgf%d" % i, [128, 512]) for i in range(2)]; bsgf = [Buf(), Buf()]
            for g in range(NG):
                sl = norm_group((sq, bsq, rs, brs), l, g, None, None, None, None, xt, bxt)
                for j in range(8):
                    s.add("dve", lambda e, j=j: e.scalar_tensor_tensor(
                        sq[:, j, :], xt[:, j, :], A2[:, l * 8 + j:l * 8 + j + 1], rs[:], ALU.mult, ALU.mult),
                        reads=[bxt, brs, bmod, bsq], writes=[bsq])
                    s.add("act", lambda e, j=j: e.activation(
                        hT2[:, j, :], sq[:, j, :], AF.Identity, bias=B2(l, j), scale=1.0),
                        reads=[bsq, bmod], writes=[bh2])
                for j in range(22):
                    pg, pu = (0, 1) if j % 2 == 0 else (2, 3)
                    for k in range(8):
                        s.add("pe", lambda e, k=k, j=j, pg=pg: e.matmul(
                            ps[pg][:], wfi[:, k, j * 128:(j + 1) * 128], hT2[:, k, :], start=(k == 0), stop=(k == 7)),
                            reads=[bwfi, bh2], writes=[bps[pg]])
                    for k in range(8):
                        s.add("pe", lambda e, k=k, j=j, pu=pu: e.matmul(
                            ps[pu][:], wfi[:, k, DFF + j * 128:DFF + (j + 1) * 128], hT2[:, k, :],
                            start=(k == 0), stop=(k == 7)), reads=[bwfi, bh2], writes=[bps[pu]])
                    sg_ = sgf[j % 2]; bsg_ = bsgf[j % 2]
                    s.add("act", lambda e, pg=pg, sg_=sg_: e.activation(sg_[:], ps[pg][:], AF.Silu),
                          reads=[bps[pg]], writes=[bsg_])
                    s.add("dve", lambda e, j=j, pu=pu, sg_=sg_: e.tensor_tensor(aT[:, j, :], sg_[:], ps[pu][:], ALU.mult),
                          reads=[bsg_, bps[pu]], writes=[baT])
                for oc in range(8):
                    pi = 4 + oc % 2
                    for k in range(22):
                        s.add("pe", lambda e, k=k, oc=oc, pi=pi: e.matmul(
                            ps[pi][:], wfo[:, k, oc * 128:(oc + 1) * 128], aT[:, k, :], start=(k == 0), stop=(k == 21)),
                            reads=[bwfo, baT], writes=[bps[pi]])
                    s.add("dve", lambda e, oc=oc, pi=pi: e.scalar_tensor_tensor(
                        xt[:, oc, :], ps[pi][:], G2(l, oc), xt[:, oc, :], ALU.mult, ALU.add),
                        reads=[bps[pi], bxt, bmod], writes=[bxt])
                s.dma("sp", xs_v[:, :, sl], xt[:], reads=[bxt], writes=[bxs])
            s.barrier(); s.flush()
        esF.close()
        ck("ffn%d" % l)

    s.muted = False
    dumpy = stop in ('mixA', 'mixB') or (stop is not None and stop.startswith('mixy'))
    with ExitStack() as es:
        xt = [sb(es, "xz%d" % i, [128, 8, 512]) for i in range(2)]; bxt = [Buf(), Buf()]
        sq = sb(es, "sqz", [128, 8, 512], BF16); bsq = Buf()
        rs = sb(es, "rsz", [128, 512]); brs = Buf()
        if dumpy:
            yt = [sb(es, "yz%d" % i, [128, 8, 512], BF16) for i in range(2)]; byt = [Buf(), Buf()]
        for g in range(NG):
            x_ = xt[g % 2]; bx_ = bxt[g % 2]
            if stop is None:
                sl = norm_group((sq, bsq, rs, brs), 0, g, None, None, None, None, x_, bx_)
                for j in range(8):
                    s.add("dve", lambda e, j=j, x_=x_: e.scalar_tensor_tensor(
                        x_[:, j, :], x_[:, j, :], gfin[:, j:j + 1], rs[:], ALU.mult, ALU.mult),
                        reads=[bx_, brs, bconst], writes=[bx_])
            elif dumpy:
                sl = slice(g * 512, (g + 1) * 512)
                y_ = yt[g % 2]; by_ = byt[g % 2]
                s.dma("sp", y_[:], ys_v[:, :, sl], writes=[by_])
                s.add("dve", lambda e, x_=x_, y_=y_: e.tensor_copy(x_[:], y_[:]), reads=[by_], writes=[bx_])
            else:
                sl = slice(g * 512, (g + 1) * 512)
                s.dma("sp", x_[:], xs_v[:, :, sl], writes=[bx_])
            s.dma("sp", out_v[:, :, sl], x_[:], reads=[bx_], writes=[])
        s.flush(final=True)
    top.close()
    s.close()
    nc._nops = s.nops
    return nc


def _pp(a):
    a = np.asarray(a, np.float32)
    lead = a.shape[:-1]
    n = a.shape[-1] // 128
    a = a.reshape(lead + (n, 128))
    a = np.moveaxis(a, -1, 0)
    return np.ascontiguousarray(a.reshape(128, -1))


def make_in_maps(inputs, depth=L, ncores=NCORES):
    f = lambda k: np.asarray(inputs[k], np.float32)
    m = _levels_masks()
    common = {
        "w_ada": f("w_ada")[:depth], "b_ada": f("b_ada")[:depth], "w_in": f("w_in")[:depth], "w_out": f("w_out")[:depth],
        "w_ffn_in": f("w_ffn_in")[:depth], "w_ffn_out": f("w_ffn_out")[:depth],
        "gm": _pp(f("norm_mix_g")), "gf": _pp(f("norm_ffn_g")), "gfin": _pp(f("final_norm_g")),
        "caw": _pp(np.transpose(f("conv_a_w"), (0, 2, 1)).reshape(L, 2, 128, 3).transpose(0, 1, 3, 2)),
        "cfw": _pp(np.transpose(f("conf_dw_w"), (0, 2, 1)).reshape(L, 2, 128, 31).transpose(0, 1, 3, 2)),
        "cfb": _pp(f("conf_dw_b")), "cfg": _pp(f("conf_ln_g")), "cfbb": _pp(f("conf_ln_b")),
        "dnw": _pp(np.transpose(f("dn_conv_w"), (0, 2, 1)).reshape(L, 12, 128, 4).transpose(0, 1, 3, 2)),
        "alog": np.ascontiguousarray(np.broadcast_to(np.tile(f("dn_a_log"), (1, NCH)).reshape(1, L * 128), (128, L * 128))),
        "dtb": np.ascontiguousarray(np.broadcast_to(np.tile(f("dn_dt_bias"), (1, NCH)).reshape(1, L * 128), (128, L * 128))),
        "gdn": np.ascontiguousarray(np.broadcast_to(f("dn_norm_g").reshape(1, L * 128), (128, L * 128))),
        "cmask": np.ascontiguousarray(np.concatenate(
            [m["ident"], m["uincl"], m["masksl"], m["negm"], m["strictu"], m["ones"]], axis=1)),
        "levU4": np.ascontiguousarray(np.concatenate([np.tile(m["levU"][i], (1, 4)) for i in range(7)], axis=1)),
        "levL4": np.ascontiguousarray(np.concatenate([np.tile(m["levU"][i].T, (1, 4)) for i in range(7)], axis=1)),
        "ident4": np.ascontiguousarray(np.tile(m["ident"], (1, 4))),
        "strictu4": np.ascontiguousarray(np.tile(m["strictu"], (1, 4))),
    }
    x = f("x"); c = f("c")
    maps = []
    for core in range(ncores):
        b = core % 4
        d = dict(common)
        d["xT"] = np.ascontiguousarray(x[b].T)
        d["cT"] = np.ascontiguousarray(c[b].reshape(8, 128).T)
        maps.append(d)
    return maps


_NC_CACHE = {}


def kernel(**inputs):
    if "nc" not in _NC_CACHE:
        _NC_CACHE["nc"] = build_program()
    nc = _NC_CACHE["nc"]
    in_maps = make_in_maps(inputs)
    res = run_bass_kernel_spmd(nc, in_maps, core_ids=list(range(NCORES)))
    out = np.stack([np.asarray(res.results[b]["outT"], np.float32).T for b in range(4)], axis=0)
    return np.ascontiguousarray(out)
```

```python
from contextlib import ExitStack

import numpy as np
import concourse.bass as bass
import concourse.mybir as mybir
from concourse.bass_utils import run_bass_kernel_spmd

F32 = mybir.dt.float32
BF16 = mybir.dt.bfloat16
AF = mybir.ActivationFunctionType
ALU = mybir.AluOpType


class Buf:
    __slots__ = ("name", "w", "r", "excl")

    def __init__(self, name="", excl=False):
        self.name = name
        self.w = None
        self.r = []
        self.excl = excl


class _Op:
    __slots__ = ("eng", "fn", "deps", "is_dma", "sem", "val", "signal", "emitted")


class Sched:
    CENG = ("pe", "act", "dve", "pool")
    ENG = ("pe", "act", "dve", "pool", "sp")
    DMAQ = ("sp", "pool", "act")

    def __init__(self, nc, strict=True, dma_k=6):
        self.nc = nc
        self.strict = strict
        self.K = dma_k
        self.es = ExitStack()
        self.csem = {e: self.es.enter_context(nc.semaphore("c_" + e)) for e in self.CENG}
        self.ccount = {e: 0 for e in self.CENG}
        self.dsem = {q: [self.es.enter_context(nc.semaphore("d_%s%d" % (q, i))) for i in range(dma_k)]
                     for q in self.DMAQ}
        self.dcount = {q: 0 for q in self.DMAQ}
        self.dhist = {q: [] for q in self.DMAQ}
        self.known = {e: {} for e in self.ENG}
        self.pending = {e: [] for e in self.ENG}
        self.last = {e: None for e in self.CENG}
        self.barrier_ops = []
        self.nops = 0
        self.muted = False

    def _mk(self, eng, fn, reads, writes, is_dma):
        op = _Op()
        op.eng = eng
        op.fn = fn
        op.is_dma = is_dma
        op.signal = is_dma
        op.sem = None
        op.val = None
        op.emitted = False
        deps = list(self.barrier_ops)
        for b in reads:
            if b.w is not None:
                deps.append(b.w)
            if b.excl:
                deps.extend(r for r in b.r if r.eng != eng)
        for b in writes:
            if b.w is not None:
                deps.append(b.w)
            deps.extend(b.r)
        op.deps = deps
        for b in reads:
            b.r.append(op)
        for b in writes:
            b.w = op
            b.r = []
        self.pending[eng].append(op)
        self.nops += 1
        return op

    def add(self, eng, fn, reads=(), writes=()):
        if self.muted:
            return None
        op = self._mk(eng, fn, reads, writes, False)
        self.last[eng] = op
        return op

    def dma(self, q, out, in_, reads=(), writes=(), **kw):
        if self.muted:
            return None
        def fn(e, out=out, in_=in_, kw=kw):
            return e.dma_start(out=out, in_=in_, **kw)
        op = self._mk(q, fn, reads, writes, True)
        i = self.dcount[q]
        self.dcount[q] += 1
        op.sem = self.dsem[q][i % self.K]
        op.val = 16 * (i // self.K + 1)
        if i >= self.K:
            op.deps.append(self.dhist[q][i - self.K])
        self.dhist[q].append(op)
        return op

    def barrier(self):
        if self.muted:
            return
        ops = [self.last[e] for e in self.CENG if self.last[e] is not None]
        for q in self.DMAQ:
            ops.extend(self.dhist[q][-self.K:])
        self.barrier_ops = ops

    def _need(self, op, dep):
        if dep.is_dma:
            return True
        if dep.eng != op.eng:
            return True
        if op.eng == "pe":
            return False
        if op.is_dma:
            return True
        return self.strict

    def flush(self, final=False):
        nc = self.nc
        if not final and not any(self.pending[e] for e in self.ENG):
            return
        for e in self.ENG:
            for op in self.pending[e]:
                for d in op.deps:
                    if not d.is_dma and not d.emitted and self._need(op, d):
                        d.signal = True
        for e in self.CENG:
            comp = [o for o in self.pending[e] if not o.is_dma]
            if comp:
                comp[-1].signal = True
        for e in self.CENG:
            c = self.ccount[e]
            comp = [o for o in self.pending[e] if not o.is_dma]
            for o in comp:
                if o.signal:
                    c += 1
                    o.val = c
                o.sem = self.csem[e]
            self.ccount[e] = c
            nxt = None
            for o in reversed(comp):
                if o.signal:
                    nxt = o.val
                else:
                    o.val = nxt
        getter = {"pe": "tensor", "act": "scalar", "dve": "vector", "pool": "gpsimd", "sp": "sync"}

        def run(e, eng):
            known = self.known[e]
            for op in self.pending[e]:
                waits = {}
                for d in op.deps:
                    if not self._need(op, d):
                        continue
                    key = id(d.sem)
                    if known.get(key, 0) >= d.val:
                        continue
                    if key not in waits or waits[key][1] < d.val:
                        waits[key] = (d.sem, d.val)
                for key, (sem, val) in waits.items():
                    eng.wait_ge(sem, val)
                    known[key] = val
                inst = op.fn(eng)
                if op.signal:
                    inst.then_inc(op.sem, 16 if op.is_dma else 1)
                op.emitted = True
                op.fn = None
            if final and e == "sp":
                for q in self.DMAQ:
                    n = self.dcount[q]
                    for j in range(self.K):
                        cnt = len(range(j, n, self.K))
                        if cnt:
                            eng.wait_ge(self.dsem[q][j], 16 * cnt)

        with nc.Block() as blk:
            for e in self.ENG:
                if not self.pending[e] and not (final and e == "sp"):
                    continue
                getattr(blk, getter[e])(lambda eng, e=e: run(e, eng))
        self.pending = {e: [] for e in self.ENG}

    def close(self):
        self.es.close()


D = 1024
T = 4096
L = 4
NG = T // 512
NCH = T // 128
DFF = 2816
INC = 3336
EPS = 1e-6
NCORES = 8


class _Stop(Exception):
    pass


def _levels_masks():
    i = np.arange(128)
    s_, c_ = np.meshgrid(i, i, indexing="ij")
    out = {}
    out["ident"] = (s_ == c_).astype(np.float32)
    out["uincl"] = (s_ <= c_).astype(np.float32)
    out["masksl"] = (s_ > c_).astype(np.float32)
    out["negm"] = np.where(c_ < s_, -30000.0, 0.0).astype(np.float32)
    out["strictu"] = (s_ < c_).astype(np.float32)
    out["ones"] = np.ones((128, 128), np.float32)
    lev = []
    for b in (1, 2, 4, 8, 16, 32, 64):
        m = ((s_ // (2 * b)) == (c_ // (2 * b))) & ((s_ // b) % 2 == 0) & ((c_ // b) % 2 == 1)
        lev.append(m.astype(np.float32))
    out["levU"] = np.stack(lev)
    return out


def build_program(depth=L, stop=None):
    nc = bass.Bass("TRN2", target_bir_lowering=False)
    dt = nc.dram_tensor

    def din(name, shape, dtype=F32):
        return dt(name, list(shape), dtype, kind="ExternalInput").ap()

    xT_in = din("xT", [D, T])
    cT_in = din("cT", [128, 8])
    w_ada = din("w_ada", [depth, D, 6 * D])
    b_ada = din("b_ada", [depth, 6 * D])
    w_in = din("w_in", [depth, D, INC])
    w_out = din("w_out", [depth, D, D])
    w_fi = din("w_ffn_in", [depth, D, 2 * DFF])
    w_fo = din("w_ffn_out", [depth, DFF, D])
    gm_in = din("gm", [128, L * 8])
    gf_in = din("gf", [128, L * 8])
    gfin_in = din("gfin", [128, 8])
    caw_in = din("caw", [128, L * 2 * 3])
    cfw_in = din("cfw", [128, L * 2 * 31])
    cfb_in = din("cfb", [128, L * 2])
    cfg_in = din("cfg", [128, L * 2])
    cfbb_in = din("cfbb", [128, L * 2])
    dnw_in = din("dnw", [128, L * 12 * 4])
    alog_in = din("alog", [128, L * 128])
    dtb_in = din("dtb", [128, L * 128])
    gdn_in = din("gdn", [128, L * 128])
    cm_in = din("cmask", [128, 6 * 128])
    lev_in = din("levU4", [128, 7 * 512])
    levT_in = din("levL4", [128, 7 * 512])
    i4_in = din("ident4", [128, 512])
    su4_in = din("strictu4", [128, 512])
    outT = dt("outT", [D, T], F32, kind="ExternalOutput").ap()

    xs = dt("xs", [D, T], F32, kind="Internal").ap()
    ys = dt("ys", [D, T], BF16, kind="Internal").ap()
    dnin = dt("dnin", [128, NCH, 16, 128], BF16, kind="Internal").ap()

    import os as _os2
    s = Sched(nc, strict=(_os2.environ.get("MK_STRICT", "1") == "1"))
    top = ExitStack()

    _cnt = [0]

    def sb(es, name, shape, dtype=F32):
        _cnt[0] += 1
        return es.enter_context(nc.sbuf_tensor("%s_%d" % (name, _cnt[0]), list(shape), dtype))

    ps = [top.enter_context(nc.psum_tensor("ps%d" % i, [128, 512], F32)) for i in range(6)]
    ps6b = top.enter_context(nc.psum_tensor("ps6b", [128, 1024], BF16))
    ps7b = top.enter_context(nc.psum_tensor("ps7b", [128, 1024], BF16))
    bps = [Buf("ps%d" % i, excl=True) for i in range(6)]
    bps6, bps7 = Buf("ps6b", excl=True), Buf("ps7b", excl=True)

    modT = sb(top, "modT", [128, L, 48])
    A1 = sb(top, "A1", [128, L * 8]); A2 = sb(top, "A2", [128, L * 8])
    gm = sb(top, "gm_t", [128, L * 8]); gf = sb(top, "gf_t", [128, L * 8]); gfin = sb(top, "gfin_t", [128, 8])
    caw = sb(top, "caw_t", [128, L * 6]); cfw = sb(top, "cfw_t", [128, L * 62])
    cfb = sb(top, "cfb_t", [128, L * 2]); cfg = sb(top, "cfg_t", [128, L * 2]); cfbb = sb(top, "cfbb_t", [128, L * 2])
    dnw = sb(top, "dnw_t", [128, L * 48])
    alog = sb(top, "alog_t", [128, L * 128]); dtb = sb(top, "dtb_t", [128, L * 128]); gdn = sb(top, "gdn_t", [128, L * 128])
    cmF = sb(top, "cmF", [128, 6 * 128])
    identB = sb(top, "identB", [128, 128], BF16)
    onesB = sb(top, "onesB", [128, 128], BF16)
    ab_tok = sb(top, "ab_tok", [128, NCH * 8])
    bconst = Buf("const")
    bmod = Buf("mod")
    bab = Buf("abtok")
    identF = cmF[:, 0:128]; uincl = cmF[:, 128:256]; masksl = cmF[:, 256:384]
    negm = cmF[:, 384:512]; strictu = cmF[:, 512:640]; onesF = cmF[:, 640:768]

    for (t_, src) in ((gm, gm_in), (gf, gf_in), (gfin, gfin_in), (caw, caw_in), (cfw, cfw_in), (cfb, cfb_in),
                      (cfg, cfg_in), (cfbb, cfbb_in), (dnw, dnw_in), (alog, alog_in), (dtb, dtb_in),
                      (gdn, gdn_in), (cmF, cm_in)):
        s.dma("sp", t_[:], src, writes=[bconst])
    s.dma("pool", identB[:], cm_in[:, 0:128], writes=[bconst])
    s.dma("pool", onesB[:], cm_in[:, 640:768], writes=[bconst])
    bxs = Buf("xs")
    for j in range(8):
        s.dma("sp", xs[j * 128:(j + 1) * 128, :], xT_in[j * 128:(j + 1) * 128, :], writes=[bxs])

    with ExitStack() as es:
        cT = sb(es, "cT_t", [128, 8]); cact = sb(es, "cact", [128, 8])
        wt = [sb(es, "wada%d" % i, [128, 2048]) for i in range(6)]
        bwt = [Buf() for _ in range(6)]
        modrow = sb(es, "modrow", [1, 6 * D]); brow = sb(es, "brow", [1, 6 * D])
        one11 = sb(es, "one11", [1, 1])
        bc, bmr, bbr = Buf(), Buf(), Buf()
        s.dma("sp", cT[:], cT_in, writes=[bc])
        s.add("act", lambda e: e.activation(cact[:], cT[:], AF.Silu), reads=[bc], writes=[bc])
        s.add("dve", lambda e: e.memset(one11[:], 1.0), writes=[bc])
        it = 0
        for l in range(depth):
            s.dma("sp", brow[:], b_ada[l:l + 1, :], reads=[], writes=[bbr])
            for cg in range(3):
                for k in range(8):
                    w_ = wt[it % 6]; bw_ = bwt[it % 6]; it += 1
                    s.dma(("sp", "act", "pool")[it % 3], w_[:], w_ada[l, k * 128:(k + 1) * 128, cg * 2048:(cg + 1) * 2048],
                          writes=[bw_])
                    for i in range(4):
                        s.add("pe", lambda e, i=i, w_=w_, k=k: e.matmul(
                            ps[i][0:1, :], cact[:, k:k + 1], w_[:, i * 512:(i + 1) * 512],
                            start=(k == 0), stop=(k == 7)), reads=[bw_, bc], writes=[bps[i]])
                for i in range(4):
                    c0 = cg * 2048 + i * 512
                    s.add("dve", lambda e, i=i, c0=c0: e.tensor_tensor(
                        modrow[0:1, c0:c0 + 512], ps[i][0:1, :], brow[0:1, c0:c0 + 512], ALU.add),
                        reads=[bps[i], bbr], writes=[bmr])
            for j in range(48):
                s.add("pe", lambda e, j=j: e.matmul(ps[4][:, j:j + 1], modrow[0:1, j * 128:(j + 1) * 128],
                                                     one11[0:1, 0:1], start=True, stop=True),
                      reads=[bmr, bc], writes=[bps[4]])
            s.add("act", lambda e, l=l: e.activation(modT[:, l, :], ps[4][:, 0:48], AF.Copy),
                  reads=[bps[4]], writes=[bmod])
            s.add("dve", lambda e, l=l: e.scalar_tensor_tensor(
                A1[:, l * 8:(l + 1) * 8], modT[:, l, 8:16], 1.0, gm[:, l * 8:(l + 1) * 8], ALU.add, ALU.mult),
                reads=[bmod, bconst], writes=[bmod])
            s.add("dve", lambda e, l=l: e.scalar_tensor_tensor(
                A2[:, l * 8:(l + 1) * 8], modT[:, l, 32:40], 1.0, gf[:, l * 8:(l + 1) * 8], ALU.add, ALU.mult),
                reads=[bmod, bconst], writes=[bmod])
        s.barrier()
        s.flush()

    def B1(l, j): return modT[:, l, j:j + 1]
    def G1(l, j): return modT[:, l, 16 + j:17 + j]
    def B2(l, j): return modT[:, l, 24 + j:25 + j]
    def G2(l, j): return modT[:, l, 40 + j:41 + j]

    xs_v = xs.rearrange("(j p) t -> p j t", p=128)
    ys_v = ys.rearrange("(j p) t -> p j t", p=128)
    out_v = outT.rearrange("(j p) t -> p j t", p=128)

    def norm_group(es_tiles, l, g, Acol, Bcol, hdst, hbuf, xt, bxt, load=True, pn=5):
        sq, bsq, rs, brs = es_tiles
        sl = slice(g * 512, (g + 1) * 512)
        if load:
            s.dma("sp", xt[:], xs_v[:, :, sl], reads=[bxs], writes=[bxt])
        s.add("act", lambda e: e.activation(sq[:], xt[:], AF.Square), reads=[bxt], writes=[bsq])
        for j in range(8):
            s.add("pe", lambda e, j=j: e.matmul(ps[pn][:], onesB[:], sq[:, j, :], start=(j == 0), stop=(j == 7)),
                  reads=[bsq, bconst], writes=[bps[pn]])
        s.add("act", lambda e: e.activation(rs[:], ps[pn][:], AF.Sqrt, bias=EPS, scale=1.0 / D),
              reads=[bps[pn]], writes=[brs])
        s.add("dve", lambda e: e.reciprocal(rs[:], rs[:]), reads=[brs], writes=[brs])
        return sl

    def ck(name):
        if stop == name:
            s.muted = True
    ck('pro')
    for l in range(depth):
        with ExitStack() as es:
            hT = sb(es, "hT", [128, 8, T], BF16); bh = Buf("hT")
            with ExitStack() as es2:
                xt = [sb(es2, "xt%d" % i, [128, 8, 512]) for i in range(3)]; bxt = [Buf(), Buf(), Buf()]
                sqs = [sb(es2, "sq", [128, 8, 512], BF16) for _ in range(2)]; bsqs = [Buf(), Buf()]
                rss = [sb(es2, "rs", [128, 512]) for _ in range(2)]; brss = [Buf(), Buf()]
                def n1_stats(g):
                    x_ = xt[g % 3]; bx_ = bxt[g % 3]
                    sq = sqs[g % 2]; bsq = bsqs[g % 2]; rs = rss[g % 2]; brs = brss[g % 2]
                    norm_group((sq, bsq, rs, brs), l, g, None, None, None, None, x_, bx_, pn=4 + g % 2)

                def n1_apply(g):
                    x_ = xt[g % 3]; bx_ = bxt[g % 3]; rs = rss[g % 2]; brs = brss[g % 2]
                    sl = slice(g * 512, (g + 1) * 512)
                    for j in range(8):
                        s.add("dve", lambda e, j=j, x_=x_, rs=rs: e.scalar_tensor_tensor(
                            x_[:, j, :], x_[:, j, :], A1[:, l * 8 + j:l * 8 + j + 1], rs[:], ALU.mult, ALU.mult),
                            reads=[bx_, brs, bmod], writes=[bx_])
                        s.add("act", lambda e, j=j, x_=x_, sl=sl: e.activation(
                            hT[:, j, sl], x_[:, j, :], AF.Identity, bias=B1(l, j), scale=1.0),
                            reads=[bx_, bmod], writes=[bh])

                n1_stats(0)
                for g in range(NG):
                    if g + 1 < NG:
                        n1_stats(g + 1)
                    n1_apply(g)
                s.barrier(); s.flush()
            ck('n1')

            wv = w_in[l].rearrange("(k p) c -> p k c", p=128)

            def proj(wtile, bw, g, pidx):
                for k in range(8):
                    s.add("pe", lambda e, k=k: e.matmul(ps[pidx][:], wtile[:, k, :], hT[:, k, g * 512:(g + 1) * 512],
                                                         start=(k == 0), stop=(k == 7)),
                          reads=[bw, bh], writes=[bps[pidx]])

            with ExitStack() as es2:
                wa = [sb(es2, "wa%d" % i, [128, 8, 128], BF16) for i in range(3)]; bwa = [Buf() for _ in range(3)]
                ppad = sb(es2, "ppad", [128, T + 2]); bpp = Buf()
                abt = sb(es2, "abt", [128, T]); bab_ = Buf()
                acc = sb(es2, "acc", [128, T]); bacc = Buf()
                ybf = sb(es2, "ybf", [128, T], BF16); bybf = Buf()
                tmpc = sb(es2, "tmpc", [128, 512]); btc = Buf()
                s.add("pool", lambda e: e.memset(ppad[:, 0:2], 0.0), writes=[bpp])
                for cc in range(2):
                    for i, base in enumerate((256, 512, 0)):
                        c0 = base + cc * 128
                        s.dma("pool", wa[i][:], wv[:, :, c0:c0 + 128], writes=[bwa[i]])
                    for g in range(NG):
                        sl = slice(g * 512, (g + 1) * 512)
                        proj(wa[0], bwa[0], g, 0)
                        proj(wa[1], bwa[1], g, 1)
                        proj(wa[2], bwa[2], g, 2)
                        s.add("act", lambda e: e.activation(tmpc[:], ps[0][:], AF.Copy), reads=[bps[0]], writes=[btc])
                        s.add("dve", lambda e, g=g: e.tensor_tensor(ppad[:, 2 + g * 512:2 + (g + 1) * 512], tmpc[:],
                                                                  ps[1][:], ALU.mult),
                              reads=[btc, bps[1]], writes=[bpp])
                        s.add("act", lambda e, sl=sl: e.activation(abt[:, sl], ps[2][:], AF.Copy),
                              reads=[bps[2]], writes=[bab_])
                    for sg in range(4):
                        o = sg * 1024
                        wc = lambda k: caw[:, l * 6 + cc * 3 + k:l * 6 + cc * 3 + k + 1]
                        s.add("dve", lambda e, o=o, w0=wc(0): e.tensor_scalar(
                            acc[:, o:o + 1024], ppad[:, o:o + 1024], w0, None, ALU.mult),
                            reads=[bpp, bconst], writes=[bacc])
                        for k in (1, 2):
                            s.add("dve", lambda e, o=o, k=k, wk=wc(k): e.scalar_tensor_tensor(
                                acc[:, o:o + 1024], ppad[:, o + k:o + k + 1024], wk, acc[:, o:o + 1024],
                                ALU.mult, ALU.add), reads=[bpp, bacc, bconst], writes=[bacc])
                        s.add("pool", lambda e, o=o: e.tensor_tensor(ybf[:, o:o + 1024], acc[:, o:o + 1024],
                                                                  abt[:, o:o + 1024], ALU.mult),
                              reads=[bacc, bab_], writes=[bybf])
                    s.dma("sp", ys[cc * 128:(cc + 1) * 128, :], ybf[:], reads=[bybf], writes=[])
                s.barrier(); s.flush()
            ck('mixA')

            with ExitStack() as es2:
                wb = [sb(es2, "wb%d" % i, [128, 8, 128], BF16) for i in range(2)]; bwb = [Buf(), Buf()]
                upad = sb(es2, "upad", [128, 2, T + 30], BF16); bup = Buf()
                dg = sb(es2, "dg", [128, 62, 128], BF16); bdg = Buf()
                v32 = sb(es2, "v32", [128, 2, T]); bv32 = Buf()
                sgt = sb(es2, "sgt", [128, 512]); bsg = Buf()
                sq2s = [sb(es2, "sq2", [128, 2, 512], BF16) for _ in range(2)]; bsq2s = [Buf(), Buf()]
                rs2s = [sb(es2, "rs2", [128, 512]) for _ in range(2)]; brs2s = [Buf(), Buf()]
                t2s = [sb(es2, "t2", [128, 512]) for _ in range(4)]; bt2s = [Buf() for _ in range(4)]
                yb2 = [sb(es2, "yb2_%d" % i, [128, 2, 512], BF16) for i in range(2)]; byb2 = [Buf(), Buf()]
                for cc in range(2):
                    s.add("pool", lambda e, cc=cc: e.memset(upad[:, cc, 0:30], 0.0), writes=[bup])
                    for k in range(31):
                        s.add("pool", lambda e, cc=cc, k=k: e.tensor_scalar(
                            dg[:, cc * 31 + k, :], identB[:], cfw[:, l * 62 + cc * 31 + k:l * 62 + cc * 31 + k + 1],
                            None, ALU.mult), reads=[bconst], writes=[bdg])
                for cc in range(2):
                    s.dma("pool", wb[0][:], wv[:, :, 768 + cc * 128:768 + (cc + 1) * 128], writes=[bwb[0]])
                    s.dma("pool", wb[1][:], wv[:, :, 1024 + cc * 128:1024 + (cc + 1) * 128], writes=[bwb[1]])
                    for g in range(NG):
                        proj(wb[0], bwb[0], g, 0)
                        proj(wb[1], bwb[1], g, 1)
                        s.add("act", lambda e: e.activation(sgt[:], ps[1][:], AF.Sigmoid), reads=[bps[1]], writes=[bsg])
                        s.add("dve", lambda e, cc=cc, g=g: e.tensor_tensor(
                            upad[:, cc, 30 + g * 512:30 + (g + 1) * 512], sgt[:], ps[0][:], ALU.mult),
                            reads=[bsg, bps[0]], writes=[bup])
                for g in range(NG):
                    sl = slice(g * 512, (g + 1) * 512)
                    for cc in range(2):
                        pi = 2 + cc
                        for k in range(31):
                            s.add("pe", lambda e, cc=cc, k=k, g=g, pi=pi: e.matmul(
                                ps[pi][:], dg[:, cc * 31 + k, :], upad[:, cc, g * 512 + k:g * 512 + k + 512],
                                start=(k == 0), stop=(k == 30)), reads=[bdg, bup], writes=[bps[pi]])
                        s.add("act", lambda e, cc=cc, sl=sl, pi=pi: e.activation(
                            v32[:, cc, sl], ps[pi][:], AF.Identity, bias=cfb[:, l * 2 + cc:l * 2 + cc + 1], scale=1.0),
                            reads=[bps[pi], bconst], writes=[bv32])
                    pm, pv = (4, 5) if g % 2 == 0 else (0, 1)
                    sq2 = sq2s[g % 2]; bsq2 = bsq2s[g % 2]; rs2 = rs2s[g % 2]; brs2 = brs2s[g % 2]
                    for cc in range(2):
                        s.add("pe", lambda e, cc=cc, sl=sl, pm=pm: e.matmul(ps[pm][:], onesF, v32[:, cc, sl],
                                                                  start=(cc == 0), stop=(cc == 1)),
                              reads=[bv32, bconst], writes=[bps[pm]])
                    for cc in range(2):
                        s.add("dve", lambda e, cc=cc, sl=sl, pm=pm: e.scalar_tensor_tensor(
                            v32[:, cc, sl], ps[pm][:], -1.0 / 256, v32[:, cc, sl], ALU.mult, ALU.add),
                            reads=[bps[pm], bv32], writes=[bv32])
                    s.add("act", lambda e, sl=sl, sq2=sq2: e.activation(sq2[:], v32[:, :, sl], AF.Square),
                          reads=[bv32], writes=[bsq2])
                    for cc in range(2):
                        s.add("pe", lambda e, cc=cc, pv=pv, sq2=sq2: e.matmul(ps[pv][:], onesB[:], sq2[:, cc, :],
                                                             start=(cc == 0), stop=(cc == 1)),
                              reads=[bsq2, bconst], writes=[bps[pv]])
                    s.add("act", lambda e, pv=pv, rs2=rs2: e.activation(rs2[:], ps[pv][:], AF.Sqrt, bias=1e-5, scale=1.0 / 256),
                          reads=[bps[pv]], writes=[brs2])
                    s.add("dve", lambda e, rs2=rs2: e.reciprocal(rs2[:], rs2[:]), reads=[brs2], writes=[brs2])
                    yb_ = yb2[g % 2]; by_ = byb2[g % 2]
                    for cc in range(2):
                        t2 = t2s[(g % 2) * 2 + cc]; bt2 = bt2s[(g % 2) * 2 + cc]
                        s.add("dve", lambda e, cc=cc, sl=sl, t2=t2, rs2=rs2: e.tensor_tensor(t2[:], v32[:, cc, sl], rs2[:], ALU.mult),
                              reads=[bv32, brs2], writes=[bt2])
                        s.add("act", lambda e, cc=cc, yb_=yb_, t2=t2: e.activation(
                            yb_[:, cc, :], t2[:], AF.Silu, bias=cfbb[:, l * 2 + cc:l * 2 + cc + 1],
                            scale=cfg[:, l * 2 + cc:l * 2 + cc + 1]), reads=[bt2, bconst], writes=[by_])
                    s.dma("sp", ys_v[:, 2:4, sl], yb_[:], reads=[by_], writes=[])
                s.barrier(); s.flush()
            ck('mixB')

            with ExitStack() as es2:
                wc_ = [sb(es2, "wc%d" % i, [128, 8, 128], BF16) for i in range(2)]; bwc = [Buf(), Buf()]
                cpads = [sb(es2, "cpad", [128, T + 3], BF16) for _ in range(2)]; bcps = [Buf(), Buf()]
                dg4s = [sb(es2, "dg4", [128, 4, 128], BF16) for _ in range(2)]; bdg4s = [Buf(), Buf()]
                silF = sb(es2, "silF", [128, T]); bsilg = [Buf() for _ in range(NG)]
                sqF = sb(es2, "sqF", [128, T], BF16); bsqg = [Buf() for _ in range(NG)]
                sils = [sb(es2, "sil", [128, 512]) for _ in range(2)]; bsils = [Buf(), Buf()]
                sq3s = [sb(es2, "sq3", [128, 512], BF16) for _ in range(2)]; bsq3s = [Buf(), Buf()]
                rs3s = [sb(es2, "rs3", [128, 512]) for _ in range(2)]; brs3s = [Buf(), Buf()]
                ob = [sb(es2, "ob%d" % i, [128, T], BF16) for i in range(2)]; bob = [Buf(), Buf()]
                wab = sb(es2, "wab", [128, 8, 8], BF16); bwab = Buf()
                for i_ in range(2):
                    s.add("pool", lambda e, i_=i_: e.memset(cpads[i_][:, 0:3], 0.0), writes=[bcps[i_]])
                jobs = []
                for h in range(4):
                    jobs.append(("k", 1792 + h * 128, 4 + h, 0 + h))
                    jobs.append(("q", 1280 + h * 128, 0 + h, 4 + h))
                    jobs.append(("v", 2304 + h * 128, 8 + h, 8 + h))
                    jobs.append(("z", 2816 + h * 128, None, 12 + h))
                for ji, (ty, c0, ci, dj) in enumerate(jobs):
                    w_ = wc_[ji % 2]; bw_ = bwc[ji % 2]
                    o_ = ob[ji % 2]; bo_ = bob[ji % 2]
                    s.dma("pool", w_[:], wv[:, :, c0:c0 + 128], writes=[bw_])
                    if ty == "z":
                        for g in range(NG):
                            sl = slice(g * 512, (g + 1) * 512)
                            proj(w_, bw_, g, g % 2)
                            s.add("act", lambda e, sl=sl, g=g, o_=o_: e.activation(o_[:, sl], ps[g % 2][:], AF.Silu),
                                  reads=[bps[g % 2]], writes=[bo_])
                    else:
                        cpad = cpads[ji % 2]; bcp = bcps[ji % 2]; dg4 = dg4s[ji % 2]; bdg4 = bdg4s[ji % 2]
                        for k in range(4):
                            s.add("pool", lambda e, k=k, ci=ci, dg4=dg4: e.tensor_scalar(
                                dg4[:, k, :], identB[:], dnw[:, l * 48 + ci * 4 + k:l * 48 + ci * 4 + k + 1],
                                None, ALU.mult), reads=[bconst], writes=[bdg4])
                        for g in range(NG):
                            proj(w_, bw_, g, g % 2)
                            s.add("act", lambda e, g=g, cpad=cpad: e.activation(cpad[:, 3 + g * 512:3 + (g + 1) * 512],
                                                                    ps[g % 2][:], AF.Copy),
                                  reads=[bps[g % 2]], writes=[bcp])
                        for g in range(NG):
                            sl = slice(g * 512, (g + 1) * 512)
                            pi = 2 + g % 2
                            for k in range(4):
                                s.add("pe", lambda e, k=k, g=g, pi=pi, dg4=dg4, cpad=cpad: e.matmul(
                                    ps[pi][:], dg4[:, k, :], cpad[:, g * 512 + k:g * 512 + k + 512],
                                    start=(k == 0), stop=(k == 3)), reads=[bdg4, bcp], writes=[bps[pi]])
                            if ty == "v":
                                s.add("act", lambda e, sl=sl, pi=pi, o_=o_: e.activation(o_[:, sl], ps[pi][:], AF.Silu),
                                      reads=[bps[pi]], writes=[bo_])
                            else:
                                s.add("act", lambda e, pi=pi, sl=sl: e.activation(silF[:, sl], ps[pi][:], AF.Silu),
                                      reads=[bps[pi]], writes=[bsilg[g]])
                                s.add("pool", lambda e, sl=sl: e.tensor_tensor(sqF[:, sl], silF[:, sl], silF[:, sl], ALU.mult),
                                      reads=[bsilg[g]], writes=[bsqg[g]])
                        if ty != "v":
                            for g in range(NG):
                                sl = slice(g * 512, (g + 1) * 512)
                                rs3 = rs3s[g % 2]; brs3 = brs3s[g % 2]; pq = 4 + g % 2
                                s.add("pe", lambda e, sl=sl, pq=pq: e.matmul(ps[pq][:], onesB[:], sqF[:, sl], start=True, stop=True),
                                      reads=[bsqg[g], bconst], writes=[bps[pq]])
                                s.add("act", lambda e, rs3=rs3, pq=pq: e.activation(rs3[:], ps[pq][:], AF.Sqrt, bias=EPS, scale=1.0),
                                      reads=[bps[pq]], writes=[brs3])
                                s.add("dve", lambda e, rs3=rs3: e.reciprocal(rs3[:], rs3[:]), reads=[brs3], writes=[brs3])
                                sc = 128.0 ** -0.5 if ty == "q" else 1.0
                                s.add("dve", lambda e, sl=sl, sc=sc, o_=o_, rs3=rs3: e.scalar_tensor_tensor(
                                    o_[:, sl], silF[:, sl], sc, rs3[:], ALU.mult, ALU.mult),
                                    reads=[bsilg[g], brs3], writes=[bo_])
                    s.dma("sp", dnin[:, :, dj, :], o_[:].rearrange("p (n t) -> p n t", t=128), reads=[bo_], writes=[])
                s.dma("pool", wab[:], wv[:, :, 3328:3336], writes=[bwab])
                for n in range(NCH):
                    for k in range(8):
                        s.add("pe", lambda e, n=n, k=k: e.matmul(
                            ps[5][:, n * 8:(n + 1) * 8], hT[:, k, n * 128:(n + 1) * 128], wab[:, k, :],
                            start=(k == 0), stop=(k == 7)), reads=[bwab, bh], writes=[bps[5]])
                s.add("act", lambda e: e.activation(ab_tok[:], ps[5][:, 0:NCH * 8], AF.Copy),
                      reads=[bps[5]], writes=[bab])
                s.barrier(); s.flush()

        ck('dnin')
        with ExitStack() as es:
            def t_(name, shape, dtype=F32):
                return sb(es, name, shape, dtype)
            mU4 = t_("mU4", [128, 7, 512], BF16); mL4 = t_("mL4", [128, 7, 512], BF16)
            i4 = t_("i4", [128, 512], BF16); su4 = t_("su4", [128, 512])
            bm = Buf("masks")
            s.dma("pool", mU4[:], lev_in.rearrange("p (a b) -> p a b", b=512), writes=[bm])
            s.dma("pool", mL4[:], levT_in.rearrange("p (a b) -> p a b", b=512), writes=[bm])
            s.dma("pool", i4[:], i4_in, writes=[bm])
            s.dma("sp", su4[:], su4_in, writes=[bm])
            beta = t_("beta", [128, NCH, 4]); gtok = t_("gtok", [128, NCH, 4])
            xsp = t_("xsp", [128, NCH, 4]); axs = t_("axs", [128, NCH, 4]); lg = t_("lg", [128, NCH, 4])
            nega = t_("nega", [128, 128])
            gam = t_("gam", [128, 128]); eg = t_("eg", [128, 128]); negeg = t_("negeg", [128, 128])
            decl = t_("decl", [128, 128]); eglast = t_("eglast", [128, 128])
            bsc = Buf("scal")
            abv = ab_tok[:].rearrange("p (n c) -> p n c", c=8)
            dtbv = dtb[:, l * 128:(l + 1) * 128].rearrange("p (n c) -> p n c", c=4)
            s.add("act", lambda e: e.activation(beta[:], abv[:, :, 4:8], AF.Sigmoid), reads=[bab], writes=[bsc])
            s.add("dve", lambda e: e.tensor_tensor(xsp[:], abv[:, :, 0:4], dtbv, ALU.add), reads=[bab, bconst], writes=[bsc])
            s.add("act", lambda e: e.activation(axs[:], xsp[:], AF.Abs), reads=[bsc], writes=[bsc])
            s.add("act", lambda e: e.activation(axs[:], axs[:], AF.Exp, scale=-1.0), reads=[bsc], writes=[bsc])
            s.add("act", lambda e: e.activation(lg[:], axs[:], AF.Ln, bias=1.0, scale=1.0), reads=[bsc], writes=[bsc])
            s.add("dve", lambda e: e.tensor_scalar(xsp[:], xsp[:], 0.0, None, ALU.max), reads=[bsc], writes=[bsc])
            s.add("dve", lambda e: e.tensor_tensor(xsp[:], xsp[:], lg[:], ALU.add), reads=[bsc], writes=[bsc])
            s.add("act", lambda e: e.activation(nega[:], alog[:, l * 128:(l + 1) * 128], AF.Exp), reads=[bconst], writes=[bsc])
            s.add("dve", lambda e: e.scalar_tensor_tensor(
                gtok[:].rearrange("p n c -> p (n c)"), xsp[:].rearrange("p n c -> p (n c)"), -1.0, nega[:],
                ALU.mult, ALU.mult), reads=[bsc], writes=[bsc])
            gflat = gtok[:].rearrange("p n c -> p (n c)")
            bflat = beta[:].rearrange("p n c -> p (n c)")
            s.add("pe", lambda e: e.matmul(ps[0][:, 0:128], uincl, gflat, start=True, stop=True),
                  reads=[bsc, bconst], writes=[bps[0]])
            s.add("pe", lambda e: e.matmul(ps[1][:, 0:128], onesF, gflat, start=True, stop=True),
                  reads=[bsc, bconst], writes=[bps[1]])
            s.add("act", lambda e: e.activation(gam[:], ps[0][:, 0:128], AF.Copy), reads=[bps[0]], writes=[bsc])
            s.add("act", lambda e: e.activation(eg[:], ps[0][:, 0:128], AF.Exp), reads=[bps[0]], writes=[bsc])
            s.add("dve", lambda e: e.tensor_scalar(negeg[:], eg[:], -1.0, None, ALU.mult), reads=[bsc], writes=[bsc])
            s.add("dve", lambda e: e.tensor_tensor(decl[:], ps[1][:, 0:128], gam[:], ALU.subtract),
                  reads=[bps[1], bsc], writes=[bsc])
            s.add("act", lambda e: e.activation(decl[:], decl[:], AF.Exp), reads=[bsc], writes=[bsc])
            s.add("act", lambda e: e.activation(eglast[:], ps[1][:, 0:128], AF.Exp), reads=[bps[1]], writes=[bsc])

            ck('D0')
            import os as _os
            NCH_RUN = int(_os.environ.get('DBG_NCH', NCH))
            NL = 3
            S32 = t_("S32", [128, 512]); bS = Buf()
            Sbf = t_("Sbf", [128, 512], BF16); bSb = Buf()
            s.add("dve", lambda e: e.memset(S32[:], 0.0), writes=[bS])
            s.add("pool", lambda e: e.memset(Sbf[:], 0.0), writes=[bSb])
            gd = gdn[:, l * 128:(l + 1) * 128]
            H = lambda a, h: a[:, h * 128:(h + 1) * 128]

            class Lane:
                pass
            lanes = []
            for li_ in range(NL):
                ln = Lane()
                ln.inp = t_("inp", [128, 16, 128], BF16); ln.binp = Buf()
                ln.Mt = t_("Mt", [128, 4, 128]); ln.bMt = Buf()
                ln.E = t_("E", [128, 512]); ln.bE = Buf()
                ln.Es = t_("Es", [128, 512]); ln.bEs = Buf()
                for nm in ("Pm", "Qm", "X", "Y", "R1", "QKm", "kdec", "rp", "vnew", "on", "yc"):
                    setattr(ln, nm, t_(nm, [128, 512], BF16)); setattr(ln, "b" + nm, Buf())
                ln.Qoff = t_("Qoff", [128, 6, 512], BF16); ln.bQoff = Buf()
                for nm in ("vtok", "o2s", "o_t"):
                    setattr(ln, nm, t_(nm, [128, 512])); setattr(ln, "b" + nm, Buf())
                ln.junk = t_("junk", [128, 128]); ln.bjunk = Buf()
                ln.ss = t_("ss", [128, 4]); ln.bss = Buf()
                ln.p = (ps[2 * li_], ps[2 * li_ + 1]); ln.bp = (bps[2 * li_], bps[2 * li_ + 1])
                if li_ % 2 == 0:
                    ln.pT = ps6b; ln.bpT = bps6
                else:
                    ln.pT = ps7b; ln.bpT = bps7
                lanes.append(ln)

            owner = {}

            def acq(me, banks):
                while any(owner.get(id(b_)) not in (None, me) for b_ in banks):
                    yield
                for b_ in banks:
                    owner[id(b_)] = me

            def rel(me, banks):
                for b_ in banks:
                    if owner.get(id(b_)) == me:
                        owner[id(b_)] = None

            def d1_gen(n, ln):
                ip = ln.inp; bip = ln.binp
                p0, p1 = ln.p; b0, b1 = ln.bp; pT = ln.pT; bT = ln.bpT
                col = lambda a, h: a[:, n * 4 + h:n * 4 + h + 1]
                s.dma("sp", ip[:], dnin[:, n, :, :], reads=[], writes=[bip])
                for h in range(4):
                    s.add("pool", lambda e, h=h, g_=col(gflat, h): e.tensor_scalar(ln.Mt[:, h, :], masksl, g_, None, ALU.mult),
                          reads=[bsc, bconst], writes=[ln.bMt])
                yield
                for h in range(4):
                    s.add("pe", lambda e, h=h: e.matmul(H(p0, h), ln.Mt[:, h, :], uincl, start=True, stop=False),
                          reads=[ln.bMt, bconst], writes=[b0])
                    s.add("pe", lambda e, h=h: e.matmul(H(p0, h), identF, negm, start=False, stop=True),
                          reads=[ln.bMt, bconst], writes=[b0])
                yield
                s.add("act", lambda e: e.activation(ln.E[:], p0[:], AF.Exp), reads=[b0], writes=[ln.bE])
                yield
                for h in range(4):
                    s.add("pe", lambda e, h=h: e.matmul(H(p0, h), ip[:, h, :], ip[:, h, :], start=True, stop=True),
                          reads=[bip], writes=[b0])
                for h in range(4):
                    s.add("pe", lambda e, h=h: e.matmul(H(p1, h), ip[:, h, :], ip[:, 4 + h, :], start=True, stop=True),
                          reads=[bip], writes=[b1])
                s.add("pool", lambda e: e.tensor_tensor(ln.Es[:], ln.E[:], su4[:], ALU.mult), reads=[ln.bE, bm], writes=[ln.bEs])
                yield
                for h in range(4):
                    s.add("dve", lambda e, h=h, b_=col(bflat, h): e.scalar_tensor_tensor(
                        H(ln.Pm, h), H(p0, h), b_, H(ln.Es, h), ALU.mult, ALU.mult),
                        reads=[b0, ln.bEs, bsc], writes=[ln.bPm])
                s.add("dve", lambda e: e.tensor_tensor(ln.QKm[:], p1[:], ln.E[:], ALU.mult), reads=[b1, ln.bE], writes=[ln.bQKm])
                yield
                yield from acq(n, (bT,))
                for h in range(4):
                    s.add("pe", lambda e, h=h: e.transpose(H(pT, h), H(ln.Pm, h), identB[:]), reads=[ln.bPm, bconst], writes=[bT])
                s.add("pool", lambda e: e.tensor_tensor(ln.X[:], ln.Pm[:], mU4[:, 0, :], ALU.mult), reads=[ln.bPm, bm], writes=[ln.bX])
                s.add("pool", lambda e: e.tensor_tensor(ln.X[:], i4[:], ln.X[:], ALU.subtract), reads=[ln.bX, bm], writes=[ln.bX])
                yield
                s.add("act", lambda e: e.activation(ln.Qm[:], pT[:, 0:512], AF.Copy), reads=[bT], writes=[ln.bQm])
                rel(n, (bT,))
                yield
                s.add("pool", lambda e: e.tensor_tensor(ln.Y[:], ln.Qm[:], mL4[:, 0, :], ALU.mult), reads=[ln.bQm, bm], writes=[ln.bY])
                s.add("pool", lambda e: e.tensor_tensor(ln.Y[:], i4[:], ln.Y[:], ALU.subtract), reads=[ln.bY, bm], writes=[ln.bY])
                for li in range(1, 7):
                    s.add("pool", lambda e, li=li: e.tensor_tensor(ln.Qoff[:, li - 1, :], ln.Qm[:], mL4[:, li, :], ALU.mult),
                          reads=[ln.bQm, bm], writes=[ln.bQoff])
                yield
                for li in range(1, 7):
                    last = (li == 6)
                    for h in range(4):
                        s.add("pe", lambda e, h=h, li=li: e.matmul(H(p0, h), ln.Qoff[:, li - 1, h * 128:(h + 1) * 128], H(ln.X, h),
                                                                 start=True, stop=True), reads=[ln.bQoff, ln.bX], writes=[b0])
                    yield
                    s.add("act", lambda e: e.activation(ln.R1[:], p0[:], AF.Copy), reads=[b0], writes=[ln.bR1])
                    yield
                    for h in range(4):
                        s.add("pe", lambda e, h=h: e.matmul(H(p0, h), H(ln.Y, h), H(ln.R1, h), start=True, stop=True),
                              reads=[ln.bY, ln.bR1], writes=[b0])
                    if not last:
                        for h in range(4):
                            s.add("pe", lambda e, h=h: e.matmul(H(p1, h), H(ln.R1, h), H(ln.Y, h), start=True, stop=True),
                                  reads=[ln.bY, ln.bR1], writes=[b1])
                    yield
                    s.add("dve", lambda e: e.tensor_tensor(ln.X[:], ln.X[:], p0[:], ALU.subtract), reads=[ln.bX, b0], writes=[ln.bX])
                    if not last:
                        s.add("dve", lambda e: e.tensor_tensor(ln.Y[:], ln.Y[:], p1[:], ALU.subtract),
                              reads=[ln.bY, b1], writes=[ln.bY])
                    yield

            def d2_gen(n, ln):
                ip = ln.inp; bip = ln.binp
                p0, p1 = ln.p; b0, b1 = ln.bp; pT = ln.pT; bT = ln.bpT
                col = lambda a, h: a[:, n * 4 + h:n * 4 + h + 1]
                yield from acq(n, (bT,))
                for h in range(4):
                    s.add("pe", lambda e, h=h: e.transpose(H(pT, h), ip[:, h, :], identB[:]), reads=[bip, bconst], writes=[bT])
                for h in range(4):
                    s.add("pe", lambda e, h=h: e.transpose(H(pT, 4 + h), ip[:, 8 + h, :], identB[:]), reads=[bip, bconst], writes=[bT])
                for h in range(4):
                    s.add("pe", lambda e, h=h: e.matmul(H(p0, h), ip[:, h, :], H(Sbf, h), start=True, stop=True),
                          reads=[bip, bSb], writes=[b0])
                for h in range(4):
                    s.add("pe", lambda e, h=h: e.matmul(H(p1, h), ip[:, 4 + h, :], H(Sbf, h), start=True, stop=True),
                          reads=[bip, bSb], writes=[b1])
                yield
                s.add("act", lambda e: e.activation(ln.vtok[:], pT[:, 512:1024], AF.Copy), reads=[bT], writes=[ln.bvtok])
                for h in range(4):
                    s.add("dve", lambda e, h=h, d_=col(decl, h): e.tensor_scalar(H(ln.kdec, h), H(pT, h), d_, None, ALU.mult),
                          reads=[bT, bsc], writes=[ln.bkdec])
                rel(n, (bT,))
                yield
                for h in range(4):
                    s.add("dve", lambda e, h=h, ne_=col(negeg, h): e.scalar_tensor_tensor(
                        H(ln.rp, h), H(p0, h), ne_, H(ln.vtok, h), ALU.mult, ALU.add),
                        reads=[b0, ln.bvtok, bsc], writes=[ln.brp])
                s.add("act", lambda e: e.activation(ln.o2s[:], p1[:], AF.Copy), reads=[b1], writes=[ln.bo2s])
                yield
                for h in range(4):
                    s.add("pe", lambda e, h=h: e.matmul(H(p0, h), H(ln.X, h), H(ln.rp, h), start=True, stop=True),
                          reads=[ln.bX, ln.brp], writes=[b0])
                yield
                for h in range(4):
                    s.add("act", lambda e, h=h, b_=col(bflat, h): e.activation(H(ln.vnew, h), H(p0, h), AF.Identity, bias=0.0, scale=b_),
                          reads=[b0, bsc], writes=[ln.bvnew])
                yield
                for h in range(4):
                    s.add("pe", lambda e, h=h: e.matmul(H(p0, h), H(ln.kdec, h), H(ln.vnew, h), start=True, stop=True),
                          reads=[ln.bkdec, ln.bvnew], writes=[b0])
                for h in range(4):
                    s.add("pe", lambda e, h=h: e.matmul(H(p1, h), H(ln.QKm, h), H(ln.vnew, h), start=True, stop=True),
                          reads=[ln.bQKm, ln.bvnew], writes=[b1])
                yield
                for h in range(4):
                    s.add("dve", lambda e, h=h, el_=col(eglast, h): e.scalar_tensor_tensor(
                        H(S32, h), H(S32, h), el_, H(p0, h), ALU.mult, ALU.add),
                        reads=[bS, b0, bsc], writes=[bS])
                yield
                s.add("pool", lambda e: e.tensor_copy(Sbf[:], S32[:]), reads=[bS], writes=[bSb])
                for h in range(4):
                    s.add("dve", lambda e, h=h, eg_=col(eg, h): e.scalar_tensor_tensor(
                        H(ln.o_t, h), H(ln.o2s, h), eg_, H(p1, h), ALU.mult, ALU.add),
                        reads=[b1, ln.bo2s, bsc], writes=[ln.bo_t])
                s.add("pool", lambda e: e.memset(ln.ss[:], 0.0), writes=[ln.bss])
                yield

            def d3_gen(n, ln):
                ip = ln.inp; bip = ln.binp
                pT = ln.pT; bT = ln.bpT
                for h in range(4):
                    s.add("act", lambda e, h=h: e.activation(ln.junk[:], H(ln.o_t, h), AF.Square, accum_out=ln.ss[:, h:h + 1]),
                          reads=[ln.bo_t, ln.bss], writes=[ln.bjunk, ln.bss])
                yield
                s.add("act", lambda e: e.activation(ln.ss[:], ln.ss[:], AF.Sqrt, bias=EPS, scale=1.0 / 128), reads=[ln.bss], writes=[ln.bss])
                yield
                s.add("dve", lambda e: e.reciprocal(ln.ss[:], ln.ss[:]), reads=[ln.bss], writes=[ln.bss])
                yield
                for h in range(4):
                    s.add("dve", lambda e, h=h: e.scalar_tensor_tensor(H(ln.on, h), H(ln.o_t, h), ln.ss[:, h:h + 1], gd, ALU.mult, ALU.mult),
                          reads=[ln.bo_t, ln.bss, bconst], writes=[ln.bon])
                yield
                yield from acq(n, (bT,))
                for h in range(4):
                    s.add("pe", lambda e, h=h: e.transpose(H(pT, 4 + h), H(ln.on, h), identB[:]), reads=[ln.bon, bconst], writes=[bT])
                yield
                s.add("dve", lambda e: e.tensor_tensor(
                    ln.yc[:], pT[:, 512:1024], ip[:, 12:16, :].rearrange("p a b -> p (a b)"), ALU.mult),
                    reads=[bT, bip], writes=[ln.byc])
                rel(n, (bT,))
                s.dma("sp", ys_v[:, 4:8, n * 128:(n + 1) * 128], ln.yc[:].rearrange("p (a b) -> p a b", b=128),
                      reads=[ln.byc], writes=[])
                yield

            def chain(n, ln):
                yield from d1_gen(n, ln)
                while d2_turn[0] != n:
                    yield
                yield from d2_gen(n, ln)
                d2_turn[0] = n + 1
                yield from d3_gen(n, ln)

            d2_turn = [0]
            active = {}
            nxt = 0
            while nxt < NCH_RUN or active:
                while nxt < NCH_RUN and (nxt % NL) not in active:
                    active[nxt % NL] = chain(nxt, lanes[nxt % NL]); nxt += 1
                    break
                for k in sorted(active.keys(), key=lambda kk: kk):
                    try:
                        next(active[k])
                    except StopIteration:
                        del active[k]
            s.barrier(); s.flush()

        ck("mixy%d" % l)
        esF = ExitStack()
        wfi = sb(esF, "wfi", [128, 8, 2 * DFF], BF16); bwfi = Buf()
        wfiv = w_fi[l].rearrange("(k p) c -> p k c", p=128)
        with ExitStack() as es:
            wo = sb(es, "wo", [128, 8, D], BF16); bwo = Buf()
            wov = w_out[l].rearrange("(k p) c -> p k c", p=128)
            for k in range(8):
                s.dma("pool", wo[:, k, :], wov[:, k, :], writes=[bwo])
            wfi_jobs = [(k, hh) for k in range(8) for hh in range(2)]
            yt = [sb(es, "yt%d" % i, [128, 8, 512], BF16) for i in range(2)]; byt = [Buf(), Buf()]
            xt = [sb(es, "xo%d" % i, [128, 8, 512]) for i in range(2)]; bxt = [Buf(), Buf()]
            for g in range(NG):
                sl = slice(g * 512, (g + 1) * 512)
                y_ = yt[g % 2]; by_ = byt[g % 2]; x_ = xt[g % 2]; bx_ = bxt[g % 2]
                s.dma("pool", y_[:], ys_v[:, :, sl], writes=[by_])
                s.dma("sp", x_[:], xs_v[:, :, sl], writes=[bx_])
                for (k, hh) in wfi_jobs[2 * g:2 * g + 2]:
                    s.dma("pool", wfi[:, k, hh * DFF:(hh + 1) * DFF], wfiv[:, k, hh * DFF:(hh + 1) * DFF], writes=[bwfi])
                for oc in range(8):
                    pi = oc % 4
                    for k in range(8):
                        s.add("pe", lambda e, k=k, oc=oc, pi=pi, y_=y_: e.matmul(
                            ps[pi][:], wo[:, k, oc * 128:(oc + 1) * 128], y_[:, k, :], start=(k == 0), stop=(k == 7)),
                            reads=[bwo, by_], writes=[bps[pi]])
                    s.add("dve", lambda e, oc=oc, pi=pi, x_=x_: e.scalar_tensor_tensor(
                        x_[:, oc, :], ps[pi][:], G1(l, oc), x_[:, oc, :], ALU.mult, ALU.add),
                        reads=[bps[pi], bx_, bmod], writes=[bx_])
                s.dma("act", xs_v[:, :, sl], x_[:], reads=[bx_], writes=[])
            s.barrier(); s.flush()
        ck("mix%d" % l)

        with ExitStack() as es:
            wfo = sb(es, "wfo", [128, 22, D], BF16); bwfo = Buf()
            wfov = w_fo[l].rearrange("(k p) c -> p k c", p=128)
            for k in range(22):
                s.dma("pool", wfo[:, k, :], wfov[:, k, :], writes=[bwfo])
            GF = 256
            NGF = T // GF
            xts = [sb(es, "xf", [128, 8, GF]) for _ in range(2)]; bxts = [Buf(), Buf()]
            hT2s = [sb(es, "hT2", [128, 8, GF], BF16) for _ in range(2)]; bh2s = [Buf(), Buf()]
            sq = sb(es, "sqf", [128, 8, GF], BF16); bsq = Buf()
            rs = sb(es, "rsf", [128, GF]); brs = Buf()
            aTs = [sb(es, "aT", [128, 22, GF], BF16) for _ in range(2)]; baTs = [Buf(), Buf()]
            sgf = [sb(es, "sgf%d" % i, [128, GF]) for i in range(2)]; bsgf = [Buf(), Buf()]

            def f_norm(g):
                xt = xts[g % 2]; bxt = bxts[g % 2]; hT2 = hT2s[g % 2]; bh2 = bh2s[g % 2]
                sl = slice(g * GF, (g + 1) * GF)
                s.dma("sp", xt[:], xs_v[:, :, sl], reads=[bxs], writes=[bxt])
                s.add("act", lambda e: e.activation(sq[:], xt[:], AF.Square), reads=[bxt], writes=[bsq])
                for j in range(8):
                    s.add("pe", lambda e, j=j: e.matmul(ps[5][:, 0:GF], onesB[:], sq[:, j, :], start=(j == 0), stop=(j == 7)),
                          reads=[bsq, bconst], writes=[bps[5]])
                s.add("act", lambda e: e.activation(rs[:], ps[5][:, 0:GF], AF.Sqrt, bias=EPS, scale=1.0 / D),
                      reads=[bps[5]], writes=[brs])
                s.add("dve", lambda e: e.reciprocal(rs[:], rs[:]), reads=[brs], writes=[brs])
                for j in range(8):
                    s.add("dve", lambda e, j=j: e.scalar_tensor_tensor(
                        sq[:, j, :], xt[:, j, :], A2[:, l * 8 + j:l * 8 + j + 1], rs[:], ALU.mult, ALU.mult),
                        reads=[bxt, brs, bmod, bsq], writes=[bsq])
                    s.add("act", lambda e, j=j: e.activation(
                        hT2[:, j, :], sq[:, j, :], AF.Identity, bias=B2(l, j), scale=1.0),
                        reads=[bsq, bmod], writes=[bh2])

            def f_in(g):
                hT2 = hT2s[g % 2]; bh2 = bh2s[g % 2]; aT = aTs[g % 2]; baT = baTs[g % 2]
                for j in range(22):
                    pg, pu = (0, 1) if j % 2 == 0 else (2, 3)
                    for k in range(8):
                        s.add("pe", lambda e, k=k, j=j, pg=pg: e.matmul(
                            ps[pg][:, 0:GF], wfi[:, k, j * 128:(j + 1) * 128], hT2[:, k, :], start=(k == 0), stop=(k == 7)),
                            reads=[bwfi, bh2], writes=[bps[pg]])
                    for k in range(8):
                        s.add("pe", lambda e, k=k, j=j, pu=pu: e.matmul(
                            ps[pu][:, 0:GF], wfi[:, k, DFF + j * 128:DFF + (j + 1) * 128], hT2[:, k, :],
                            start=(k == 0), stop=(k == 7)), reads=[bwfi, bh2], writes=[bps[pu]])
                    sg_ = sgf[j % 2]; bsg_ = bsgf[j % 2]
                    s.add("act", lambda e, pg=pg, sg_=sg_: e.activation(sg_[:], ps[pg][:, 0:GF], AF.Silu),
                          reads=[bps[pg]], writes=[bsg_])
                    s.add("dve", lambda e, j=j, pu=pu, sg_=sg_: e.tensor_tensor(aT[:, j, :], sg_[:], ps[pu][:, 0:GF], ALU.mult),
                          reads=[bsg_, bps[pu]], writes=[baT])

            def f_out(g):
                xt = xts[g % 2]; bxt = bxts[g % 2]; aT = aTs[g % 2]; baT = baTs[g % 2]
                sl = slice(g * GF, (g + 1) * GF)
                for oc in range(8):
                    pi = 4 if oc % 2 == 0 else (0, 2)[(oc // 2) % 2]
                    for k in range(22):
                        s.add("pe", lambda e, k=k, oc=oc, pi=pi: e.matmul(
                            ps[pi][:, 0:GF], wfo[:, k, oc * 128:(oc + 1) * 128], aT[:, k, :], start=(k == 0), stop=(k == 21)),
                            reads=[bwfo, baT], writes=[bps[pi]])
                    s.add("dve", lambda e, oc=oc, pi=pi: e.scalar_tensor_tensor(
                        xt[:, oc, :], ps[pi][:, 0:GF], G2(l, oc), xt[:, oc, :], ALU.mult, ALU.add),
                        reads=[bps[pi], bxt, bmod], writes=[bxt])
                s.dma("pool", xs_v[:, :, sl], xt[:], reads=[bxt], writes=[bxs])

            f_norm(0)
            for g in range(NGF):
                f_in(g)
                if g + 1 < NGF:
                    f_norm(g + 1)
                f_out(g)
            s.barrier(); s.flush()
        esF.close()
        ck("ffn%d" % l)

    s.muted = False
    dumpy = stop in ('mixA', 'mixB') or (stop is not None and stop.startswith('mixy'))
    with ExitStack() as es:
        xt = [sb(es, "xz%d" % i, [128, 8, 512]) for i in range(2)]; bxt = [Buf(), Buf()]
        sq = sb(es, "sqz", [128, 8, 512], BF16); bsq = Buf()
        rs = sb(es, "rsz", [128, 512]); brs = Buf()
        if dumpy:
            yt = [sb(es, "yz%d" % i, [128, 8, 512], BF16) for i in range(2)]; byt = [Buf(), Buf()]
        for g in range(NG):
            x_ = xt[g % 2]; bx_ = bxt[g % 2]
            if stop is None:
                sl = norm_group((sq, bsq, rs, brs), 0, g, None, None, None, None, x_, bx_)
                for j in range(8):
                    s.add("dve", lambda e, j=j, x_=x_: e.scalar_tensor_tensor(
                        x_[:, j, :], x_[:, j, :], gfin[:, j:j + 1], rs[:], ALU.mult, ALU.mult),
                        reads=[bx_, brs, bconst], writes=[bx_])
            elif dumpy:
                sl = slice(g * 512, (g + 1) * 512)
                y_ = yt[g % 2]; by_ = byt[g % 2]
                s.dma("sp", y_[:], ys_v[:, :, sl], writes=[by_])
                s.add("dve", lambda e, x_=x_, y_=y_: e.tensor_copy(x_[:], y_[:]), reads=[by_], writes=[bx_])
            else:
                sl = slice(g * 512, (g + 1) * 512)
                s.dma("sp", x_[:], xs_v[:, :, sl], writes=[bx_])
            s.dma("sp", out_v[:, :, sl], x_[:], reads=[bx_], writes=[])
        s.flush(final=True)
    top.close()
    s.close()
    nc._nops = s.nops
    return nc


def _pp(a):
    a = np.asarray(a, np.float32)
    lead = a.shape[:-1]
    n = a.shape[-1] // 128
    a = a.reshape(lead + (n, 128))
    a = np.moveaxis(a, -1, 0)
    return np.ascontiguousarray(a.reshape(128, -1))


def make_in_maps(inputs, depth=L, ncores=NCORES):
    f = lambda k: np.asarray(inputs[k], np.float32)
    m = _levels_masks()
    common = {
        "w_ada": f("w_ada")[:depth], "b_ada": f("b_ada")[:depth], "w_in": f("w_in")[:depth], "w_out": f("w_out")[:depth],
        "w_ffn_in": f("w_ffn_in")[:depth], "w_ffn_out": f("w_ffn_out")[:depth],
        "gm": _pp(f("norm_mix_g")), "gf": _pp(f("norm_ffn_g")), "gfin": _pp(f("final_norm_g")),
        "caw": _pp(np.transpose(f("conv_a_w"), (0, 2, 1)).reshape(L, 2, 128, 3).transpose(0, 1, 3, 2)),
        "cfw": _pp(np.transpose(f("conf_dw_w"), (0, 2, 1)).reshape(L, 2, 128, 31).transpose(0, 1, 3, 2)),
        "cfb": _pp(f("conf_dw_b")), "cfg": _pp(f("conf_ln_g")), "cfbb": _pp(f("conf_ln_b")),
        "dnw": _pp(np.transpose(f("dn_conv_w"), (0, 2, 1)).reshape(L, 12, 128, 4).transpose(0, 1, 3, 2)),
        "alog": np.ascontiguousarray(np.broadcast_to(np.tile(f("dn_a_log"), (1, NCH)).reshape(1, L * 128), (128, L * 128))),
        "dtb": np.ascontiguousarray(np.broadcast_to(np.tile(f("dn_dt_bias"), (1, NCH)).reshape(1, L * 128), (128, L * 128))),
        "gdn": np.ascontiguousarray(np.broadcast_to(f("dn_norm_g").reshape(1, L * 128), (128, L * 128))),
        "cmask": np.ascontiguousarray(np.concatenate(
            [m["ident"], m["uincl"], m["masksl"], m["negm"], m["strictu"], m["ones"]], axis=1)),
        "levU4": np.ascontiguousarray(np.concatenate([np.tile(m["levU"][i], (1, 4)) for i in range(7)], axis=1)),
        "levL4": np.ascontiguousarray(np.concatenate([np.tile(m["levU"][i].T, (1, 4)) for i in range(7)], axis=1)),
        "ident4": np.ascontiguousarray(np.tile(m["ident"], (1, 4))),
        "strictu4": np.ascontiguousarray(np.tile(m["strictu"], (1, 4))),
    }
    x = f("x"); c = f("c")
    maps = []
    for core in range(ncores):
        b = core % 4
        d = dict(common)
        d["xT"] = np.ascontiguousarray(x[b].T)
        d["cT"] = np.ascontiguousarray(c[b].reshape(8, 128).T)
        maps.append(d)
    return maps


_NC_CACHE = {}


def kernel(**inputs):
    if "nc" not in _NC_CACHE:
        _NC_CACHE["nc"] = build_program()
    nc = _NC_CACHE["nc"]
    in_maps = make_in_maps(inputs)
    res = run_bass_kernel_spmd(nc, in_maps, core_ids=list(range(NCORES)))
    out = np.stack([np.asarray(res.results[b]["outT"], np.float32).T for b in range(4)], axis=0)
    return np.ascontiguousarray(out)
```

```python
from contextlib import ExitStack

import numpy as np
import concourse.bass as bass
import concourse.mybir as mybir
from concourse.bass_utils import run_bass_kernel_spmd

F32 = mybir.dt.float32
BF16 = mybir.dt.bfloat16
AF = mybir.ActivationFunctionType
ALU = mybir.AluOpType


class Buf:
    __slots__ = ("name", "w", "r", "excl")

    def __init__(self, name="", excl=False):
        self.name = name
        self.w = None
        self.r = []
        self.excl = excl


class _Op:
    __slots__ = ("eng", "fn", "deps", "is_dma", "sem", "val", "signal", "emitted")


class Sched:
    CENG = ("pe", "act", "dve", "pool")
    ENG = ("pe", "act", "dve", "pool", "sp")
    DMAQ = ("sp", "pool", "act")

    def __init__(self, nc, strict=True, dma_k=6):
        self.nc = nc
        self.strict = strict
        self.K = dma_k
        self.es = ExitStack()
        self.csem = {e: self.es.enter_context(nc.semaphore("c_" + e)) for e in self.CENG}
        self.ccount = {e: 0 for e in self.CENG}
        self.dsem = {q: [self.es.enter_context(nc.semaphore("d_%s%d" % (q, i))) for i in range(dma_k)]
                     for q in self.DMAQ}
        self.dcount = {q: 0 for q in self.DMAQ}
        self.dhist = {q: [] for q in self.DMAQ}
        self.known = {e: {} for e in self.ENG}
        self.pending = {e: [] for e in self.ENG}
        self.last = {e: None for e in self.CENG}
        self.barrier_ops = []
        self.nops = 0
        self.muted = False

    def _mk(self, eng, fn, reads, writes, is_dma):
        op = _Op()
        op.eng = eng
        op.fn = fn
        op.is_dma = is_dma
        op.signal = is_dma
        op.sem = None
        op.val = None
        op.emitted = False
        deps = list(self.barrier_ops)
        for b in reads:
            if b.w is not None:
                deps.append(b.w)
            if b.excl:
                deps.extend(r for r in b.r if r.eng != eng)
        for b in writes:
            if b.w is not None:
                deps.append(b.w)
            deps.extend(b.r)
        op.deps = deps
        for b in reads:
            b.r.append(op)
        for b in writes:
            b.w = op
            b.r = []
        self.pending[eng].append(op)
        self.nops += 1
        return op

    def add(self, eng, fn, reads=(), writes=()):
        if self.muted:
            return None
        op = self._mk(eng, fn, reads, writes, False)
        self.last[eng] = op
        return op

    def dma(self, q, out, in_, reads=(), writes=(), **kw):
        if self.muted:
            return None
        def fn(e, out=out, in_=in_, kw=kw):
            return e.dma_start(out=out, in_=in_, **kw)
        op = self._mk(q, fn, reads, writes, True)
        i = self.dcount[q]
        self.dcount[q] += 1
        op.sem = self.dsem[q][i % self.K]
        op.val = 16 * (i // self.K + 1)
        if i >= self.K:
            op.deps.append(self.dhist[q][i - self.K])
        self.dhist[q].append(op)
        return op

    def barrier(self):
        if self.muted:
            return
        ops = [self.last[e] for e in self.CENG if self.last[e] is not None]
        for q in self.DMAQ:
            ops.extend(self.dhist[q][-self.K:])
        self.barrier_ops = ops

    def _need(self, op, dep):
        if dep.is_dma:
            return True
        if dep.eng != op.eng:
            return True
        if op.eng == "pe":
            return False
        if op.is_dma:
            return True
        return self.strict

    def flush(self, final=False):
        nc = self.nc
        if not final and not any(self.pending[e] for e in self.ENG):
            return
        for e in self.ENG:
            for op in self.pending[e]:
                for d in op.deps:
                    if not d.is_dma and not d.emitted and self._need(op, d):
                        d.signal = True
        for e in self.CENG:
            comp = [o for o in self.pending[e] if not o.is_dma]
            if comp:
                comp[-1].signal = True
        for e in self.CENG:
            c = self.ccount[e]
            comp = [o for o in self.pending[e] if not o.is_dma]
            for o in comp:
                if o.signal:
                    c += 1
                    o.val = c
                o.sem = self.csem[e]
            self.ccount[e] = c
            nxt = None
            for o in reversed(comp):
                if o.signal:
                    nxt = o.val
                else:
                    o.val = nxt
        getter = {"pe": "tensor", "act": "scalar", "dve": "vector", "pool": "gpsimd", "sp": "sync"}

        def run(e, eng):
            known = self.known[e]
            for op in self.pending[e]:
                waits = {}
                for d in op.deps:
                    if not self._need(op, d):
                        continue
                    key = id(d.sem)
                    if known.get(key, 0) >= d.val:
                        continue
                    if key not in waits or waits[key][1] < d.val:
                        waits[key] = (d.sem, d.val)
                for key, (sem, val) in waits.items():
                    eng.wait_ge(sem, val)
                    known[key] = val
                inst = op.fn(eng)
                if op.signal:
                    inst.then_inc(op.sem, 16 if op.is_dma else 1)
                op.emitted = True
                op.fn = None
            if final and e == "sp":
                for q in self.DMAQ:
                    n = self.dcount[q]
                    for j in range(self.K):
                        cnt = len(range(j, n, self.K))
                        if cnt:
                            eng.wait_ge(self.dsem[q][j], 16 * cnt)

        with nc.Block() as blk:
            for e in self.ENG:
                if not self.pending[e] and not (final and e == "sp"):
                    continue
                getattr(blk, getter[e])(lambda eng, e=e: run(e, eng))
        self.pending = {e: [] for e in self.ENG}

    def close(self):
        self.es.close()


D = 1024
T = 4096
L = 4
NG = T // 512
NCH = T // 128
DFF = 2816
INC = 3336
EPS = 1e-6
NCORES = 8


class _Stop(Exception):
    pass


def _levels_masks():
    i = np.arange(128)
    s_, c_ = np.meshgrid(i, i, indexing="ij")
    out = {}
    out["ident"] = (s_ == c_).astype(np.float32)
    out["uincl"] = (s_ <= c_).astype(np.float32)
    out["masksl"] = (s_ > c_).astype(np.float32)
    out["negm"] = np.where(c_ < s_, -30000.0, 0.0).astype(np.float32)
    out["strictu"] = (s_ < c_).astype(np.float32)
    out["ones"] = np.ones((128, 128), np.float32)
    lev = []
    for b in (1, 2, 4, 8, 16, 32, 64):
        m = ((s_ // (2 * b)) == (c_ // (2 * b))) & ((s_ // b) % 2 == 0) & ((c_ // b) % 2 == 1)
        lev.append(m.astype(np.float32))
    out["levU"] = np.stack(lev)
    return out


def build_program(depth=L, stop=None):
    nc = bass.Bass("TRN2", target_bir_lowering=False)
    dt = nc.dram_tensor

    def din(name, shape, dtype=F32):
        return dt(name, list(shape), dtype, kind="ExternalInput").ap()

    xT_in = din("xT", [D, T])
    cT_in = din("cT", [128, 8])
    w_ada = din("w_ada", [depth, D, 6 * D])
    b_ada = din("b_ada", [depth, 6 * D])
    w_in = din("w_in", [depth, D, INC])
    w_out = din("w_out", [depth, D, D])
    w_fi = din("w_ffn_in", [depth, D, 2 * DFF])
    w_fo = din("w_ffn_out", [depth, DFF, D])
    gm_in = din("gm", [128, L * 8])
    gf_in = din("gf", [128, L * 8])
    gfin_in = din("gfin", [128, 8])
    caw_in = din("caw", [128, L * 2 * 3])
    cfw_in = din("cfw", [128, L * 2 * 31])
    cfb_in = din("cfb", [128, L * 2])
    cfg_in = din("cfg", [128, L * 2])
    cfbb_in = din("cfbb", [128, L * 2])
    dnw_in = din("dnw", [128, L * 12 * 4])
    alog_in = din("alog", [128, L * 128])
    dtb_in = din("dtb", [128, L * 128])
    gdn_in = din("gdn", [128, L * 128])
    cm_in = din("cmask", [128, 6 * 128])
    lev_in = din("levU4", [128, 7 * 512])
    levT_in = din("levL4", [128, 7 * 512])
    i4_in = din("ident4", [128, 512])
    su4_in = din("strictu4", [128, 512])
    outT = dt("outT", [D, T], F32, kind="ExternalOutput").ap()

    xs = dt("xs", [D, T], F32, kind="Internal").ap()
    ys = dt("ys", [D, T], BF16, kind="Internal").ap()
    dnin = dt("dnin", [128, NCH, 16, 128], BF16, kind="Internal").ap()

    import os as _os2
    s = Sched(nc, strict=(_os2.environ.get("MK_STRICT", "1") == "1"))
    top = ExitStack()

    _cnt = [0]

    def sb(es, name, shape, dtype=F32):
        _cnt[0] += 1
        return es.enter_context(nc.sbuf_tensor("%s_%d" % (name, _cnt[0]), list(shape), dtype))

    ps = [top.enter_context(nc.psum_tensor("ps%d" % i, [128, 512], F32)) for i in range(6)]
    ps6b = top.enter_context(nc.psum_tensor("ps6b", [128, 1024], BF16))
    ps7b = top.enter_context(nc.psum_tensor("ps7b", [128, 1024], BF16))
    bps = [Buf("ps%d" % i, excl=True) for i in range(6)]
    bps6, bps7 = Buf("ps6b", excl=True), Buf("ps7b", excl=True)

    modT = sb(top, "modT", [128, L, 48])
    A1 = sb(top, "A1", [128, L * 8]); A2 = sb(top, "A2", [128, L * 8])
    gm = sb(top, "gm_t", [128, L * 8]); gf = sb(top, "gf_t", [128, L * 8]); gfin = sb(top, "gfin_t", [128, 8])
    caw = sb(top, "caw_t", [128, L * 6]); cfw = sb(top, "cfw_t", [128, L * 62])
    cfb = sb(top, "cfb_t", [128, L * 2]); cfg = sb(top, "cfg_t", [128, L * 2]); cfbb = sb(top, "cfbb_t", [128, L * 2])
    dnw = sb(top, "dnw_t", [128, L * 48])
    alog = sb(top, "alog_t", [128, L * 128]); dtb = sb(top, "dtb_t", [128, L * 128]); gdn = sb(top, "gdn_t", [128, L * 128])
    cmF = sb(top, "cmF", [128, 6 * 128])
    identB = sb(top, "identB", [128, 128], BF16)
    onesB = sb(top, "onesB", [128, 128], BF16)
    ab_tok = sb(top, "ab_tok", [128, NCH * 8])
    bconst = Buf("const")
    bmod = Buf("mod")
    bab = Buf("abtok")
    identF = cmF[:, 0:128]; uincl = cmF[:, 128:256]; masksl = cmF[:, 256:384]
    negm = cmF[:, 384:512]; strictu = cmF[:, 512:640]; onesF = cmF[:, 640:768]

    for (t_, src) in ((gm, gm_in), (gf, gf_in), (gfin, gfin_in), (caw, caw_in), (cfw, cfw_in), (cfb, cfb_in),
                      (cfg, cfg_in), (cfbb, cfbb_in), (dnw, dnw_in), (alog, alog_in), (dtb, dtb_in),
                      (gdn, gdn_in), (cmF, cm_in)):
        s.dma("sp", t_[:], src, writes=[bconst])
    s.dma("pool", identB[:], cm_in[:, 0:128], writes=[bconst])
    s.dma("pool", onesB[:], cm_in[:, 640:768], writes=[bconst])
    bxs = Buf("xs")
    for j in range(8):
        s.dma("sp", xs[j * 128:(j + 1) * 128, :], xT_in[j * 128:(j + 1) * 128, :], writes=[bxs])

    with ExitStack() as es:
        cT = sb(es, "cT_t", [128, 8]); cact = sb(es, "cact", [128, 8])
        wt = [sb(es, "wada%d" % i, [128, 2048]) for i in range(6)]
        bwt = [Buf() for _ in range(6)]
        modrow = sb(es, "modrow", [1, 6 * D]); brow = sb(es, "brow", [1, 6 * D])
        one11 = sb(es, "one11", [1, 1])
        bc, bmr, bbr = Buf(), Buf(), Buf()
        s.dma("sp", cT[:], cT_in, writes=[bc])
        s.add("act", lambda e: e.activation(cact[:], cT[:], AF.Silu), reads=[bc], writes=[bc])
        s.add("dve", lambda e: e.memset(one11[:], 1.0), writes=[bc])
        it = 0
        for l in range(depth):
            s.dma("sp", brow[:], b_ada[l:l + 1, :], reads=[], writes=[bbr])
            for cg in range(3):
                for k in range(8):
                    w_ = wt[it % 6]; bw_ = bwt[it % 6]; it += 1
                    s.dma(("sp", "act", "pool")[it % 3], w_[:], w_ada[l, k * 128:(k + 1) * 128, cg * 2048:(cg + 1) * 2048],
                          writes=[bw_])
                    for i in range(4):
                        s.add("pe", lambda e, i=i, w_=w_, k=k: e.matmul(
                            ps[i][0:1, :], cact[:, k:k + 1], w_[:, i * 512:(i + 1) * 512],
                            start=(k == 0), stop=(k == 7)), reads=[bw_, bc], writes=[bps[i]])
                for i in range(4):
                    c0 = cg * 2048 + i * 512
                    s.add("dve", lambda e, i=i, c0=c0: e.tensor_tensor(
                        modrow[0:1, c0:c0 + 512], ps[i][0:1, :], brow[0:1, c0:c0 + 512], ALU.add),
                        reads=[bps[i], bbr], writes=[bmr])
            for j in range(48):
                s.add("pe", lambda e, j=j: e.matmul(ps[4][:, j:j + 1], modrow[0:1, j * 128:(j + 1) * 128],
                                                     one11[0:1, 0:1], start=True, stop=True),
                      reads=[bmr, bc], writes=[bps[4]])
            s.add("act", lambda e, l=l: e.activation(modT[:, l, :], ps[4][:, 0:48], AF.Copy),
                  reads=[bps[4]], writes=[bmod])
            s.add("dve", lambda e, l=l: e.scalar_tensor_tensor(
                A1[:, l * 8:(l + 1) * 8], modT[:, l, 8:16], 1.0, gm[:, l * 8:(l + 1) * 8], ALU.add, ALU.mult),
                reads=[bmod, bconst], writes=[bmod])
            s.add("dve", lambda e, l=l: e.scalar_tensor_tensor(
                A2[:, l * 8:(l + 1) * 8], modT[:, l, 32:40], 1.0, gf[:, l * 8:(l + 1) * 8], ALU.add, ALU.mult),
                reads=[bmod, bconst], writes=[bmod])
        s.barrier()
        s.flush()

    def B1(l, j): return modT[:, l, j:j + 1]
    def G1(l, j): return modT[:, l, 16 + j:17 + j]
    def B2(l, j): return modT[:, l, 24 + j:25 + j]
    def G2(l, j): return modT[:, l, 40 + j:41 + j]

    xs_v = xs.rearrange("(j p) t -> p j t", p=128)
    ys_v = ys.rearrange("(j p) t -> p j t", p=128)
    out_v = outT.rearrange("(j p) t -> p j t", p=128)

    def norm_group(es_tiles, l, g, Acol, Bcol, hdst, hbuf, xt, bxt, load=True, pn=5):
        sq, bsq, rs, brs = es_tiles
        sl = slice(g * 512, (g + 1) * 512)
        if load:
            s.dma("sp", xt[:], xs_v[:, :, sl], reads=[bxs], writes=[bxt])
        s.add("act", lambda e: e.activation(sq[:], xt[:], AF.Square), reads=[bxt], writes=[bsq])
        for j in range(8):
            s.add("pe", lambda e, j=j: e.matmul(ps[pn][:], onesB[:], sq[:, j, :], start=(j == 0), stop=(j == 7)),
                  reads=[bsq, bconst], writes=[bps[pn]])
        s.add("act", lambda e: e.activation(rs[:], ps[pn][:], AF.Sqrt, bias=EPS, scale=1.0 / D),
              reads=[bps[pn]], writes=[brs])
        s.add("dve", lambda e: e.reciprocal(rs[:], rs[:]), reads=[brs], writes=[brs])
        return sl

    def ck(name):
        if stop == name:
            s.muted = True
    ck('pro')
    for l in range(depth):
        with ExitStack() as es:
            hT = sb(es, "hT", [128, 8, T], BF16); bh = Buf("hT")
            with ExitStack() as es2:
                xt = [sb(es2, "xt%d" % i, [128, 8, 512]) for i in range(3)]; bxt = [Buf(), Buf(), Buf()]
                sqs = [sb(es2, "sq", [128, 8, 512], BF16) for _ in range(2)]; bsqs = [Buf(), Buf()]
                rss = [sb(es2, "rs", [128, 512]) for _ in range(2)]; brss = [Buf(), Buf()]
                def n1_stats(g):
                    x_ = xt[g % 3]; bx_ = bxt[g % 3]
                    sq = sqs[g % 2]; bsq = bsqs[g % 2]; rs = rss[g % 2]; brs = brss[g % 2]
                    norm_group((sq, bsq, rs, brs), l, g, None, None, None, None, x_, bx_, pn=4 + g % 2)

                def n1_apply(g):
                    x_ = xt[g % 3]; bx_ = bxt[g % 3]; rs = rss[g % 2]; brs = brss[g % 2]
                    sl = slice(g * 512, (g + 1) * 512)
                    for j in range(8):
                        s.add("dve", lambda e, j=j, x_=x_, rs=rs: e.scalar_tensor_tensor(
                            x_[:, j, :], x_[:, j, :], A1[:, l * 8 + j:l * 8 + j + 1], rs[:], ALU.mult, ALU.mult),
                            reads=[bx_, brs, bmod], writes=[bx_])
                        s.add("act", lambda e, j=j, x_=x_, sl=sl: e.activation(
                            hT[:, j, sl], x_[:, j, :], AF.Identity, bias=B1(l, j), scale=1.0),
                            reads=[bx_, bmod], writes=[bh])

                n1_stats(0)
                for g in range(NG):
                    if g + 1 < NG:
                        n1_stats(g + 1)
                    n1_apply(g)
                s.barrier(); s.flush()
            ck('n1')

            wv = w_in[l].rearrange("(k p) c -> p k c", p=128)

            def proj(wtile, bw, g, pidx):
                for k in range(8):
                    s.add("pe", lambda e, k=k: e.matmul(ps[pidx][:], wtile[:, k, :], hT[:, k, g * 512:(g + 1) * 512],
                                                         start=(k == 0), stop=(k == 7)),
                          reads=[bw, bh], writes=[bps[pidx]])

            with ExitStack() as es2:
                wa = [sb(es2, "wa%d" % i, [128, 8, 128], BF16) for i in range(3)]; bwa = [Buf() for _ in range(3)]
                ppad = sb(es2, "ppad", [128, T + 2]); bpp = Buf()
                abt = sb(es2, "abt", [128, T]); bab_ = Buf()
                acc = sb(es2, "acc", [128, T]); bacc = Buf()
                ybf = sb(es2, "ybf", [128, T], BF16); bybf = Buf()
                tmpc = sb(es2, "tmpc", [128, 512]); btc = Buf()
                s.add("pool", lambda e: e.memset(ppad[:, 0:2], 0.0), writes=[bpp])
                for cc in range(2):
                    for i, base in enumerate((256, 512, 0)):
                        c0 = base + cc * 128
                        s.dma("pool", wa[i][:], wv[:, :, c0:c0 + 128], writes=[bwa[i]])
                    for g in range(NG):
                        sl = slice(g * 512, (g + 1) * 512)
                        proj(wa[0], bwa[0], g, 0)
                        proj(wa[1], bwa[1], g, 1)
                        proj(wa[2], bwa[2], g, 2)
                        s.add("act", lambda e: e.activation(tmpc[:], ps[0][:], AF.Copy), reads=[bps[0]], writes=[btc])
                        s.add("dve", lambda e, g=g: e.tensor_tensor(ppad[:, 2 + g * 512:2 + (g + 1) * 512], tmpc[:],
                                                                  ps[1][:], ALU.mult),
                              reads=[btc, bps[1]], writes=[bpp])
                        s.add("act", lambda e, sl=sl: e.activation(abt[:, sl], ps[2][:], AF.Copy),
                              reads=[bps[2]], writes=[bab_])
                    for sg in range(4):
                        o = sg * 1024
                        wc = lambda k: caw[:, l * 6 + cc * 3 + k:l * 6 + cc * 3 + k + 1]
                        s.add("dve", lambda e, o=o, w0=wc(0): e.tensor_scalar(
                            acc[:, o:o + 1024], ppad[:, o:o + 1024], w0, None, ALU.mult),
                            reads=[bpp, bconst], writes=[bacc])
                        for k in (1, 2):
                            s.add("dve", lambda e, o=o, k=k, wk=wc(k): e.scalar_tensor_tensor(
                                acc[:, o:o + 1024], ppad[:, o + k:o + k + 1024], wk, acc[:, o:o + 1024],
                                ALU.mult, ALU.add), reads=[bpp, bacc, bconst], writes=[bacc])
                        s.add("pool", lambda e, o=o: e.tensor_tensor(ybf[:, o:o + 1024], acc[:, o:o + 1024],
                                                                  abt[:, o:o + 1024], ALU.mult),
                              reads=[bacc, bab_], writes=[bybf])
                    s.dma("sp", ys[cc * 128:(cc + 1) * 128, :], ybf[:], reads=[bybf], writes=[])
                s.barrier(); s.flush()
            ck('mixA')

            with ExitStack() as es2:
                wb = [sb(es2, "wb%d" % i, [128, 8, 128], BF16) for i in range(2)]; bwb = [Buf(), Buf()]
                upad = sb(es2, "upad", [128, 2, T + 30], BF16); bup = Buf()
                dg = sb(es2, "dg", [128, 62, 128], BF16); bdg = Buf()
                v32 = sb(es2, "v32", [128, 2, T]); bv32 = Buf()
                sgt = sb(es2, "sgt", [128, 512]); bsg = Buf()
                sq2s = [sb(es2, "sq2", [128, 2, 512], BF16) for _ in range(2)]; bsq2s = [Buf(), Buf()]
                rs2s = [sb(es2, "rs2", [128, 512]) for _ in range(2)]; brs2s = [Buf(), Buf()]
                t2s = [sb(es2, "t2", [128, 512]) for _ in range(4)]; bt2s = [Buf() for _ in range(4)]
                yb2 = [sb(es2, "yb2_%d" % i, [128, 2, 512], BF16) for i in range(2)]; byb2 = [Buf(), Buf()]
                for cc in range(2):
                    s.add("pool", lambda e, cc=cc: e.memset(upad[:, cc, 0:30], 0.0), writes=[bup])
                    for k in range(31):
                        s.add("pool", lambda e, cc=cc, k=k: e.tensor_scalar(
                            dg[:, cc * 31 + k, :], identB[:], cfw[:, l * 62 + cc * 31 + k:l * 62 + cc * 31 + k + 1],
                            None, ALU.mult), reads=[bconst], writes=[bdg])
                for cc in range(2):
                    s.dma("pool", wb[0][:], wv[:, :, 768 + cc * 128:768 + (cc + 1) * 128], writes=[bwb[0]])
                    s.dma("pool", wb[1][:], wv[:, :, 1024 + cc * 128:1024 + (cc + 1) * 128], writes=[bwb[1]])
                    for g in range(NG):
                        proj(wb[0], bwb[0], g, 0)
                        proj(wb[1], bwb[1], g, 1)
                        s.add("act", lambda e: e.activation(sgt[:], ps[1][:], AF.Sigmoid), reads=[bps[1]], writes=[bsg])
                        s.add("dve", lambda e, cc=cc, g=g: e.tensor_tensor(
                            upad[:, cc, 30 + g * 512:30 + (g + 1) * 512], sgt[:], ps[0][:], ALU.mult),
                            reads=[bsg, bps[0]], writes=[bup])
                for g in range(NG):
                    sl = slice(g * 512, (g + 1) * 512)
                    for cc in range(2):
                        pi = 2 + cc
                        for k in range(31):
                            s.add("pe", lambda e, cc=cc, k=k, g=g, pi=pi: e.matmul(
                                ps[pi][:], dg[:, cc * 31 + k, :], upad[:, cc, g * 512 + k:g * 512 + k + 512],
                                start=(k == 0), stop=(k == 30)), reads=[bdg, bup], writes=[bps[pi]])
                        s.add("act", lambda e, cc=cc, sl=sl, pi=pi: e.activation(
                            v32[:, cc, sl], ps[pi][:], AF.Identity, bias=cfb[:, l * 2 + cc:l * 2 + cc + 1], scale=1.0),
                            reads=[bps[pi], bconst], writes=[bv32])
                    pm, pv = (4, 5) if g % 2 == 0 else (0, 1)
                    sq2 = sq2s[g % 2]; bsq2 = bsq2s[g % 2]; rs2 = rs2s[g % 2]; brs2 = brs2s[g % 2]
                    for cc in range(2):
                        s.add("pe", lambda e, cc=cc, sl=sl, pm=pm: e.matmul(ps[pm][:], onesF, v32[:, cc, sl],
                                                                  start=(cc == 0), stop=(cc == 1)),
                              reads=[bv32, bconst], writes=[bps[pm]])
                    for cc in range(2):
                        s.add("dve", lambda e, cc=cc, sl=sl, pm=pm: e.scalar_tensor_tensor(
                            v32[:, cc, sl], ps[pm][:], -1.0 / 256, v32[:, cc, sl], ALU.mult, ALU.add),
                            reads=[bps[pm], bv32], writes=[bv32])
                    s.add("act", lambda e, sl=sl, sq2=sq2: e.activation(sq2[:], v32[:, :, sl], AF.Square),
                          reads=[bv32], writes=[bsq2])
                    for cc in range(2):
                        s.add("pe", lambda e, cc=cc, pv=pv, sq2=sq2: e.matmul(ps[pv][:], onesB[:], sq2[:, cc, :],
                                                             start=(cc == 0), stop=(cc == 1)),
                              reads=[bsq2, bconst], writes=[bps[pv]])
                    s.add("act", lambda e, pv=pv, rs2=rs2: e.activation(rs2[:], ps[pv][:], AF.Sqrt, bias=1e-5, scale=1.0 / 256),
                          reads=[bps[pv]], writes=[brs2])
                    s.add("dve", lambda e, rs2=rs2: e.reciprocal(rs2[:], rs2[:]), reads=[brs2], writes=[brs2])
                    yb_ = yb2[g % 2]; by_ = byb2[g % 2]
                    for cc in range(2):
                        t2 = t2s[(g % 2) * 2 + cc]; bt2 = bt2s[(g % 2) * 2 + cc]
                        s.add("dve", lambda e, cc=cc, sl=sl, t2=t2, rs2=rs2: e.tensor_tensor(t2[:], v32[:, cc, sl], rs2[:], ALU.mult),
                              reads=[bv32, brs2], writes=[bt2])
                        s.add("act", lambda e, cc=cc, yb_=yb_, t2=t2: e.activation(
                            yb_[:, cc, :], t2[:], AF.Silu, bias=cfbb[:, l * 2 + cc:l * 2 + cc + 1],
                            scale=cfg[:, l * 2 + cc:l * 2 + cc + 1]), reads=[bt2, bconst], writes=[by_])
                    s.dma("sp", ys_v[:, 2:4, sl], yb_[:], reads=[by_], writes=[])
                s.barrier(); s.flush()
            ck('mixB')

            with ExitStack() as es2:
                wc_ = [sb(es2, "wc%d" % i, [128, 8, 128], BF16) for i in range(2)]; bwc = [Buf(), Buf()]
                cpads = [sb(es2, "cpad", [128, T + 3], BF16) for _ in range(2)]; bcps = [Buf(), Buf()]
                dg4s = [sb(es2, "dg4", [128, 4, 128], BF16) for _ in range(2)]; bdg4s = [Buf(), Buf()]
                silF = sb(es2, "silF", [128, T]); bsilg = [Buf() for _ in range(NG)]
                sqF = sb(es2, "sqF", [128, T], BF16); bsqg = [Buf() for _ in range(NG)]
                sils = [sb(es2, "sil", [128, 512]) for _ in range(2)]; bsils = [Buf(), Buf()]
                sq3s = [sb(es2, "sq3", [128, 512], BF16) for _ in range(2)]; bsq3s = [Buf(), Buf()]
                rs3s = [sb(es2, "rs3", [128, 512]) for _ in range(2)]; brs3s = [Buf(), Buf()]
                ob = [sb(es2, "ob%d" % i, [128, T], BF16) for i in range(2)]; bob = [Buf(), Buf()]
                wab = sb(es2, "wab", [128, 8, 8], BF16); bwab = Buf()
                for i_ in range(2):
                    s.add("pool", lambda e, i_=i_: e.memset(cpads[i_][:, 0:3], 0.0), writes=[bcps[i_]])
                jobs = []
                for h in range(4):
                    jobs.append(("k", 1792 + h * 128, 4 + h, 0 + h))
                    jobs.append(("q", 1280 + h * 128, 0 + h, 4 + h))
                    jobs.append(("v", 2304 + h * 128, 8 + h, 8 + h))
                    jobs.append(("z", 2816 + h * 128, None, 12 + h))
                for ji, (ty, c0, ci, dj) in enumerate(jobs):
                    w_ = wc_[ji % 2]; bw_ = bwc[ji % 2]
                    o_ = ob[ji % 2]; bo_ = bob[ji % 2]
                    s.dma("pool", w_[:], wv[:, :, c0:c0 + 128], writes=[bw_])
                    if ty == "z":
                        for g in range(NG):
                            sl = slice(g * 512, (g + 1) * 512)
                            proj(w_, bw_, g, g % 2)
                            s.add("act", lambda e, sl=sl, g=g, o_=o_: e.activation(o_[:, sl], ps[g % 2][:], AF.Silu),
                                  reads=[bps[g % 2]], writes=[bo_])
                    else:
                        cpad = cpads[ji % 2]; bcp = bcps[ji % 2]; dg4 = dg4s[ji % 2]; bdg4 = bdg4s[ji % 2]
                        for k in range(4):
                            s.add("pool", lambda e, k=k, ci=ci, dg4=dg4: e.tensor_scalar(
                                dg4[:, k, :], identB[:], dnw[:, l * 48 + ci * 4 + k:l * 48 + ci * 4 + k + 1],
                                None, ALU.mult), reads=[bconst], writes=[bdg4])
                        for g in range(NG):
                            proj(w_, bw_, g, g % 2)
                            s.add("act", lambda e, g=g, cpad=cpad: e.activation(cpad[:, 3 + g * 512:3 + (g + 1) * 512],
                                                                    ps[g % 2][:], AF.Copy),
                                  reads=[bps[g % 2]], writes=[bcp])
                        for g in range(NG):
                            sl = slice(g * 512, (g + 1) * 512)
                            pi = 2 + g % 2
                            for k in range(4):
                                s.add("pe", lambda e, k=k, g=g, pi=pi, dg4=dg4, cpad=cpad: e.matmul(
                                    ps[pi][:], dg4[:, k, :], cpad[:, g * 512 + k:g * 512 + k + 512],
                                    start=(k == 0), stop=(k == 3)), reads=[bdg4, bcp], writes=[bps[pi]])
                            if ty == "v":
                                s.add("act", lambda e, sl=sl, pi=pi, o_=o_: e.activation(o_[:, sl], ps[pi][:], AF.Silu),
                                      reads=[bps[pi]], writes=[bo_])
                            else:
                                s.add("act", lambda e, pi=pi, sl=sl: e.activation(silF[:, sl], ps[pi][:], AF.Silu),
                                      reads=[bps[pi]], writes=[bsilg[g]])
                                s.add("pool", lambda e, sl=sl: e.tensor_tensor(sqF[:, sl], silF[:, sl], silF[:, sl], ALU.mult),
                                      reads=[bsilg[g]], writes=[bsqg[g]])
                        if ty != "v":
                            for g in range(NG):
                                sl = slice(g * 512, (g + 1) * 512)
                                rs3 = rs3s[g % 2]; brs3 = brs3s[g % 2]; pq = 4 + g % 2
                                s.add("pe", lambda e, sl=sl, pq=pq: e.matmul(ps[pq][:], onesB[:], sqF[:, sl], start=True, stop=True),
                                      reads=[bsqg[g], bconst], writes=[bps[pq]])
                                s.add("act", lambda e, rs3=rs3, pq=pq: e.activation(rs3[:], ps[pq][:], AF.Sqrt, bias=EPS, scale=1.0),
                                      reads=[bps[pq]], writes=[brs3])
                                s.add("dve", lambda e, rs3=rs3: e.reciprocal(rs3[:], rs3[:]), reads=[brs3], writes=[brs3])
                                sc = 128.0 ** -0.5 if ty == "q" else 1.0
                                s.add("dve", lambda e, sl=sl, sc=sc, o_=o_, rs3=rs3: e.scalar_tensor_tensor(
                                    o_[:, sl], silF[:, sl], sc, rs3[:], ALU.mult, ALU.mult),
                                    reads=[bsilg[g], brs3], writes=[bo_])
                    s.dma("sp", dnin[:, :, dj, :], o_[:].rearrange("p (n t) -> p n t", t=128), reads=[bo_], writes=[])
                s.dma("pool", wab[:], wv[:, :, 3328:3336], writes=[bwab])
                for n in range(NCH):
                    for k in range(8):
                        s.add("pe", lambda e, n=n, k=k: e.matmul(
                            ps[5][:, n * 8:(n + 1) * 8], hT[:, k, n * 128:(n + 1) * 128], wab[:, k, :],
                            start=(k == 0), stop=(k == 7)), reads=[bwab, bh], writes=[bps[5]])
                s.add("act", lambda e: e.activation(ab_tok[:], ps[5][:, 0:NCH * 8], AF.Copy),
                      reads=[bps[5]], writes=[bab])
                s.barrier(); s.flush()

        ck('dnin')
        with ExitStack() as es:
            def t_(name, shape, dtype=F32):
                return sb(es, name, shape, dtype)
            mU4 = t_("mU4", [128, 7, 512], BF16); mL4 = t_("mL4", [128, 7, 512], BF16)
            i4 = t_("i4", [128, 512], BF16); su4 = t_("su4", [128, 512])
            bm = Buf("masks")
            s.dma("pool", mU4[:], lev_in.rearrange("p (a b) -> p a b", b=512), writes=[bm])
            s.dma("pool", mL4[:], levT_in.rearrange("p (a b) -> p a b", b=512), writes=[bm])
            s.dma("pool", i4[:], i4_in, writes=[bm])
            s.dma("sp", su4[:], su4_in, writes=[bm])
            beta = t_("beta", [128, NCH, 4]); gtok = t_("gtok", [128, NCH, 4])
            xsp = t_("xsp", [128, NCH, 4]); axs = t_("axs", [128, NCH, 4]); lg = t_("lg", [128, NCH, 4])
            nega = t_("nega", [128, 128])
            gam = t_("gam", [128, 128]); eg = t_("eg", [128, 128]); negeg = t_("negeg", [128, 128])
            decl = t_("decl", [128, 128]); eglast = t_("eglast", [128, 128])
            bsc = Buf("scal")
            abv = ab_tok[:].rearrange("p (n c) -> p n c", c=8)
            dtbv = dtb[:, l * 128:(l + 1) * 128].rearrange("p (n c) -> p n c", c=4)
            s.add("act", lambda e: e.activation(beta[:], abv[:, :, 4:8], AF.Sigmoid), reads=[bab], writes=[bsc])
            s.add("dve", lambda e: e.tensor_tensor(xsp[:], abv[:, :, 0:4], dtbv, ALU.add), reads=[bab, bconst], writes=[bsc])
            s.add("act", lambda e: e.activation(axs[:], xsp[:], AF.Abs), reads=[bsc], writes=[bsc])
            s.add("act", lambda e: e.activation(axs[:], axs[:], AF.Exp, scale=-1.0), reads=[bsc], writes=[bsc])
            s.add("act", lambda e: e.activation(lg[:], axs[:], AF.Ln, bias=1.0, scale=1.0), reads=[bsc], writes=[bsc])
            s.add("dve", lambda e: e.tensor_scalar(xsp[:], xsp[:], 0.0, None, ALU.max), reads=[bsc], writes=[bsc])
            s.add("dve", lambda e: e.tensor_tensor(xsp[:], xsp[:], lg[:], ALU.add), reads=[bsc], writes=[bsc])
            s.add("act", lambda e: e.activation(nega[:], alog[:, l * 128:(l + 1) * 128], AF.Exp), reads=[bconst], writes=[bsc])
            s.add("dve", lambda e: e.scalar_tensor_tensor(
                gtok[:].rearrange("p n c -> p (n c)"), xsp[:].rearrange("p n c -> p (n c)"), -1.0, nega[:],
                ALU.mult, ALU.mult), reads=[bsc], writes=[bsc])
            gflat = gtok[:].rearrange("p n c -> p (n c)")
            bflat = beta[:].rearrange("p n c -> p (n c)")
            s.add("pe", lambda e: e.matmul(ps[0][:, 0:128], uincl, gflat, start=True, stop=True),
                  reads=[bsc, bconst], writes=[bps[0]])
            s.add("pe", lambda e: e.matmul(ps[1][:, 0:128], onesF, gflat, start=True, stop=True),
                  reads=[bsc, bconst], writes=[bps[1]])
            s.add("act", lambda e: e.activation(gam[:], ps[0][:, 0:128], AF.Copy), reads=[bps[0]], writes=[bsc])
            s.add("act", lambda e: e.activation(eg[:], ps[0][:, 0:128], AF.Exp), reads=[bps[0]], writes=[bsc])
            s.add("dve", lambda e: e.tensor_scalar(negeg[:], eg[:], -1.0, None, ALU.mult), reads=[bsc], writes=[bsc])
            s.add("dve", lambda e: e.tensor_tensor(decl[:], ps[1][:, 0:128], gam[:], ALU.subtract),
                  reads=[bps[1], bsc], writes=[bsc])
            s.add("act", lambda e: e.activation(decl[:], decl[:], AF.Exp), reads=[bsc], writes=[bsc])
            s.add("act", lambda e: e.activation(eglast[:], ps[1][:, 0:128], AF.Exp), reads=[bps[1]], writes=[bsc])

            ck('D0')
            import os as _os
            NCH_RUN = int(_os.environ.get('DBG_NCH', NCH))
            NL = 3
            S32 = t_("S32", [128, 512]); bS = Buf()
            Sbf = t_("Sbf", [128, 512], BF16); bSb = Buf()
            s.add("dve", lambda e: e.memset(S32[:], 0.0), writes=[bS])
            s.add("pool", lambda e: e.memset(Sbf[:], 0.0), writes=[bSb])
            gd = gdn[:, l * 128:(l + 1) * 128]
            H = lambda a, h: a[:, h * 128:(h + 1) * 128]

            class Lane:
                pass
            lanes = []
            for li_ in range(NL):
                ln = Lane()
                ln.inp = t_("inp", [128, 16, 128], BF16); ln.binp = Buf()
                ln.Mt = t_("Mt", [128, 4, 128]); ln.bMt = Buf()
                ln.E = t_("E", [128, 512]); ln.bE = Buf()
                ln.Es = t_("Es", [128, 512]); ln.bEs = Buf()
                for nm in ("Pm", "Qm", "X", "Y", "R1", "QKm", "kdec", "rp", "vnew", "on", "yc"):
                    setattr(ln, nm, t_(nm, [128, 512], BF16)); setattr(ln, "b" + nm, Buf())
                ln.Qoff = t_("Qoff", [128, 6, 512], BF16); ln.bQoff = Buf()
                for nm in ("vtok", "o2s", "o_t"):
                    setattr(ln, nm, t_(nm, [128, 512])); setattr(ln, "b" + nm, Buf())
                ln.junk = t_("junk", [128, 128]); ln.bjunk = Buf()
                ln.ss = t_("ss", [128, 4]); ln.bss = Buf()
                ln.p = (ps[2 * li_], ps[2 * li_ + 1]); ln.bp = (bps[2 * li_], bps[2 * li_ + 1])
                if li_ % 2 == 0:
                    ln.pT = ps6b; ln.bpT = bps6
                else:
                    ln.pT = ps7b; ln.bpT = bps7
                lanes.append(ln)

            owner = {}

            def acq(me, banks):
                while any(owner.get(id(b_)) not in (None, me) for b_ in banks):
                    yield
                for b_ in banks:
                    owner[id(b_)] = me

            def rel(me, banks):
                for b_ in banks:
                    if owner.get(id(b_)) == me:
                        owner[id(b_)] = None

            def d1_gen(n, ln):
                ip = ln.inp; bip = ln.binp
                p0, p1 = ln.p; b0, b1 = ln.bp; pT = ln.pT; bT = ln.bpT
                col = lambda a, h: a[:, n * 4 + h:n * 4 + h + 1]
                s.dma("sp", ip[:], dnin[:, n, :, :], reads=[], writes=[bip])
                for h in range(4):
                    s.add("act", lambda e, h=h, g_=col(gflat, h): e.activation(ln.Mt[:, h, :], masksl, AF.Identity, bias=0.0, scale=g_),
                          reads=[bsc, bconst], writes=[ln.bMt])
                yield
                for h in range(4):
                    s.add("pe", lambda e, h=h: e.matmul(H(p0, h), ln.Mt[:, h, :], uincl, start=True, stop=False),
                          reads=[ln.bMt, bconst], writes=[b0])
                    s.add("pe", lambda e, h=h: e.matmul(H(p0, h), identF, negm, start=False, stop=True),
                          reads=[ln.bMt, bconst], writes=[b0])
                yield
                s.add("act", lambda e: e.activation(ln.E[:], p0[:], AF.Exp), reads=[b0], writes=[ln.bE])
                yield
                for h in range(4):
                    s.add("pe", lambda e, h=h: e.matmul(H(p0, h), ip[:, h, :], ip[:, h, :], start=True, stop=True),
                          reads=[bip], writes=[b0])
                for h in range(4):
                    s.add("pe", lambda e, h=h: e.matmul(H(p1, h), ip[:, h, :], ip[:, 4 + h, :], start=True, stop=True),
                          reads=[bip], writes=[b1])
                yield
                for h in range(4):
                    s.add("dve", lambda e, h=h, b_=col(bflat, h): e.scalar_tensor_tensor(
                        H(ln.Pm, h), H(p0, h), b_, H(ln.E, h), ALU.mult, ALU.mult),
                        reads=[b0, ln.bE, bsc], writes=[ln.bPm])
                s.add("dve", lambda e: e.tensor_tensor(ln.QKm[:], p1[:], ln.E[:], ALU.mult), reads=[b1, ln.bE], writes=[ln.bQKm])
                yield
                yield from acq(n, (bT,))
                for h in range(4):
                    s.add("pe", lambda e, h=h: e.transpose(H(pT, h), H(ln.Pm, h), identB[:]), reads=[ln.bPm, bconst], writes=[bT])
                s.add("pool", lambda e: e.tensor_tensor(ln.X[:], ln.Pm[:], mU4[:, 0, :], ALU.mult), reads=[ln.bPm, bm], writes=[ln.bX])
                s.add("pool", lambda e: e.tensor_tensor(ln.X[:], i4[:], ln.X[:], ALU.subtract), reads=[ln.bX, bm], writes=[ln.bX])
                yield
                s.add("act", lambda e: e.activation(ln.Qm[:], pT[:, 0:512], AF.Copy), reads=[bT], writes=[ln.bQm])
                rel(n, (bT,))
                yield
                s.add("pool", lambda e: e.tensor_tensor(ln.Y[:], ln.Qm[:], mL4[:, 0, :], ALU.mult), reads=[ln.bQm, bm], writes=[ln.bY])
                s.add("pool", lambda e: e.tensor_tensor(ln.Y[:], i4[:], ln.Y[:], ALU.subtract), reads=[ln.bY, bm], writes=[ln.bY])
                yield
                for li in range(1, 7):
                    last = (li == 6)
                    for h in range(4):
                        s.add("pe", lambda e, h=h, li=li: e.matmul(H(p0, h), H(ln.Qm, h), H(ln.X, h),
                                                                 start=True, stop=True), reads=[ln.bQm, ln.bX], writes=[b0])
                    yield
                    s.add("dve", lambda e, li=li: e.tensor_tensor(ln.R1[:], p0[:], mU4[:, li, :], ALU.mult),
                          reads=[b0, bm], writes=[ln.bR1])
                    yield
                    for h in range(4):
                        s.add("pe", lambda e, h=h: e.matmul(H(p0, h), H(ln.Y, h), H(ln.R1, h), start=True, stop=True),
                              reads=[ln.bY, ln.bR1], writes=[b0])
                    if not last:
                        for h in range(4):
                            s.add("pe", lambda e, h=h: e.matmul(H(p1, h), H(ln.R1, h), H(ln.Y, h), start=True, stop=True),
                                  reads=[ln.bY, ln.bR1], writes=[b1])
                    yield
                    s.add("dve", lambda e: e.tensor_tensor(ln.X[:], ln.X[:], p0[:], ALU.subtract), reads=[ln.bX, b0], writes=[ln.bX])
                    if not last:
                        s.add("dve", lambda e: e.tensor_tensor(ln.Y[:], ln.Y[:], p1[:], ALU.subtract),
                              reads=[ln.bY, b1], writes=[ln.bY])
                    yield

            def d2_gen(n, ln):
                ip = ln.inp; bip = ln.binp
                p0, p1 = ln.p; b0, b1 = ln.bp; pT = ln.pT; bT = ln.bpT
                col = lambda a, h: a[:, n * 4 + h:n * 4 + h + 1]
                yield from acq(n, (bT,))
                for h in range(4):
                    s.add("pe", lambda e, h=h: e.transpose(H(pT, h), ip[:, h, :], identB[:]), reads=[bip, bconst], writes=[bT])
                for h in range(4):
                    s.add("pe", lambda e, h=h: e.transpose(H(pT, 4 + h), ip[:, 8 + h, :], identB[:]), reads=[bip, bconst], writes=[bT])
                for h in range(4):
                    s.add("pe", lambda e, h=h: e.matmul(H(p0, h), ip[:, h, :], H(Sbf, h), start=True, stop=True),
                          reads=[bip, bSb], writes=[b0])
                for h in range(4):
                    s.add("pe", lambda e, h=h: e.matmul(H(p1, h), ip[:, 4 + h, :], H(Sbf, h), start=True, stop=True),
                          reads=[bip, bSb], writes=[b1])
                yield
                s.add("act", lambda e: e.activation(ln.vtok[:], pT[:, 512:1024], AF.Copy), reads=[bT], writes=[ln.bvtok])
                for h in range(4):
                    s.add("dve", lambda e, h=h, d_=col(decl, h): e.tensor_scalar(H(ln.kdec, h), H(pT, h), d_, None, ALU.mult),
                          reads=[bT, bsc], writes=[ln.bkdec])
                rel(n, (bT,))
                yield
                for h in range(4):
                    s.add("dve", lambda e, h=h, ne_=col(negeg, h): e.scalar_tensor_tensor(
                        H(ln.rp, h), H(p0, h), ne_, H(ln.vtok, h), ALU.mult, ALU.add),
                        reads=[b0, ln.bvtok, bsc], writes=[ln.brp])
                s.add("act", lambda e: e.activation(ln.o2s[:], p1[:], AF.Copy), reads=[b1], writes=[ln.bo2s])
                yield
                for h in range(4):
                    s.add("pe", lambda e, h=h: e.matmul(H(p0, h), H(ln.X, h), H(ln.rp, h), start=True, stop=True),
                          reads=[ln.bX, ln.brp], writes=[b0])
                yield
                for h in range(4):
                    s.add("act", lambda e, h=h, b_=col(bflat, h): e.activation(H(ln.vnew, h), H(p0, h), AF.Identity, bias=0.0, scale=b_),
                          reads=[b0, bsc], writes=[ln.bvnew])
                yield
                for h in range(4):
                    s.add("pe", lambda e, h=h: e.matmul(H(p0, h), H(ln.kdec, h), H(ln.vnew, h), start=True, stop=True),
                          reads=[ln.bkdec, ln.bvnew], writes=[b0])
                for h in range(4):
                    s.add("pe", lambda e, h=h: e.matmul(H(p1, h), H(ln.QKm, h), H(ln.vnew, h), start=True, stop=True),
                          reads=[ln.bQKm, ln.bvnew], writes=[b1])
                yield
                for h in range(4):
                    s.add("dve", lambda e, h=h, el_=col(eglast, h): e.scalar_tensor_tensor(
                        H(S32, h), H(S32, h), el_, H(p0, h), ALU.mult, ALU.add),
                        reads=[bS, b0, bsc], writes=[bS])
                yield
                s.add("act", lambda e: e.activation(Sbf[:], S32[:], AF.Copy), reads=[bS], writes=[bSb])
                for h in range(4):
                    s.add("dve", lambda e, h=h, eg_=col(eg, h): e.scalar_tensor_tensor(
                        H(ln.o_t, h), H(ln.o2s, h), eg_, H(p1, h), ALU.mult, ALU.add),
                        reads=[b1, ln.bo2s, bsc], writes=[ln.bo_t])
                s.add("pool", lambda e: e.memset(ln.ss[:], 0.0), writes=[ln.bss])
                yield

            def d3_gen(n, ln):
                ip = ln.inp; bip = ln.binp
                pT = ln.pT; bT = ln.bpT
                for h in range(4):
                    s.add("act", lambda e, h=h: e.activation(ln.junk[:], H(ln.o_t, h), AF.Square, accum_out=ln.ss[:, h:h + 1]),
                          reads=[ln.bo_t, ln.bss], writes=[ln.bjunk, ln.bss])
                yield
                s.add("act", lambda e: e.activation(ln.ss[:], ln.ss[:], AF.Sqrt, bias=EPS, scale=1.0 / 128), reads=[ln.bss], writes=[ln.bss])
                yield
                s.add("dve", lambda e: e.reciprocal(ln.ss[:], ln.ss[:]), reads=[ln.bss], writes=[ln.bss])
                yield
                for h in range(4):
                    s.add("dve", lambda e, h=h: e.scalar_tensor_tensor(H(ln.on, h), H(ln.o_t, h), ln.ss[:, h:h + 1], gd, ALU.mult, ALU.mult),
                          reads=[ln.bo_t, ln.bss, bconst], writes=[ln.bon])
                yield
                yield from acq(n, (bT,))
                for h in range(4):
                    s.add("pe", lambda e, h=h: e.transpose(H(pT, 4 + h), H(ln.on, h), identB[:]), reads=[ln.bon, bconst], writes=[bT])
                yield
                s.add("dve", lambda e: e.tensor_tensor(
                    ln.yc[:], pT[:, 512:1024], ip[:, 12:16, :].rearrange("p a b -> p (a b)"), ALU.mult),
                    reads=[bT, bip], writes=[ln.byc])
                rel(n, (bT,))
                s.dma("sp", ys_v[:, 4:8, n * 128:(n + 1) * 128], ln.yc[:].rearrange("p (a b) -> p a b", b=128),
                      reads=[ln.byc], writes=[])
                yield

            def chain(n, ln):
                yield from d1_gen(n, ln)
                while d2_turn[0] != n:
                    yield
                yield from d2_gen(n, ln)
                d2_turn[0] = n + 1
                yield from d3_gen(n, ln)

            d2_turn = [0]
            active = {}
            nxt = 0
            while nxt < NCH_RUN or active:
                while nxt < NCH_RUN and (nxt % NL) not in active:
                    active[nxt % NL] = chain(nxt, lanes[nxt % NL]); nxt += 1
                    break
                for k in sorted(active.keys(), key=lambda kk: kk):
                    try:
                        next(active[k])
                    except StopIteration:
                        del active[k]
            s.barrier(); s.flush()

        ck("mixy%d" % l)
        esF = ExitStack()
        wfi = sb(esF, "wfi", [128, 8, 2 * DFF], BF16); bwfi = Buf()
        wfiv = w_fi[l].rearrange("(k p) c -> p k c", p=128)
        with ExitStack() as es:
            wo = sb(es, "wo", [128, 8, D], BF16); bwo = Buf()
            wov = w_out[l].rearrange("(k p) c -> p k c", p=128)
            for k in range(8):
                s.dma("pool", wo[:, k, :], wov[:, k, :], writes=[bwo])
            wfi_jobs = [(k, hh) for k in range(8) for hh in range(2)]
            yt = [sb(es, "yt%d" % i, [128, 8, 512], BF16) for i in range(2)]; byt = [Buf(), Buf()]
            xt = [sb(es, "xo%d" % i, [128, 8, 512]) for i in range(2)]; bxt = [Buf(), Buf()]
            for g in range(NG):
                sl = slice(g * 512, (g + 1) * 512)
                y_ = yt[g % 2]; by_ = byt[g % 2]; x_ = xt[g % 2]; bx_ = bxt[g % 2]
                s.dma("pool", y_[:], ys_v[:, :, sl], writes=[by_])
                s.dma("sp", x_[:], xs_v[:, :, sl], writes=[bx_])
                for (k, hh) in wfi_jobs[2 * g:2 * g + 2]:
                    s.dma("pool", wfi[:, k, hh * DFF:(hh + 1) * DFF], wfiv[:, k, hh * DFF:(hh + 1) * DFF], writes=[bwfi])
                for oc in range(8):
                    pi = oc % 4
                    for k in range(8):
                        s.add("pe", lambda e, k=k, oc=oc, pi=pi, y_=y_: e.matmul(
                            ps[pi][:], wo[:, k, oc * 128:(oc + 1) * 128], y_[:, k, :], start=(k == 0), stop=(k == 7)),
                            reads=[bwo, by_], writes=[bps[pi]])
                    s.add("dve", lambda e, oc=oc, pi=pi, x_=x_: e.scalar_tensor_tensor(
                        x_[:, oc, :], ps[pi][:], G1(l, oc), x_[:, oc, :], ALU.mult, ALU.add),
                        reads=[bps[pi], bx_, bmod], writes=[bx_])
                s.dma("act", xs_v[:, :, sl], x_[:], reads=[bx_], writes=[])
            s.barrier(); s.flush()
        ck("mix%d" % l)

        with ExitStack() as es:
            wfo = sb(es, "wfo", [128, 22, D], BF16); bwfo = Buf()
            wfov = w_fo[l].rearrange("(k p) c -> p k c", p=128)
            for k in range(22):
                s.dma("pool", wfo[:, k, :], wfov[:, k, :], writes=[bwfo])
            GF = 256
            NGF = T // GF
            xts = [sb(es, "xf", [128, 8, GF]) for _ in range(2)]; bxts = [Buf(), Buf()]
            hT2s = [sb(es, "hT2", [128, 8, GF], BF16) for _ in range(2)]; bh2s = [Buf(), Buf()]
            sq = sb(es, "sqf", [128, 8, GF], BF16); bsq = Buf()
            rs = sb(es, "rsf", [128, GF]); brs = Buf()
            aTs = [sb(es, "aT", [128, 22, GF], BF16) for _ in range(2)]; baTs = [Buf(), Buf()]
            sgf = [sb(es, "sgf%d" % i, [128, GF]) for i in range(2)]; bsgf = [Buf(), Buf()]

            def f_norm(g):
                xt = xts[g % 2]; bxt = bxts[g % 2]; hT2 = hT2s[g % 2]; bh2 = bh2s[g % 2]
                sl = slice(g * GF, (g + 1) * GF)
                s.dma("sp", xt[:], xs_v[:, :, sl], reads=[bxs], writes=[bxt])
                s.add("act", lambda e: e.activation(sq[:], xt[:], AF.Square), reads=[bxt], writes=[bsq])
                for j in range(8):
                    s.add("pe", lambda e, j=j: e.matmul(ps[5][:, 0:GF], onesB[:], sq[:, j, :], start=(j == 0), stop=(j == 7)),
                          reads=[bsq, bconst], writes=[bps[5]])
                s.add("act", lambda e: e.activation(rs[:], ps[5][:, 0:GF], AF.Sqrt, bias=EPS, scale=1.0 / D),
                      reads=[bps[5]], writes=[brs])
                s.add("dve", lambda e: e.reciprocal(rs[:], rs[:]), reads=[brs], writes=[brs])
                for j in range(8):
                    s.add("dve", lambda e, j=j: e.scalar_tensor_tensor(
                        sq[:, j, :], xt[:, j, :], A2[:, l * 8 + j:l * 8 + j + 1], rs[:], ALU.mult, ALU.mult),
                        reads=[bxt, brs, bmod, bsq], writes=[bsq])
                    s.add("act", lambda e, j=j: e.activation(
                        hT2[:, j, :], sq[:, j, :], AF.Identity, bias=B2(l, j), scale=1.0),
                        reads=[bsq, bmod], writes=[bh2])

            def f_in(g):
                hT2 = hT2s[g % 2]; bh2 = bh2s[g % 2]; aT = aTs[g % 2]; baT = baTs[g % 2]
                for j in range(22):
                    pg, pu = (0, 1) if j % 2 == 0 else (2, 3)
                    for k in range(8):
                        s.add("pe", lambda e, k=k, j=j, pg=pg: e.matmul(
                            ps[pg][:, 0:GF], wfi[:, k, j * 128:(j + 1) * 128], hT2[:, k, :], start=(k == 0), stop=(k == 7)),
                            reads=[bwfi, bh2], writes=[bps[pg]])
                    for k in range(8):
                        s.add("pe", lambda e, k=k, j=j, pu=pu: e.matmul(
                            ps[pu][:, 0:GF], wfi[:, k, DFF + j * 128:DFF + (j + 1) * 128], hT2[:, k, :],
                            start=(k == 0), stop=(k == 7)), reads=[bwfi, bh2], writes=[bps[pu]])
                    sg_ = sgf[j % 2]; bsg_ = bsgf[j % 2]
                    s.add("act", lambda e, pg=pg, sg_=sg_: e.activation(sg_[:], ps[pg][:, 0:GF], AF.Silu),
                          reads=[bps[pg]], writes=[bsg_])
                    s.add("dve", lambda e, j=j, pu=pu, sg_=sg_: e.tensor_tensor(aT[:, j, :], sg_[:], ps[pu][:, 0:GF], ALU.mult),
                          reads=[bsg_, bps[pu]], writes=[baT])

            def f_out(g):
                xt = xts[g % 2]; bxt = bxts[g % 2]; aT = aTs[g % 2]; baT = baTs[g % 2]
                sl = slice(g * GF, (g + 1) * GF)
                for oc in range(8):
                    pi = 4 if oc % 2 == 0 else (0, 2)[(oc // 2) % 2]
                    for k in range(22):
                        s.add("pe", lambda e, k=k, oc=oc, pi=pi: e.matmul(
                            ps[pi][:, 0:GF], wfo[:, k, oc * 128:(oc + 1) * 128], aT[:, k, :], start=(k == 0), stop=(k == 21)),
                            reads=[bwfo, baT], writes=[bps[pi]])
                    s.add("dve", lambda e, oc=oc, pi=pi: e.scalar_tensor_tensor(
                        xt[:, oc, :], ps[pi][:, 0:GF], G2(l, oc), xt[:, oc, :], ALU.mult, ALU.add),
                        reads=[bps[pi], bxt, bmod], writes=[bxt])
                s.dma("pool", xs_v[:, :, sl], xt[:], reads=[bxt], writes=[bxs])

            f_norm(0)
            for g in range(NGF):
                f_in(g)
                if g + 1 < NGF:
                    f_norm(g + 1)
                f_out(g)
            s.barrier(); s.flush()
        esF.close()
        ck("ffn%d" % l)

    s.muted = False
    dumpy = stop in ('mixA', 'mixB') or (stop is not None and stop.startswith('mixy'))
    with ExitStack() as es:
        xt = [sb(es, "xz%d" % i, [128, 8, 512]) for i in range(2)]; bxt = [Buf(), Buf()]
        sq = sb(es, "sqz", [128, 8, 512], BF16); bsq = Buf()
        rs = sb(es, "rsz", [128, 512]); brs = Buf()
        if dumpy:
            yt = [sb(es, "yz%d" % i, [128, 8, 512], BF16) for i in range(2)]; byt = [Buf(), Buf()]
        for g in range(NG):
            x_ = xt[g % 2]; bx_ = bxt[g % 2]
            if stop is None:
                sl = norm_group((sq, bsq, rs, brs), 0, g, None, None, None, None, x_, bx_)
                for j in range(8):
                    s.add("dve", lambda e, j=j, x_=x_: e.scalar_tensor_tensor(
                        x_[:, j, :], x_[:, j, :], gfin[:, j:j + 1], rs[:], ALU.mult, ALU.mult),
                        reads=[bx_, brs, bconst], writes=[bx_])
            elif dumpy:
                sl = slice(g * 512, (g + 1) * 512)
                y_ = yt[g % 2]; by_ = byt[g % 2]
                s.dma("sp", y_[:], ys_v[:, :, sl], writes=[by_])
                s.add("dve", lambda e, x_=x_, y_=y_: e.tensor_copy(x_[:], y_[:]), reads=[by_], writes=[bx_])
            else:
                sl = slice(g * 512, (g + 1) * 512)
                s.dma("sp", x_[:], xs_v[:, :, sl], writes=[bx_])
            s.dma("sp", out_v[:, :, sl], x_[:], reads=[bx_], writes=[])
        s.flush(final=True)
    top.close()
    s.close()
    nc._nops = s.nops
    return nc


def _pp(a):
    a = np.asarray(a, np.float32)
    lead = a.shape[:-1]
    n = a.shape[-1] // 128
    a = a.reshape(lead + (n, 128))
    a = np.moveaxis(a, -1, 0)
    return np.ascontiguousarray(a.reshape(128, -1))


def make_in_maps(inputs, depth=L, ncores=NCORES):
    f = lambda k: np.asarray(inputs[k], np.float32)
    m = _levels_masks()
    common = {
        "w_ada": f("w_ada")[:depth], "b_ada": f("b_ada")[:depth], "w_in": f("w_in")[:depth], "w_out": f("w_out")[:depth],
        "w_ffn_in": f("w_ffn_in")[:depth], "w_ffn_out": f("w_ffn_out")[:depth],
        "gm": _pp(f("norm_mix_g")), "gf": _pp(f("norm_ffn_g")), "gfin": _pp(f("final_norm_g")),
        "caw": _pp(np.transpose(f("conv_a_w"), (0, 2, 1)).reshape(L, 2, 128, 3).transpose(0, 1, 3, 2)),
        "cfw": _pp(np.transpose(f("conf_dw_w"), (0, 2, 1)).reshape(L, 2, 128, 31).transpose(0, 1, 3, 2)),
        "cfb": _pp(f("conf_dw_b")), "cfg": _pp(f("conf_ln_g")), "cfbb": _pp(f("conf_ln_b")),
        "dnw": _pp(np.transpose(f("dn_conv_w"), (0, 2, 1)).reshape(L, 12, 128, 4).transpose(0, 1, 3, 2)),
        "alog": np.ascontiguousarray(np.broadcast_to(np.tile(f("dn_a_log"), (1, NCH)).reshape(1, L * 128), (128, L * 128))),
        "dtb": np.ascontiguousarray(np.broadcast_to(np.tile(f("dn_dt_bias"), (1, NCH)).reshape(1, L * 128), (128, L * 128))),
        "gdn": np.ascontiguousarray(np.broadcast_to(f("dn_norm_g").reshape(1, L * 128), (128, L * 128))),
        "cmask": np.ascontiguousarray(np.concatenate(
            [m["ident"], m["uincl"], m["masksl"], m["negm"], m["strictu"], m["ones"]], axis=1)),
        "levU4": np.ascontiguousarray(np.concatenate([np.tile(m["levU"][i], (1, 4)) for i in range(7)], axis=1)),
        "levL4": np.ascontiguousarray(np.concatenate([np.tile(m["levU"][i].T, (1, 4)) for i in range(7)], axis=1)),
        "ident4": np.ascontiguousarray(np.tile(m["ident"], (1, 4))),
        "strictu4": np.ascontiguousarray(np.tile(m["strictu"], (1, 4))),
    }
    x = f("x"); c = f("c")
    maps = []
    for core in range(ncores):
        b = core % 4
        d = dict(common)
        d["xT"] = np.ascontiguousarray(x[b].T)
        d["cT"] = np.ascontiguousarray(c[b].reshape(8, 128).T)
        maps.append(d)
    return maps


_NC_CACHE = {}


def kernel(**inputs):
    if "nc" not in _NC_CACHE:
        _NC_CACHE["nc"] = build_program()
    nc = _NC_CACHE["nc"]
    in_maps = make_in_maps(inputs)
    res = run_bass_kernel_spmd(nc, in_maps, core_ids=list(range(NCORES)))
    out = np.stack([np.asarray(res.results[b]["outT"], np.float32).T for b in range(4)], axis=0)
    return np.ascontiguousarray(out)
```

```python
from contextlib import ExitStack

import numpy as np
import concourse.bass as bass
import concourse.mybir as mybir
from concourse.bass_utils import run_bass_kernel_spmd

F32 = mybir.dt.float32
BF16 = mybir.dt.bfloat16
AF = mybir.ActivationFunctionType
ALU = mybir.AluOpType


class Buf:
    __slots__ = ("name", "w", "r", "excl")

    def __init__(self, name="", excl=False):
        self.name = name
        self.w = None
        self.r = []
        self.excl = excl


class _Op:
    __slots__ = ("eng", "fn", "deps", "is_dma", "sem", "val", "signal", "emitted")


class Sched:
    CENG = ("pe", "act", "dve", "pool")
    ENG = ("pe", "act", "dve", "pool", "sp")
    DMAQ = ("sp", "pool", "act")

    def __init__(self, nc, strict=True, dma_k=6):
        self.nc = nc
        self.strict = strict
        self.K = dma_k
        self.es = ExitStack()
        self.csem = {e: self.es.enter_context(nc.semaphore("c_" + e)) for e in self.CENG}
        self.ccount = {e: 0 for e in self.CENG}
        self.dsem = {q: [self.es.enter_context(nc.semaphore("d_%s%d" % (q, i))) for i in range(dma_k)]
                     for q in self.DMAQ}
        self.dcount = {q: 0 for q in self.DMAQ}
        self.dhist = {q: [] for q in self.DMAQ}
        self.known = {e: {} for e in self.ENG}
        self.pending = {e: [] for e in self.ENG}
        self.last = {e: None for e in self.CENG}
        self.barrier_ops = []
        self.nops = 0
        self.muted = False

    def _mk(self, eng, fn, reads, writes, is_dma):
        op = _Op()
        op.eng = eng
        op.fn = fn
        op.is_dma = is_dma
        op.signal = is_dma
        op.sem = None
        op.val = None
        op.emitted = False
        deps = list(self.barrier_ops)
        for b in reads:
            if b.w is not None:
                deps.append(b.w)
            if b.excl:
                deps.extend(r for r in b.r if r.eng != eng)
        for b in writes:
            if b.w is not None:
                deps.append(b.w)
            deps.extend(b.r)
        op.deps = deps
        for b in reads:
            b.r.append(op)
        for b in writes:
            b.w = op
            b.r = []
        self.pending[eng].append(op)
        self.nops += 1
        return op

    def add(self, eng, fn, reads=(), writes=()):
        if self.muted:
            return None
        op = self._mk(eng, fn, reads, writes, False)
        self.last[eng] = op
        return op

    def dma(self, q, out, in_, reads=(), writes=(), **kw):
        if self.muted:
            return None
        def fn(e, out=out, in_=in_, kw=kw):
            return e.dma_start(out=out, in_=in_, **kw)
        op = self._mk(q, fn, reads, writes, True)
        i = self.dcount[q]
        self.dcount[q] += 1
        op.sem = self.dsem[q][i % self.K]
        op.val = 16 * (i // self.K + 1)
        if i >= self.K:
            op.deps.append(self.dhist[q][i - self.K])
        self.dhist[q].append(op)
        return op

    def barrier(self):
        if self.muted:
            return
        ops = [self.last[e] for e in self.CENG if self.last[e] is not None]
        for q in self.DMAQ:
            ops.extend(self.dhist[q][-self.K:])
        self.barrier_ops = ops

    def _need(self, op, dep):
        if dep.is_dma:
            return True
        if dep.eng != op.eng:
            return True
        if op.eng == "pe":
            return False
        if op.is_dma:
            return True
        return self.strict

    def flush(self, final=False):
        nc = self.nc
        if not final and not any(self.pending[e] for e in self.ENG):
            return
        for e in self.ENG:
            for op in self.pending[e]:
                for d in op.deps:
                    if not d.is_dma and not d.emitted and self._need(op, d):
                        d.signal = True
        for e in self.CENG:
            comp = [o for o in self.pending[e] if not o.is_dma]
            if comp:
                comp[-1].signal = True
        for e in self.CENG:
            c = self.ccount[e]
            comp = [o for o in self.pending[e] if not o.is_dma]
            for o in comp:
                if o.signal:
                    c += 1
                    o.val = c
                o.sem = self.csem[e]
            self.ccount[e] = c
            nxt = None
            for o in reversed(comp):
                if o.signal:
                    nxt = o.val
                else:
                    o.val = nxt
        getter = {"pe": "tensor", "act": "scalar", "dve": "vector", "pool": "gpsimd", "sp": "sync"}

        def run(e, eng):
            known = self.known[e]
            for op in self.pending[e]:
                waits = {}
                for d in op.deps:
                    if not self._need(op, d):
                        continue
                    key = id(d.sem)
                    if known.get(key, 0) >= d.val:
                        continue
                    if key not in waits or waits[key][1] < d.val:
                        waits[key] = (d.sem, d.val)
                for key, (sem, val) in waits.items():
                    eng.wait_ge(sem, val)
                    known[key] = val
                inst = op.fn(eng)
                if op.signal:
                    inst.then_inc(op.sem, 16 if op.is_dma else 1)
                op.emitted = True
                op.fn = None
            if final and e == "sp":
                for q in self.DMAQ:
                    n = self.dcount[q]
                    for j in range(self.K):
                        cnt = len(range(j, n, self.K))
                        if cnt:
                            eng.wait_ge(self.dsem[q][j], 16 * cnt)

        with nc.Block() as blk:
            for e in self.ENG:
                if not self.pending[e] and not (final and e == "sp"):
                    continue
                getattr(blk, getter[e])(lambda eng, e=e: run(e, eng))
        self.pending = {e: [] for e in self.ENG}

    def close(self):
        self.es.close()


D = 1024
T = 4096
L = 4
NG = T // 512
NCH = T // 128
DFF = 2816
INC = 3336
EPS = 1e-6
NCORES = 8


class _Stop(Exception):
    pass


def _levels_masks():
    i = np.arange(128)
    s_, c_ = np.meshgrid(i, i, indexing="ij")
    out = {}
    out["ident"] = (s_ == c_).astype(np.float32)
    out["uincl"] = (s_ <= c_).astype(np.float32)
    out["masksl"] = (s_ > c_).astype(np.float32)
    out["negm"] = np.where(c_ < s_, -30000.0, 0.0).astype(np.float32)
    out["strictu"] = (s_ < c_).astype(np.float32)
    out["ones"] = np.ones((128, 128), np.float32)
    lev = []
    for b in (1, 2, 4, 8, 16, 32, 64):
        m = ((s_ // (2 * b)) == (c_ // (2 * b))) & ((s_ // b) % 2 == 0) & ((c_ // b) % 2 == 1)
        lev.append(m.astype(np.float32))
    out["levU"] = np.stack(lev)
    return out


def build_program(depth=L, stop=None):
    nc = bass.Bass("TRN2", target_bir_lowering=False)
    dt = nc.dram_tensor

    def din(name, shape, dtype=F32):
        return dt(name, list(shape), dtype, kind="ExternalInput").ap()

    xT_in = din("xT", [D, T])
    cT_in = din("cT", [128, 8])
    w_ada = din("w_ada", [depth, D, 6 * D])
    b_ada = din("b_ada", [depth, 6 * D])
    w_in = din("w_in", [depth, D, INC])
    w_out = din("w_out", [depth, D, D])
    w_fi = din("w_ffn_in", [depth, D, 2 * DFF])
    w_fo = din("w_ffn_out", [depth, DFF, D])
    gm_in = din("gm", [128, L * 8])
    gf_in = din("gf", [128, L * 8])
    gfin_in = din("gfin", [128, 8])
    caw_in = din("caw", [128, L * 2 * 3])
    cfw_in = din("cfw", [128, L * 2 * 31])
    cfb_in = din("cfb", [128, L * 2])
    cfg_in = din("cfg", [128, L * 2])
    cfbb_in = din("cfbb", [128, L * 2])
    dnw_in = din("dnw", [128, L * 12 * 4])
    alog_in = din("alog", [128, L * 128])
    dtb_in = din("dtb", [128, L * 128])
    gdn_in = din("gdn", [128, L * 128])
    cm_in = din("cmask", [128, 6 * 128])
    lev_in = din("levU4", [128, 7 * 512])
    levT_in = din("levL4", [128, 7 * 512])
    i4_in = din("ident4", [128, 512])
    su4_in = din("strictu4", [128, 512])
    outT = dt("outT", [D, T], F32, kind="ExternalOutput").ap()

    xs = dt("xs", [D, T], F32, kind="Internal").ap()
    ys = dt("ys", [D, T], BF16, kind="Internal").ap()
    dnin = dt("dnin", [128, NCH, 16, 128], BF16, kind="Internal").ap()

    import os as _os2
    s = Sched(nc, strict=(_os2.environ.get("MK_STRICT", "1") == "1"))
    top = ExitStack()

    _cnt = [0]

    def sb(es, name, shape, dtype=F32):
        _cnt[0] += 1
        return es.enter_context(nc.sbuf_tensor("%s_%d" % (name, _cnt[0]), list(shape), dtype))

    ps = [top.enter_context(nc.psum_tensor("ps%d" % i, [128, 512], F32)) for i in range(6)]
    ps6b = top.enter_context(nc.psum_tensor("ps6b", [128, 1024], BF16))
    ps7b = top.enter_context(nc.psum_tensor("ps7b", [128, 1024], BF16))
    bps = [Buf("ps%d" % i, excl=True) for i in range(6)]
    bps6, bps7 = Buf("ps6b", excl=True), Buf("ps7b", excl=True)

    modT = sb(top, "modT", [128, L, 48])
    A1 = sb(top, "A1", [128, L * 8]); A2 = sb(top, "A2", [128, L * 8])
    gm = sb(top, "gm_t", [128, L * 8]); gf = sb(top, "gf_t", [128, L * 8]); gfin = sb(top, "gfin_t", [128, 8])
    caw = sb(top, "caw_t", [128, L * 6]); cfw = sb(top, "cfw_t", [128, L * 62])
    cfb = sb(top, "cfb_t", [128, L * 2]); cfg = sb(top, "cfg_t", [128, L * 2]); cfbb = sb(top, "cfbb_t", [128, L * 2])
    dnw = sb(top, "dnw_t", [128, L * 48])
    alog = sb(top, "alog_t", [128, L * 128]); dtb = sb(top, "dtb_t", [128, L * 128]); gdn = sb(top, "gdn_t", [128, L * 128])
    cmF = sb(top, "cmF", [128, 6 * 128])
    identB = sb(top, "identB", [128, 128], BF16)
    onesB = sb(top, "onesB", [128, 128], BF16)
    ab_tok = sb(top, "ab_tok", [128, NCH * 8])
    bconst = Buf("const")
    bmod = Buf("mod")
    bab = Buf("abtok")
    identF = cmF[:, 0:128]; uincl = cmF[:, 128:256]; masksl = cmF[:, 256:384]
    negm = cmF[:, 384:512]; strictu = cmF[:, 512:640]; onesF = cmF[:, 640:768]

    for (t_, src) in ((gm, gm_in), (gf, gf_in), (gfin, gfin_in), (caw, caw_in), (cfw, cfw_in), (cfb, cfb_in),
                      (cfg, cfg_in), (cfbb, cfbb_in), (dnw, dnw_in), (alog, alog_in), (dtb, dtb_in),
                      (gdn, gdn_in), (cmF, cm_in)):
        s.dma("sp", t_[:], src, writes=[bconst])
    s.dma("pool", identB[:], cm_in[:, 0:128], writes=[bconst])
    s.dma("pool", onesB[:], cm_in[:, 640:768], writes=[bconst])
    bxs = Buf("xs")
    for j in range(8):
        s.dma("sp", xs[j * 128:(j + 1) * 128, :], xT_in[j * 128:(j + 1) * 128, :], writes=[bxs])

    with ExitStack() as es:
        cT = sb(es, "cT_t", [128, 8]); cact = sb(es, "cact", [128, 8])
        wt = [sb(es, "wada%d" % i, [128, 2048]) for i in range(6)]
        bwt = [Buf() for _ in range(6)]
        modrow = sb(es, "modrow", [1, 6 * D]); brow = sb(es, "brow", [1, 6 * D])
        one11 = sb(es, "one11", [1, 1])
        bc, bmr, bbr = Buf(), Buf(), Buf()
        s.dma("sp", cT[:], cT_in, writes=[bc])
        s.add("act", lambda e: e.activation(cact[:], cT[:], AF.Silu), reads=[bc], writes=[bc])
        s.add("dve", lambda e: e.memset(one11[:], 1.0), writes=[bc])
        it = 0
        for l in range(depth):
            s.dma("sp", brow[:], b_ada[l:l + 1, :], reads=[], writes=[bbr])
            for cg in range(3):
                for k in range(8):
                    w_ = wt[it % 6]; bw_ = bwt[it % 6]; it += 1
                    s.dma(("sp", "act", "pool")[it % 3], w_[:], w_ada[l, k * 128:(k + 1) * 128, cg * 2048:(cg + 1) * 2048],
                          writes=[bw_])
                    for i in range(4):
                        s.add("pe", lambda e, i=i, w_=w_, k=k: e.matmul(
                            ps[i][0:1, :], cact[:, k:k + 1], w_[:, i * 512:(i + 1) * 512],
                            start=(k == 0), stop=(k == 7)), reads=[bw_, bc], writes=[bps[i]])
                for i in range(4):
                    c0 = cg * 2048 + i * 512
                    s.add("dve", lambda e, i=i, c0=c0: e.tensor_tensor(
                        modrow[0:1, c0:c0 + 512], ps[i][0:1, :], brow[0:1, c0:c0 + 512], ALU.add),
                        reads=[bps[i], bbr], writes=[bmr])
            for j in range(48):
                s.add("pe", lambda e, j=j: e.matmul(ps[4][:, j:j + 1], modrow[0:1, j * 128:(j + 1) * 128],
                                                     one11[0:1, 0:1], start=True, stop=True),
                      reads=[bmr, bc], writes=[bps[4]])
            s.add("act", lambda e, l=l: e.activation(modT[:, l, :], ps[4][:, 0:48], AF.Copy),
                  reads=[bps[4]], writes=[bmod])
            s.add("dve", lambda e, l=l: e.scalar_tensor_tensor(
                A1[:, l * 8:(l + 1) * 8], modT[:, l, 8:16], 1.0, gm[:, l * 8:(l + 1) * 8], ALU.add, ALU.mult),
                reads=[bmod, bconst], writes=[bmod])
            s.add("dve", lambda e, l=l: e.scalar_tensor_tensor(
                A2[:, l * 8:(l + 1) * 8], modT[:, l, 32:40], 1.0, gf[:, l * 8:(l + 1) * 8], ALU.add, ALU.mult),
                reads=[bmod, bconst], writes=[bmod])
        s.barrier()
        s.flush()

    def B1(l, j): return modT[:, l, j:j + 1]
    def G1(l, j): return modT[:, l, 16 + j:17 + j]
    def B2(l, j): return modT[:, l, 24 + j:25 + j]
    def G2(l, j): return modT[:, l, 40 + j:41 + j]

    xs_v = xs.rearrange("(j p) t -> p j t", p=128)
    ys_v = ys.rearrange("(j p) t -> p j t", p=128)
    out_v = outT.rearrange("(j p) t -> p j t", p=128)

    def norm_group(es_tiles, l, g, Acol, Bcol, hdst, hbuf, xt, bxt, load=True, pn=5, extra_w=()):
        sq, bsq, rs, brs = es_tiles
        sl = slice(g * 512, (g + 1) * 512)
        if load:
            s.dma("sp" if g % 2 == 0 else "pool", xt[:], xs_v[:, :, sl], reads=[bxs], writes=[bxt] + list(extra_w))
        s.add("act", lambda e: e.activation(sq[:], xt[:], AF.Square), reads=[bxt], writes=[bsq])
        for j in range(8):
            s.add("pe", lambda e, j=j: e.matmul(ps[pn][:], onesB[:], sq[:, j, :], start=(j == 0), stop=(j == 7)),
                  reads=[bsq, bconst], writes=[bps[pn]])
        s.add("act", lambda e: e.activation(rs[:], ps[pn][:], AF.Sqrt, bias=EPS, scale=1.0 / D),
              reads=[bps[pn]], writes=[brs])
        s.add("dve", lambda e: e.reciprocal(rs[:], rs[:]), reads=[brs], writes=[brs])
        return sl

    def ck(name):
        if stop == name:
            s.muted = True
    ck('pro')
    for l in range(depth):
        with ExitStack() as es:
            hT = sb(es, "hT", [128, 8, T], BF16); bh = Buf("hT")
            with ExitStack() as es2:
                xt = [sb(es2, "xt%d" % i, [128, 8, 512]) for i in range(3)]; bxt = [Buf(), Buf(), Buf()]
                bxj = [[Buf() for _ in range(8)] for _ in range(3)]
                sqs = [sb(es2, "sq", [128, 8, 512], BF16) for _ in range(2)]; bsqs = [Buf(), Buf()]
                rss = [sb(es2, "rs", [128, 512]) for _ in range(2)]; brss = [Buf(), Buf()]
                def n1_stats(g):
                    x_ = xt[g % 3]; bx_ = bxt[g % 3]
                    sq = sqs[g % 2]; bsq = bsqs[g % 2]; rs = rss[g % 2]; brs = brss[g % 2]
                    norm_group((sq, bsq, rs, brs), l, g, None, None, None, None, x_, bx_, pn=4 + g % 2, extra_w=bxj[g % 3])

                def n1_apply(g):
                    x_ = xt[g % 3]; bx_ = bxt[g % 3]; rs = rss[g % 2]; brs = brss[g % 2]
                    sl = slice(g * 512, (g + 1) * 512)
                    for j in range(8):
                        s.add("dve", lambda e, j=j, x_=x_, rs=rs: e.scalar_tensor_tensor(
                            x_[:, j, :], x_[:, j, :], A1[:, l * 8 + j:l * 8 + j + 1], rs[:], ALU.mult, ALU.mult),
                            reads=[bx_, brs, bmod], writes=[bxj[g % 3][j]])
                        s.add("act", lambda e, j=j, x_=x_, sl=sl: e.activation(
                            hT[:, j, sl], x_[:, j, :], AF.Identity, bias=B1(l, j), scale=1.0),
                            reads=[bxj[g % 3][j], bmod], writes=[bh])

                n1_stats(0)
                for g in range(NG):
                    if g + 1 < NG:
                        n1_stats(g + 1)
                    n1_apply(g)
                s.barrier(); s.flush()
            ck('n1')

            wv = w_in[l].rearrange("(k p) c -> p k c", p=128)

            def proj(wtile, bw, g, pidx):
                for k in range(8):
                    s.add("pe", lambda e, k=k: e.matmul(ps[pidx][:], wtile[:, k, :], hT[:, k, g * 512:(g + 1) * 512],
                                                         start=(k == 0), stop=(k == 7)),
                          reads=[bw, bh], writes=[bps[pidx]])

            with ExitStack() as es2:
                wa = [sb(es2, "wa%d" % i, [128, 8, 128], BF16) for i in range(3)]; bwa = [Buf() for _ in range(3)]
                ppad = sb(es2, "ppad", [128, T + 2]); bpp = Buf()
                abt = sb(es2, "abt", [128, T]); bab_ = Buf()
                acc = sb(es2, "acc", [128, T]); bacc = Buf()
                ybf = sb(es2, "ybf", [128, T], BF16); bybf = Buf()
                tmpc = sb(es2, "tmpc", [128, 512]); btc = Buf()
                s.add("pool", lambda e: e.memset(ppad[:, 0:2], 0.0), writes=[bpp])
                for cc in range(2):
                    for i, base in enumerate((256, 512, 0)):
                        c0 = base + cc * 128
                        s.dma("pool", wa[i][:], wv[:, :, c0:c0 + 128], writes=[bwa[i]])
                    for g in range(NG):
                        sl = slice(g * 512, (g + 1) * 512)
                        proj(wa[0], bwa[0], g, 0)
                        proj(wa[1], bwa[1], g, 1)
                        proj(wa[2], bwa[2], g, 2)
                        s.add("act", lambda e: e.activation(tmpc[:], ps[0][:], AF.Copy), reads=[bps[0]], writes=[btc])
                        s.add("dve", lambda e, g=g: e.tensor_tensor(ppad[:, 2 + g * 512:2 + (g + 1) * 512], tmpc[:],
                                                                  ps[1][:], ALU.mult),
                              reads=[btc, bps[1]], writes=[bpp])
                        s.add("act", lambda e, sl=sl: e.activation(abt[:, sl], ps[2][:], AF.Copy),
                              reads=[bps[2]], writes=[bab_])
                    for sg in range(4):
                        o = sg * 1024
                        wc = lambda k: caw[:, l * 6 + cc * 3 + k:l * 6 + cc * 3 + k + 1]
                        s.add("dve", lambda e, o=o, w0=wc(0): e.tensor_scalar(
                            acc[:, o:o + 1024], ppad[:, o:o + 1024], w0, None, ALU.mult),
                            reads=[bpp, bconst], writes=[bacc])
                        for k in (1, 2):
                            s.add("dve", lambda e, o=o, k=k, wk=wc(k): e.scalar_tensor_tensor(
                                acc[:, o:o + 1024], ppad[:, o + k:o + k + 1024], wk, acc[:, o:o + 1024],
                                ALU.mult, ALU.add), reads=[bpp, bacc, bconst], writes=[bacc])
                        s.add("pool", lambda e, o=o: e.tensor_tensor(ybf[:, o:o + 1024], acc[:, o:o + 1024],
                                                                  abt[:, o:o + 1024], ALU.mult),
                              reads=[bacc, bab_], writes=[bybf])
                    s.dma("sp", ys[cc * 128:(cc + 1) * 128, :], ybf[:], reads=[bybf], writes=[])
                s.barrier(); s.flush()
            ck('mixA')

            with ExitStack() as es2:
                wb = [sb(es2, "wb%d" % i, [128, 8, 128], BF16) for i in range(2)]; bwb = [Buf(), Buf()]
                upad = sb(es2, "upad", [128, 2, T + 30], BF16); bup = Buf()
                dg = sb(es2, "dg", [128, 62, 128], BF16); bdg = Buf()
                v32 = sb(es2, "v32", [128, 2, T]); bv32 = Buf()
                sgt = sb(es2, "sgt", [128, 512]); bsg = Buf()
                sq2s = [sb(es2, "sq2", [128, 2, 512], BF16) for _ in range(2)]; bsq2s = [Buf(), Buf()]
                rs2s = [sb(es2, "rs2", [128, 512]) for _ in range(2)]; brs2s = [Buf(), Buf()]
                t2s = [sb(es2, "t2", [128, 512]) for _ in range(4)]; bt2s = [Buf() for _ in range(4)]
                yb2 = [sb(es2, "yb2_%d" % i, [128, 2, 512], BF16) for i in range(2)]; byb2 = [Buf(), Buf()]
                for cc in range(2):
                    s.add("pool", lambda e, cc=cc: e.memset(upad[:, cc, 0:30], 0.0), writes=[bup])
                    for k in range(31):
                        s.add("pool", lambda e, cc=cc, k=k: e.tensor_scalar(
                            dg[:, cc * 31 + k, :], identB[:], cfw[:, l * 62 + cc * 31 + k:l * 62 + cc * 31 + k + 1],
                            None, ALU.mult), reads=[bconst], writes=[bdg])
                for cc in range(2):
                    s.dma("pool", wb[0][:], wv[:, :, 768 + cc * 128:768 + (cc + 1) * 128], writes=[bwb[0]])
                    s.dma("pool", wb[1][:], wv[:, :, 1024 + cc * 128:1024 + (cc + 1) * 128], writes=[bwb[1]])
                    for g in range(NG):
                        proj(wb[0], bwb[0], g, 0)
                        proj(wb[1], bwb[1], g, 1)
                        s.add("act", lambda e: e.activation(sgt[:], ps[1][:], AF.Sigmoid), reads=[bps[1]], writes=[bsg])
                        s.add("dve", lambda e, cc=cc, g=g: e.tensor_tensor(
                            upad[:, cc, 30 + g * 512:30 + (g + 1) * 512], sgt[:], ps[0][:], ALU.mult),
                            reads=[bsg, bps[0]], writes=[bup])
                for g in range(NG):
                    sl = slice(g * 512, (g + 1) * 512)
                    for cc in range(2):
                        pi = 2 + cc
                        for k in range(31):
                            s.add("pe", lambda e, cc=cc, k=k, g=g, pi=pi: e.matmul(
                                ps[pi][:], dg[:, cc * 31 + k, :], upad[:, cc, g * 512 + k:g * 512 + k + 512],
                                start=(k == 0), stop=(k == 30)), reads=[bdg, bup], writes=[bps[pi]])
                        s.add("act", lambda e, cc=cc, sl=sl, pi=pi: e.activation(
                            v32[:, cc, sl], ps[pi][:], AF.Identity, bias=cfb[:, l * 2 + cc:l * 2 + cc + 1], scale=1.0),
                            reads=[bps[pi], bconst], writes=[bv32])
                    pm, pv = (4, 5) if g % 2 == 0 else (0, 1)
                    sq2 = sq2s[g % 2]; bsq2 = bsq2s[g % 2]; rs2 = rs2s[g % 2]; brs2 = brs2s[g % 2]
                    for cc in range(2):
                        s.add("pe", lambda e, cc=cc, sl=sl, pm=pm: e.matmul(ps[pm][:], onesF, v32[:, cc, sl],
                                                                  start=(cc == 0), stop=(cc == 1)),
                              reads=[bv32, bconst], writes=[bps[pm]])
                    for cc in range(2):
                        s.add("dve", lambda e, cc=cc, sl=sl, pm=pm: e.scalar_tensor_tensor(
                            v32[:, cc, sl], ps[pm][:], -1.0 / 256, v32[:, cc, sl], ALU.mult, ALU.add),
                            reads=[bps[pm], bv32], writes=[bv32])
                    s.add("act", lambda e, sl=sl, sq2=sq2: e.activation(sq2[:], v32[:, :, sl], AF.Square),
                          reads=[bv32], writes=[bsq2])
                    for cc in range(2):
                        s.add("pe", lambda e, cc=cc, pv=pv, sq2=sq2: e.matmul(ps[pv][:], onesB[:], sq2[:, cc, :],
                                                             start=(cc == 0), stop=(cc == 1)),
                              reads=[bsq2, bconst], writes=[bps[pv]])
                    s.add("act", lambda e, pv=pv, rs2=rs2: e.activation(rs2[:], ps[pv][:], AF.Sqrt, bias=1e-5, scale=1.0 / 256),
                          reads=[bps[pv]], writes=[brs2])
                    s.add("dve", lambda e, rs2=rs2: e.reciprocal(rs2[:], rs2[:]), reads=[brs2], writes=[brs2])
                    yb_ = yb2[g % 2]; by_ = byb2[g % 2]
                    for cc in range(2):
                        t2 = t2s[(g % 2) * 2 + cc]; bt2 = bt2s[(g % 2) * 2 + cc]
                        s.add("dve", lambda e, cc=cc, sl=sl, t2=t2, rs2=rs2: e.tensor_tensor(t2[:], v32[:, cc, sl], rs2[:], ALU.mult),
                              reads=[bv32, brs2], writes=[bt2])
                        s.add("act", lambda e, cc=cc, yb_=yb_, t2=t2: e.activation(
                            yb_[:, cc, :], t2[:], AF.Silu, bias=cfbb[:, l * 2 + cc:l * 2 + cc + 1],
                            scale=cfg[:, l * 2 + cc:l * 2 + cc + 1]), reads=[bt2, bconst], writes=[by_])
                    s.dma("sp", ys_v[:, 2:4, sl], yb_[:], reads=[by_], writes=[])
                s.barrier(); s.flush()
            ck('mixB')

            with ExitStack() as es2:
                wc_ = [sb(es2, "wc%d" % i, [128, 8, 128], BF16) for i in range(2)]; bwc = [Buf(), Buf()]
                cpads = [sb(es2, "cpad", [128, T + 3], BF16) for _ in range(2)]; bcps = [Buf(), Buf()]
                dg4s = [sb(es2, "dg4", [128, 4, 128], BF16) for _ in range(2)]; bdg4s = [Buf(), Buf()]
                silF = sb(es2, "silF", [128, T]); bsilg = [Buf() for _ in range(NG)]
                sqF = sb(es2, "sqF", [128, T], BF16); bsqg = [Buf() for _ in range(NG)]
                sils = [sb(es2, "sil", [128, 512]) for _ in range(2)]; bsils = [Buf(), Buf()]
                sq3s = [sb(es2, "sq3", [128, 512], BF16) for _ in range(2)]; bsq3s = [Buf(), Buf()]
                rs3s = [sb(es2, "rs3", [128, 512]) for _ in range(2)]; brs3s = [Buf(), Buf()]
                ob = [sb(es2, "ob%d" % i, [128, T], BF16) for i in range(2)]; bob = [Buf(), Buf()]
                wab = sb(es2, "wab", [128, 8, 8], BF16); bwab = Buf()
                for i_ in range(2):
                    s.add("pool", lambda e, i_=i_: e.memset(cpads[i_][:, 0:3], 0.0), writes=[bcps[i_]])
                jobs = []
                for h in range(4):
                    jobs.append(("k", 1792 + h * 128, 4 + h, 0 + h))
                    jobs.append(("q", 1280 + h * 128, 0 + h, 4 + h))
                    jobs.append(("v", 2304 + h * 128, 8 + h, 8 + h))
                    jobs.append(("z", 2816 + h * 128, None, 12 + h))
                for ji, (ty, c0, ci, dj) in enumerate(jobs):
                    w_ = wc_[ji % 2]; bw_ = bwc[ji % 2]
                    o_ = ob[ji % 2]; bo_ = bob[ji % 2]
                    s.dma("pool", w_[:], wv[:, :, c0:c0 + 128], writes=[bw_])
                    if ty == "z":
                        for g in range(NG):
                            sl = slice(g * 512, (g + 1) * 512)
                            proj(w_, bw_, g, g % 2)
                            s.add("act", lambda e, sl=sl, g=g, o_=o_: e.activation(o_[:, sl], ps[g % 2][:], AF.Silu),
                                  reads=[bps[g % 2]], writes=[bo_])
                    else:
                        cpad = cpads[ji % 2]; bcp = bcps[ji % 2]; dg4 = dg4s[ji % 2]; bdg4 = bdg4s[ji % 2]
                        for k in range(4):
                            s.add("pool", lambda e, k=k, ci=ci, dg4=dg4: e.tensor_scalar(
                                dg4[:, k, :], identB[:], dnw[:, l * 48 + ci * 4 + k:l * 48 + ci * 4 + k + 1],
                                None, ALU.mult), reads=[bconst], writes=[bdg4])
                        for g in range(NG):
                            proj(w_, bw_, g, g % 2)
                            s.add("act", lambda e, g=g, cpad=cpad: e.activation(cpad[:, 3 + g * 512:3 + (g + 1) * 512],
                                                                    ps[g % 2][:], AF.Copy),
                                  reads=[bps[g % 2]], writes=[bcp])
                        for g in range(NG):
                            sl = slice(g * 512, (g + 1) * 512)
                            pi = 2 + g % 2
                            for k in range(4):
                                s.add("pe", lambda e, k=k, g=g, pi=pi, dg4=dg4, cpad=cpad: e.matmul(
                                    ps[pi][:], dg4[:, k, :], cpad[:, g * 512 + k:g * 512 + k + 512],
                                    start=(k == 0), stop=(k == 3)), reads=[bdg4, bcp], writes=[bps[pi]])
                            if ty == "v":
                                s.add("act", lambda e, sl=sl, pi=pi, o_=o_: e.activation(o_[:, sl], ps[pi][:], AF.Silu),
                                      reads=[bps[pi]], writes=[bo_])
                            else:
                                s.add("act", lambda e, pi=pi, sl=sl: e.activation(silF[:, sl], ps[pi][:], AF.Silu),
                                      reads=[bps[pi]], writes=[bsilg[g]])
                                s.add("pool", lambda e, sl=sl: e.tensor_tensor(sqF[:, sl], silF[:, sl], silF[:, sl], ALU.mult),
                                      reads=[bsilg[g]], writes=[bsqg[g]])
                        if ty != "v":
                            for g in range(NG):
                                sl = slice(g * 512, (g + 1) * 512)
                                rs3 = rs3s[g % 2]; brs3 = brs3s[g % 2]; pq = 4 + g % 2
                                s.add("pe", lambda e, sl=sl, pq=pq: e.matmul(ps[pq][:], onesB[:], sqF[:, sl], start=True, stop=True),
                                      reads=[bsqg[g], bconst], writes=[bps[pq]])
                                s.add("act", lambda e, rs3=rs3, pq=pq: e.activation(rs3[:], ps[pq][:], AF.Sqrt, bias=EPS, scale=1.0),
                                      reads=[bps[pq]], writes=[brs3])
                                s.add("dve", lambda e, rs3=rs3: e.reciprocal(rs3[:], rs3[:]), reads=[brs3], writes=[brs3])
                                sc = 128.0 ** -0.5 if ty == "q" else 1.0
                                s.add("dve", lambda e, sl=sl, sc=sc, o_=o_, rs3=rs3: e.scalar_tensor_tensor(
                                    o_[:, sl], silF[:, sl], sc, rs3[:], ALU.mult, ALU.mult),
                                    reads=[bsilg[g], brs3], writes=[bo_])
                    s.dma("sp", dnin[:, :, dj, :], o_[:].rearrange("p (n t) -> p n t", t=128), reads=[bo_], writes=[])
                s.dma("pool", wab[:], wv[:, :, 3328:3336], writes=[bwab])
                for n in range(NCH):
                    for k in range(8):
                        s.add("pe", lambda e, n=n, k=k: e.matmul(
                            ps[5][:, n * 8:(n + 1) * 8], hT[:, k, n * 128:(n + 1) * 128], wab[:, k, :],
                            start=(k == 0), stop=(k == 7)), reads=[bwab, bh], writes=[bps[5]])
                s.add("act", lambda e: e.activation(ab_tok[:], ps[5][:, 0:NCH * 8], AF.Copy),
                      reads=[bps[5]], writes=[bab])
                s.barrier(); s.flush()

        ck('dnin')
        with ExitStack() as es:
            def t_(name, shape, dtype=F32):
                return sb(es, name, shape, dtype)
            mU4 = t_("mU4", [128, 7, 512], BF16); mL4 = t_("mL4", [128, 7, 512], BF16)
            i4 = t_("i4", [128, 512], BF16); su4 = t_("su4", [128, 512])
            bm = Buf("masks")
            s.dma("pool", mU4[:], lev_in.rearrange("p (a b) -> p a b", b=512), writes=[bm])
            s.dma("pool", mL4[:], levT_in.rearrange("p (a b) -> p a b", b=512), writes=[bm])
            s.dma("pool", i4[:], i4_in, writes=[bm])
            s.dma("sp", su4[:], su4_in, writes=[bm])
            beta = t_("beta", [128, NCH, 4]); gtok = t_("gtok", [128, NCH, 4])
            xsp = t_("xsp", [128, NCH, 4]); axs = t_("axs", [128, NCH, 4]); lg = t_("lg", [128, NCH, 4])
            nega = t_("nega", [128, 128])
            gam = t_("gam", [128, 128]); eg = t_("eg", [128, 128]); negeg = t_("negeg", [128, 128])
            decl = t_("decl", [128, 128]); eglast = t_("eglast", [128, 128])
            bsc = Buf("scal")
            abv = ab_tok[:].rearrange("p (n c) -> p n c", c=8)
            dtbv = dtb[:, l * 128:(l + 1) * 128].rearrange("p (n c) -> p n c", c=4)
            s.add("act", lambda e: e.activation(beta[:], abv[:, :, 4:8], AF.Sigmoid), reads=[bab], writes=[bsc])
            s.add("dve", lambda e: e.tensor_tensor(xsp[:], abv[:, :, 0:4], dtbv, ALU.add), reads=[bab, bconst], writes=[bsc])
            s.add("act", lambda e: e.activation(axs[:], xsp[:], AF.Abs), reads=[bsc], writes=[bsc])
            s.add("act", lambda e: e.activation(axs[:], axs[:], AF.Exp, scale=-1.0), reads=[bsc], writes=[bsc])
            s.add("act", lambda e: e.activation(lg[:], axs[:], AF.Ln, bias=1.0, scale=1.0), reads=[bsc], writes=[bsc])
            s.add("dve", lambda e: e.tensor_scalar(xsp[:], xsp[:], 0.0, None, ALU.max), reads=[bsc], writes=[bsc])
            s.add("dve", lambda e: e.tensor_tensor(xsp[:], xsp[:], lg[:], ALU.add), reads=[bsc], writes=[bsc])
            s.add("act", lambda e: e.activation(nega[:], alog[:, l * 128:(l + 1) * 128], AF.Exp), reads=[bconst], writes=[bsc])
            s.add("dve", lambda e: e.scalar_tensor_tensor(
                gtok[:].rearrange("p n c -> p (n c)"), xsp[:].rearrange("p n c -> p (n c)"), -1.0, nega[:],
                ALU.mult, ALU.mult), reads=[bsc], writes=[bsc])
            gflat = gtok[:].rearrange("p n c -> p (n c)")
            bflat = beta[:].rearrange("p n c -> p (n c)")
            s.add("pe", lambda e: e.matmul(ps[0][:, 0:128], uincl, gflat, start=True, stop=True),
                  reads=[bsc, bconst], writes=[bps[0]])
            s.add("pe", lambda e: e.matmul(ps[1][:, 0:128], onesF, gflat, start=True, stop=True),
                  reads=[bsc, bconst], writes=[bps[1]])
            s.add("act", lambda e: e.activation(gam[:], ps[0][:, 0:128], AF.Copy), reads=[bps[0]], writes=[bsc])
            s.add("act", lambda e: e.activation(eg[:], ps[0][:, 0:128], AF.Exp), reads=[bps[0]], writes=[bsc])
            s.add("dve", lambda e: e.tensor_scalar(negeg[:], eg[:], -1.0, None, ALU.mult), reads=[bsc], writes=[bsc])
            s.add("dve", lambda e: e.tensor_tensor(decl[:], ps[1][:, 0:128], gam[:], ALU.subtract),
                  reads=[bps[1], bsc], writes=[bsc])
            s.add("act", lambda e: e.activation(decl[:], decl[:], AF.Exp), reads=[bsc], writes=[bsc])
            s.add("act", lambda e: e.activation(eglast[:], ps[1][:, 0:128], AF.Exp), reads=[bps[1]], writes=[bsc])

            ck('D0')
            import os as _os
            NCH_RUN = int(_os.environ.get('DBG_NCH', NCH))
            NL = 3
            S32 = t_("S32", [128, 512]); bS = Buf()
            Sbf = t_("Sbf", [128, 512], BF16); bSb = Buf()
            s.add("dve", lambda e: e.memset(S32[:], 0.0), writes=[bS])
            s.add("pool", lambda e: e.memset(Sbf[:], 0.0), writes=[bSb])
            gd = gdn[:, l * 128:(l + 1) * 128]
            H = lambda a, h: a[:, h * 128:(h + 1) * 128]

            class Lane:
                pass
            lanes = []
            for li_ in range(NL):
                ln = Lane()
                ln.inp = t_("inp", [128, 16, 128], BF16); ln.binp = Buf()
                ln.Mt = t_("Mt", [128, 4, 128]); ln.bMt = Buf()
                ln.E = t_("E", [128, 512]); ln.bE = Buf()
                ln.Es = t_("Es", [128, 512]); ln.bEs = Buf()
                for nm in ("Pm", "Qm", "X", "Y", "R1", "QKm", "kdec", "rp", "vnew", "on", "yc"):
                    setattr(ln, nm, t_(nm, [128, 512], BF16)); setattr(ln, "b" + nm, Buf())
                ln.Qoff = t_("Qoff", [128, 6, 512], BF16); ln.bQoff = Buf()
                for nm in ("vtok", "o2s", "o_t"):
                    setattr(ln, nm, t_(nm, [128, 512])); setattr(ln, "b" + nm, Buf())
                ln.junk = t_("junk", [128, 128]); ln.bjunk = Buf()
                ln.ss = t_("ss", [128, 4]); ln.bss = Buf()
                ln.p = (ps[2 * li_], ps[2 * li_ + 1]); ln.bp = (bps[2 * li_], bps[2 * li_ + 1])
                if li_ % 2 == 0:
                    ln.pT = ps6b; ln.bpT = bps6
                else:
                    ln.pT = ps7b; ln.bpT = bps7
                lanes.append(ln)

            owner = {}

            def acq(me, banks):
                while any(owner.get(id(b_)) not in (None, me) for b_ in banks):
                    yield
                for b_ in banks:
                    owner[id(b_)] = me

            def rel(me, banks):
                for b_ in banks:
                    if owner.get(id(b_)) == me:
                        owner[id(b_)] = None

            def d1_gen(n, ln):
                ip = ln.inp; bip = ln.binp
                p0, p1 = ln.p; b0, b1 = ln.bp; pT = ln.pT; bT = ln.bpT
                col = lambda a, h: a[:, n * 4 + h:n * 4 + h + 1]
                s.dma("sp", ip[:], dnin[:, n, :, :], reads=[], writes=[bip])
                for h in range(4):
                    s.add("act", lambda e, h=h, g_=col(gflat, h): e.activation(ln.Mt[:, h, :], masksl, AF.Identity, bias=0.0, scale=g_),
                          reads=[bsc, bconst], writes=[ln.bMt])
                yield
                for h in range(4):
                    s.add("pe", lambda e, h=h: e.matmul(H(p0, h), ln.Mt[:, h, :], uincl, start=True, stop=False),
                          reads=[ln.bMt, bconst], writes=[b0])
                    s.add("pe", lambda e, h=h: e.matmul(H(p0, h), identF, negm, start=False, stop=True),
                          reads=[ln.bMt, bconst], writes=[b0])
                yield
                s.add("act", lambda e: e.activation(ln.E[:], p0[:], AF.Exp), reads=[b0], writes=[ln.bE])
                yield
                for h in range(4):
                    s.add("pe", lambda e, h=h: e.matmul(H(p0, h), ip[:, h, :], ip[:, h, :], start=True, stop=True),
                          reads=[bip], writes=[b0])
                for h in range(4):
                    s.add("pe", lambda e, h=h: e.matmul(H(p1, h), ip[:, h, :], ip[:, 4 + h, :], start=True, stop=True),
                          reads=[bip], writes=[b1])
                yield
                for h in range(4):
                    s.add("dve", lambda e, h=h, b_=col(bflat, h): e.scalar_tensor_tensor(
                        H(ln.Pm, h), H(p0, h), b_, H(ln.E, h), ALU.mult, ALU.mult),
                        reads=[b0, ln.bE, bsc], writes=[ln.bPm])
                s.add("dve", lambda e: e.tensor_tensor(ln.QKm[:], p1[:], ln.E[:], ALU.mult), reads=[b1, ln.bE], writes=[ln.bQKm])
                yield
                yield from acq(n, (bT,))
                for h in range(4):
                    s.add("pe", lambda e, h=h: e.transpose(H(pT, h), H(ln.Pm, h), identB[:]), reads=[ln.bPm, bconst], writes=[bT])
                s.add("pool", lambda e: e.tensor_tensor(ln.X[:], ln.Pm[:], mU4[:, 0, :], ALU.mult), reads=[ln.bPm, bm], writes=[ln.bX])
                s.add("pool", lambda e: e.tensor_tensor(ln.X[:], i4[:], ln.X[:], ALU.subtract), reads=[ln.bX, bm], writes=[ln.bX])
                yield
                s.add("act", lambda e: e.activation(ln.Qm[:], pT[:, 0:512], AF.Copy), reads=[bT], writes=[ln.bQm])
                rel(n, (bT,))
                yield
                s.add("pool", lambda e: e.tensor_tensor(ln.Y[:], ln.Qm[:], mL4[:, 0, :], ALU.mult), reads=[ln.bQm, bm], writes=[ln.bY])
                s.add("pool", lambda e: e.tensor_tensor(ln.Y[:], i4[:], ln.Y[:], ALU.subtract), reads=[ln.bY, bm], writes=[ln.bY])
                yield
                for li in range(1, 7):
                    last = (li == 6)
                    for h in range(4):
                        s.add("pe", lambda e, h=h, li=li: e.matmul(H(p0, h), H(ln.Qm, h), H(ln.X, h),
                                                                 start=True, stop=True), reads=[ln.bQm, ln.bX], writes=[b0])
                    yield
                    s.add("dve", lambda e, li=li: e.tensor_tensor(ln.R1[:], p0[:], mU4[:, li, :], ALU.mult),
                          reads=[b0, bm], writes=[ln.bR1])
                    yield
                    for h in range(4):
                        s.add("pe", lambda e, h=h: e.matmul(H(p0, h), H(ln.Y, h), H(ln.R1, h), start=True, stop=True),
                              reads=[ln.bY, ln.bR1], writes=[b0])
                    if not last:
                        for h in range(4):
                            s.add("pe", lambda e, h=h: e.matmul(H(p1, h), H(ln.R1, h), H(ln.Y, h), start=True, stop=True),
                                  reads=[ln.bY, ln.bR1], writes=[b1])
                    yield
                    s.add("dve", lambda e: e.tensor_tensor(ln.X[:], ln.X[:], p0[:], ALU.subtract), reads=[ln.bX, b0], writes=[ln.bX])
                    if not last:
                        s.add("dve", lambda e: e.tensor_tensor(ln.Y[:], ln.Y[:], p1[:], ALU.subtract),
                              reads=[ln.bY, b1], writes=[ln.bY])
                    yield

            def d2_gen(n, ln):
                ip = ln.inp; bip = ln.binp
                p0, p1 = ln.p; b0, b1 = ln.bp; pT = ln.pT; bT = ln.bpT
                col = lambda a, h: a[:, n * 4 + h:n * 4 + h + 1]
                yield from acq(n, (bT,))
                for h in range(4):
                    s.add("pe", lambda e, h=h: e.transpose(H(pT, h), ip[:, h, :], identB[:]), reads=[bip, bconst], writes=[bT])
                for h in range(4):
                    s.add("pe", lambda e, h=h: e.transpose(H(pT, 4 + h), ip[:, 8 + h, :], identB[:]), reads=[bip, bconst], writes=[bT])
                for h in range(4):
                    s.add("pe", lambda e, h=h: e.matmul(H(p0, h), ip[:, h, :], H(Sbf, h), start=True, stop=True),
                          reads=[bip, bSb], writes=[b0])
                for h in range(4):
                    s.add("pe", lambda e, h=h: e.matmul(H(p1, h), ip[:, 4 + h, :], H(Sbf, h), start=True, stop=True),
                          reads=[bip, bSb], writes=[b1])
                yield
                s.add("act", lambda e: e.activation(ln.vtok[:], pT[:, 512:1024], AF.Copy), reads=[bT], writes=[ln.bvtok])
                for h in range(4):
                    s.add("act", lambda e, h=h, d_=col(decl, h): e.activation(H(ln.kdec, h), H(pT, h), AF.Identity, bias=0.0, scale=d_),
                          reads=[bT, bsc], writes=[ln.bkdec])
                rel(n, (bT,))
                yield
                for h in range(4):
                    s.add("dve", lambda e, h=h, ne_=col(negeg, h): e.scalar_tensor_tensor(
                        H(ln.rp, h), H(p0, h), ne_, H(ln.vtok, h), ALU.mult, ALU.add),
                        reads=[b0, ln.bvtok, bsc], writes=[ln.brp])
                s.add("act", lambda e: e.activation(ln.o2s[:], p1[:], AF.Copy), reads=[b1], writes=[ln.bo2s])
                yield
                for h in range(4):
                    s.add("pe", lambda e, h=h: e.matmul(H(p0, h), H(ln.X, h), H(ln.rp, h), start=True, stop=True),
                          reads=[ln.bX, ln.brp], writes=[b0])
                yield
                for h in range(4):
                    s.add("act", lambda e, h=h, b_=col(bflat, h): e.activation(H(ln.vnew, h), H(p0, h), AF.Identity, bias=0.0, scale=b_),
                          reads=[b0, bsc], writes=[ln.bvnew])
                yield
                for h in range(4):
                    s.add("pe", lambda e, h=h: e.matmul(H(p0, h), H(ln.kdec, h), H(ln.vnew, h), start=True, stop=True),
                          reads=[ln.bkdec, ln.bvnew], writes=[b0])
                for h in range(4):
                    s.add("pe", lambda e, h=h: e.matmul(H(p1, h), H(ln.QKm, h), H(ln.vnew, h), start=True, stop=True),
                          reads=[ln.bQKm, ln.bvnew], writes=[b1])
                yield
                for h in range(4):
                    s.add("dve", lambda e, h=h, el_=col(eglast, h): e.scalar_tensor_tensor(
                        H(S32, h), H(S32, h), el_, H(p0, h), ALU.mult, ALU.add),
                        reads=[bS, b0, bsc], writes=[bS])
                yield
                s.add("act", lambda e: e.activation(Sbf[:], S32[:], AF.Copy), reads=[bS], writes=[bSb])
                for h in range(4):
                    s.add("dve", lambda e, h=h, eg_=col(eg, h): e.scalar_tensor_tensor(
                        H(ln.o_t, h), H(ln.o2s, h), eg_, H(p1, h), ALU.mult, ALU.add),
                        reads=[b1, ln.bo2s, bsc], writes=[ln.bo_t])
                s.add("pool", lambda e: e.memset(ln.ss[:], 0.0), writes=[ln.bss])
                yield

            def d3_gen(n, ln):
                ip = ln.inp; bip = ln.binp
                pT = ln.pT; bT = ln.bpT
                for h in range(4):
                    s.add("act", lambda e, h=h: e.activation(ln.junk[:], H(ln.o_t, h), AF.Square, accum_out=ln.ss[:, h:h + 1]),
                          reads=[ln.bo_t, ln.bss], writes=[ln.bjunk, ln.bss])
                yield
                s.add("act", lambda e: e.activation(ln.ss[:], ln.ss[:], AF.Sqrt, bias=EPS, scale=1.0 / 128), reads=[ln.bss], writes=[ln.bss])
                yield
                s.add("dve", lambda e: e.reciprocal(ln.ss[:], ln.ss[:]), reads=[ln.bss], writes=[ln.bss])
                yield
                for h in range(4):
                    s.add("dve", lambda e, h=h: e.scalar_tensor_tensor(H(ln.on, h), H(ln.o_t, h), ln.ss[:, h:h + 1], gd, ALU.mult, ALU.mult),
                          reads=[ln.bo_t, ln.bss, bconst], writes=[ln.bon])
                yield
                yield from acq(n, (bT,))
                for h in range(4):
                    s.add("pe", lambda e, h=h: e.transpose(H(pT, 4 + h), H(ln.on, h), identB[:]), reads=[ln.bon, bconst], writes=[bT])
                yield
                s.add("dve", lambda e: e.tensor_tensor(
                    ln.yc[:], pT[:, 512:1024], ip[:, 12:16, :].rearrange("p a b -> p (a b)"), ALU.mult),
                    reads=[bT, bip], writes=[ln.byc])
                rel(n, (bT,))
                s.dma("sp", ys_v[:, 4:8, n * 128:(n + 1) * 128], ln.yc[:].rearrange("p (a b) -> p a b", b=128),
                      reads=[ln.byc], writes=[])
                yield

            def chain(n, ln):
                yield from d1_gen(n, ln)
                while d2_turn[0] != n:
                    yield
                yield from d2_gen(n, ln)
                d2_turn[0] = n + 1
                yield from d3_gen(n, ln)

            d2_turn = [0]
            active = {}
            nxt = 0
            while nxt < NCH_RUN or active:
                while nxt < NCH_RUN and (nxt % NL) not in active:
                    active[nxt % NL] = chain(nxt, lanes[nxt % NL]); nxt += 1
                    break
                for k in sorted(active.keys(), key=lambda kk: kk):
                    try:
                        next(active[k])
                    except StopIteration:
                        del active[k]
            s.barrier(); s.flush()

        ck("mixy%d" % l)
        esF = ExitStack()
        wfi = sb(esF, "wfi", [128, 8, 2 * DFF], BF16); bwfi = Buf()
        wfiv = w_fi[l].rearrange("(k p) c -> p k c", p=128)
        with ExitStack() as es:
            wo = sb(es, "wo", [128, 8, D], BF16); bwo = Buf()
            wov = w_out[l].rearrange("(k p) c -> p k c", p=128)
            for k in range(8):
                s.dma("pool", wo[:, k, :], wov[:, k, :], writes=[bwo])
            wfi_jobs = [(k, hh) for k in range(8) for hh in range(2)]
            yt = [sb(es, "yt%d" % i, [128, 8, 512], BF16) for i in range(2)]; byt = [Buf(), Buf()]
            xt = [sb(es, "xo%d" % i, [128, 8, 512]) for i in range(2)]; bxt = [Buf(), Buf()]
            for g in range(NG):
                sl = slice(g * 512, (g + 1) * 512)
                y_ = yt[g % 2]; by_ = byt[g % 2]; x_ = xt[g % 2]; bx_ = bxt[g % 2]
                s.dma("pool", y_[:], ys_v[:, :, sl], writes=[by_])
                s.dma("sp", x_[:], xs_v[:, :, sl], writes=[bx_])
                for (k, hh) in wfi_jobs[2 * g:2 * g + 2]:
                    s.dma("pool", wfi[:, k, hh * DFF:(hh + 1) * DFF], wfiv[:, k, hh * DFF:(hh + 1) * DFF], writes=[bwfi])
                for oc in range(8):
                    pi = oc % 4
                    for k in range(8):
                        s.add("pe", lambda e, k=k, oc=oc, pi=pi, y_=y_: e.matmul(
                            ps[pi][:], wo[:, k, oc * 128:(oc + 1) * 128], y_[:, k, :], start=(k == 0), stop=(k == 7)),
                            reads=[bwo, by_], writes=[bps[pi]])
                    s.add("dve", lambda e, oc=oc, pi=pi, x_=x_: e.scalar_tensor_tensor(
                        x_[:, oc, :], ps[pi][:], G1(l, oc), x_[:, oc, :], ALU.mult, ALU.add),
                        reads=[bps[pi], bx_, bmod], writes=[bx_])
                s.dma("act", xs_v[:, :, sl], x_[:], reads=[bx_], writes=[])
            s.barrier(); s.flush()
        ck("mix%d" % l)

        with ExitStack() as es:
            wfo = sb(es, "wfo", [128, 22, D], BF16); bwfo = Buf()
            wfov = w_fo[l].rearrange("(k p) c -> p k c", p=128)
            for k in range(22):
                s.dma("pool", wfo[:, k, :], wfov[:, k, :], writes=[bwfo])
            GF = 256
            NGF = T // GF
            xts = [sb(es, "xf", [128, 8, GF]) for _ in range(2)]; bxts = [Buf(), Buf()]
            hT2s = [sb(es, "hT2", [128, 8, GF], BF16) for _ in range(2)]; bh2s = [Buf(), Buf()]
            sq = sb(es, "sqf", [128, 8, GF], BF16); bsq = Buf(); bsqj = [Buf() for _ in range(8)]
            rs = sb(es, "rsf", [128, GF]); brs = Buf()
            aTs = [sb(es, "aT", [128, 22, GF], BF16) for _ in range(2)]; baTs = [Buf(), Buf()]
            sgf = [sb(es, "sgf%d" % i, [128, GF]) for i in range(2)]; bsgf = [Buf(), Buf()]

            def f_norm(g):
                xt = xts[g % 2]; bxt = bxts[g % 2]; hT2 = hT2s[g % 2]; bh2 = bh2s[g % 2]
                sl = slice(g * GF, (g + 1) * GF)
                s.dma("sp", xt[:], xs_v[:, :, sl], reads=[bxs], writes=[bxt])
                s.add("act", lambda e: e.activation(sq[:], xt[:], AF.Square), reads=[bxt], writes=[bsq] + bsqj)
                for j in range(8):
                    s.add("pe", lambda e, j=j: e.matmul(ps[5][:, 0:GF], onesB[:], sq[:, j, :], start=(j == 0), stop=(j == 7)),
                          reads=[bsq, bconst], writes=[bps[5]])
                s.add("act", lambda e: e.activation(rs[:], ps[5][:, 0:GF], AF.Sqrt, bias=EPS, scale=1.0 / D),
                      reads=[bps[5]], writes=[brs])
                s.add("dve", lambda e: e.reciprocal(rs[:], rs[:]), reads=[brs], writes=[brs])
                for j in range(8):
                    s.add("dve", lambda e, j=j: e.scalar_tensor_tensor(
                        sq[:, j, :], xt[:, j, :], A2[:, l * 8 + j:l * 8 + j + 1], rs[:], ALU.mult, ALU.mult),
                        reads=[bxt, brs, bmod, bsq], writes=[bsqj[j]])
                    s.add("act", lambda e, j=j: e.activation(
                        hT2[:, j, :], sq[:, j, :], AF.Identity, bias=B2(l, j), scale=1.0),
                        reads=[bsqj[j], bmod], writes=[bh2])

            def f_in(g):
                hT2 = hT2s[g % 2]; bh2 = bh2s[g % 2]; aT = aTs[g % 2]; baT = baTs[g % 2]
                for j in range(22):
                    pg, pu = (0, 1) if j % 2 == 0 else (2, 3)
                    for k in range(8):
                        s.add("pe", lambda e, k=k, j=j, pg=pg: e.matmul(
                            ps[pg][:, 0:GF], wfi[:, k, j * 128:(j + 1) * 128], hT2[:, k, :], start=(k == 0), stop=(k == 7)),
                            reads=[bwfi, bh2], writes=[bps[pg]])
                    for k in range(8):
                        s.add("pe", lambda e, k=k, j=j, pu=pu: e.matmul(
                            ps[pu][:, 0:GF], wfi[:, k, DFF + j * 128:DFF + (j + 1) * 128], hT2[:, k, :],
                            start=(k == 0), stop=(k == 7)), reads=[bwfi, bh2], writes=[bps[pu]])
                    sg_ = sgf[j % 2]; bsg_ = bsgf[j % 2]
                    s.add("act", lambda e, pg=pg, sg_=sg_: e.activation(sg_[:], ps[pg][:, 0:GF], AF.Silu),
                          reads=[bps[pg]], writes=[bsg_])
                    s.add("dve", lambda e, j=j, pu=pu, sg_=sg_: e.tensor_tensor(aT[:, j, :], sg_[:], ps[pu][:, 0:GF], ALU.mult),
                          reads=[bsg_, bps[pu]], writes=[baT])

            def f_out(g):
                xt = xts[g % 2]; bxt = bxts[g % 2]; aT = aTs[g % 2]; baT = baTs[g % 2]
                sl = slice(g * GF, (g + 1) * GF)
                for oc in range(8):
                    pi = 4 if oc % 2 == 0 else (0, 2)[(oc // 2) % 2]
                    for k in range(22):
                        s.add("pe", lambda e, k=k, oc=oc, pi=pi: e.matmul(
                            ps[pi][:, 0:GF], wfo[:, k, oc * 128:(oc + 1) * 128], aT[:, k, :], start=(k == 0), stop=(k == 21)),
                            reads=[bwfo, baT], writes=[bps[pi]])
                    s.add("dve", lambda e, oc=oc, pi=pi: e.scalar_tensor_tensor(
                        xt[:, oc, :], ps[pi][:, 0:GF], G2(l, oc), xt[:, oc, :], ALU.mult, ALU.add),
                        reads=[bps[pi], bxt, bmod], writes=[bxt])
                s.dma("pool", xs_v[:, :, sl], xt[:], reads=[bxt], writes=[bxs])

            f_norm(0)
            for g in range(NGF):
                f_in(g)
                if g + 1 < NGF:
                    f_norm(g + 1)
                f_out(g)
            s.barrier(); s.flush()
        esF.close()
        ck("ffn%d" % l)

    s.muted = False
    dumpy = stop in ('mixA', 'mixB') or (stop is not None and stop.startswith('mixy'))
    with ExitStack() as es:
        xt = [sb(es, "xz%d" % i, [128, 8, 512]) for i in range(2)]; bxt = [Buf(), Buf()]
        sq = sb(es, "sqz", [128, 8, 512], BF16); bsq = Buf()
        rs = sb(es, "rsz", [128, 512]); brs = Buf()
        if dumpy:
            yt = [sb(es, "yz%d" % i, [128, 8, 512], BF16) for i in range(2)]; byt = [Buf(), Buf()]
        for g in range(NG):
            x_ = xt[g % 2]; bx_ = bxt[g % 2]
            if stop is None:
                sl = norm_group((sq, bsq, rs, brs), 0, g, None, None, None, None, x_, bx_)
                for j in range(8):
                    s.add("dve", lambda e, j=j, x_=x_: e.scalar_tensor_tensor(
                        x_[:, j, :], x_[:, j, :], gfin[:, j:j + 1], rs[:], ALU.mult, ALU.mult),
                        reads=[bx_, brs, bconst], writes=[bx_])
            elif dumpy:
                sl = slice(g * 512, (g + 1) * 512)
                y_ = yt[g % 2]; by_ = byt[g % 2]
                s.dma("sp", y_[:], ys_v[:, :, sl], writes=[by_])
                s.add("dve", lambda e, x_=x_, y_=y_: e.tensor_copy(x_[:], y_[:]), reads=[by_], writes=[bx_])
            else:
                sl = slice(g * 512, (g + 1) * 512)
                s.dma("sp", x_[:], xs_v[:, :, sl], writes=[bx_])
            s.dma("act" if g % 2 == 0 else "sp", out_v[:, :, sl], x_[:], reads=[bx_], writes=[])
        s.flush(final=True)
    top.close()
    s.close()
    nc._nops = s.nops
    return nc


def _pp(a):
    a = np.asarray(a, np.float32)
    lead = a.shape[:-1]
    n = a.shape[-1] // 128
    a = a.reshape(lead + (n, 128))
    a = np.moveaxis(a, -1, 0)
    return np.ascontiguousarray(a.reshape(128, -1))


def make_in_maps(inputs, depth=L, ncores=NCORES):
    f = lambda k: np.asarray(inputs[k], np.float32)
    m = _levels_masks()
    common = {
        "w_ada": f("w_ada")[:depth], "b_ada": f("b_ada")[:depth], "w_in": f("w_in")[:depth], "w_out": f("w_out")[:depth],
        "w_ffn_in": f("w_ffn_in")[:depth], "w_ffn_out": f("w_ffn_out")[:depth],
        "gm": _pp(f("norm_mix_g")), "gf": _pp(f("norm_ffn_g")), "gfin": _pp(f("final_norm_g")),
        "caw": _pp(np.transpose(f("conv_a_w"), (0, 2, 1)).reshape(L, 2, 128, 3).transpose(0, 1, 3, 2)),
        "cfw": _pp(np.transpose(f("conf_dw_w"), (0, 2, 1)).reshape(L, 2, 128, 31).transpose(0, 1, 3, 2)),
        "cfb": _pp(f("conf_dw_b")), "cfg": _pp(f("conf_ln_g")), "cfbb": _pp(f("conf_ln_b")),
        "dnw": _pp(np.transpose(f("dn_conv_w"), (0, 2, 1)).reshape(L, 12, 128, 4).transpose(0, 1, 3, 2)),
        "alog": np.ascontiguousarray(np.broadcast_to(np.tile(f("dn_a_log"), (1, NCH)).reshape(1, L * 128), (128, L * 128))),
        "dtb": np.ascontiguousarray(np.broadcast_to(np.tile(f("dn_dt_bias"), (1, NCH)).reshape(1, L * 128), (128, L * 128))),
        "gdn": np.ascontiguousarray(np.broadcast_to(f("dn_norm_g").reshape(1, L * 128), (128, L * 128))),
        "cmask": np.ascontiguousarray(np.concatenate(
            [m["ident"], m["uincl"], m["masksl"], m["negm"], m["strictu"], m["ones"]], axis=1)),
        "levU4": np.ascontiguousarray(np.concatenate([np.tile(m["levU"][i], (1, 4)) for i in range(7)], axis=1)),
        "levL4": np.ascontiguousarray(np.concatenate([np.tile(m["levU"][i].T, (1, 4)) for i in range(7)], axis=1)),
        "ident4": np.ascontiguousarray(np.tile(m["ident"], (1, 4))),
        "strictu4": np.ascontiguousarray(np.tile(m["strictu"], (1, 4))),
    }
    x = f("x"); c = f("c")
    maps = []
    for core in range(ncores):
        b = core % 4
        d = dict(common)
        d["xT"] = np.ascontiguousarray(x[b].T)
        d["cT"] = np.ascontiguousarray(c[b].reshape(8, 128).T)
        maps.append(d)
    return maps


_NC_CACHE = {}


def kernel(**inputs):
    if "nc" not in _NC_CACHE:
        _NC_CACHE["nc"] = build_program()
    nc = _NC_CACHE["nc"]
    in_maps = make_in_maps(inputs)
    res = run_bass_kernel_spmd(nc, in_maps, core_ids=list(range(NCORES)))
    out = np.stack([np.asarray(res.results[b]["outT"], np.float32).T for b in range(4)], axis=0)
    return np.ascontiguousarray(out)
```

```python
from contextlib import ExitStack

import numpy as np
import concourse.bass as bass
import concourse.mybir as mybir
from concourse.bass_utils import run_bass_kernel_spmd

F32 = mybir.dt.float32
BF16 = mybir.dt.bfloat16
AF = mybir.ActivationFunctionType
ALU = mybir.AluOpType


class Buf:
    __slots__ = ("name", "w", "r", "excl")

    def __init__(self, name="", excl=False):
        self.name = name
        self.w = None
        self.r = []
        self.excl = excl


class _Op:
    __slots__ = ("eng", "fn", "deps", "is_dma", "sem", "val", "signal", "emitted")


class Sched:
    CENG = ("pe", "act", "dve", "pool")
    ENG = ("pe", "act", "dve", "pool", "sp")
    DMAQ = ("sp", "pool", "act")

    def __init__(self, nc, strict=True, dma_k=6):
        self.nc = nc
        self.strict = strict
        self.K = dma_k
        self.es = ExitStack()
        self.csem = {e: self.es.enter_context(nc.semaphore("c_" + e)) for e in self.CENG}
        self.ccount = {e: 0 for e in self.CENG}
        self.dsem = {q: [self.es.enter_context(nc.semaphore("d_%s%d" % (q, i))) for i in range(dma_k)]
                     for q in self.DMAQ}
        self.dcount = {q: 0 for q in self.DMAQ}
        self.dhist = {q: [] for q in self.DMAQ}
        self.known = {e: {} for e in self.ENG}
        self.pending = {e: [] for e in self.ENG}
        self.last = {e: None for e in self.CENG}
        self.barrier_ops = []
        self.nops = 0
        self.muted = False

    def _mk(self, eng, fn, reads, writes, is_dma):
        op = _Op()
        op.eng = eng
        op.fn = fn
        op.is_dma = is_dma
        op.signal = is_dma
        op.sem = None
        op.val = None
        op.emitted = False
        deps = list(self.barrier_ops)
        for b in reads:
            if b.w is not None:
                deps.append(b.w)
            if b.excl:
                deps.extend(r for r in b.r if r.eng != eng)
        for b in writes:
            if b.w is not None:
                if not (b.w.eng == eng and not b.w.is_dma and not is_dma and eng != "pool"):
                    deps.append(b.w)
            deps.extend(b.r)
        op.deps = deps
        for b in reads:
            b.r.append(op)
        for b in writes:
            b.w = op
            b.r = []
        self.pending[eng].append(op)
        self.nops += 1
        return op

    def add(self, eng, fn, reads=(), writes=()):
        if self.muted:
            return None
        op = self._mk(eng, fn, reads, writes, False)
        self.last[eng] = op
        return op

    def dma(self, q, out, in_, reads=(), writes=(), **kw):
        if self.muted:
            return None
        def fn(e, out=out, in_=in_, kw=kw):
            return e.dma_start(out=out, in_=in_, **kw)
        op = self._mk(q, fn, reads, writes, True)
        i = self.dcount[q]
        self.dcount[q] += 1
        op.sem = self.dsem[q][i % self.K]
        op.val = 16 * (i // self.K + 1)
        if i >= self.K:
            op.deps.append(self.dhist[q][i - self.K])
        self.dhist[q].append(op)
        return op

    def barrier(self):
        if self.muted:
            return
        ops = [self.last[e] for e in self.CENG if self.last[e] is not None]
        for q in self.DMAQ:
            ops.extend(self.dhist[q][-self.K:])
        self.barrier_ops = ops

    def _need(self, op, dep):
        if dep.is_dma:
            return True
        if dep.eng != op.eng:
            return True
        if op.eng == "pe":
            return False
        if op.is_dma:
            return True
        return self.strict

    def flush(self, final=False):
        nc = self.nc
        if not final and not any(self.pending[e] for e in self.ENG):
            return
        for e in self.ENG:
            for op in self.pending[e]:
                for d in op.deps:
                    if not d.is_dma and not d.emitted and self._need(op, d):
                        d.signal = True
        for e in self.CENG:
            comp = [o for o in self.pending[e] if not o.is_dma]
            if comp:
                comp[-1].signal = True
        for e in self.CENG:
            c = self.ccount[e]
            comp = [o for o in self.pending[e] if not o.is_dma]
            for o in comp:
                if o.signal:
                    c += 1
                    o.val = c
                o.sem = self.csem[e]
            self.ccount[e] = c
            nxt = None
            for o in reversed(comp):
                if o.signal:
                    nxt = o.val
                else:
                    o.val = nxt
        getter = {"pe": "tensor", "act": "scalar", "dve": "vector", "pool": "gpsimd", "sp": "sync"}

        def run(e, eng):
            known = self.known[e]
            for op in self.pending[e]:
                waits = {}
                for d in op.deps:
                    if not self._need(op, d):
                        continue
                    key = id(d.sem)
                    if known.get(key, 0) >= d.val:
                        continue
                    if key not in waits or waits[key][1] < d.val:
                        waits[key] = (d.sem, d.val)
                for key, (sem, val) in waits.items():
                    eng.wait_ge(sem, val)
                    known[key] = val
                inst = op.fn(eng)
                if op.signal:
                    inst.then_inc(op.sem, 16 if op.is_dma else 1)
                op.emitted = True
                op.fn = None
            if final and e == "sp":
                for q in self.DMAQ:
                    n = self.dcount[q]
                    for j in range(self.K):
                        cnt = len(range(j, n, self.K))
                        if cnt:
                            eng.wait_ge(self.dsem[q][j], 16 * cnt)

        with nc.Block() as blk:
            for e in self.ENG:
                if not self.pending[e] and not (final and e == "sp"):
                    continue
                getattr(blk, getter[e])(lambda eng, e=e: run(e, eng))
        self.pending = {e: [] for e in self.ENG}

    def close(self):
        self.es.close()


D = 1024
T = 4096
L = 4
NG = T // 512
NCH = T // 128
DFF = 2816
INC = 3336
EPS = 1e-6
NCORES = 8


class _Stop(Exception):
    pass


def _levels_masks():
    i = np.arange(128)
    s_, c_ = np.meshgrid(i, i, indexing="ij")
    out = {}
    out["ident"] = (s_ == c_).astype(np.float32)
    out["uincl"] = (s_ <= c_).astype(np.float32)
    out["masksl"] = (s_ > c_).astype(np.float32)
    out["negm"] = np.where(c_ < s_, -30000.0, 0.0).astype(np.float32)
    out["strictu"] = (s_ < c_).astype(np.float32)
    out["ones"] = np.ones((128, 128), np.float32)
    lev = []
    for b in (1, 2, 4, 8, 16, 32, 64):
        m = ((s_ // (2 * b)) == (c_ // (2 * b))) & ((s_ // b) % 2 == 0) & ((c_ // b) % 2 == 1)
        lev.append(m.astype(np.float32))
    out["levU"] = np.stack(lev)
    return out


def build_program(depth=L, stop=None):
    nc = bass.Bass("TRN2", target_bir_lowering=False)
    dt = nc.dram_tensor

    def din(name, shape, dtype=F32):
        return dt(name, list(shape), dtype, kind="ExternalInput").ap()

    xT_in = din("xT", [D, T])
    cT_in = din("cT", [128, 8])
    w_ada = din("w_ada", [depth, D, 6 * D])
    b_ada = din("b_ada", [depth, 6 * D])
    w_in = din("w_in", [depth, D, INC])
    w_out = din("w_out", [depth, D, D])
    w_fi = din("w_ffn_in", [depth, D, 2 * DFF])
    w_fo = din("w_ffn_out", [depth, DFF, D])
    gm_in = din("gm", [128, L * 8])
    gf_in = din("gf", [128, L * 8])
    gfin_in = din("gfin", [128, 8])
    caw_in = din("caw", [128, L * 2 * 3])
    cfw_in = din("cfw", [128, L * 2 * 31])
    cfb_in = din("cfb", [128, L * 2])
    cfg_in = din("cfg", [128, L * 2])
    cfbb_in = din("cfbb", [128, L * 2])
    dnw_in = din("dnw", [128, L * 12 * 4])
    alog_in = din("alog", [128, L * 128])
    dtb_in = din("dtb", [128, L * 128])
    gdn_in = din("gdn", [128, L * 128])
    cm_in = din("cmask", [128, 6 * 128])
    lev_in = din("levU4", [128, 7 * 512])
    levT_in = din("levL4", [128, 7 * 512])
    i4_in = din("ident4", [128, 512])
    su4_in = din("strictu4", [128, 512])
    outT = dt("outT", [D, T], F32, kind="ExternalOutput").ap()

    xs = dt("xs", [D, T], F32, kind="Internal").ap()
    ys = dt("ys", [D, T], BF16, kind="Internal").ap()
    dnin = dt("dnin", [128, NCH, 16, 128], BF16, kind="Internal").ap()

    import os as _os2
    s = Sched(nc, strict=(_os2.environ.get("MK_STRICT", "1") == "1"))
    top = ExitStack()

    _cnt = [0]

    def sb(es, name, shape, dtype=F32):
        _cnt[0] += 1
        return es.enter_context(nc.sbuf_tensor("%s_%d" % (name, _cnt[0]), list(shape), dtype))

    ps = [top.enter_context(nc.psum_tensor("ps%d" % i, [128, 512], F32)) for i in range(6)]
    ps6b = top.enter_context(nc.psum_tensor("ps6b", [128, 1024], BF16))
    ps7b = top.enter_context(nc.psum_tensor("ps7b", [128, 1024], BF16))
    bps = [Buf("ps%d" % i, excl=True) for i in range(6)]
    bps6, bps7 = Buf("ps6b", excl=True), Buf("ps7b", excl=True)

    modT = sb(top, "modT", [128, L, 48])
    A1 = sb(top, "A1", [128, L * 8]); A2 = sb(top, "A2", [128, L * 8])
    gm = sb(top, "gm_t", [128, L * 8]); gf = sb(top, "gf_t", [128, L * 8]); gfin = sb(top, "gfin_t", [128, 8])
    caw = sb(top, "caw_t", [128, L * 6]); cfw = sb(top, "cfw_t", [128, L * 62])
    cfb = sb(top, "cfb_t", [128, L * 2]); cfg = sb(top, "cfg_t", [128, L * 2]); cfbb = sb(top, "cfbb_t", [128, L * 2])
    dnw = sb(top, "dnw_t", [128, L * 48])
    alog = sb(top, "alog_t", [128, L * 128]); dtb = sb(top, "dtb_t", [128, L * 128]); gdn = sb(top, "gdn_t", [128, L * 128])
    cmF = sb(top, "cmF", [128, 6 * 128])
    identB = sb(top, "identB", [128, 128], BF16)
    onesB = sb(top, "onesB", [128, 128], BF16)
    ab_tok = sb(top, "ab_tok", [128, NCH * 8])
    bconst = Buf("const")
    bmod = Buf("mod")
    bab = Buf("abtok")
    identF = cmF[:, 0:128]; uincl = cmF[:, 128:256]; masksl = cmF[:, 256:384]
    negm = cmF[:, 384:512]; strictu = cmF[:, 512:640]; onesF = cmF[:, 640:768]

    for (t_, src) in ((gm, gm_in), (gf, gf_in), (gfin, gfin_in), (caw, caw_in), (cfw, cfw_in), (cfb, cfb_in),
                      (cfg, cfg_in), (cfbb, cfbb_in), (dnw, dnw_in), (alog, alog_in), (dtb, dtb_in),
                      (gdn, gdn_in), (cmF, cm_in)):
        s.dma("sp", t_[:], src, writes=[bconst])
    s.dma("pool", identB[:], cm_in[:, 0:128], writes=[bconst])
    s.dma("pool", onesB[:], cm_in[:, 640:768], writes=[bconst])
    bxs = Buf("xs")
    for j in range(8):
        s.dma("sp", xs[j * 128:(j + 1) * 128, :], xT_in[j * 128:(j + 1) * 128, :], writes=[bxs])

    with ExitStack() as es:
        cT = sb(es, "cT_t", [128, 8]); cact = sb(es, "cact", [128, 8])
        wt = [sb(es, "wada%d" % i, [128, 2048]) for i in range(6)]
        bwt = [Buf() for _ in range(6)]
        modrow = sb(es, "modrow", [1, 6 * D]); brow = sb(es, "brow", [1, 6 * D])
        one11 = sb(es, "one11", [1, 1])
        bc, bmr, bbr = Buf(), Buf(), Buf()
        s.dma("sp", cT[:], cT_in, writes=[bc])
        s.add("act", lambda e: e.activation(cact[:], cT[:], AF.Silu), reads=[bc], writes=[bc])
        s.add("dve", lambda e: e.memset(one11[:], 1.0), writes=[bc])
        it = 0
        for l in range(depth):
            s.dma("sp", brow[:], b_ada[l:l + 1, :], reads=[], writes=[bbr])
            for cg in range(3):
                for k in range(8):
                    w_ = wt[it % 6]; bw_ = bwt[it % 6]; it += 1
                    s.dma(("sp", "act", "pool")[it % 3], w_[:], w_ada[l, k * 128:(k + 1) * 128, cg * 2048:(cg + 1) * 2048],
                          writes=[bw_])
                    for i in range(4):
                        s.add("pe", lambda e, i=i, w_=w_, k=k: e.matmul(
                            ps[i][0:1, :], cact[:, k:k + 1], w_[:, i * 512:(i + 1) * 512],
                            start=(k == 0), stop=(k == 7)), reads=[bw_, bc], writes=[bps[i]])
                for i in range(4):
                    c0 = cg * 2048 + i * 512
                    s.add("dve", lambda e, i=i, c0=c0: e.tensor_tensor(
                        modrow[0:1, c0:c0 + 512], ps[i][0:1, :], brow[0:1, c0:c0 + 512], ALU.add),
                        reads=[bps[i], bbr], writes=[bmr])
            for j in range(48):
                s.add("pe", lambda e, j=j: e.matmul(ps[4][:, j:j + 1], modrow[0:1, j * 128:(j + 1) * 128],
                                                     one11[0:1, 0:1], start=True, stop=True),
                      reads=[bmr, bc], writes=[bps[4]])
            s.add("act", lambda e, l=l: e.activation(modT[:, l, :], ps[4][:, 0:48], AF.Copy),
                  reads=[bps[4]], writes=[bmod])
            s.add("dve", lambda e, l=l: e.scalar_tensor_tensor(
                A1[:, l * 8:(l + 1) * 8], modT[:, l, 8:16], 1.0, gm[:, l * 8:(l + 1) * 8], ALU.add, ALU.mult),
                reads=[bmod, bconst], writes=[bmod])
            s.add("dve", lambda e, l=l: e.scalar_tensor_tensor(
                A2[:, l * 8:(l + 1) * 8], modT[:, l, 32:40], 1.0, gf[:, l * 8:(l + 1) * 8], ALU.add, ALU.mult),
                reads=[bmod, bconst], writes=[bmod])
        s.barrier()
        s.flush()

    def B1(l, j): return modT[:, l, j:j + 1]
    def G1(l, j): return modT[:, l, 16 + j:17 + j]
    def B2(l, j): return modT[:, l, 24 + j:25 + j]
    def G2(l, j): return modT[:, l, 40 + j:41 + j]

    xs_v = xs.rearrange("(j p) t -> p j t", p=128)
    ys_v = ys.rearrange("(j p) t -> p j t", p=128)
    out_v = outT.rearrange("(j p) t -> p j t", p=128)

    def norm_group(es_tiles, l, g, Acol, Bcol, hdst, hbuf, xt, bxt, load=True, pn=5, extra_w=()):
        sq, bsq, rs, brs = es_tiles
        sl = slice(g * 512, (g + 1) * 512)
        if load:
            s.dma("sp" if g % 2 == 0 else "pool", xt[:], xs_v[:, :, sl], reads=[bxs], writes=[bxt] + list(extra_w))
        s.add("act", lambda e: e.activation(sq[:], xt[:], AF.Square), reads=[bxt], writes=[bsq])
        for j in range(8):
            s.add("pe", lambda e, j=j: e.matmul(ps[pn][:], onesB[:], sq[:, j, :], start=(j == 0), stop=(j == 7)),
                  reads=[bsq, bconst], writes=[bps[pn]])
        s.add("act", lambda e: e.activation(rs[:], ps[pn][:], AF.Sqrt, bias=EPS, scale=1.0 / D),
              reads=[bps[pn]], writes=[brs])
        s.add("dve", lambda e: e.reciprocal(rs[:], rs[:]), reads=[brs], writes=[brs])
        return sl

    def ck(name):
        if stop == name:
            s.muted = True
    ck('pro')
    for l in range(depth):
        with ExitStack() as es:
            hT = sb(es, "hT", [128, 8, T], BF16); bh = Buf("hT")
            with ExitStack() as es2:
                xt = [sb(es2, "xt%d" % i, [128, 8, 512]) for i in range(3)]; bxt = [Buf(), Buf(), Buf()]
                bxj = [[Buf() for _ in range(8)] for _ in range(3)]
                sqs = [sb(es2, "sq", [128, 8, 512], BF16) for _ in range(2)]; bsqs = [Buf(), Buf()]
                rss = [sb(es2, "rs", [128, 512]) for _ in range(2)]; brss = [Buf(), Buf()]
                def n1_stats(g):
                    x_ = xt[g % 3]; bx_ = bxt[g % 3]
                    sq = sqs[g % 2]; bsq = bsqs[g % 2]; rs = rss[g % 2]; brs = brss[g % 2]
                    norm_group((sq, bsq, rs, brs), l, g, None, None, None, None, x_, bx_, pn=4 + g % 2, extra_w=bxj[g % 3])

                def n1_apply(g):
                    x_ = xt[g % 3]; bx_ = bxt[g % 3]; rs = rss[g % 2]; brs = brss[g % 2]
                    sl = slice(g * 512, (g + 1) * 512)
                    for j in range(8):
                        s.add("dve", lambda e, j=j, x_=x_, rs=rs: e.scalar_tensor_tensor(
                            x_[:, j, :], x_[:, j, :], A1[:, l * 8 + j:l * 8 + j + 1], rs[:], ALU.mult, ALU.mult),
                            reads=[bx_, brs, bmod], writes=[bxj[g % 3][j]])
                        s.add("act", lambda e, j=j, x_=x_, sl=sl: e.activation(
                            hT[:, j, sl], x_[:, j, :], AF.Identity, bias=B1(l, j), scale=1.0),
                            reads=[bxj[g % 3][j], bmod], writes=[bh])

                n1_stats(0)
                for g in range(NG):
                    if g + 1 < NG:
                        n1_stats(g + 1)
                    n1_apply(g)
                s.barrier(); s.flush()
            ck('n1')

            wv = w_in[l].rearrange("(k p) c -> p k c", p=128)

            def proj(wtile, bw, g, pidx):
                for k in range(8):
                    s.add("pe", lambda e, k=k: e.matmul(ps[pidx][:], wtile[:, k, :], hT[:, k, g * 512:(g + 1) * 512],
                                                         start=(k == 0), stop=(k == 7)),
                          reads=[bw, bh], writes=[bps[pidx]])

            with ExitStack() as es2:
                wa = [sb(es2, "wa%d" % i, [128, 8, 128], BF16) for i in range(3)]; bwa = [Buf() for _ in range(3)]
                ppad = sb(es2, "ppad", [128, T + 2]); bpp = Buf()
                abt = sb(es2, "abt", [128, T]); bab_ = Buf()
                acc = sb(es2, "acc", [128, T]); bacc = Buf()
                ybf = sb(es2, "ybf", [128, T], BF16); bybf = Buf()
                tmpc = sb(es2, "tmpc", [128, 512]); btc = Buf()
                s.add("pool", lambda e: e.memset(ppad[:, 0:2], 0.0), writes=[bpp])
                for cc in range(2):
                    for i, base in enumerate((256, 512, 0)):
                        c0 = base + cc * 128
                        s.dma("pool", wa[i][:], wv[:, :, c0:c0 + 128], writes=[bwa[i]])
                    for g in range(NG):
                        sl = slice(g * 512, (g + 1) * 512)
                        proj(wa[0], bwa[0], g, 0)
                        proj(wa[1], bwa[1], g, 1)
                        proj(wa[2], bwa[2], g, 2)
                        s.add("act", lambda e: e.activation(tmpc[:], ps[0][:], AF.Copy), reads=[bps[0]], writes=[btc])
                        s.add("dve", lambda e, g=g: e.tensor_tensor(ppad[:, 2 + g * 512:2 + (g + 1) * 512], tmpc[:],
                                                                  ps[1][:], ALU.mult),
                              reads=[btc, bps[1]], writes=[bpp])
                        s.add("act", lambda e, sl=sl: e.activation(abt[:, sl], ps[2][:], AF.Copy),
                              reads=[bps[2]], writes=[bab_])
                    for sg in range(4):
                        o = sg * 1024
                        wc = lambda k: caw[:, l * 6 + cc * 3 + k:l * 6 + cc * 3 + k + 1]
                        s.add("dve", lambda e, o=o, w0=wc(0): e.tensor_scalar(
                            acc[:, o:o + 1024], ppad[:, o:o + 1024], w0, None, ALU.mult),
                            reads=[bpp, bconst], writes=[bacc])
                        for k in (1, 2):
                            s.add("dve", lambda e, o=o, k=k, wk=wc(k): e.scalar_tensor_tensor(
                                acc[:, o:o + 1024], ppad[:, o + k:o + k + 1024], wk, acc[:, o:o + 1024],
                                ALU.mult, ALU.add), reads=[bpp, bacc, bconst], writes=[bacc])
                        s.add("pool", lambda e, o=o: e.tensor_tensor(ybf[:, o:o + 1024], acc[:, o:o + 1024],
                                                                  abt[:, o:o + 1024], ALU.mult),
                              reads=[bacc, bab_], writes=[bybf])
                    s.dma("sp", ys[cc * 128:(cc + 1) * 128, :], ybf[:], reads=[bybf], writes=[])
                s.barrier(); s.flush()
            ck('mixA')

            with ExitStack() as es2:
                wb = [sb(es2, "wb%d" % i, [128, 8, 128], BF16) for i in range(2)]; bwb = [Buf(), Buf()]
                upad = sb(es2, "upad", [128, 2, T + 30], BF16); bup = Buf()
                dg = sb(es2, "dg", [128, 62, 128], BF16); bdg = Buf(); bdgk = [Buf() for _ in range(62)]
                v32 = sb(es2, "v32", [128, 2, T]); bv32g = [Buf() for _ in range(NG)]
                sgts = [sb(es2, "sgt", [128, 512]) for _ in range(2)]; bsgs = [Buf(), Buf()]
                sq2s = [sb(es2, "sq2", [128, 2, 512], BF16) for _ in range(2)]; bsq2s = [Buf(), Buf()]
                rs2s = [sb(es2, "rs2", [128, 512]) for _ in range(2)]; brs2s = [Buf(), Buf()]
                t2s = [sb(es2, "t2", [128, 512]) for _ in range(4)]; bt2s = [Buf() for _ in range(4)]
                yb2 = [sb(es2, "yb2_%d" % i, [128, 2, 512], BF16) for i in range(2)]; byb2 = [Buf(), Buf()]
                for cc in range(2):
                    s.add("pool", lambda e, cc=cc: e.memset(upad[:, cc, 0:30], 0.0), writes=[bup])
                    for k in range(31):
                        s.add("pool" if k % 2 == 0 else "dve", lambda e, cc=cc, k=k: e.tensor_scalar(
                            dg[:, cc * 31 + k, :], identB[:], cfw[:, l * 62 + cc * 31 + k:l * 62 + cc * 31 + k + 1],
                            None, ALU.mult), reads=[bconst], writes=[bdgk[cc * 31 + k]])
                for cc in range(2):
                    s.dma("pool", wb[0][:], wv[:, :, 768 + cc * 128:768 + (cc + 1) * 128], writes=[bwb[0]])
                    s.dma("pool", wb[1][:], wv[:, :, 1024 + cc * 128:1024 + (cc + 1) * 128], writes=[bwb[1]])
                    for g in range(NG):
                        proj(wb[0], bwb[0], g, 0)
                        proj(wb[1], bwb[1], g, 1)
                        sgt = sgts[g % 2]; bsg = bsgs[g % 2]
                        s.add("act", lambda e, sgt=sgt: e.activation(sgt[:], ps[1][:], AF.Sigmoid), reads=[bps[1]], writes=[bsg])
                        s.add("dve", lambda e, cc=cc, g=g, sgt=sgt: e.tensor_tensor(
                            upad[:, cc, 30 + g * 512:30 + (g + 1) * 512], sgt[:], ps[0][:], ALU.mult),
                            reads=[bsg, bps[0]], writes=[bup])
                for g in range(NG):
                    sl = slice(g * 512, (g + 1) * 512)
                    bv32 = bv32g[g]
                    for cc in range(2):
                        pi = 2 + cc
                        for k in range(31):
                            s.add("pe", lambda e, cc=cc, k=k, g=g, pi=pi: e.matmul(
                                ps[pi][:], dg[:, cc * 31 + k, :], upad[:, cc, g * 512 + k:g * 512 + k + 512],
                                start=(k == 0), stop=(k == 30)), reads=[bdgk[cc * 31 + k], bup], writes=[bps[pi]])
                        s.add("act", lambda e, cc=cc, sl=sl, pi=pi: e.activation(
                            v32[:, cc, sl], ps[pi][:], AF.Identity, bias=cfb[:, l * 2 + cc:l * 2 + cc + 1], scale=1.0),
                            reads=[bps[pi], bconst], writes=[bv32])
                    pm, pv = (4, 5) if g % 2 == 0 else (0, 1)
                    sq2 = sq2s[g % 2]; bsq2 = bsq2s[g % 2]; rs2 = rs2s[g % 2]; brs2 = brs2s[g % 2]
                    for cc in range(2):
                        s.add("pe", lambda e, cc=cc, sl=sl, pm=pm: e.matmul(ps[pm][:], onesF, v32[:, cc, sl],
                                                                  start=(cc == 0), stop=(cc == 1)),
                              reads=[bv32, bconst], writes=[bps[pm]])
                    for cc in range(2):
                        s.add("dve", lambda e, cc=cc, sl=sl, pm=pm: e.scalar_tensor_tensor(
                            v32[:, cc, sl], ps[pm][:], -1.0 / 256, v32[:, cc, sl], ALU.mult, ALU.add),
                            reads=[bps[pm], bv32], writes=[bv32])
                    s.add("act", lambda e, sl=sl, sq2=sq2: e.activation(sq2[:], v32[:, :, sl], AF.Square),
                          reads=[bv32], writes=[bsq2])
                    for cc in range(2):
                        s.add("pe", lambda e, cc=cc, pv=pv, sq2=sq2: e.matmul(ps[pv][:], onesB[:], sq2[:, cc, :],
                                                             start=(cc == 0), stop=(cc == 1)),
                              reads=[bsq2, bconst], writes=[bps[pv]])
                    s.add("act", lambda e, pv=pv, rs2=rs2: e.activation(rs2[:], ps[pv][:], AF.Sqrt, bias=1e-5, scale=1.0 / 256),
                          reads=[bps[pv]], writes=[brs2])
                    s.add("dve", lambda e, rs2=rs2: e.reciprocal(rs2[:], rs2[:]), reads=[brs2], writes=[brs2])
                    yb_ = yb2[g % 2]; by_ = byb2[g % 2]
                    for cc in range(2):
                        t2 = t2s[(g % 2) * 2 + cc]; bt2 = bt2s[(g % 2) * 2 + cc]
                        s.add("dve", lambda e, cc=cc, sl=sl, t2=t2, rs2=rs2: e.tensor_tensor(t2[:], v32[:, cc, sl], rs2[:], ALU.mult),
                              reads=[bv32, brs2], writes=[bt2])
                        s.add("act", lambda e, cc=cc, yb_=yb_, t2=t2: e.activation(
                            yb_[:, cc, :], t2[:], AF.Silu, bias=cfbb[:, l * 2 + cc:l * 2 + cc + 1],
                            scale=cfg[:, l * 2 + cc:l * 2 + cc + 1]), reads=[bt2, bconst], writes=[by_])
                    s.dma("sp", ys_v[:, 2:4, sl], yb_[:], reads=[by_], writes=[])
                s.barrier(); s.flush()
            ck('mixB')

            with ExitStack() as es2:
                wc_ = [sb(es2, "wc%d" % i, [128, 8, 128], BF16) for i in range(2)]; bwc = [Buf(), Buf()]
                cpads = [sb(es2, "cpad", [128, T + 3], BF16) for _ in range(2)]; bcps = [Buf(), Buf()]
                dg4s = [sb(es2, "dg4", [128, 4, 128], BF16) for _ in range(2)]; bdg4s = [Buf(), Buf()]
                silF = sb(es2, "silF", [128, T]); bsilg = [Buf() for _ in range(NG)]
                sqF = sb(es2, "sqF", [128, T], BF16); bsqg = [Buf() for _ in range(NG)]
                sils = [sb(es2, "sil", [128, 512]) for _ in range(2)]; bsils = [Buf(), Buf()]
                sq3s = [sb(es2, "sq3", [128, 512], BF16) for _ in range(2)]; bsq3s = [Buf(), Buf()]
                rs3s = [sb(es2, "rs3", [128, 512]) for _ in range(2)]; brs3s = [Buf(), Buf()]
                ob = [sb(es2, "ob%d" % i, [128, T], BF16) for i in range(2)]; bob = [Buf(), Buf()]
                wab = sb(es2, "wab", [128, 8, 8], BF16); bwab = Buf()
                for i_ in range(2):
                    s.add("pool", lambda e, i_=i_: e.memset(cpads[i_][:, 0:3], 0.0), writes=[bcps[i_]])
                jobs = []
                for h in range(4):
                    jobs.append(("k", 1792 + h * 128, 4 + h, 0 + h))
                    jobs.append(("q", 1280 + h * 128, 0 + h, 4 + h))
                    jobs.append(("v", 2304 + h * 128, 8 + h, 8 + h))
                    jobs.append(("z", 2816 + h * 128, None, 12 + h))
                for ji, (ty, c0, ci, dj) in enumerate(jobs):
                    w_ = wc_[ji % 2]; bw_ = bwc[ji % 2]
                    o_ = ob[ji % 2]; bo_ = bob[ji % 2]
                    s.dma("pool", w_[:], wv[:, :, c0:c0 + 128], writes=[bw_])
                    if ty == "z":
                        for g in range(NG):
                            sl = slice(g * 512, (g + 1) * 512)
                            proj(w_, bw_, g, g % 2)
                            s.add("act", lambda e, sl=sl, g=g, o_=o_: e.activation(o_[:, sl], ps[g % 2][:], AF.Silu),
                                  reads=[bps[g % 2]], writes=[bo_])
                    else:
                        cpad = cpads[ji % 2]; bcp = bcps[ji % 2]; dg4 = dg4s[ji % 2]; bdg4 = bdg4s[ji % 2]
                        for k in range(4):
                            s.add("pool", lambda e, k=k, ci=ci, dg4=dg4: e.tensor_scalar(
                                dg4[:, k, :], identB[:], dnw[:, l * 48 + ci * 4 + k:l * 48 + ci * 4 + k + 1],
                                None, ALU.mult), reads=[bconst], writes=[bdg4])
                        for g in range(NG):
                            proj(w_, bw_, g, g % 2)
                            s.add("act", lambda e, g=g, cpad=cpad: e.activation(cpad[:, 3 + g * 512:3 + (g + 1) * 512],
                                                                    ps[g % 2][:], AF.Copy),
                                  reads=[bps[g % 2]], writes=[bcp])
                        for g in range(NG):
                            sl = slice(g * 512, (g + 1) * 512)
                            pi = 2 + g % 2
                            for k in range(4):
                                s.add("pe", lambda e, k=k, g=g, pi=pi, dg4=dg4, cpad=cpad: e.matmul(
                                    ps[pi][:], dg4[:, k, :], cpad[:, g * 512 + k:g * 512 + k + 512],
                                    start=(k == 0), stop=(k == 3)), reads=[bdg4, bcp], writes=[bps[pi]])
                            if ty == "v":
                                s.add("act", lambda e, sl=sl, pi=pi, o_=o_: e.activation(o_[:, sl], ps[pi][:], AF.Silu),
                                      reads=[bps[pi]], writes=[bo_])
                            else:
                                s.add("act", lambda e, pi=pi, sl=sl: e.activation(silF[:, sl], ps[pi][:], AF.Silu),
                                      reads=[bps[pi]], writes=[bsilg[g]])
                                s.add("pool", lambda e, sl=sl: e.tensor_tensor(sqF[:, sl], silF[:, sl], silF[:, sl], ALU.mult),
                                      reads=[bsilg[g]], writes=[bsqg[g]])
                        if ty != "v":
                            for g in range(NG):
                                sl = slice(g * 512, (g + 1) * 512)
                                rs3 = rs3s[g % 2]; brs3 = brs3s[g % 2]; pq = 4 + g % 2
                                s.add("pe", lambda e, sl=sl, pq=pq: e.matmul(ps[pq][:], onesB[:], sqF[:, sl], start=True, stop=True),
                                      reads=[bsqg[g], bconst], writes=[bps[pq]])
                                s.add("act", lambda e, rs3=rs3, pq=pq: e.activation(rs3[:], ps[pq][:], AF.Sqrt, bias=EPS, scale=1.0),
                                      reads=[bps[pq]], writes=[brs3])
                                s.add("dve", lambda e, rs3=rs3: e.reciprocal(rs3[:], rs3[:]), reads=[brs3], writes=[brs3])
                                sc = 128.0 ** -0.5 if ty == "q" else 1.0
                                s.add("dve", lambda e, sl=sl, sc=sc, o_=o_, rs3=rs3: e.scalar_tensor_tensor(
                                    o_[:, sl], silF[:, sl], sc, rs3[:], ALU.mult, ALU.mult),
                                    reads=[bsilg[g], brs3], writes=[bo_])
                    s.dma("sp", dnin[:, :, dj, :], o_[:].rearrange("p (n t) -> p n t", t=128), reads=[bo_], writes=[])
                s.dma("pool", wab[:], wv[:, :, 3328:3336], writes=[bwab])
                for n in range(NCH):
                    for k in range(8):
                        s.add("pe", lambda e, n=n, k=k: e.matmul(
                            ps[5][:, n * 8:(n + 1) * 8], hT[:, k, n * 128:(n + 1) * 128], wab[:, k, :],
                            start=(k == 0), stop=(k == 7)), reads=[bwab, bh], writes=[bps[5]])
                s.add("act", lambda e: e.activation(ab_tok[:], ps[5][:, 0:NCH * 8], AF.Copy),
                      reads=[bps[5]], writes=[bab])
                s.barrier(); s.flush()

        ck('dnin')
        with ExitStack() as es:
            def t_(name, shape, dtype=F32):
                return sb(es, name, shape, dtype)
            mU4 = t_("mU4", [128, 7, 512], BF16); mL4 = t_("mL4", [128, 7, 512], BF16)
            i4 = t_("i4", [128, 512], BF16); su4 = t_("su4", [128, 512])
            bm = Buf("masks")
            s.dma("pool", mU4[:], lev_in.rearrange("p (a b) -> p a b", b=512), writes=[bm])
            s.dma("pool", mL4[:], levT_in.rearrange("p (a b) -> p a b", b=512), writes=[bm])
            s.dma("pool", i4[:], i4_in, writes=[bm])
            s.dma("sp", su4[:], su4_in, writes=[bm])
            beta = t_("beta", [128, NCH, 4]); gtok = t_("gtok", [128, NCH, 4])
            xsp = t_("xsp", [128, NCH, 4]); axs = t_("axs", [128, NCH, 4]); lg = t_("lg", [128, NCH, 4])
            nega = t_("nega", [128, 128])
            gam = t_("gam", [128, 128]); eg = t_("eg", [128, 128]); negeg = t_("negeg", [128, 128])
            decl = t_("decl", [128, 128]); eglast = t_("eglast", [128, 128])
            bsc = Buf("scal")
            abv = ab_tok[:].rearrange("p (n c) -> p n c", c=8)
            dtbv = dtb[:, l * 128:(l + 1) * 128].rearrange("p (n c) -> p n c", c=4)
            s.add("act", lambda e: e.activation(beta[:], abv[:, :, 4:8], AF.Sigmoid), reads=[bab], writes=[bsc])
            s.add("dve", lambda e: e.tensor_tensor(xsp[:], abv[:, :, 0:4], dtbv, ALU.add), reads=[bab, bconst], writes=[bsc])
            s.add("act", lambda e: e.activation(axs[:], xsp[:], AF.Abs), reads=[bsc], writes=[bsc])
            s.add("act", lambda e: e.activation(axs[:], axs[:], AF.Exp, scale=-1.0), reads=[bsc], writes=[bsc])
            s.add("act", lambda e: e.activation(lg[:], axs[:], AF.Ln, bias=1.0, scale=1.0), reads=[bsc], writes=[bsc])
            s.add("dve", lambda e: e.tensor_scalar(xsp[:], xsp[:], 0.0, None, ALU.max), reads=[bsc], writes=[bsc])
            s.add("dve", lambda e: e.tensor_tensor(xsp[:], xsp[:], lg[:], ALU.add), reads=[bsc], writes=[bsc])
            s.add("act", lambda e: e.activation(nega[:], alog[:, l * 128:(l + 1) * 128], AF.Exp), reads=[bconst], writes=[bsc])
            s.add("dve", lambda e: e.scalar_tensor_tensor(
                gtok[:].rearrange("p n c -> p (n c)"), xsp[:].rearrange("p n c -> p (n c)"), -1.0, nega[:],
                ALU.mult, ALU.mult), reads=[bsc], writes=[bsc])
            gflat = gtok[:].rearrange("p n c -> p (n c)")
            bflat = beta[:].rearrange("p n c -> p (n c)")
            s.add("pe", lambda e: e.matmul(ps[0][:, 0:128], uincl, gflat, start=True, stop=True),
                  reads=[bsc, bconst], writes=[bps[0]])
            s.add("pe", lambda e: e.matmul(ps[1][:, 0:128], onesF, gflat, start=True, stop=True),
                  reads=[bsc, bconst], writes=[bps[1]])
            s.add("act", lambda e: e.activation(gam[:], ps[0][:, 0:128], AF.Copy), reads=[bps[0]], writes=[bsc])
            s.add("act", lambda e: e.activation(eg[:], ps[0][:, 0:128], AF.Exp), reads=[bps[0]], writes=[bsc])
            s.add("dve", lambda e: e.tensor_scalar(negeg[:], eg[:], -1.0, None, ALU.mult), reads=[bsc], writes=[bsc])
            s.add("dve", lambda e: e.tensor_tensor(decl[:], ps[1][:, 0:128], gam[:], ALU.subtract),
                  reads=[bps[1], bsc], writes=[bsc])
            s.add("act", lambda e: e.activation(decl[:], decl[:], AF.Exp), reads=[bsc], writes=[bsc])
            s.add("act", lambda e: e.activation(eglast[:], ps[1][:, 0:128], AF.Exp), reads=[bps[1]], writes=[bsc])

            ck('D0')
            import os as _os
            NCH_RUN = int(_os.environ.get('DBG_NCH', NCH))
            NL = 3
            S32 = t_("S32", [128, 512]); bS = Buf()
            Sbf = t_("Sbf", [128, 512], BF16); bSb = Buf()
            s.add("dve", lambda e: e.memset(S32[:], 0.0), writes=[bS])
            s.add("pool", lambda e: e.memset(Sbf[:], 0.0), writes=[bSb])
            gd = gdn[:, l * 128:(l + 1) * 128]
            H = lambda a, h: a[:, h * 128:(h + 1) * 128]

            class Lane:
                pass
            lanes = []
            for li_ in range(NL):
                ln = Lane()
                ln.inp = t_("inp", [128, 16, 128], BF16); ln.binp = Buf()
                ln.Mt = t_("Mt", [128, 4, 128]); ln.bMt = Buf()
                ln.E = t_("E", [128, 512]); ln.bE = Buf()
                ln.Es = t_("Es", [128, 512]); ln.bEs = Buf()
                for nm in ("Pm", "Qm", "X", "Y", "R1", "QKm", "kdec", "rp", "vnew", "on", "yc"):
                    setattr(ln, nm, t_(nm, [128, 512], BF16)); setattr(ln, "b" + nm, Buf())
                ln.Qoff = t_("Qoff", [128, 6, 512], BF16); ln.bQoff = Buf()
                for nm in ("vtok", "o2s", "o_t"):
                    setattr(ln, nm, t_(nm, [128, 512])); setattr(ln, "b" + nm, Buf())
                ln.junk = t_("junk", [128, 128]); ln.bjunk = Buf()
                ln.ss = t_("ss", [128, 4]); ln.bss = Buf()
                ln.p = (ps[2 * li_], ps[2 * li_ + 1]); ln.bp = (bps[2 * li_], bps[2 * li_ + 1])
                if li_ % 2 == 0:
                    ln.pT = ps6b; ln.bpT = bps6
                else:
                    ln.pT = ps7b; ln.bpT = bps7
                lanes.append(ln)

            owner = {}

            def acq(me, banks):
                while any(owner.get(id(b_)) not in (None, me) for b_ in banks):
                    yield
                for b_ in banks:
                    owner[id(b_)] = me

            def rel(me, banks):
                for b_ in banks:
                    if owner.get(id(b_)) == me:
                        owner[id(b_)] = None

            def d1_gen(n, ln):
                ip = ln.inp; bip = ln.binp
                p0, p1 = ln.p; b0, b1 = ln.bp; pT = ln.pT; bT = ln.bpT
                col = lambda a, h: a[:, n * 4 + h:n * 4 + h + 1]
                s.dma("sp", ip[:], dnin[:, n, :, :], reads=[], writes=[bip])
                for h in range(4):
                    s.add("act", lambda e, h=h, g_=col(gflat, h): e.activation(ln.Mt[:, h, :], masksl, AF.Identity, bias=0.0, scale=g_),
                          reads=[bsc, bconst], writes=[ln.bMt])
                yield
                for h in range(4):
                    s.add("pe", lambda e, h=h: e.matmul(H(p0, h), ln.Mt[:, h, :], uincl, start=True, stop=False),
                          reads=[ln.bMt, bconst], writes=[b0])
                    s.add("pe", lambda e, h=h: e.matmul(H(p0, h), identF, negm, start=False, stop=True),
                          reads=[ln.bMt, bconst], writes=[b0])
                yield
                s.add("act", lambda e: e.activation(ln.E[:], p0[:], AF.Exp), reads=[b0], writes=[ln.bE])
                yield
                for h in range(4):
                    s.add("pe", lambda e, h=h: e.matmul(H(p0, h), ip[:, h, :], ip[:, h, :], start=True, stop=True),
                          reads=[bip], writes=[b0])
                for h in range(4):
                    s.add("pe", lambda e, h=h: e.matmul(H(p1, h), ip[:, h, :], ip[:, 4 + h, :], start=True, stop=True),
                          reads=[bip], writes=[b1])
                yield
                for h in range(4):
                    s.add("dve", lambda e, h=h, b_=col(bflat, h): e.scalar_tensor_tensor(
                        H(ln.Pm, h), H(p0, h), b_, H(ln.E, h), ALU.mult, ALU.mult),
                        reads=[b0, ln.bE, bsc], writes=[ln.bPm])
                s.add("dve", lambda e: e.tensor_tensor(ln.QKm[:], p1[:], ln.E[:], ALU.mult), reads=[b1, ln.bE], writes=[ln.bQKm])
                yield
                yield from acq(n, (bT,))
                for h in range(4):
                    s.add("pe", lambda e, h=h: e.transpose(H(pT, h), H(ln.Pm, h), identB[:]), reads=[ln.bPm, bconst], writes=[bT])
                s.add("pool", lambda e: e.tensor_tensor(ln.X[:], ln.Pm[:], mU4[:, 0, :], ALU.mult), reads=[ln.bPm, bm], writes=[ln.bX])
                s.add("pool", lambda e: e.tensor_tensor(ln.X[:], i4[:], ln.X[:], ALU.subtract), reads=[ln.bX, bm], writes=[ln.bX])
                yield
                s.add("act", lambda e: e.activation(ln.Qm[:], pT[:, 0:512], AF.Copy), reads=[bT], writes=[ln.bQm])
                rel(n, (bT,))
                yield
                s.add("pool", lambda e: e.tensor_tensor(ln.Y[:], ln.Qm[:], mL4[:, 0, :], ALU.mult), reads=[ln.bQm, bm], writes=[ln.bY])
                s.add("pool", lambda e: e.tensor_tensor(ln.Y[:], i4[:], ln.Y[:], ALU.subtract), reads=[ln.bY, bm], writes=[ln.bY])
                yield
                for li in range(1, 7):
                    last = (li == 6)
                    for h in range(4):
                        s.add("pe", lambda e, h=h, li=li: e.matmul(H(p0, h), H(ln.Qm, h), H(ln.X, h),
                                                                 start=True, stop=True), reads=[ln.bQm, ln.bX], writes=[b0])
                    yield
                    s.add("dve", lambda e, li=li: e.tensor_tensor(ln.R1[:], p0[:], mU4[:, li, :], ALU.mult),
                          reads=[b0, bm], writes=[ln.bR1])
                    yield
                    for h in range(4):
                        s.add("pe", lambda e, h=h: e.matmul(H(p0, h), H(ln.Y, h), H(ln.R1, h), start=True, stop=True),
                              reads=[ln.bY, ln.bR1], writes=[b0])
                    if not last:
                        for h in range(4):
                            s.add("pe", lambda e, h=h: e.matmul(H(p1, h), H(ln.R1, h), H(ln.Y, h), start=True, stop=True),
                                  reads=[ln.bY, ln.bR1], writes=[b1])
                    yield
                    s.add("dve", lambda e: e.tensor_tensor(ln.X[:], ln.X[:], p0[:], ALU.subtract), reads=[ln.bX, b0], writes=[ln.bX])
                    if not last:
                        s.add("dve", lambda e: e.tensor_tensor(ln.Y[:], ln.Y[:], p1[:], ALU.subtract),
                              reads=[ln.bY, b1], writes=[ln.bY])
                    yield

            def d2_gen(n, ln):
                ip = ln.inp; bip = ln.binp
                p0, p1 = ln.p; b0, b1 = ln.bp; pT = ln.pT; bT = ln.bpT
                col = lambda a, h: a[:, n * 4 + h:n * 4 + h + 1]
                yield from acq(n, (bT,))
                for h in range(4):
                    s.add("pe", lambda e, h=h: e.transpose(H(pT, h), ip[:, h, :], identB[:]), reads=[bip, bconst], writes=[bT])
                for h in range(4):
                    s.add("pe", lambda e, h=h: e.transpose(H(pT, 4 + h), ip[:, 8 + h, :], identB[:]), reads=[bip, bconst], writes=[bT])
                for h in range(4):
                    s.add("pe", lambda e, h=h: e.matmul(H(p0, h), ip[:, h, :], H(Sbf, h), start=True, stop=True),
                          reads=[bip, bSb], writes=[b0])
                for h in range(4):
                    s.add("pe", lambda e, h=h: e.matmul(H(p1, h), ip[:, 4 + h, :], H(Sbf, h), start=True, stop=True),
                          reads=[bip, bSb], writes=[b1])
                yield
                s.add("act", lambda e: e.activation(ln.vtok[:], pT[:, 512:1024], AF.Copy), reads=[bT], writes=[ln.bvtok])
                for h in range(4):
                    s.add("act", lambda e, h=h, d_=col(decl, h): e.activation(H(ln.kdec, h), H(pT, h), AF.Identity, bias=0.0, scale=d_),
                          reads=[bT, bsc], writes=[ln.bkdec])
                rel(n, (bT,))
                yield
                for h in range(4):
                    s.add("dve", lambda e, h=h, ne_=col(negeg, h): e.scalar_tensor_tensor(
                        H(ln.rp, h), H(p0, h), ne_, H(ln.vtok, h), ALU.mult, ALU.add),
                        reads=[b0, ln.bvtok, bsc], writes=[ln.brp])
                s.add("act", lambda e: e.activation(ln.o2s[:], p1[:], AF.Copy), reads=[b1], writes=[ln.bo2s])
                yield
                for h in range(4):
                    s.add("pe", lambda e, h=h: e.matmul(H(p0, h), H(ln.X, h), H(ln.rp, h), start=True, stop=True),
                          reads=[ln.bX, ln.brp], writes=[b0])
                yield
                for h in range(4):
                    s.add("act", lambda e, h=h, b_=col(bflat, h): e.activation(H(ln.vnew, h), H(p0, h), AF.Identity, bias=0.0, scale=b_),
                          reads=[b0, bsc], writes=[ln.bvnew])
                yield
                for h in range(4):
                    s.add("pe", lambda e, h=h: e.matmul(H(p0, h), H(ln.kdec, h), H(ln.vnew, h), start=True, stop=True),
                          reads=[ln.bkdec, ln.bvnew], writes=[b0])
                for h in range(4):
                    s.add("pe", lambda e, h=h: e.matmul(H(p1, h), H(ln.QKm, h), H(ln.vnew, h), start=True, stop=True),
                          reads=[ln.bQKm, ln.bvnew], writes=[b1])
                yield
                for h in range(4):
                    s.add("dve", lambda e, h=h, el_=col(eglast, h): e.scalar_tensor_tensor(
                        H(S32, h), H(S32, h), el_, H(p0, h), ALU.mult, ALU.add),
                        reads=[bS, b0, bsc], writes=[bS])
                yield
                s.add("act", lambda e: e.activation(Sbf[:], S32[:], AF.Copy), reads=[bS], writes=[bSb])
                for h in range(4):
                    s.add("dve", lambda e, h=h, eg_=col(eg, h): e.scalar_tensor_tensor(
                        H(ln.o_t, h), H(ln.o2s, h), eg_, H(p1, h), ALU.mult, ALU.add),
                        reads=[b1, ln.bo2s, bsc], writes=[ln.bo_t])
                s.add("pool", lambda e: e.memset(ln.ss[:], 0.0), writes=[ln.bss])
                yield

            def d3_gen(n, ln):
                ip = ln.inp; bip = ln.binp
                pT = ln.pT; bT = ln.bpT
                for h in range(4):
                    s.add("act", lambda e, h=h: e.activation(ln.junk[:], H(ln.o_t, h), AF.Square, accum_out=ln.ss[:, h:h + 1]),
                          reads=[ln.bo_t, ln.bss], writes=[ln.bjunk, ln.bss])
                yield
                s.add("act", lambda e: e.activation(ln.ss[:], ln.ss[:], AF.Sqrt, bias=EPS, scale=1.0 / 128), reads=[ln.bss], writes=[ln.bss])
                yield
                s.add("dve", lambda e: e.reciprocal(ln.ss[:], ln.ss[:]), reads=[ln.bss], writes=[ln.bss])
                yield
                for h in range(4):
                    s.add("dve", lambda e, h=h: e.scalar_tensor_tensor(H(ln.on, h), H(ln.o_t, h), ln.ss[:, h:h + 1], gd, ALU.mult, ALU.mult),
                          reads=[ln.bo_t, ln.bss, bconst], writes=[ln.bon])
                yield
                yield from acq(n, (bT,))
                for h in range(4):
                    s.add("pe", lambda e, h=h: e.transpose(H(pT, 4 + h), H(ln.on, h), identB[:]), reads=[ln.bon, bconst], writes=[bT])
                yield
                s.add("dve", lambda e: e.tensor_tensor(
                    ln.yc[:], pT[:, 512:1024], ip[:, 12:16, :].rearrange("p a b -> p (a b)"), ALU.mult),
                    reads=[bT, bip], writes=[ln.byc])
                rel(n, (bT,))
                s.dma("sp", ys_v[:, 4:8, n * 128:(n + 1) * 128], ln.yc[:].rearrange("p (a b) -> p a b", b=128),
                      reads=[ln.byc], writes=[])
                yield

            def chain(n, ln):
                yield from d1_gen(n, ln)
                while d2_turn[0] != n:
                    yield
                yield from d2_gen(n, ln)
                d2_turn[0] = n + 1
                yield from d3_gen(n, ln)

            d2_turn = [0]
            active = {}
            nxt = 0
            while nxt < NCH_RUN or active:
                while nxt < NCH_RUN and (nxt % NL) not in active:
                    active[nxt % NL] = chain(nxt, lanes[nxt % NL]); nxt += 1
                    break
                for k in sorted(active.keys(), key=lambda kk: kk):
                    try:
                        next(active[k])
                    except StopIteration:
                        del active[k]
            s.barrier(); s.flush()

        ck("mixy%d" % l)
        esF = ExitStack()
        wfi = sb(esF, "wfi", [128, 8, 2 * DFF], BF16); bwfi = Buf()
        wfiv = w_fi[l].rearrange("(k p) c -> p k c", p=128)
        with ExitStack() as es:
            wo = sb(es, "wo", [128, 8, D], BF16); bwo = Buf()
            wov = w_out[l].rearrange("(k p) c -> p k c", p=128)
            for k in range(8):
                s.dma("pool", wo[:, k, :], wov[:, k, :], writes=[bwo])
            wfi_jobs = [(k, hh) for k in range(8) for hh in range(2)]
            yt = [sb(es, "yt%d" % i, [128, 8, 512], BF16) for i in range(2)]; byt = [Buf(), Buf()]
            xt = [sb(es, "xo%d" % i, [128, 8, 512]) for i in range(2)]; bxt = [Buf(), Buf()]
            for g in range(NG):
                sl = slice(g * 512, (g + 1) * 512)
                y_ = yt[g % 2]; by_ = byt[g % 2]; x_ = xt[g % 2]; bx_ = bxt[g % 2]
                s.dma("pool", y_[:], ys_v[:, :, sl], writes=[by_])
                s.dma("sp", x_[:], xs_v[:, :, sl], writes=[bx_])
                for (k, hh) in wfi_jobs[2 * g:2 * g + 2]:
                    s.dma("pool", wfi[:, k, hh * DFF:(hh + 1) * DFF], wfiv[:, k, hh * DFF:(hh + 1) * DFF], writes=[bwfi])
                for oc in range(8):
                    pi = oc % 4
                    for k in range(8):
                        s.add("pe", lambda e, k=k, oc=oc, pi=pi, y_=y_: e.matmul(
                            ps[pi][:], wo[:, k, oc * 128:(oc + 1) * 128], y_[:, k, :], start=(k == 0), stop=(k == 7)),
                            reads=[bwo, by_], writes=[bps[pi]])
                    s.add("dve", lambda e, oc=oc, pi=pi, x_=x_: e.scalar_tensor_tensor(
                        x_[:, oc, :], ps[pi][:], G1(l, oc), x_[:, oc, :], ALU.mult, ALU.add),
                        reads=[bps[pi], bx_, bmod], writes=[bx_])
                s.dma("act", xs_v[:, :, sl], x_[:], reads=[bx_], writes=[])
            s.barrier(); s.flush()
        ck("mix%d" % l)

        with ExitStack() as es:
            wfo = sb(es, "wfo", [128, 22, D], BF16); bwfo = Buf()
            wfov = w_fo[l].rearrange("(k p) c -> p k c", p=128)
            for k in range(22):
                s.dma("pool", wfo[:, k, :], wfov[:, k, :], writes=[bwfo])
            GF = 256
            NGF = T // GF
            xts = [sb(es, "xf", [128, 8, GF]) for _ in range(2)]; bxts = [Buf(), Buf()]
            hT2s = [sb(es, "hT2", [128, 8, GF], BF16) for _ in range(2)]; bh2s = [Buf(), Buf()]
            sq = sb(es, "sqf", [128, 8, GF], BF16); bsq = Buf(); bsqj = [Buf() for _ in range(8)]
            rs = sb(es, "rsf", [128, GF]); brs = Buf()
            aTs = [sb(es, "aT", [128, 22, GF], BF16) for _ in range(2)]; baTs = [Buf(), Buf()]
            sgf = [sb(es, "sgf%d" % i, [128, GF]) for i in range(2)]; bsgf = [Buf(), Buf()]

            def f_norm(g):
                xt = xts[g % 2]; bxt = bxts[g % 2]; hT2 = hT2s[g % 2]; bh2 = bh2s[g % 2]
                sl = slice(g * GF, (g + 1) * GF)
                s.dma("sp", xt[:], xs_v[:, :, sl], reads=[bxs], writes=[bxt])
                s.add("act", lambda e: e.activation(sq[:], xt[:], AF.Square), reads=[bxt], writes=[bsq] + bsqj)
                for j in range(8):
                    s.add("pe", lambda e, j=j: e.matmul(ps[5][:, 0:GF], onesB[:], sq[:, j, :], start=(j == 0), stop=(j == 7)),
                          reads=[bsq, bconst], writes=[bps[5]])
                s.add("act", lambda e: e.activation(rs[:], ps[5][:, 0:GF], AF.Sqrt, bias=EPS, scale=1.0 / D),
                      reads=[bps[5]], writes=[brs])
                s.add("dve", lambda e: e.reciprocal(rs[:], rs[:]), reads=[brs], writes=[brs])
                for j in range(8):
                    s.add("dve", lambda e, j=j: e.scalar_tensor_tensor(
                        sq[:, j, :], xt[:, j, :], A2[:, l * 8 + j:l * 8 + j + 1], rs[:], ALU.mult, ALU.mult),
                        reads=[bxt, brs, bmod, bsq], writes=[bsqj[j]])
                    s.add("act", lambda e, j=j: e.activation(
                        hT2[:, j, :], sq[:, j, :], AF.Identity, bias=B2(l, j), scale=1.0),
                        reads=[bsqj[j], bmod], writes=[bh2])

            def f_in(g):
                hT2 = hT2s[g % 2]; bh2 = bh2s[g % 2]; aT = aTs[g % 2]; baT = baTs[g % 2]
                for j in range(22):
                    pg, pu = (0, 1) if j % 2 == 0 else (2, 3)
                    for k in range(8):
                        s.add("pe", lambda e, k=k, j=j, pg=pg: e.matmul(
                            ps[pg][:, 0:GF], wfi[:, k, j * 128:(j + 1) * 128], hT2[:, k, :], start=(k == 0), stop=(k == 7)),
                            reads=[bwfi, bh2], writes=[bps[pg]])
                    for k in range(8):
                        s.add("pe", lambda e, k=k, j=j, pu=pu: e.matmul(
                            ps[pu][:, 0:GF], wfi[:, k, DFF + j * 128:DFF + (j + 1) * 128], hT2[:, k, :],
                            start=(k == 0), stop=(k == 7)), reads=[bwfi, bh2], writes=[bps[pu]])
                    sg_ = sgf[j % 2]; bsg_ = bsgf[j % 2]
                    s.add("act", lambda e, pg=pg, sg_=sg_: e.activation(sg_[:], ps[pg][:, 0:GF], AF.Silu),
                          reads=[bps[pg]], writes=[bsg_])
                    s.add("dve", lambda e, j=j, pu=pu, sg_=sg_: e.tensor_tensor(aT[:, j, :], sg_[:], ps[pu][:, 0:GF], ALU.mult),
                          reads=[bsg_, bps[pu]], writes=[baT])

            def f_out(g):
                xt = xts[g % 2]; bxt = bxts[g % 2]; aT = aTs[g % 2]; baT = baTs[g % 2]
                sl = slice(g * GF, (g + 1) * GF)
                for oc in range(8):
                    pi = 4 if oc % 2 == 0 else (0, 2)[(oc // 2) % 2]
                    for k in range(22):
                        s.add("pe", lambda e, k=k, oc=oc, pi=pi: e.matmul(
                            ps[pi][:, 0:GF], wfo[:, k, oc * 128:(oc + 1) * 128], aT[:, k, :], start=(k == 0), stop=(k == 21)),
                            reads=[bwfo, baT], writes=[bps[pi]])
                    s.add("dve", lambda e, oc=oc, pi=pi: e.scalar_tensor_tensor(
                        xt[:, oc, :], ps[pi][:, 0:GF], G2(l, oc), xt[:, oc, :], ALU.mult, ALU.add),
                        reads=[bps[pi], bxt, bmod], writes=[bxt])
                s.dma("pool", xs_v[:, :, sl], xt[:], reads=[bxt], writes=[bxs])

            f_norm(0)
            for g in range(NGF):
                f_in(g)
                if g + 1 < NGF:
                    f_norm(g + 1)
                f_out(g)
            s.barrier(); s.flush()
        esF.close()
        ck("ffn%d" % l)

    s.muted = False
    dumpy = stop in ('mixA', 'mixB') or (stop is not None and stop.startswith('mixy'))
    with ExitStack() as es:
        xt = [sb(es, "xz%d" % i, [128, 8, 512]) for i in range(2)]; bxt = [Buf(), Buf()]
        sq = sb(es, "sqz", [128, 8, 512], BF16); bsq = Buf()
        rs = sb(es, "rsz", [128, 512]); brs = Buf()
        if dumpy:
            yt = [sb(es, "yz%d" % i, [128, 8, 512], BF16) for i in range(2)]; byt = [Buf(), Buf()]
        for g in range(NG):
            x_ = xt[g % 2]; bx_ = bxt[g % 2]
            if stop is None:
                sl = norm_group((sq, bsq, rs, brs), 0, g, None, None, None, None, x_, bx_)
                for j in range(8):
                    s.add("dve", lambda e, j=j, x_=x_: e.scalar_tensor_tensor(
                        x_[:, j, :], x_[:, j, :], gfin[:, j:j + 1], rs[:], ALU.mult, ALU.mult),
                        reads=[bx_, brs, bconst], writes=[bx_])
            elif dumpy:
                sl = slice(g * 512, (g + 1) * 512)
                y_ = yt[g % 2]; by_ = byt[g % 2]
                s.dma("sp", y_[:], ys_v[:, :, sl], writes=[by_])
                s.add("dve", lambda e, x_=x_, y_=y_: e.tensor_copy(x_[:], y_[:]), reads=[by_], writes=[bx_])
            else:
                sl = slice(g * 512, (g + 1) * 512)
                s.dma("sp", x_[:], xs_v[:, :, sl], writes=[bx_])
            s.dma("act" if g % 2 == 0 else "sp", out_v[:, :, sl], x_[:], reads=[bx_], writes=[])
        s.flush(final=True)
    top.close()
    s.close()
    nc._nops = s.nops
    return nc


def _pp(a):
    a = np.asarray(a, np.float32)
    lead = a.shape[:-1]
    n = a.shape[-1] // 128
    a = a.reshape(lead + (n, 128))
    a = np.moveaxis(a, -1, 0)
    return np.ascontiguousarray(a.reshape(128, -1))


def make_in_maps(inputs, depth=L, ncores=NCORES):
    f = lambda k: np.asarray(inputs[k], np.float32)
    m = _levels_masks()
    common = {
        "w_ada": f("w_ada")[:depth], "b_ada": f("b_ada")[:depth], "w_in": f("w_in")[:depth], "w_out": f("w_out")[:depth],
        "w_ffn_in": f("w_ffn_in")[:depth], "w_ffn_out": f("w_ffn_out")[:depth],
        "gm": _pp(f("norm_mix_g")), "gf": _pp(f("norm_ffn_g")), "gfin": _pp(f("final_norm_g")),
        "caw": _pp(np.transpose(f("conv_a_w"), (0, 2, 1)).reshape(L, 2, 128, 3).transpose(0, 1, 3, 2)),
        "cfw": _pp(np.transpose(f("conf_dw_w"), (0, 2, 1)).reshape(L, 2, 128, 31).transpose(0, 1, 3, 2)),
        "cfb": _pp(f("conf_dw_b")), "cfg": _pp(f("conf_ln_g")), "cfbb": _pp(f("conf_ln_b")),
        "dnw": _pp(np.transpose(f("dn_conv_w"), (0, 2, 1)).reshape(L, 12, 128, 4).transpose(0, 1, 3, 2)),
        "alog": np.ascontiguousarray(np.broadcast_to(np.tile(f("dn_a_log"), (1, NCH)).reshape(1, L * 128), (128, L * 128))),
        "dtb": np.ascontiguousarray(np.broadcast_to(np.tile(f("dn_dt_bias"), (1, NCH)).reshape(1, L * 128), (128, L * 128))),
        "gdn": np.ascontiguousarray(np.broadcast_to(f("dn_norm_g").reshape(1, L * 128), (128, L * 128))),
        "cmask": np.ascontiguousarray(np.concatenate(
            [m["ident"], m["uincl"], m["masksl"], m["negm"], m["strictu"], m["ones"]], axis=1)),
        "levU4": np.ascontiguousarray(np.concatenate([np.tile(m["levU"][i], (1, 4)) for i in range(7)], axis=1)),
        "levL4": np.ascontiguousarray(np.concatenate([np.tile(m["levU"][i].T, (1, 4)) for i in range(7)], axis=1)),
        "ident4": np.ascontiguousarray(np.tile(m["ident"], (1, 4))),
        "strictu4": np.ascontiguousarray(np.tile(m["strictu"], (1, 4))),
    }
    x = f("x"); c = f("c")
    maps = []
    for core in range(ncores):
        b = core % 4
        d = dict(common)
        d["xT"] = np.ascontiguousarray(x[b].T)
        d["cT"] = np.ascontiguousarray(c[b].reshape(8, 128).T)
        maps.append(d)
    return maps


_NC_CACHE = {}


def kernel(**inputs):
    if "nc" not in _NC_CACHE:
        _NC_CACHE["nc"] = build_program()
    nc = _NC_CACHE["nc"]
    in_maps = make_in_maps(inputs)
    res = run_bass_kernel_spmd(nc, in_maps, core_ids=list(range(NCORES)))
    out = np.stack([np.asarray(res.results[b]["outT"], np.float32).T for b in range(4)], axis=0)
    return np.ascontiguousarray(out)
```

```python
from contextlib import ExitStack

import numpy as np
import concourse.bass as bass
import concourse.mybir as mybir
from concourse.bass_utils import run_bass_kernel_spmd

F32 = mybir.dt.float32
BF16 = mybir.dt.bfloat16
AF = mybir.ActivationFunctionType
ALU = mybir.AluOpType


class Buf:
    __slots__ = ("name", "w", "r", "excl")

    def __init__(self, name="", excl=False):
        self.name = name
        self.w = None
        self.r = []
        self.excl = excl


class _Op:
    __slots__ = ("eng", "fn", "deps", "is_dma", "sem", "val", "signal", "emitted")


class Sched:
    CENG = ("pe", "act", "dve", "pool")
    ENG = ("pe", "act", "dve", "pool", "sp")
    DMAQ = ("sp", "pool", "act")

    def __init__(self, nc, strict=True, dma_k=8):
        self.nc = nc
        self.strict = strict
        self.K = dma_k
        self.es = ExitStack()
        self.csem = {e: self.es.enter_context(nc.semaphore("c_" + e)) for e in self.CENG}
        self.ccount = {e: 0 for e in self.CENG}
        self.dsem = {q: [self.es.enter_context(nc.semaphore("d_%s%d" % (q, i))) for i in range(dma_k)]
                     for q in self.DMAQ}
        self.dcount = {q: 0 for q in self.DMAQ}
        self.dhist = {q: [] for q in self.DMAQ}
        self.known = {e: {} for e in self.ENG}
        self.pending = {e: [] for e in self.ENG}
        self.last = {e: None for e in self.CENG}
        self.barrier_ops = []
        self.nops = 0
        self.muted = False

    def _mk(self, eng, fn, reads, writes, is_dma):
        op = _Op()
        op.eng = eng
        op.fn = fn
        op.is_dma = is_dma
        op.signal = is_dma
        op.sem = None
        op.val = None
        op.emitted = False
        deps = list(self.barrier_ops)
        for b in reads:
            if b.w is not None:
                deps.append(b.w)
            if b.excl:
                deps.extend(r for r in b.r if r.eng != eng)
        for b in writes:
            if b.w is not None:
                if not (b.w.eng == eng and not b.w.is_dma and not is_dma and eng != "pool"):
                    deps.append(b.w)
            deps.extend(b.r)
        op.deps = deps
        for b in reads:
            b.r.append(op)
        for b in writes:
            b.w = op
            b.r = []
        self.pending[eng].append(op)
        self.nops += 1
        return op

    def add(self, eng, fn, reads=(), writes=()):
        if self.muted:
            return None
        op = self._mk(eng, fn, reads, writes, False)
        self.last[eng] = op
        return op

    def dma(self, q, out, in_, reads=(), writes=(), **kw):
        if self.muted:
            return None
        def fn(e, out=out, in_=in_, kw=kw):
            return e.dma_start(out=out, in_=in_, **kw)
        op = self._mk(q, fn, reads, writes, True)
        i = self.dcount[q]
        self.dcount[q] += 1
        op.sem = self.dsem[q][i % self.K]
        op.val = 16 * (i // self.K + 1)
        if i >= self.K:
            op.deps.append(self.dhist[q][i - self.K])
        self.dhist[q].append(op)
        return op

    def barrier(self):
        if self.muted:
            return
        ops = [self.last[e] for e in self.CENG if self.last[e] is not None]
        for q in self.DMAQ:
            ops.extend(self.dhist[q][-self.K:])
        self.barrier_ops = ops

    def _need(self, op, dep):
        if dep.is_dma:
            return True
        if dep.eng != op.eng:
            return True
        if op.eng == "pe":
            return False
        if op.is_dma:
            return True
        return self.strict

    def flush(self, final=False):
        nc = self.nc
        if not final and not any(self.pending[e] for e in self.ENG):
            return
        for e in self.ENG:
            for op in self.pending[e]:
                for d in op.deps:
                    if not d.is_dma and not d.emitted and self._need(op, d):
                        d.signal = True
        for e in self.CENG:
            comp = [o for o in self.pending[e] if not o.is_dma]
            if comp:
                comp[-1].signal = True
        for e in self.CENG:
            c = self.ccount[e]
            comp = [o for o in self.pending[e] if not o.is_dma]
            for o in comp:
                if o.signal:
                    c += 1
                    o.val = c
                o.sem = self.csem[e]
            self.ccount[e] = c
            nxt = None
            for o in reversed(comp):
                if o.signal:
                    nxt = o.val
                else:
                    o.val = nxt
        getter = {"pe": "tensor", "act": "scalar", "dve": "vector", "pool": "gpsimd", "sp": "sync"}

        def run(e, eng):
            known = self.known[e]
            for op in self.pending[e]:
                waits = {}
                for d in op.deps:
                    if not self._need(op, d):
                        continue
                    key = id(d.sem)
                    if known.get(key, 0) >= d.val:
                        continue
                    if key not in waits or waits[key][1] < d.val:
                        waits[key] = (d.sem, d.val)
                for key, (sem, val) in waits.items():
                    eng.wait_ge(sem, val)
                    known[key] = val
                inst = op.fn(eng)
                if op.signal:
                    inst.then_inc(op.sem, 16 if op.is_dma else 1)
                op.emitted = True
                op.fn = None
            if final and e == "sp":
                for q in self.DMAQ:
                    n = self.dcount[q]
                    for j in range(self.K):
                        cnt = len(range(j, n, self.K))
                        if cnt:
                            eng.wait_ge(self.dsem[q][j], 16 * cnt)

        with nc.Block() as blk:
            for e in self.ENG:
                if not self.pending[e] and not (final and e == "sp"):
                    continue
                getattr(blk, getter[e])(lambda eng, e=e: run(e, eng))
        self.pending = {e: [] for e in self.ENG}

    def close(self):
        self.es.close()


D = 1024
T = 4096
L = 4
NG = T // 512
NCH = T // 128
DFF = 2816
INC = 3336
EPS = 1e-6
NCORES = 8


class _Stop(Exception):
    pass


def _levels_masks():
    i = np.arange(128)
    s_, c_ = np.meshgrid(i, i, indexing="ij")
    out = {}
    out["ident"] = (s_ == c_).astype(np.float32)
    out["uincl"] = (s_ <= c_).astype(np.float32)
    out["masksl"] = (s_ > c_).astype(np.float32)
    out["negm"] = np.where(c_ < s_, -30000.0, 0.0).astype(np.float32)
    out["strictu"] = (s_ < c_).astype(np.float32)
    out["ones"] = np.ones((128, 128), np.float32)
    lev = []
    for b in (1, 2, 4, 8, 16, 32, 64):
        m = ((s_ // (2 * b)) == (c_ // (2 * b))) & ((s_ // b) % 2 == 0) & ((c_ // b) % 2 == 1)
        lev.append(m.astype(np.float32))
    out["levU"] = np.stack(lev)
    return out


def build_program(depth=L, stop=None):
    nc = bass.Bass("TRN2", target_bir_lowering=False)
    dt = nc.dram_tensor

    def din(name, shape, dtype=F32):
        return dt(name, list(shape), dtype, kind="ExternalInput").ap()

    xT_in = din("xT", [D, T])
    cT_in = din("cT", [128, 8])
    w_ada = din("w_ada", [depth, D, 6 * D])
    b_ada = din("b_ada", [depth, 6 * D])
    w_in = din("w_in", [depth, D, INC])
    w_out = din("w_out", [depth, D, D])
    w_fi = din("w_ffn_in", [depth, D, 2 * DFF])
    w_fo = din("w_ffn_out", [depth, DFF, D])
    gm_in = din("gm", [128, L * 8])
    gf_in = din("gf", [128, L * 8])
    gfin_in = din("gfin", [128, 8])
    caw_in = din("caw", [128, L * 2 * 3])
    cfw_in = din("cfw", [128, L * 2 * 31])
    cfb_in = din("cfb", [128, L * 2])
    cfg_in = din("cfg", [128, L * 2])
    cfbb_in = din("cfbb", [128, L * 2])
    dnw_in = din("dnw", [128, L * 12 * 4])
    alog_in = din("alog", [128, L * 128])
    dtb_in = din("dtb", [128, L * 128])
    gdn_in = din("gdn", [128, L * 128])
    cm_in = din("cmask", [128, 6 * 128])
    lev_in = din("levU4", [128, 7 * 512])
    levT_in = din("levL4", [128, 7 * 512])
    i4_in = din("ident4", [128, 512])
    su4_in = din("strictu4", [128, 512])
    outT = dt("outT", [D, T], F32, kind="ExternalOutput").ap()

    xs = dt("xs", [D, T], F32, kind="Internal").ap()
    ys = dt("ys", [D, T], BF16, kind="Internal").ap()
    dnin = dt("dnin", [128, NCH, 16, 128], BF16, kind="Internal").ap()

    import os as _os2
    s = Sched(nc, strict=(_os2.environ.get("MK_STRICT", "1") == "1"))
    top = ExitStack()

    _cnt = [0]

    def sb(es, name, shape, dtype=F32):
        _cnt[0] += 1
        return es.enter_context(nc.sbuf_tensor("%s_%d" % (name, _cnt[0]), list(shape), dtype))

    ps = [top.enter_context(nc.psum_tensor("ps%d" % i, [128, 512], F32)) for i in range(6)]
    ps6b = top.enter_context(nc.psum_tensor("ps6b", [128, 1024], BF16))
    ps7b = top.enter_context(nc.psum_tensor("ps7b", [128, 1024], BF16))
    bps = [Buf("ps%d" % i, excl=True) for i in range(6)]
    bps6, bps7 = Buf("ps6b", excl=True), Buf("ps7b", excl=True)

    modT = sb(top, "modT", [128, L, 48])
    A1 = sb(top, "A1", [128, L * 8]); A2 = sb(top, "A2", [128, L * 8])
    gm = sb(top, "gm_t", [128, L * 8]); gf = sb(top, "gf_t", [128, L * 8]); gfin = sb(top, "gfin_t", [128, 8])
    caw = sb(top, "caw_t", [128, L * 6]); cfw = sb(top, "cfw_t", [128, L * 62])
    cfb = sb(top, "cfb_t", [128, L * 2]); cfg = sb(top, "cfg_t", [128, L * 2]); cfbb = sb(top, "cfbb_t", [128, L * 2])
    dnw = sb(top, "dnw_t", [128, L * 48])
    alog = sb(top, "alog_t", [128, L * 128]); dtb = sb(top, "dtb_t", [128, L * 128]); gdn = sb(top, "gdn_t", [128, L * 128])
    cmF = sb(top, "cmF", [128, 6 * 128])
    identB = sb(top, "identB", [128, 128], BF16)
    onesB = sb(top, "onesB", [128, 128], BF16)
    ab_tok = sb(top, "ab_tok", [128, NCH * 8])
    bconst = Buf("const")
    bmod = Buf("mod")
    bab = Buf("abtok")
    identF = cmF[:, 0:128]; uincl = cmF[:, 128:256]; masksl = cmF[:, 256:384]
    negm = cmF[:, 384:512]; strictu = cmF[:, 512:640]; onesF = cmF[:, 640:768]

    for (t_, src) in ((gm, gm_in), (gf, gf_in), (gfin, gfin_in), (caw, caw_in), (cfw, cfw_in), (cfb, cfb_in),
                      (cfg, cfg_in), (cfbb, cfbb_in), (dnw, dnw_in), (alog, alog_in), (dtb, dtb_in),
                      (gdn, gdn_in), (cmF, cm_in)):
        s.dma("sp", t_[:], src, writes=[bconst])
    s.dma("pool", identB[:], cm_in[:, 0:128], writes=[bconst])
    s.dma("pool", onesB[:], cm_in[:, 640:768], writes=[bconst])
    bxs = Buf("xs")
    for j in range(8):
        s.dma("sp", xs[j * 128:(j + 1) * 128, :], xT_in[j * 128:(j + 1) * 128, :], writes=[bxs])

    with ExitStack() as es:
        cT = sb(es, "cT_t", [128, 8]); cact = sb(es, "cact", [128, 8])
        wt = [sb(es, "wada%d" % i, [128, 2048]) for i in range(6)]
        bwt = [Buf() for _ in range(6)]
        modrow = sb(es, "modrow", [1, 6 * D]); brow = sb(es, "brow", [1, 6 * D])
        one11 = sb(es, "one11", [1, 1])
        bc, bmr, bbr = Buf(), Buf(), Buf()
        s.dma("sp", cT[:], cT_in, writes=[bc])
        s.add("act", lambda e: e.activation(cact[:], cT[:], AF.Silu), reads=[bc], writes=[bc])
        s.add("dve", lambda e: e.memset(one11[:], 1.0), writes=[bc])
        it = 0
        for l in range(depth):
            s.dma("sp", brow[:], b_ada[l:l + 1, :], reads=[], writes=[bbr])
            for cg in range(3):
                for k in range(8):
                    w_ = wt[it % 6]; bw_ = bwt[it % 6]; it += 1
                    s.dma(("sp", "act", "pool")[it % 3], w_[:], w_ada[l, k * 128:(k + 1) * 128, cg * 2048:(cg + 1) * 2048],
                          writes=[bw_])
                    for i in range(4):
                        s.add("pe", lambda e, i=i, w_=w_, k=k: e.matmul(
                            ps[i][0:1, :], cact[:, k:k + 1], w_[:, i * 512:(i + 1) * 512],
                            start=(k == 0), stop=(k == 7)), reads=[bw_, bc], writes=[bps[i]])
                for i in range(4):
                    c0 = cg * 2048 + i * 512
                    s.add("dve", lambda e, i=i, c0=c0: e.tensor_tensor(
                        modrow[0:1, c0:c0 + 512], ps[i][0:1, :], brow[0:1, c0:c0 + 512], ALU.add),
                        reads=[bps[i], bbr], writes=[bmr])
            for j in range(48):
                s.add("pe", lambda e, j=j: e.matmul(ps[4][:, j:j + 1], modrow[0:1, j * 128:(j + 1) * 128],
                                                     one11[0:1, 0:1], start=True, stop=True),
                      reads=[bmr, bc], writes=[bps[4]])
            s.add("act", lambda e, l=l: e.activation(modT[:, l, :], ps[4][:, 0:48], AF.Copy),
                  reads=[bps[4]], writes=[bmod])
            s.add("dve", lambda e, l=l: e.scalar_tensor_tensor(
                A1[:, l * 8:(l + 1) * 8], modT[:, l, 8:16], 1.0, gm[:, l * 8:(l + 1) * 8], ALU.add, ALU.mult),
                reads=[bmod, bconst], writes=[bmod])
            s.add("dve", lambda e, l=l: e.scalar_tensor_tensor(
                A2[:, l * 8:(l + 1) * 8], modT[:, l, 32:40], 1.0, gf[:, l * 8:(l + 1) * 8], ALU.add, ALU.mult),
                reads=[bmod, bconst], writes=[bmod])
        s.barrier()
        s.flush()

    def B1(l, j): return modT[:, l, j:j + 1]
    def G1(l, j): return modT[:, l, 16 + j:17 + j]
    def B2(l, j): return modT[:, l, 24 + j:25 + j]
    def G2(l, j): return modT[:, l, 40 + j:41 + j]

    xs_v = xs.rearrange("(j p) t -> p j t", p=128)
    ys_v = ys.rearrange("(j p) t -> p j t", p=128)
    out_v = outT.rearrange("(j p) t -> p j t", p=128)

    def norm_group(es_tiles, l, g, Acol, Bcol, hdst, hbuf, xt, bxt, load=True, pn=5, extra_w=()):
        sq, bsq, rs, brs = es_tiles
        sl = slice(g * 512, (g + 1) * 512)
        if load:
            s.dma("sp" if g % 2 == 0 else "pool", xt[:], xs_v[:, :, sl], reads=[bxs], writes=[bxt] + list(extra_w))
        s.add("act", lambda e: e.activation(sq[:], xt[:], AF.Square), reads=[bxt], writes=[bsq])
        for j in range(8):
            s.add("pe", lambda e, j=j: e.matmul(ps[pn][:], onesB[:], sq[:, j, :], start=(j == 0), stop=(j == 7)),
                  reads=[bsq, bconst], writes=[bps[pn]])
        s.add("act", lambda e: e.activation(rs[:], ps[pn][:], AF.Sqrt, bias=EPS, scale=1.0 / D),
              reads=[bps[pn]], writes=[brs])
        s.add("dve", lambda e: e.reciprocal(rs[:], rs[:]), reads=[brs], writes=[brs])
        return sl

    def ck(name):
        if stop == name:
            s.muted = True
    ck('pro')
    for l in range(depth):
        with ExitStack() as es:
            hT = sb(es, "hT", [128, 8, T], BF16); bh = Buf("hT")
            with ExitStack() as es2:
                xt = [sb(es2, "xt%d" % i, [128, 8, 512]) for i in range(3)]; bxt = [Buf(), Buf(), Buf()]
                bxj = [[Buf() for _ in range(8)] for _ in range(3)]
                sqs = [sb(es2, "sq", [128, 8, 512], BF16) for _ in range(2)]; bsqs = [Buf(), Buf()]
                rss = [sb(es2, "rs", [128, 512]) for _ in range(2)]; brss = [Buf(), Buf()]
                def n1_stats(g):
                    x_ = xt[g % 3]; bx_ = bxt[g % 3]
                    sq = sqs[g % 2]; bsq = bsqs[g % 2]; rs = rss[g % 2]; brs = brss[g % 2]
                    norm_group((sq, bsq, rs, brs), l, g, None, None, None, None, x_, bx_, pn=4 + g % 2, extra_w=bxj[g % 3])

                def n1_apply(g):
                    x_ = xt[g % 3]; bx_ = bxt[g % 3]; rs = rss[g % 2]; brs = brss[g % 2]
                    sl = slice(g * 512, (g + 1) * 512)
                    for j in range(8):
                        s.add("dve", lambda e, j=j, x_=x_, rs=rs: e.scalar_tensor_tensor(
                            x_[:, j, :], x_[:, j, :], A1[:, l * 8 + j:l * 8 + j + 1], rs[:], ALU.mult, ALU.mult),
                            reads=[bx_, brs, bmod], writes=[bxj[g % 3][j]])
                        s.add("act", lambda e, j=j, x_=x_, sl=sl: e.activation(
                            hT[:, j, sl], x_[:, j, :], AF.Identity, bias=B1(l, j), scale=1.0),
                            reads=[bxj[g % 3][j], bmod], writes=[bh])

                n1_stats(0)
                for g in range(NG):
                    if g + 1 < NG:
                        n1_stats(g + 1)
                    n1_apply(g)
                s.barrier(); s.flush()
            ck('n1')

            wv = w_in[l].rearrange("(k p) c -> p k c", p=128)

            def proj(wtile, bw, g, pidx):
                for k in range(8):
                    s.add("pe", lambda e, k=k: e.matmul(ps[pidx][:], wtile[:, k, :], hT[:, k, g * 512:(g + 1) * 512],
                                                         start=(k == 0), stop=(k == 7)),
                          reads=[bw, bh], writes=[bps[pidx]])

            with ExitStack() as es2:
                wa = [sb(es2, "wa%d" % i, [128, 8, 128], BF16) for i in range(3)]; bwa = [Buf() for _ in range(3)]
                ppad = sb(es2, "ppad", [128, T + 2]); bpp = Buf()
                abt = sb(es2, "abt", [128, T]); bab_ = Buf()
                acc = sb(es2, "acc", [128, T]); bacc = Buf()
                ybf = sb(es2, "ybf", [128, T], BF16); bybf = Buf()
                tmpc = sb(es2, "tmpc", [128, 512]); btc = Buf()
                s.add("pool", lambda e: e.memset(ppad[:, 0:2], 0.0), writes=[bpp])
                for cc in range(2):
                    for i, base in enumerate((256, 512, 0)):
                        c0 = base + cc * 128
                        s.dma("pool", wa[i][:], wv[:, :, c0:c0 + 128], writes=[bwa[i]])
                    for g in range(NG):
                        sl = slice(g * 512, (g + 1) * 512)
                        proj(wa[0], bwa[0], g, 0)
                        proj(wa[1], bwa[1], g, 1)
                        proj(wa[2], bwa[2], g, 2)
                        s.add("act", lambda e: e.activation(tmpc[:], ps[0][:], AF.Copy), reads=[bps[0]], writes=[btc])
                        s.add("dve", lambda e, g=g: e.tensor_tensor(ppad[:, 2 + g * 512:2 + (g + 1) * 512], tmpc[:],
                                                                  ps[1][:], ALU.mult),
                              reads=[btc, bps[1]], writes=[bpp])
                        s.add("act", lambda e, sl=sl: e.activation(abt[:, sl], ps[2][:], AF.Copy),
                              reads=[bps[2]], writes=[bab_])
                    for sg in range(4):
                        o = sg * 1024
                        wc = lambda k: caw[:, l * 6 + cc * 3 + k:l * 6 + cc * 3 + k + 1]
                        s.add("dve", lambda e, o=o, w0=wc(0): e.tensor_scalar(
                            acc[:, o:o + 1024], ppad[:, o:o + 1024], w0, None, ALU.mult),
                            reads=[bpp, bconst], writes=[bacc])
                        for k in (1, 2):
                            s.add("dve", lambda e, o=o, k=k, wk=wc(k): e.scalar_tensor_tensor(
                                acc[:, o:o + 1024], ppad[:, o + k:o + k + 1024], wk, acc[:, o:o + 1024],
                                ALU.mult, ALU.add), reads=[bpp, bacc, bconst], writes=[bacc])
                        s.add("pool", lambda e, o=o: e.tensor_tensor(ybf[:, o:o + 1024], acc[:, o:o + 1024],
                                                                  abt[:, o:o + 1024], ALU.mult),
                              reads=[bacc, bab_], writes=[bybf])
                    s.dma("sp", ys[cc * 128:(cc + 1) * 128, :], ybf[:], reads=[bybf], writes=[])
                s.barrier(); s.flush()
            ck('mixA')

            with ExitStack() as es2:
                wb = [sb(es2, "wb%d" % i, [128, 8, 128], BF16) for i in range(2)]; bwb = [Buf(), Buf()]
                upad = sb(es2, "upad", [128, 2, T + 30], BF16); bup = Buf()
                dg = sb(es2, "dg", [128, 62, 128], BF16); bdg = Buf(); bdgk = [Buf() for _ in range(62)]
                v32 = sb(es2, "v32", [128, 2, T]); bv32g = [Buf() for _ in range(NG)]
                sgts = [sb(es2, "sgt", [128, 512]) for _ in range(2)]; bsgs = [Buf(), Buf()]
                sq2s = [sb(es2, "sq2", [128, 2, 512], BF16) for _ in range(2)]; bsq2s = [Buf(), Buf()]
                rs2s = [sb(es2, "rs2", [128, 512]) for _ in range(2)]; brs2s = [Buf(), Buf()]
                t2s = [sb(es2, "t2", [128, 512]) for _ in range(4)]; bt2s = [Buf() for _ in range(4)]
                yb2 = [sb(es2, "yb2_%d" % i, [128, 2, 512], BF16) for i in range(2)]; byb2 = [Buf(), Buf()]
                for cc in range(2):
                    s.add("pool", lambda e, cc=cc: e.memset(upad[:, cc, 0:30], 0.0), writes=[bup])
                    for k in range(31):
                        s.add("pool" if k % 2 == 0 else "dve", lambda e, cc=cc, k=k: e.tensor_scalar(
                            dg[:, cc * 31 + k, :], identB[:], cfw[:, l * 62 + cc * 31 + k:l * 62 + cc * 31 + k + 1],
                            None, ALU.mult), reads=[bconst], writes=[bdgk[cc * 31 + k]])
                for cc in range(2):
                    s.dma("pool", wb[0][:], wv[:, :, 768 + cc * 128:768 + (cc + 1) * 128], writes=[bwb[0]])
                    s.dma("pool", wb[1][:], wv[:, :, 1024 + cc * 128:1024 + (cc + 1) * 128], writes=[bwb[1]])
                    for g in range(NG):
                        proj(wb[0], bwb[0], g, 0)
                        proj(wb[1], bwb[1], g, 1)
                        sgt = sgts[g % 2]; bsg = bsgs[g % 2]
                        s.add("act", lambda e, sgt=sgt: e.activation(sgt[:], ps[1][:], AF.Sigmoid), reads=[bps[1]], writes=[bsg])
                        s.add("dve", lambda e, cc=cc, g=g, sgt=sgt: e.tensor_tensor(
                            upad[:, cc, 30 + g * 512:30 + (g + 1) * 512], sgt[:], ps[0][:], ALU.mult),
                            reads=[bsg, bps[0]], writes=[bup])
                for g in range(NG):
                    sl = slice(g * 512, (g + 1) * 512)
                    bv32 = bv32g[g]
                    for cc in range(2):
                        pi = 2 + cc
                        for k in range(31):
                            s.add("pe", lambda e, cc=cc, k=k, g=g, pi=pi: e.matmul(
                                ps[pi][:], dg[:, cc * 31 + k, :], upad[:, cc, g * 512 + k:g * 512 + k + 512],
                                start=(k == 0), stop=(k == 30)), reads=[bdgk[cc * 31 + k], bup], writes=[bps[pi]])
                        s.add("act", lambda e, cc=cc, sl=sl, pi=pi: e.activation(
                            v32[:, cc, sl], ps[pi][:], AF.Identity, bias=cfb[:, l * 2 + cc:l * 2 + cc + 1], scale=1.0),
                            reads=[bps[pi], bconst], writes=[bv32])
                    pm, pv = (4, 5) if g % 2 == 0 else (0, 1)
                    sq2 = sq2s[g % 2]; bsq2 = bsq2s[g % 2]; rs2 = rs2s[g % 2]; brs2 = brs2s[g % 2]
                    for cc in range(2):
                        s.add("pe", lambda e, cc=cc, sl=sl, pm=pm: e.matmul(ps[pm][:], onesF, v32[:, cc, sl],
                                                                  start=(cc == 0), stop=(cc == 1)),
                              reads=[bv32, bconst], writes=[bps[pm]])
                    for cc in range(2):
                        s.add("dve", lambda e, cc=cc, sl=sl, pm=pm: e.scalar_tensor_tensor(
                            v32[:, cc, sl], ps[pm][:], -1.0 / 256, v32[:, cc, sl], ALU.mult, ALU.add),
                            reads=[bps[pm], bv32], writes=[bv32])
                    s.add("act", lambda e, sl=sl, sq2=sq2: e.activation(sq2[:], v32[:, :, sl], AF.Square),
                          reads=[bv32], writes=[bsq2])
                    for cc in range(2):
                        s.add("pe", lambda e, cc=cc, pv=pv, sq2=sq2: e.matmul(ps[pv][:], onesB[:], sq2[:, cc, :],
                                                             start=(cc == 0), stop=(cc == 1)),
                              reads=[bsq2, bconst], writes=[bps[pv]])
                    s.add("act", lambda e, pv=pv, rs2=rs2: e.activation(rs2[:], ps[pv][:], AF.Sqrt, bias=1e-5, scale=1.0 / 256),
                          reads=[bps[pv]], writes=[brs2])
                    s.add("dve", lambda e, rs2=rs2: e.reciprocal(rs2[:], rs2[:]), reads=[brs2], writes=[brs2])
                    yb_ = yb2[g % 2]; by_ = byb2[g % 2]
                    for cc in range(2):
                        t2 = t2s[(g % 2) * 2 + cc]; bt2 = bt2s[(g % 2) * 2 + cc]
                        s.add("dve", lambda e, cc=cc, sl=sl, t2=t2, rs2=rs2: e.tensor_tensor(t2[:], v32[:, cc, sl], rs2[:], ALU.mult),
                              reads=[bv32, brs2], writes=[bt2])
                        s.add("act", lambda e, cc=cc, yb_=yb_, t2=t2: e.activation(
                            yb_[:, cc, :], t2[:], AF.Silu, bias=cfbb[:, l * 2 + cc:l * 2 + cc + 1],
                            scale=cfg[:, l * 2 + cc:l * 2 + cc + 1]), reads=[bt2, bconst], writes=[by_])
                    s.dma("sp", ys_v[:, 2:4, sl], yb_[:], reads=[by_], writes=[])
                s.barrier(); s.flush()
            ck('mixB')

            with ExitStack() as es2:
                wc_ = [sb(es2, "wc%d" % i, [128, 8, 128], BF16) for i in range(2)]; bwc = [Buf(), Buf()]
                cpads = [sb(es2, "cpad", [128, T + 3], BF16) for _ in range(2)]; bcps = [Buf(), Buf()]
                dg4s = [sb(es2, "dg4", [128, 4, 128], BF16) for _ in range(2)]; bdg4s = [Buf(), Buf()]
                silF = sb(es2, "silF", [128, T]); bsilg = [Buf() for _ in range(NG)]
                sqF = sb(es2, "sqF", [128, T], BF16); bsqg = [Buf() for _ in range(NG)]
                sils = [sb(es2, "sil", [128, 512]) for _ in range(2)]; bsils = [Buf(), Buf()]
                sq3s = [sb(es2, "sq3", [128, 512], BF16) for _ in range(2)]; bsq3s = [Buf(), Buf()]
                rs3s = [sb(es2, "rs3", [128, 512]) for _ in range(2)]; brs3s = [Buf(), Buf()]
                ob = [sb(es2, "ob%d" % i, [128, T], BF16) for i in range(2)]; bob = [Buf(), Buf()]
                wab = sb(es2, "wab", [128, 8, 8], BF16); bwab = Buf()
                for i_ in range(2):
                    s.add("pool", lambda e, i_=i_: e.memset(cpads[i_][:, 0:3], 0.0), writes=[bcps[i_]])
                jobs = []
                for h in range(4):
                    jobs.append(("k", 1792 + h * 128, 4 + h, 0 + h))
                    jobs.append(("q", 1280 + h * 128, 0 + h, 4 + h))
                    jobs.append(("v", 2304 + h * 128, 8 + h, 8 + h))
                    jobs.append(("z", 2816 + h * 128, None, 12 + h))
                for ji, (ty, c0, ci, dj) in enumerate(jobs):
                    w_ = wc_[ji % 2]; bw_ = bwc[ji % 2]
                    o_ = ob[ji % 2]; bo_ = bob[ji % 2]
                    s.dma("pool", w_[:], wv[:, :, c0:c0 + 128], writes=[bw_])
                    if ty == "z":
                        for g in range(NG):
                            sl = slice(g * 512, (g + 1) * 512)
                            proj(w_, bw_, g, g % 2)
                            s.add("act", lambda e, sl=sl, g=g, o_=o_: e.activation(o_[:, sl], ps[g % 2][:], AF.Silu),
                                  reads=[bps[g % 2]], writes=[bo_])
                    else:
                        cpad = cpads[ji % 2]; bcp = bcps[ji % 2]; dg4 = dg4s[ji % 2]; bdg4 = bdg4s[ji % 2]
                        for k in range(4):
                            s.add("pool", lambda e, k=k, ci=ci, dg4=dg4: e.tensor_scalar(
                                dg4[:, k, :], identB[:], dnw[:, l * 48 + ci * 4 + k:l * 48 + ci * 4 + k + 1],
                                None, ALU.mult), reads=[bconst], writes=[bdg4])
                        for g in range(NG):
                            proj(w_, bw_, g, g % 2)
                            s.add("act", lambda e, g=g, cpad=cpad: e.activation(cpad[:, 3 + g * 512:3 + (g + 1) * 512],
                                                                    ps[g % 2][:], AF.Copy),
                                  reads=[bps[g % 2]], writes=[bcp])
                        for g in range(NG):
                            sl = slice(g * 512, (g + 1) * 512)
                            pi = 2 + g % 2
                            for k in range(4):
                                s.add("pe", lambda e, k=k, g=g, pi=pi, dg4=dg4, cpad=cpad: e.matmul(
                                    ps[pi][:], dg4[:, k, :], cpad[:, g * 512 + k:g * 512 + k + 512],
                                    start=(k == 0), stop=(k == 3)), reads=[bdg4, bcp], writes=[bps[pi]])
                            if ty == "v":
                                s.add("act", lambda e, sl=sl, pi=pi, o_=o_: e.activation(o_[:, sl], ps[pi][:], AF.Silu),
                                      reads=[bps[pi]], writes=[bo_])
                            else:
                                s.add("act", lambda e, pi=pi, sl=sl: e.activation(silF[:, sl], ps[pi][:], AF.Silu),
                                      reads=[bps[pi]], writes=[bsilg[g]])
                                s.add("pool", lambda e, sl=sl: e.tensor_tensor(sqF[:, sl], silF[:, sl], silF[:, sl], ALU.mult),
                                      reads=[bsilg[g]], writes=[bsqg[g]])
                        if ty != "v":
                            for g in range(NG):
                                sl = slice(g * 512, (g + 1) * 512)
                                rs3 = rs3s[g % 2]; brs3 = brs3s[g % 2]; pq = 4 + g % 2
                                s.add("pe", lambda e, sl=sl, pq=pq: e.matmul(ps[pq][:], onesB[:], sqF[:, sl], start=True, stop=True),
                                      reads=[bsqg[g], bconst], writes=[bps[pq]])
                                s.add("act", lambda e, rs3=rs3, pq=pq: e.activation(rs3[:], ps[pq][:], AF.Sqrt, bias=EPS, scale=1.0),
                                      reads=[bps[pq]], writes=[brs3])
                                s.add("dve", lambda e, rs3=rs3: e.reciprocal(rs3[:], rs3[:]), reads=[brs3], writes=[brs3])
                                sc = 128.0 ** -0.5 if ty == "q" else 1.0
                                s.add("dve", lambda e, sl=sl, sc=sc, o_=o_, rs3=rs3: e.scalar_tensor_tensor(
                                    o_[:, sl], silF[:, sl], sc, rs3[:], ALU.mult, ALU.mult),
                                    reads=[bsilg[g], brs3], writes=[bo_])
                    s.dma("sp", dnin[:, :, dj, :], o_[:].rearrange("p (n t) -> p n t", t=128), reads=[bo_], writes=[])
                s.dma("pool", wab[:], wv[:, :, 3328:3336], writes=[bwab])
                for n in range(NCH):
                    for k in range(8):
                        s.add("pe", lambda e, n=n, k=k: e.matmul(
                            ps[5][:, n * 8:(n + 1) * 8], hT[:, k, n * 128:(n + 1) * 128], wab[:, k, :],
                            start=(k == 0), stop=(k == 7)), reads=[bwab, bh], writes=[bps[5]])
                s.add("act", lambda e: e.activation(ab_tok[:], ps[5][:, 0:NCH * 8], AF.Copy),
                      reads=[bps[5]], writes=[bab])
                s.barrier(); s.flush()

        ck('dnin')
        with ExitStack() as es:
            def t_(name, shape, dtype=F32):
                return sb(es, name, shape, dtype)
            mU4 = t_("mU4", [128, 7, 512], BF16); mL4 = t_("mL4", [128, 7, 512], BF16)
            i4 = t_("i4", [128, 512], BF16); su4 = t_("su4", [128, 512])
            bm = Buf("masks")
            s.dma("pool", mU4[:], lev_in.rearrange("p (a b) -> p a b", b=512), writes=[bm])
            s.dma("pool", mL4[:], levT_in.rearrange("p (a b) -> p a b", b=512), writes=[bm])
            s.dma("pool", i4[:], i4_in, writes=[bm])
            s.dma("sp", su4[:], su4_in, writes=[bm])
            beta = t_("beta", [128, NCH, 4]); gtok = t_("gtok", [128, NCH, 4])
            xsp = t_("xsp", [128, NCH, 4]); axs = t_("axs", [128, NCH, 4]); lg = t_("lg", [128, NCH, 4])
            nega = t_("nega", [128, 128])
            gam = t_("gam", [128, 128]); eg = t_("eg", [128, 128]); negeg = t_("negeg", [128, 128])
            decl = t_("decl", [128, 128]); eglast = t_("eglast", [128, 128])
            bsc = Buf("scal")
            abv = ab_tok[:].rearrange("p (n c) -> p n c", c=8)
            dtbv = dtb[:, l * 128:(l + 1) * 128].rearrange("p (n c) -> p n c", c=4)
            s.add("act", lambda e: e.activation(beta[:], abv[:, :, 4:8], AF.Sigmoid), reads=[bab], writes=[bsc])
            s.add("dve", lambda e: e.tensor_tensor(xsp[:], abv[:, :, 0:4], dtbv, ALU.add), reads=[bab, bconst], writes=[bsc])
            s.add("act", lambda e: e.activation(axs[:], xsp[:], AF.Abs), reads=[bsc], writes=[bsc])
            s.add("act", lambda e: e.activation(axs[:], axs[:], AF.Exp, scale=-1.0), reads=[bsc], writes=[bsc])
            s.add("act", lambda e: e.activation(lg[:], axs[:], AF.Ln, bias=1.0, scale=1.0), reads=[bsc], writes=[bsc])
            s.add("dve", lambda e: e.tensor_scalar(xsp[:], xsp[:], 0.0, None, ALU.max), reads=[bsc], writes=[bsc])
            s.add("dve", lambda e: e.tensor_tensor(xsp[:], xsp[:], lg[:], ALU.add), reads=[bsc], writes=[bsc])
            s.add("act", lambda e: e.activation(nega[:], alog[:, l * 128:(l + 1) * 128], AF.Exp), reads=[bconst], writes=[bsc])
            s.add("dve", lambda e: e.scalar_tensor_tensor(
                gtok[:].rearrange("p n c -> p (n c)"), xsp[:].rearrange("p n c -> p (n c)"), -1.0, nega[:],
                ALU.mult, ALU.mult), reads=[bsc], writes=[bsc])
            gflat = gtok[:].rearrange("p n c -> p (n c)")
            bflat = beta[:].rearrange("p n c -> p (n c)")
            s.add("pe", lambda e: e.matmul(ps[0][:, 0:128], uincl, gflat, start=True, stop=True),
                  reads=[bsc, bconst], writes=[bps[0]])
            s.add("pe", lambda e: e.matmul(ps[1][:, 0:128], onesF, gflat, start=True, stop=True),
                  reads=[bsc, bconst], writes=[bps[1]])
            s.add("act", lambda e: e.activation(gam[:], ps[0][:, 0:128], AF.Copy), reads=[bps[0]], writes=[bsc])
            s.add("act", lambda e: e.activation(eg[:], ps[0][:, 0:128], AF.Exp), reads=[bps[0]], writes=[bsc])
            s.add("dve", lambda e: e.tensor_scalar(negeg[:], eg[:], -1.0, None, ALU.mult), reads=[bsc], writes=[bsc])
            s.add("dve", lambda e: e.tensor_tensor(decl[:], ps[1][:, 0:128], gam[:], ALU.subtract),
                  reads=[bps[1], bsc], writes=[bsc])
            s.add("act", lambda e: e.activation(decl[:], decl[:], AF.Exp), reads=[bsc], writes=[bsc])
            s.add("act", lambda e: e.activation(eglast[:], ps[1][:, 0:128], AF.Exp), reads=[bps[1]], writes=[bsc])

            ck('D0')
            import os as _os
            NCH_RUN = int(_os.environ.get('DBG_NCH', NCH))
            NL = 3
            S32 = t_("S32", [128, 512]); bS = Buf()
            Sbf = t_("Sbf", [128, 512], BF16); bSb = Buf()
            s.add("dve", lambda e: e.memset(S32[:], 0.0), writes=[bS])
            s.add("pool", lambda e: e.memset(Sbf[:], 0.0), writes=[bSb])
            gd = gdn[:, l * 128:(l + 1) * 128]
            H = lambda a, h: a[:, h * 128:(h + 1) * 128]

            class Lane:
                pass
            lanes = []
            for li_ in range(NL):
                ln = Lane()
                ln.inp = t_("inp", [128, 16, 128], BF16); ln.binp = Buf()
                ln.Mt = t_("Mt", [128, 4, 128]); ln.bMt = Buf()
                ln.E = t_("E", [128, 512]); ln.bE = Buf()
                ln.Es = t_("Es", [128, 512]); ln.bEs = Buf()
                for nm in ("Pm", "Qm", "X", "Y", "R1", "QKm", "kdec", "rp", "vnew", "on", "yc"):
                    setattr(ln, nm, t_(nm, [128, 512], BF16)); setattr(ln, "b" + nm, Buf())
                ln.Qoff = t_("Qoff", [128, 6, 512], BF16); ln.bQoff = Buf()
                for nm in ("vtok", "o2s", "o_t"):
                    setattr(ln, nm, t_(nm, [128, 512])); setattr(ln, "b" + nm, Buf())
                ln.junk = t_("junk", [128, 128]); ln.bjunk = Buf()
                ln.ss = t_("ss", [128, 4]); ln.bss = Buf()
                ln.p = (ps[2 * li_], ps[2 * li_ + 1]); ln.bp = (bps[2 * li_], bps[2 * li_ + 1])
                if li_ % 2 == 0:
                    ln.pT = ps6b; ln.bpT = bps6
                else:
                    ln.pT = ps7b; ln.bpT = bps7
                lanes.append(ln)

            owner = {}

            def acq(me, banks):
                while any(owner.get(id(b_)) not in (None, me) for b_ in banks):
                    yield
                for b_ in banks:
                    owner[id(b_)] = me

            def rel(me, banks):
                for b_ in banks:
                    if owner.get(id(b_)) == me:
                        owner[id(b_)] = None

            def d1_gen(n, ln):
                ip = ln.inp; bip = ln.binp
                p0, p1 = ln.p; b0, b1 = ln.bp; pT = ln.pT; bT = ln.bpT
                col = lambda a, h: a[:, n * 4 + h:n * 4 + h + 1]
                s.dma("sp", ip[:], dnin[:, n, :, :], reads=[], writes=[bip])
                for h in range(4):
                    s.add("act", lambda e, h=h, g_=col(gflat, h): e.activation(ln.Mt[:, h, :], masksl, AF.Identity, bias=0.0, scale=g_),
                          reads=[bsc, bconst], writes=[ln.bMt])
                yield
                for h in range(4):
                    s.add("pe", lambda e, h=h: e.matmul(H(p0, h), ln.Mt[:, h, :], uincl, start=True, stop=False),
                          reads=[ln.bMt, bconst], writes=[b0])
                    s.add("pe", lambda e, h=h: e.matmul(H(p0, h), identF, negm, start=False, stop=True),
                          reads=[ln.bMt, bconst], writes=[b0])
                yield
                s.add("act", lambda e: e.activation(ln.E[:], p0[:], AF.Exp), reads=[b0], writes=[ln.bE])
                yield
                for h in range(4):
                    s.add("pe", lambda e, h=h: e.matmul(H(p0, h), ip[:, h, :], ip[:, h, :], start=True, stop=True),
                          reads=[bip], writes=[b0])
                for h in range(4):
                    s.add("pe", lambda e, h=h: e.matmul(H(p1, h), ip[:, h, :], ip[:, 4 + h, :], start=True, stop=True),
                          reads=[bip], writes=[b1])
                yield
                for h in range(4):
                    s.add("dve", lambda e, h=h, b_=col(bflat, h): e.scalar_tensor_tensor(
                        H(ln.Pm, h), H(p0, h), b_, H(ln.E, h), ALU.mult, ALU.mult),
                        reads=[b0, ln.bE, bsc], writes=[ln.bPm])
                s.add("dve", lambda e: e.tensor_tensor(ln.QKm[:], p1[:], ln.E[:], ALU.mult), reads=[b1, ln.bE], writes=[ln.bQKm])
                yield
                yield from acq(n, (bT,))
                for h in range(4):
                    s.add("pe", lambda e, h=h: e.transpose(H(pT, h), H(ln.Pm, h), identB[:]), reads=[ln.bPm, bconst], writes=[bT])
                s.add("pool", lambda e: e.tensor_tensor(ln.X[:], ln.Pm[:], mU4[:, 0, :], ALU.mult), reads=[ln.bPm, bm], writes=[ln.bX])
                s.add("pool", lambda e: e.tensor_tensor(ln.X[:], i4[:], ln.X[:], ALU.subtract), reads=[ln.bX, bm], writes=[ln.bX])
                yield
                s.add("act", lambda e: e.activation(ln.Qm[:], pT[:, 0:512], AF.Copy), reads=[bT], writes=[ln.bQm])
                rel(n, (bT,))
                yield
                s.add("pool", lambda e: e.tensor_tensor(ln.Y[:], ln.Qm[:], mL4[:, 0, :], ALU.mult), reads=[ln.bQm, bm], writes=[ln.bY])
                s.add("pool", lambda e: e.tensor_tensor(ln.Y[:], i4[:], ln.Y[:], ALU.subtract), reads=[ln.bY, bm], writes=[ln.bY])
                yield
                for li in range(1, 7):
                    last = (li == 6)
                    for h in range(4):
                        s.add("pe", lambda e, h=h, li=li: e.matmul(H(p0, h), H(ln.Qm, h), H(ln.X, h),
                                                                 start=True, stop=True), reads=[ln.bQm, ln.bX], writes=[b0])
                    yield
                    s.add("dve", lambda e, li=li: e.tensor_tensor(ln.R1[:], p0[:], mU4[:, li, :], ALU.mult),
                          reads=[b0, bm], writes=[ln.bR1])
                    yield
                    for h in range(4):
                        s.add("pe", lambda e, h=h: e.matmul(H(p0, h), H(ln.Y, h), H(ln.R1, h), start=True, stop=True),
                              reads=[ln.bY, ln.bR1], writes=[b0])
                    if not last:
                        for h in range(4):
                            s.add("pe", lambda e, h=h: e.matmul(H(p1, h), H(ln.R1, h), H(ln.Y, h), start=True, stop=True),
                                  reads=[ln.bY, ln.bR1], writes=[b1])
                    yield
                    s.add("dve", lambda e: e.tensor_tensor(ln.X[:], ln.X[:], p0[:], ALU.subtract), reads=[ln.bX, b0], writes=[ln.bX])
                    if not last:
                        s.add("dve", lambda e: e.tensor_tensor(ln.Y[:], ln.Y[:], p1[:], ALU.subtract),
                              reads=[ln.bY, b1], writes=[ln.bY])
                    yield

            def d2_gen(n, ln):
                ip = ln.inp; bip = ln.binp
                p0, p1 = ln.p; b0, b1 = ln.bp; pT = ln.pT; bT = ln.bpT
                col = lambda a, h: a[:, n * 4 + h:n * 4 + h + 1]
                yield from acq(n, (bT,))
                for h in range(4):
                    s.add("pe", lambda e, h=h: e.transpose(H(pT, h), ip[:, h, :], identB[:]), reads=[bip, bconst], writes=[bT])
                for h in range(4):
                    s.add("pe", lambda e, h=h: e.transpose(H(pT, 4 + h), ip[:, 8 + h, :], identB[:]), reads=[bip, bconst], writes=[bT])
                for h in range(4):
                    s.add("pe", lambda e, h=h: e.matmul(H(p0, h), ip[:, h, :], H(Sbf, h), start=True, stop=True),
                          reads=[bip, bSb], writes=[b0])
                for h in range(4):
                    s.add("pe", lambda e, h=h: e.matmul(H(p1, h), ip[:, 4 + h, :], H(Sbf, h), start=True, stop=True),
                          reads=[bip, bSb], writes=[b1])
                yield
                s.add("act", lambda e: e.activation(ln.vtok[:], pT[:, 512:1024], AF.Copy), reads=[bT], writes=[ln.bvtok])
                for h in range(4):
                    s.add("act", lambda e, h=h, d_=col(decl, h): e.activation(H(ln.kdec, h), H(pT, h), AF.Identity, bias=0.0, scale=d_),
                          reads=[bT, bsc], writes=[ln.bkdec])
                rel(n, (bT,))
                yield
                for h in range(4):
                    s.add("dve", lambda e, h=h, ne_=col(negeg, h): e.scalar_tensor_tensor(
                        H(ln.rp, h), H(p0, h), ne_, H(ln.vtok, h), ALU.mult, ALU.add),
                        reads=[b0, ln.bvtok, bsc], writes=[ln.brp])
                s.add("act", lambda e: e.activation(ln.o2s[:], p1[:], AF.Copy), reads=[b1], writes=[ln.bo2s])
                yield
                for h in range(4):
                    s.add("pe", lambda e, h=h: e.matmul(H(p0, h), H(ln.X, h), H(ln.rp, h), start=True, stop=True),
                          reads=[ln.bX, ln.brp], writes=[b0])
                yield
                for h in range(4):
                    s.add("act", lambda e, h=h, b_=col(bflat, h): e.activation(H(ln.vnew, h), H(p0, h), AF.Identity, bias=0.0, scale=b_),
                          reads=[b0, bsc], writes=[ln.bvnew])
                yield
                for h in range(4):
                    s.add("pe", lambda e, h=h: e.matmul(H(p0, h), H(ln.kdec, h), H(ln.vnew, h), start=True, stop=True),
                          reads=[ln.bkdec, ln.bvnew], writes=[b0])
                for h in range(4):
                    s.add("pe", lambda e, h=h: e.matmul(H(p1, h), H(ln.QKm, h), H(ln.vnew, h), start=True, stop=True),
                          reads=[ln.bQKm, ln.bvnew], writes=[b1])
                yield
                for h in range(4):
                    s.add("dve", lambda e, h=h, el_=col(eglast, h): e.scalar_tensor_tensor(
                        H(S32, h), H(S32, h), el_, H(p0, h), ALU.mult, ALU.add),
                        reads=[bS, b0, bsc], writes=[bS])
                yield
                s.add("act", lambda e: e.activation(Sbf[:], S32[:], AF.Copy), reads=[bS], writes=[bSb])
                for h in range(4):
                    s.add("dve", lambda e, h=h, eg_=col(eg, h): e.scalar_tensor_tensor(
                        H(ln.o_t, h), H(ln.o2s, h), eg_, H(p1, h), ALU.mult, ALU.add),
                        reads=[b1, ln.bo2s, bsc], writes=[ln.bo_t])
                s.add("pool", lambda e: e.memset(ln.ss[:], 0.0), writes=[ln.bss])
                yield

            def d3_gen(n, ln):
                ip = ln.inp; bip = ln.binp
                pT = ln.pT; bT = ln.bpT
                for h in range(4):
                    s.add("act", lambda e, h=h: e.activation(ln.junk[:], H(ln.o_t, h), AF.Square, accum_out=ln.ss[:, h:h + 1]),
                          reads=[ln.bo_t, ln.bss], writes=[ln.bjunk, ln.bss])
                yield
                s.add("act", lambda e: e.activation(ln.ss[:], ln.ss[:], AF.Sqrt, bias=EPS, scale=1.0 / 128), reads=[ln.bss], writes=[ln.bss])
                yield
                s.add("dve", lambda e: e.reciprocal(ln.ss[:], ln.ss[:]), reads=[ln.bss], writes=[ln.bss])
                yield
                for h in range(4):
                    s.add("dve", lambda e, h=h: e.scalar_tensor_tensor(H(ln.on, h), H(ln.o_t, h), ln.ss[:, h:h + 1], gd, ALU.mult, ALU.mult),
                          reads=[ln.bo_t, ln.bss, bconst], writes=[ln.bon])
                yield
                yield from acq(n, (bT,))
                for h in range(4):
                    s.add("pe", lambda e, h=h: e.transpose(H(pT, 4 + h), H(ln.on, h), identB[:]), reads=[ln.bon, bconst], writes=[bT])
                yield
                s.add("dve", lambda e: e.tensor_tensor(
                    ln.yc[:], pT[:, 512:1024], ip[:, 12:16, :].rearrange("p a b -> p (a b)"), ALU.mult),
                    reads=[bT, bip], writes=[ln.byc])
                rel(n, (bT,))
                s.dma("sp", ys_v[:, 4:8, n * 128:(n + 1) * 128], ln.yc[:].rearrange("p (a b) -> p a b", b=128),
                      reads=[ln.byc], writes=[])
                yield

            def chain(n, ln):
                yield from d1_gen(n, ln)
                while d2_turn[0] != n:
                    yield
                yield from d2_gen(n, ln)
                d2_turn[0] = n + 1
                yield from d3_gen(n, ln)

            d2_turn = [0]
            active = {}
            nxt = 0
            while nxt < NCH_RUN or active:
                while nxt < NCH_RUN and (nxt % NL) not in active:
                    active[nxt % NL] = chain(nxt, lanes[nxt % NL]); nxt += 1
                    break
                for k in sorted(active.keys(), key=lambda kk: kk):
                    try:
                        next(active[k])
                    except StopIteration:
                        del active[k]
            s.barrier(); s.flush()

        ck("mixy%d" % l)
        esF = ExitStack()
        wfi = sb(esF, "wfi", [128, 8, 2 * DFF], BF16); bwfi = Buf()
        wfiv = w_fi[l].rearrange("(k p) c -> p k c", p=128)
        with ExitStack() as es:
            wo = sb(es, "wo", [128, 8, D], BF16); bwo = Buf()
            wov = w_out[l].rearrange("(k p) c -> p k c", p=128)
            for k in range(8):
                s.dma("pool", wo[:, k, :], wov[:, k, :], writes=[bwo])
            wfi_jobs = [(k, hh) for k in range(8) for hh in range(2)]
            yt = [sb(es, "yt%d" % i, [128, 8, 512], BF16) for i in range(2)]; byt = [Buf(), Buf()]
            xt = [sb(es, "xo%d" % i, [128, 8, 512]) for i in range(2)]; bxt = [Buf(), Buf()]
            for g in range(NG):
                sl = slice(g * 512, (g + 1) * 512)
                y_ = yt[g % 2]; by_ = byt[g % 2]; x_ = xt[g % 2]; bx_ = bxt[g % 2]
                s.dma("pool", y_[:], ys_v[:, :, sl], writes=[by_])
                s.dma("sp", x_[:], xs_v[:, :, sl], writes=[bx_])
                for (k, hh) in wfi_jobs[2 * g:2 * g + 2]:
                    s.dma("pool", wfi[:, k, hh * DFF:(hh + 1) * DFF], wfiv[:, k, hh * DFF:(hh + 1) * DFF], writes=[bwfi])
                for oc in range(8):
                    pi = oc % 4
                    for k in range(8):
                        s.add("pe", lambda e, k=k, oc=oc, pi=pi, y_=y_: e.matmul(
                            ps[pi][:], wo[:, k, oc * 128:(oc + 1) * 128], y_[:, k, :], start=(k == 0), stop=(k == 7)),
                            reads=[bwo, by_], writes=[bps[pi]])
                    s.add("dve", lambda e, oc=oc, pi=pi, x_=x_: e.scalar_tensor_tensor(
                        x_[:, oc, :], ps[pi][:], G1(l, oc), x_[:, oc, :], ALU.mult, ALU.add),
                        reads=[bps[pi], bx_, bmod], writes=[bx_])
                s.dma("act", xs_v[:, :, sl], x_[:], reads=[bx_], writes=[])
            s.barrier(); s.flush()
        ck("mix%d" % l)

        with ExitStack() as es:
            wfo = sb(es, "wfo", [128, 22, D], BF16); bwfo = Buf()
            wfov = w_fo[l].rearrange("(k p) c -> p k c", p=128)
            for k in range(22):
                s.dma("pool", wfo[:, k, :], wfov[:, k, :], writes=[bwfo])
            GF = 256
            NGF = T // GF
            xts = [sb(es, "xf", [128, 8, GF]) for _ in range(2)]; bxts = [Buf(), Buf()]
            hT2s = [sb(es, "hT2", [128, 8, GF], BF16) for _ in range(2)]; bh2s = [Buf(), Buf()]
            sq = sb(es, "sqf", [128, 8, GF], BF16); bsq = Buf(); bsqj = [Buf() for _ in range(8)]
            rs = sb(es, "rsf", [128, GF]); brs = Buf()
            aTs = [sb(es, "aT", [128, 22, GF], BF16) for _ in range(2)]; baTs = [Buf(), Buf()]
            sgf = [sb(es, "sgf%d" % i, [128, GF]) for i in range(2)]; bsgf = [Buf(), Buf()]

            def f_norm(g):
                xt = xts[g % 2]; bxt = bxts[g % 2]; hT2 = hT2s[g % 2]; bh2 = bh2s[g % 2]
                sl = slice(g * GF, (g + 1) * GF)
                s.dma("sp", xt[:], xs_v[:, :, sl], reads=[bxs], writes=[bxt])
                s.add("act", lambda e: e.activation(sq[:], xt[:], AF.Square), reads=[bxt], writes=[bsq] + bsqj)
                for j in range(8):
                    s.add("pe", lambda e, j=j: e.matmul(ps[5][:, 0:GF], onesB[:], sq[:, j, :], start=(j == 0), stop=(j == 7)),
                          reads=[bsq, bconst], writes=[bps[5]])
                s.add("act", lambda e: e.activation(rs[:], ps[5][:, 0:GF], AF.Sqrt, bias=EPS, scale=1.0 / D),
                      reads=[bps[5]], writes=[brs])
                s.add("dve", lambda e: e.reciprocal(rs[:], rs[:]), reads=[brs], writes=[brs])
                for j in range(8):
                    s.add("dve", lambda e, j=j: e.scalar_tensor_tensor(
                        sq[:, j, :], xt[:, j, :], A2[:, l * 8 + j:l * 8 + j + 1], rs[:], ALU.mult, ALU.mult),
                        reads=[bxt, brs, bmod, bsq], writes=[bsqj[j]])
                    s.add("act", lambda e, j=j: e.activation(
                        hT2[:, j, :], sq[:, j, :], AF.Identity, bias=B2(l, j), scale=1.0),
                        reads=[bsqj[j], bmod], writes=[bh2])

            def f_in(g):
                hT2 = hT2s[g % 2]; bh2 = bh2s[g % 2]; aT = aTs[g % 2]; baT = baTs[g % 2]
                for j in range(22):
                    pg, pu = (0, 1) if j % 2 == 0 else (2, 3)
                    for k in range(8):
                        s.add("pe", lambda e, k=k, j=j, pg=pg: e.matmul(
                            ps[pg][:, 0:GF], wfi[:, k, j * 128:(j + 1) * 128], hT2[:, k, :], start=(k == 0), stop=(k == 7)),
                            reads=[bwfi, bh2], writes=[bps[pg]])
                    for k in range(8):
                        s.add("pe", lambda e, k=k, j=j, pu=pu: e.matmul(
                            ps[pu][:, 0:GF], wfi[:, k, DFF + j * 128:DFF + (j + 1) * 128], hT2[:, k, :],
                            start=(k == 0), stop=(k == 7)), reads=[bwfi, bh2], writes=[bps[pu]])
                    sg_ = sgf[j % 2]; bsg_ = bsgf[j % 2]
                    s.add("act", lambda e, pg=pg, sg_=sg_: e.activation(sg_[:], ps[pg][:, 0:GF], AF.Silu),
                          reads=[bps[pg]], writes=[bsg_])
                    s.add("dve", lambda e, j=j, pu=pu, sg_=sg_: e.tensor_tensor(aT[:, j, :], sg_[:], ps[pu][:, 0:GF], ALU.mult),
                          reads=[bsg_, bps[pu]], writes=[baT])

            def f_out(g):
                xt = xts[g % 2]; bxt = bxts[g % 2]; aT = aTs[g % 2]; baT = baTs[g % 2]
                sl = slice(g * GF, (g + 1) * GF)
                for oc in range(8):
                    pi = 4 if oc % 2 == 0 else (0, 2)[(oc // 2) % 2]
                    for k in range(22):
                        s.add("pe", lambda e, k=k, oc=oc, pi=pi: e.matmul(
                            ps[pi][:, 0:GF], wfo[:, k, oc * 128:(oc + 1) * 128], aT[:, k, :], start=(k == 0), stop=(k == 21)),
                            reads=[bwfo, baT], writes=[bps[pi]])
                    s.add("dve", lambda e, oc=oc, pi=pi: e.scalar_tensor_tensor(
                        xt[:, oc, :], ps[pi][:, 0:GF], G2(l, oc), xt[:, oc, :], ALU.mult, ALU.add),
                        reads=[bps[pi], bxt, bmod], writes=[bxt])
                s.dma("pool", xs_v[:, :, sl], xt[:], reads=[bxt], writes=[bxs])

            f_norm(0)
            for g in range(NGF):
                f_in(g)
                if g + 1 < NGF:
                    f_norm(g + 1)
                f_out(g)
            s.barrier(); s.flush()
        esF.close()
        ck("ffn%d" % l)

    s.muted = False
    dumpy = stop in ('mixA', 'mixB') or (stop is not None and stop.startswith('mixy'))
    with ExitStack() as es:
        xt = [sb(es, "xz%d" % i, [128, 8, 512]) for i in range(2)]; bxt = [Buf(), Buf()]
        sq = sb(es, "sqz", [128, 8, 512], BF16); bsq = Buf()
        rs = sb(es, "rsz", [128, 512]); brs = Buf()
        if dumpy:
            yt = [sb(es, "yz%d" % i, [128, 8, 512], BF16) for i in range(2)]; byt = [Buf(), Buf()]
        for g in range(NG):
            x_ = xt[g % 2]; bx_ = bxt[g % 2]
            if stop is None:
                sl = norm_group((sq, bsq, rs, brs), 0, g, None, None, None, None, x_, bx_)
                for j in range(8):
                    s.add("dve", lambda e, j=j, x_=x_: e.scalar_tensor_tensor(
                        x_[:, j, :], x_[:, j, :], gfin[:, j:j + 1], rs[:], ALU.mult, ALU.mult),
                        reads=[bx_, brs, bconst], writes=[bx_])
            elif dumpy:
                sl = slice(g * 512, (g + 1) * 512)
                y_ = yt[g % 2]; by_ = byt[g % 2]
                s.dma("sp", y_[:], ys_v[:, :, sl], writes=[by_])
                s.add("dve", lambda e, x_=x_, y_=y_: e.tensor_copy(x_[:], y_[:]), reads=[by_], writes=[bx_])
            else:
                sl = slice(g * 512, (g + 1) * 512)
                s.dma("sp", x_[:], xs_v[:, :, sl], writes=[bx_])
            s.dma("act" if g % 2 == 0 else "sp", out_v[:, :, sl], x_[:], reads=[bx_], writes=[])
        s.flush(final=True)
    top.close()
    s.close()
    nc._nops = s.nops
    return nc


def _pp(a):
    a = np.asarray(a, np.float32)
    lead = a.shape[:-1]
    n = a.shape[-1] // 128
    a = a.reshape(lead + (n, 128))
    a = np.moveaxis(a, -1, 0)
    return np.ascontiguousarray(a.reshape(128, -1))


def make_in_maps(inputs, depth=L, ncores=NCORES):
    f = lambda k: np.asarray(inputs[k], np.float32)
    m = _levels_masks()
    common = {
        "w_ada": f("w_ada")[:depth], "b_ada": f("b_ada")[:depth], "w_in": f("w_in")[:depth], "w_out": f("w_out")[:depth],
        "w_ffn_in": f("w_ffn_in")[:depth], "w_ffn_out": f("w_ffn_out")[:depth],
        "gm": _pp(f("norm_mix_g")), "gf": _pp(f("norm_ffn_g")), "gfin": _pp(f("final_norm_g")),
        "caw": _pp(np.transpose(f("conv_a_w"), (0, 2, 1)).reshape(L, 2, 128, 3).transpose(0, 1, 3, 2)),
        "cfw": _pp(np.transpose(f("conf_dw_w"), (0, 2, 1)).reshape(L, 2, 128, 31).transpose(0, 1, 3, 2)),
        "cfb": _pp(f("conf_dw_b")), "cfg": _pp(f("conf_ln_g")), "cfbb": _pp(f("conf_ln_b")),
        "dnw": _pp(np.transpose(f("dn_conv_w"), (0, 2, 1)).reshape(L, 12, 128, 4).transpose(0, 1, 3, 2)),
        "alog": np.ascontiguousarray(np.broadcast_to(np.tile(f("dn_a_log"), (1, NCH)).reshape(1, L * 128), (128, L * 128))),
        "dtb": np.ascontiguousarray(np.broadcast_to(np.tile(f("dn_dt_bias"), (1, NCH)).reshape(1, L * 128), (128, L * 128))),
        "gdn": np.ascontiguousarray(np.broadcast_to(f("dn_norm_g").reshape(1, L * 128), (128, L * 128))),
        "cmask": np.ascontiguousarray(np.concatenate(
            [m["ident"], m["uincl"], m["masksl"], m["negm"], m["strictu"], m["ones"]], axis=1)),
        "levU4": np.ascontiguousarray(np.concatenate([np.tile(m["levU"][i], (1, 4)) for i in range(7)], axis=1)),
        "levL4": np.ascontiguousarray(np.concatenate([np.tile(m["levU"][i].T, (1, 4)) for i in range(7)], axis=1)),
        "ident4": np.ascontiguousarray(np.tile(m["ident"], (1, 4))),
        "strictu4": np.ascontiguousarray(np.tile(m["strictu"], (1, 4))),
    }
    x = f("x"); c = f("c")
    maps = []
    for core in range(ncores):
        b = core % 4
        d = dict(common)
        d["xT"] = np.ascontiguousarray(x[b].T)
        d["cT"] = np.ascontiguousarray(c[b].reshape(8, 128).T)
        maps.append(d)
    return maps


_NC_CACHE = {}


def kernel(**inputs):
    if "nc" not in _NC_CACHE:
        _NC_CACHE["nc"] = build_program()
    nc = _NC_CACHE["nc"]
    in_maps = make_in_maps(inputs)
    res = run_bass_kernel_spmd(nc, in_maps, core_ids=list(range(NCORES)))
    out = np.stack([np.asarray(res.results[b]["outT"], np.float32).T for b in range(4)], axis=0)
    return np.ascontiguousarray(out)
```
